# Optimizing a Trainium2 kernel written in Bass

```python
import math
import jax, jax.numpy as jnp
from jax import lax
import numpy as np

D_MODEL = 1024
BATCH = 4
SEQ = 8192
DEPTH = 1
DEC_BATCH = 16
DEC_SEQ = 2048
PAST_LEN = 128

HEAD_DIM = 64
A_HEADS = 8
A_WIDTH = A_HEADS * HEAD_DIM
B_HEADS = 4
B_QK_WIDTH = 2 * B_HEADS * HEAD_DIM
B_WIDTH = B_HEADS * 2 * HEAD_DIM
D_MIX = A_WIDTH + B_WIDTH
IN_SPLITS = (A_WIDTH, A_WIDTH, A_WIDTH, A_WIDTH, B_QK_WIDTH, B_QK_WIDTH, B_WIDTH, B_WIDTH)
D_IN_PROJ = A_WIDTH * 4 + B_QK_WIDTH * 2 + B_WIDTH * 2
DILATED_PATTERNS = ((128, 1), (512, 4), (2048, 16))
BAND_BLOCK = 64
DENSE_Q_BLOCK = 128
ROPE_THETA = 500000.0
ROPE_DIM = HEAD_DIM // 4
NORM_EPS = 1e-6
MASK_VALUE = -1e30

kernel_name = "hybrid_dilated_diff_encoder"


def rmsnorm(x, g):
    xf = x.astype(jnp.float32)
    var = jnp.mean(xf * xf, axis=-1, keepdims=True)
    return (xf * lax.rsqrt(var + NORM_EPS) * g.astype(jnp.float32)).astype(x.dtype)


def rope_partial(x, pos):
    half = ROPE_DIM // 2
    inv = ROPE_THETA ** (-jnp.arange(half, dtype=jnp.float32) / half)
    ang = pos.astype(jnp.float32)[:, None] * inv[None, :]
    cos = jnp.cos(ang)[:, None, :]
    sin = jnp.sin(ang)[:, None, :]
    xr = x[..., :ROPE_DIM].astype(jnp.float32)
    x1, x2 = xr[..., :half], xr[..., half:]
    rot = jnp.concatenate([x1 * cos - x2 * sin, x2 * cos + x1 * sin], axis=-1)
    return jnp.concatenate([rot.astype(x.dtype), x[..., ROPE_DIM:]], axis=-1)


def dilated_window_attention(q, k, v, window, dil):
    bn, s, h, dh = q.shape
    half = window // (2 * dil)
    L = s // dil
    nb = -(-L // BAND_BLOCK)
    lp = nb * BAND_BLOCK
    kb_len = BAND_BLOCK + 2 * half

    def split(t):
        return t.reshape(bn, L, dil, h, dh).transpose(0, 2, 1, 3, 4)

    qs = jnp.pad(split(q), ((0, 0), (0, 0), (0, lp - L), (0, 0), (0, 0)))
    qs = qs.reshape(bn, dil, nb, BAND_BLOCK, h, dh)
    kpad = ((0, 0), (0, 0), (half, lp - L + half), (0, 0), (0, 0))
    idx = jnp.arange(nb)[:, None] * BAND_BLOCK + jnp.arange(kb_len)[None, :]
    ks = jnp.take(jnp.pad(split(k), kpad), idx, axis=2)
    vs = jnp.take(jnp.pad(split(v), kpad), idx, axis=2)

    sc = jnp.einsum('brnqhd,brnkhd->brnhqk', qs, ks).astype(jnp.float32) * (dh ** -0.5)
    qi = jnp.arange(nb)[:, None] * BAND_BLOCK + jnp.arange(BAND_BLOCK)[None, :]
    ki = idx - half
    rel = ki[:, None, :] - qi[:, :, None]
    valid = (jnp.abs(rel) <= half) & (ki[:, None, :] >= 0) & (ki[:, None, :] < L)
    sc = jnp.where(valid[:, None], sc, MASK_VALUE)
    lse = jax.nn.logsumexp(sc, axis=-1)
    p = jnp.exp(sc - lse[..., None])
    o = jnp.einsum('brnhqk,brnkhd->brnqhd', p.astype(v.dtype), vs)
    o = o.reshape(bn, dil, lp, h, dh)[:, :, :L].transpose(0, 2, 1, 3, 4).reshape(bn, s, h, dh)
    lse = lse.transpose(0, 1, 2, 4, 3).reshape(bn, dil, lp, h)[:, :, :L]
    lse = lse.transpose(0, 2, 1, 3).reshape(bn, s, h)
    return o, lse


def dilated_mixture(q, k, v):
    outs, lses = [], []
    for window, dil in DILATED_PATTERNS:
        o, lse = dilated_window_attention(q, k, v, window, dil)
        outs.append(o)
        lses.append(lse)
    w = jax.nn.softmax(jnp.stack(lses, axis=0), axis=0)
    o = jnp.sum(w[..., None] * jnp.stack(outs, axis=0).astype(jnp.float32), axis=0)
    return o.astype(q.dtype)


def diff_attention(q, k, v, lam, g_sub, lam_init):
    bn, s, h, _, dh = q.shape
    nq = s // DENSE_Q_BLOCK
    qb = q.reshape(bn, nq, DENSE_Q_BLOCK, h, 2, dh).transpose(1, 0, 2, 3, 4, 5)
    scale = dh ** -0.5

    def block(qblk):
        sc = jnp.einsum('bqhmd,bkhmd->bhmqk', qblk, k).astype(jnp.float32) * scale
        p = jax.nn.softmax(sc, axis=-1)
        a = p[:, :, 0] - lam * p[:, :, 1]
        return jnp.einsum('bhqk,bkhe->bqhe', a.astype(v.dtype), v)

    o = lax.map(block, qb)
    o = o.transpose(1, 0, 2, 3, 4).reshape(bn, s, h, 2 * dh)
    return rmsnorm(o, g_sub) * (1.0 - lam_init)


def encoder_layer(x, c, w_in, w_out, g_pre, g_post, w_ada, b_ada,
                  lam_q1, lam_k1, lam_q2, lam_k2, g_sub, lam_init):
    bn, s, _ = x.shape
    pos = jnp.arange(s)
    mod = jax.nn.silu(c) @ w_ada + b_ada
    shift, scale, gate = jnp.split(mod, 3, axis=-1)
    h = rmsnorm(x, g_pre) * (1.0 + scale[:, None, :]) + shift[:, None, :]
    proj = h @ w_in
    parts, off = [], 0
    for wdt in IN_SPLITS:
        parts.append(proj[..., off:off + wdt])
        off += wdt
    qa, ka, va, ga, qb, kb, vb, gb = parts

    qa = rope_partial(qa.reshape(bn, s, A_HEADS, HEAD_DIM), pos)
    ka = rope_partial(ka.reshape(bn, s, A_HEADS, HEAD_DIM), pos)
    va = va.reshape(bn, s, A_HEADS, HEAD_DIM)
    ya = dilated_mixture(qa, ka, va).reshape(bn, s, A_WIDTH) * jax.nn.silu(ga)

    qb = rope_partial(qb.reshape(bn, s, 2 * B_HEADS, HEAD_DIM), pos).reshape(bn, s, B_HEADS, 2, HEAD_DIM)
    kb = rope_partial(kb.reshape(bn, s, 2 * B_HEADS, HEAD_DIM), pos).reshape(bn, s, B_HEADS, 2, HEAD_DIM)
    vb = vb.reshape(bn, s, B_HEADS, 2 * HEAD_DIM)
    lam = (jnp.exp(jnp.sum(lam_q1.astype(jnp.float32) * lam_k1.astype(jnp.float32)))
           - jnp.exp(jnp.sum(lam_q2.astype(jnp.float32) * lam_k2.astype(jnp.float32))) + lam_init)
    yb = diff_attention(qb, kb, vb, lam, g_sub, lam_init).reshape(bn, s, B_WIDTH) * jax.nn.silu(gb)

    y = jnp.concatenate([ya, yb], axis=-1) @ w_out
    return x + gate[:, None, :] * rmsnorm(y, g_post)


def setup_inputs(seed: int = 0) -> dict:
    key = jax.random.key(seed)
    ks = jax.random.split(key, 16)
    f32 = jnp.float32
    return {
        "x_prompt": jax.random.normal(ks[0], (BATCH, SEQ, D_MODEL), f32),
        "x_sample": jax.random.normal(ks[1], (DEC_BATCH, DEC_SEQ, D_MODEL), f32),
        "c_prompt": jax.random.normal(ks[2], (BATCH, D_MODEL), f32),
        "c_sample": jax.random.normal(ks[3], (DEC_BATCH, D_MODEL), f32),
        "w_in": jax.random.normal(ks[4], (DEPTH, D_MODEL, D_IN_PROJ), f32) * D_MODEL ** -0.5,
        "w_out": jax.random.normal(ks[5], (DEPTH, D_MIX, D_MODEL), f32) * D_MIX ** -0.5,
        "g_pre": 1.0 + 0.02 * jax.random.normal(ks[6], (DEPTH, D_MODEL), f32),
        "g_post": 1.0 + 0.02 * jax.random.normal(ks[7], (DEPTH, D_MODEL), f32),
        "w_ada": jax.random.normal(ks[8], (DEPTH, D_MODEL, 3 * D_MODEL), f32) * (0.5 * D_MODEL ** -0.5),
        "b_ada": 0.01 * jax.random.normal(ks[9], (DEPTH, 3 * D_MODEL), f32),
        "lam_q1": 0.1 * jax.random.normal(ks[10], (DEPTH, HEAD_DIM), f32),
        "lam_k1": 0.1 * jax.random.normal(ks[11], (DEPTH, HEAD_DIM), f32),
        "lam_q2": 0.1 * jax.random.normal(ks[12], (DEPTH, HEAD_DIM), f32),
        "lam_k2": 0.1 * jax.random.normal(ks[13], (DEPTH, HEAD_DIM), f32),
        "g_sub": 1.0 + 0.02 * jax.random.normal(ks[14], (DEPTH, 2 * HEAD_DIM), f32),
    }


def reference(x_prompt, x_sample, c_prompt, c_sample, w_in, w_out, g_pre, g_post,
              w_ada, b_ada, lam_q1, lam_k1, lam_q2, lam_k2, g_sub):
    y_prompt = x_prompt
    y_sample = x_sample
    for l in range(DEPTH):
        lam_init = 0.8 - 0.6 * math.exp(-0.3 * l)
        y_prompt = encoder_layer(y_prompt, c_prompt, w_in[l], w_out[l], g_pre[l], g_post[l],
                                 w_ada[l], b_ada[l], lam_q1[l], lam_k1[l], lam_q2[l], lam_k2[l],
                                 g_sub[l], lam_init)
        y_sample = encoder_layer(y_sample, c_sample, w_in[l], w_out[l], g_pre[l], g_post[l],
                                 w_ada[l], b_ada[l], lam_q1[l], lam_k1[l], lam_q2[l], lam_k2[l],
                                 g_sub[l], lam_init)
    return (y_prompt, y_sample)
```

```python
import contextlib
import numpy as np
import concourse.bass as bass
import concourse.mybir as mybir
from concourse.bass_utils import run_bass_kernel_spmd

F32 = mybir.dt.float32
BF16 = mybir.dt.bfloat16
AF = mybir.ActivationFunctionType
ALU = mybir.AluOpType

PE, ACT, DVE, POOL, SP = "tensor", "scalar", "vector", "gpsimd", "sync"
ENGINES = (PE, ACT, DVE, POOL, SP)

D = 1024
SEQ = 8192
DSEQ = 2048
HALF = 4096
PADK = 1024
PATTERNS = ((128, 1), (512, 4), (2048, 16))
EPS = 1e-6
LAM_INIT = 0.2
NROWS = HALF + HALF + 2 * PADK + 2 * DSEQ
ROW_OWN, ROW_OTHER, ROW_HALO, ROW_S0, ROW_S1 = 0, 4096, 8192, 10240, 12288
NOUT = HALF + 2 * DSEQ
WCOLS = 4096 + 4 * 192


class Buf:
    __slots__ = ("last_write", "reads")

    def __init__(self):
        self.last_write = None
        self.reads = []


class Rec:
    __slots__ = ("eng", "fn", "deps", "is_dma", "signal", "count", "dsem", "dval", "epoch")

    def __init__(self, eng, fn, is_dma):
        self.eng, self.fn, self.is_dma = eng, fn, is_dma
        self.deps = []
        self.signal = False
        self.count = 0
        self.dsem = None
        self.dval = 0
        self.epoch = 0


class Sched:
    EPOCH_MAX = 30000

    def __init__(self, nc, n_dma_sems=20):
        self.nc = nc
        self.q = {e: [] for e in ENGINES}
        self.n_dma_sems = n_dma_sems
        self.dma_rr = {e: 0 for e in ENGINES}
        self.dma_last = {}
        self.last_compute = {}

    def op(self, eng, meth, kw, reads=(), writes=(), dma=False, extra=()):
        r = Rec(eng, (meth, kw), dma)
        deps = list(extra)
        for b in reads:
            if b.last_write is not None:
                deps.append(b.last_write)
        for b in writes:
            if b.last_write is not None:
                deps.append(b.last_write)
            deps.extend(b.reads)
        if dma:
            slot = self.dma_rr[eng] % self.n_dma_sems
            self.dma_rr[eng] += 1
            prev = self.dma_last.get((eng, slot))
            if prev is not None:
                deps.append(prev)
            self.dma_last[(eng, slot)] = r
            r.dsem = (eng, slot)
        seen = set()
        for d in deps:
            if d is r or id(d) in seen:
                continue
            if (not d.is_dma) and (not dma) and d.eng == PE and eng == PE:
                continue
            seen.add(id(d))
            r.deps.append(d)
        for b in reads:
            b.reads.append(r)
        for b in writes:
            b.last_write = r
            b.reads = []
        self.q[eng].append(r)
        if not dma:
            self.last_compute[eng] = r
        return r

    def dma(self, meth, kw, reads=(), writes=(), eng=SP):
        return self.op(eng, meth, kw, reads, writes, dma=True)

    def barrier(self):
        deps = list(self.last_compute.values()) + list(self.dma_last.values())
        for e in ENGINES:
            r = Rec(e, None, False)
            for d in deps:
                if (not d.is_dma) and d.eng == e:
                    continue
                r.deps.append(d)
            self.q[e].append(r)

    def run(self, final_wait=()):
        nc = self.nc
        for e in ENGINES:
            for r in self.q[e]:
                for d in r.deps:
                    if not d.is_dma:
                        d.signal = True
        n_epochs = {}
        for e in ENGINES:
            cnt, ep = 0, 0
            for r in self.q[e]:
                if r.is_dma or r.fn is None:
                    continue
                if r.signal:
                    if cnt >= self.EPOCH_MAX:
                        ep += 1
                        cnt = 0
                    cnt += 1
                    r.count = cnt
                    r.epoch = ep
            n_epochs[e] = ep + 1
        dcount = {}
        for e in ENGINES:
            for r in self.q[e]:
                if r.is_dma:
                    dcount[r.dsem] = dcount.get(r.dsem, 0) + 16
                    r.dval = dcount[r.dsem]
        with contextlib.ExitStack() as st:
            csem = {}
            for e in ENGINES:
                for ep in range(n_epochs[e]):
                    if any((not r.is_dma) and r.signal and r.epoch == ep for r in self.q[e]):
                        csem[(e, ep)] = st.enter_context(nc.semaphore(f"c_{e}_{ep}"))
            dsem = {}
            for key in dcount:
                dsem[key] = st.enter_context(nc.semaphore(f"d_{key[0]}_{key[1]}"))
            block = st.enter_context(nc.Block())

            def emit(e, engine):
                waited = {}

                def do_wait(d):
                    if d.is_dma:
                        s, v, k = dsem[d.dsem], d.dval, ("d",) + d.dsem
                    else:
                        s, v, k = csem[(d.eng, d.epoch)], d.count, ("c", d.eng, d.epoch)
                    if waited.get(k, 0) >= v:
                        return
                    waited[k] = v
                    engine.wait_ge(s, v)

                for r in self.q[e]:
                    for d in r.deps:
                        do_wait(d)
                    if r.fn is None:
                        continue
                    ins = getattr(engine, r.fn[0])(**r.fn[1])
                    if r.is_dma:
                        ins.then_inc(dsem[r.dsem], 16)
                    elif r.signal:
                        ins.then_inc(csem[(e, r.epoch)], 1)
                if e == SP:
                    for d in final_wait:
                        do_wait(d)

            @block.tensor
            def _(eng):
                emit(PE, eng)

            @block.scalar
            def _(eng):
                emit(ACT, eng)

            @block.vector
            def _(eng):
                emit(DVE, eng)

            @block.gpsimd
            def _(eng):
                emit(POOL, eng)

            @block.sync
            def _(eng):
                emit(SP, eng)


class Ring:
    def __init__(self, tiles):
        self.tiles = tiles
        self.bufs = [Buf() for _ in tiles]
        self.i = 0

    def next(self):
        k = self.i % len(self.tiles)
        self.i += 1
        return self.tiles[k], self.bufs[k]


def amask_tiles(sq):
    out = []
    for pi, (w, dil) in enumerate(PATTERNS):
        nblk = sq // dil // 128
        je0 = PADK // dil - 64
        for r in range(dil):
            for t in range(nblk + 1):
                out.append((pi, dil, r, t, nblk, je0))
    return out


def build_program(stop_after=None, jobs=(0, 1, 2)):
    nc = bass.Bass("TRN2", target_bir_lowering=False)
    lvl = {'p0': 0, 'w': 1, 'p1': 2, '2b': 3, '2a': 4, None: 5}[stop_after]

    def din(name, shape, dt=F32):
        return nc.dram_tensor(name, list(shape), dt, kind="ExternalInput").ap()

    X = din("x_all", [NROWS, D])
    CT = din("rope_c", [128, NROWS])
    ST = din("rope_s", [128, NROWS])
    CTT = din("c_t", [128, 24])
    W_IN = din("w_in", [D, 4096])
    W_ROT = din("w_rot", [D, 768])
    W_OUT = din("w_out", [D, D])
    W_ADA = din("w_ada", [D, 3 * D])
    GPRE = din("g_pre_t", [128, 8])
    BADA = din("b_ada_t", [128, 16])
    BGATE = din("b_gate", [1, D])
    GPOST = din("g_post", [1, D])
    GSUB = din("g_sub_t", [128, 1])
    LAMV = din("lam_v", [4, 64])
    IDENT = din("ident", [128, 128])
    BAND = din("band", [128, 512])
    SELS = din("sels", [65, 512])
    AM0 = din("amask0", [128, 117])
    AM1 = din("amask1", [128, 69])
    Y = nc.dram_tensor("y_all", [NOUT, D], F32, kind="ExternalOutput").ap()

    JOBS = [
        dict(s=0, sq=HALF, skv=SEQ, sext=HALF + 2 * PADK, xrow=ROW_OWN, yrow=0, am=AM0),
        dict(s=1, sq=DSEQ, skv=DSEQ, sext=DSEQ + 2 * PADK, xrow=ROW_S0, yrow=HALF, am=AM1),
        dict(s=2, sq=DSEQ, skv=DSEQ, sext=DSEQ + 2 * PADK, xrow=ROW_S1, yrow=HALF + DSEQ, am=AM1),
    ]
    def scr(name, rows, cols):
        return nc.dram_tensor(name, [rows, cols], BF16, kind="Internal").ap()

    SC = dict(
        QA=scr("s_qa", 512, HALF), QAr=scr("s_qar", 128, HALF), GA=scr("s_ga", 512, HALF),
        KA=scr("s_ka", 512, HALF + 2 * PADK), KAr=scr("s_kar", 128, HALF + 2 * PADK),
        VA=scr("s_va", 512, HALF + 2 * PADK),
        QB=scr("s_qb", 512, HALF), QBr=scr("s_qbr", 128, HALF), GB=scr("s_gb", 512, HALF),
        KB=scr("s_kb", 512, SEQ), KBr=scr("s_kbr", 128, SEQ), VB=scr("s_vb", 512, SEQ),
        YM=scr("s_ym", 1024, HALF),
    )
    SCB = {k: Buf() for k in SC}

    S = Sched(nc)
    out_dmas = []
    top = contextlib.ExitStack()
    with top:
        def sb(st, name, shape, dt):
            return st.enter_context(nc.sbuf_tensor("sb_" + name, list(shape), dt))

        def ps(st, name, shape, dt):
            return st.enter_context(nc.psum_tensor("ps_" + name, list(shape), dt))

        uid = [0]

        def nm(p):
            uid[0] += 1
            return f"{p}{uid[0]}"

        ident_f = sb(top, "ident_f", [128, 128], F32)
        ident = sb(top, "ident", [128, 128], BF16)
        band_f = sb(top, "band_f", [128, 512], F32)
        band2 = sb(top, "band2", [128, 2, 256], BF16)
        sels = sb(top, "sels", [65, 4, 128], F32)
        am0 = sb(top, "am0", [128, 117], F32)
        am1 = sb(top, "am1", [128, 69], F32)
        ones_f = sb(top, "ones_f", [128, 128], F32)
        mhalf = sb(top, "mhalf", [128, 1], F32)
        zer = sb(top, "zer", [1, 512], BF16)
        gsub2 = sb(top, "gsub2", [128, 1], F32)
        neglam = sb(top, "neglam", [128, 1], F32)
        lamt = sb(top, "lamt", [128, 4, 64], F32)
        lamj = sb(top, "lamj", [128, 64], F32)
        lams = sb(top, "lams", [128, 4], F32)
        gg = [sb(top, f"gg{s}", [128, D], F32) for s in range(3)]
        gsT = sb(top, "gsT", [128, 8, 4], F32)
        shT = sb(top, "shT", [128, 8, 4], F32)
        wout = sb(top, "wout", [128, 8, D], BF16)
        bconst = Buf()
        bgg = [Buf() for _ in range(3)]
        bmod = Buf()
        bwout = Buf()

        S.dma("dma_start", dict(out=ident_f[:], in_=IDENT[:, :]), writes=[bconst])
        S.dma("dma_start", dict(out=band_f[:], in_=BAND[:, :]), writes=[bconst])
        S.dma("dma_start", dict(out=sels[:].rearrange("p a b -> p (a b)"), in_=SELS[:, :]), writes=[bconst])
        S.dma("dma_start", dict(out=am0[:], in_=AM0[:, :]), writes=[bconst])
        S.dma("dma_start", dict(out=am1[:], in_=AM1[:, :]), writes=[bconst])
        S.dma("dma_start", dict(out=gsub2[:], in_=GSUB[:, :]), writes=[bconst])
        for i in range(4):
            S.dma("dma_start", dict(out=lamt[:, i, :], in_=LAMV[i:i + 1, :].broadcast_to([128, 64])), writes=[bconst])
        S.op(DVE, "tensor_copy", dict(out=ident[:], in_=ident_f[:]), reads=[bconst], writes=[bconst])
        S.op(DVE, "tensor_copy", dict(out=band2[:].rearrange("p a b -> p (a b)"), in_=band_f[:]), reads=[bconst], writes=[bconst])
        S.op(POOL, "memset", dict(ap=ones_f[:], constant=1.0), writes=[bconst])
        S.op(POOL, "memset", dict(ap=mhalf[:], constant=-0.5), writes=[bconst])
        S.op(POOL, "memset", dict(ap=zer[:], constant=0.0), writes=[bconst])
        S.op(DVE, "tensor_scalar", dict(out=gsub2[:], in0=gsub2[:], scalar1=(1.0 - LAM_INIT) * 0.5, scalar2=None, op0=ALU.mult), reads=[bconst], writes=[bconst])
        for i in range(2):
            S.op(DVE, "tensor_tensor", dict(out=lamj[:], in0=lamt[:, 2 * i, :], in1=lamt[:, 2 * i + 1, :], op=ALU.mult), reads=[bconst], writes=[bconst])
            S.op(ACT, "activation", dict(out=lamj[:], in_=lamj[:], func=AF.Identity, accum_out=lams[:, i:i + 1]), reads=[bconst], writes=[bconst])
        S.op(ACT, "activation", dict(out=lams[:, 2:4], in_=lams[:, 0:2], func=AF.Exp), reads=[bconst], writes=[bconst])
        S.op(DVE, "tensor_tensor", dict(out=neglam[:], in0=lams[:, 3:4], in1=lams[:, 2:3], op=ALU.subtract), reads=[bconst], writes=[bconst])
        S.op(DVE, "tensor_scalar", dict(out=neglam[:], in0=neglam[:], scalar1=-LAM_INIT, scalar2=None, op0=ALU.add), reads=[bconst], writes=[bconst])

        with contextlib.ExitStack() as p0:
            stg = [sb(p0, f"stg0_{i}", [128, 8, 512], F32) for i in range(2)]
            bstg = [Buf() for _ in range(2)]
            ct = sb(p0, "ct", [128, 24], F32)
            th = sb(p0, "th0", [128, 24], F32)
            scT = sb(p0, "scT", [128, 8, 4], F32)
            screp = sb(p0, "screp", [128, 24, 128], F32)
            gpre = sb(p0, "gpre", [128, 8], F32)
            bada = sb(p0, "bada", [128, 16], F32)
            bgate = sb(p0, "bgate", [128, D], F32)
            gpost = sb(p0, "gpost", [128, D], F32)
            tmpm = sb(p0, "tmpm", [128, 8], F32)
            tmpg = sb(p0, "tmpg", [128, 512], F32)
            pmod = ps(p0, "pmod", [128, 16, 4], F32)
            pg = [ps(p0, f"pg{i}", [128, 512], F32) for i in range(3)]
            b0 = Buf()
            bpm = Buf()
            bpg = [Buf() for _ in range(3)]
            S.dma("dma_start", dict(out=ct[:], in_=CTT[:, :]), writes=[b0])
            S.dma("dma_start", dict(out=gpre[:], in_=GPRE[:, :]), writes=[b0])
            S.dma("dma_start", dict(out=bada[:], in_=BADA[:, :]), writes=[b0])
            S.dma("dma_start", dict(out=bgate[:], in_=BGATE[0:1, :].broadcast_to([128, D])), writes=[b0])
            S.dma("dma_start", dict(out=gpost[:], in_=GPOST[0:1, :].broadcast_to([128, D])), writes=[b0])
            S.op(ACT, "activation", dict(out=th[:], in_=ct[:], func=AF.Tanh, scale=0.5), reads=[b0], writes=[b0])
            S.op(DVE, "scalar_tensor_tensor", dict(out=th[:], in0=th[:], scalar=1.0, in1=ct[:], op0=ALU.add, op1=ALU.mult), reads=[b0], writes=[b0])
            S.op(POOL, "memset", dict(ap=scT[:], constant=0.0), writes=[b0])
            S.op(POOL, "memset", dict(ap=shT[:], constant=0.0), writes=[bmod])
            S.op(POOL, "memset", dict(ap=gsT[:], constant=0.0), writes=[bmod])
            S.op(DVE, "tensor_scalar", dict(out=scT[:, :, 0:3], in0=th[:].rearrange("p (a b) -> p a b", b=3), scalar1=0.5, scalar2=None, op0=ALU.mult), reads=[b0], writes=[b0])
            for i in range(24):
                kc, s = divmod(i, 3)
                S.op(DVE, "tensor_scalar", dict(out=screp[:, i, :], in0=ones_f[:], scalar1=scT[:, kc, s:s + 1], scalar2=None, op0=ALU.mult), reads=[b0, bconst], writes=[b0])
            for pc in range(6):
                k = pc % 2
                S.dma("dma_start", dict(out=stg[k][:], in_=W_ADA[:, pc * 512:(pc + 1) * 512].rearrange("(kc p) n -> p kc n", p=128)), writes=[bstg[k]])
                if pc < 4:
                    for fb in range(4):
                        blk = pc * 4 + fb
                        for kc in range(8):
                            S.op(PE, "matmul", dict(out=pmod[:, blk, :], lhsT=stg[k][:, kc, fb * 128:(fb + 1) * 128], rhs=scT[:, kc, :], start=(kc == 0), stop=(kc == 7)), reads=[bstg[k], b0], writes=[bpm])
                else:
                    hf = pc - 4
                    for s in range(3):
                        for h2 in range(2):
                            for kc in range(8):
                                S.op(PE, "matmul", dict(out=pg[s][:, h2 * 256:(h2 + 1) * 256], lhsT=screp[:, kc * 3 + s, :], rhs=stg[k][:, kc, h2 * 256:(h2 + 1) * 256], start=(kc == 0), stop=(kc == 7)), reads=[bstg[k], b0], writes=[bpg[s]])
                        S.op(DVE, "tensor_tensor", dict(out=tmpg[:], in0=pg[s][:], in1=bgate[:, hf * 512:(hf + 1) * 512], op=ALU.add), reads=[bpg[s], b0], writes=[b0])
                        S.op(DVE, "tensor_tensor", dict(out=gg[s][:, hf * 512:(hf + 1) * 512], in0=tmpg[:], in1=gpost[:, hf * 512:(hf + 1) * 512], op=ALU.mult), reads=[b0], writes=[bgg[s]])
            for s in range(3):
                S.op(DVE, "tensor_tensor", dict(out=shT[:, :, s], in0=pmod[:, 0:8, s], in1=bada[:, 0:8], op=ALU.add), reads=[bpm, b0], writes=[bmod])
                S.op(DVE, "scalar_tensor_tensor", dict(out=tmpm[:], in0=pmod[:, 8:16, s], scalar=1.0, in1=bada[:, 8:16], op0=ALU.add, op1=ALU.add), reads=[bpm, b0], writes=[b0])
                S.op(DVE, "tensor_tensor", dict(out=gsT[:, :, s], in0=tmpm[:], in1=gpre[:], op=ALU.mult), reads=[b0], writes=[bmod])
            for pc in range(2):
                k = pc % 2
                S.dma("dma_start", dict(out=stg[k][:], in_=W_OUT[:, pc * 512:(pc + 1) * 512].rearrange("(kc p) n -> p kc n", p=128)), writes=[bstg[k]])
                S.op(DVE, "tensor_copy", dict(out=wout[:, :, pc * 512:(pc + 1) * 512], in_=stg[k][:]), reads=[bstg[k]], writes=[bwout])
            S.barrier()

        for job in [JOBS[j_] for j_ in jobs] if stop_after != 'p0' else []:
            s, sq, skv, sext = job["s"], job["sq"], job["skv"], job["sext"]
            am = am0 if job["am"] is AM0 else am1
            with contextlib.ExitStack() as p1:
                wp = sb(p1, nm("wp"), [128, 8, WCOLS], BF16)
                biasT = sb(p1, nm("biasT"), [128, 40], F32)
                bwp = Buf()
                bbias = Buf()
                with contextlib.ExitStack() as pw:
                    stg = [sb(pw, nm("stgw"), [128, 8, 512], F32) for _ in range(2)]
                    bstg = [Buf() for _ in range(2)]
                    pb = ps(pw, nm("pb"), [128, 40, 4], F32)
                    bpb = Buf()
                    pieces = [(W_IN, pc * 512, 512, pc * 512) for pc in range(8)] + [(W_ROT, 0, 384, 4096), (W_ROT, 384, 384, 4096 + 384)]
                    for pi_, (src, c0, ncol, dcol) in enumerate(pieces):
                        k = pi_ % 2
                        S.dma("dma_start", dict(out=stg[k][:, :, 0:ncol], in_=src[:, c0:c0 + ncol].rearrange("(kc p) n -> p kc n", p=128)), writes=[bstg[k]])
                        for kc in range(8):
                            eng = DVE if kc % 2 == 0 else POOL
                            S.op(eng, "tensor_scalar", dict(out=wp[:, kc, dcol:dcol + ncol], in0=stg[k][:, kc, 0:ncol], scalar1=gsT[:, kc, s:s + 1], scalar2=None, op0=ALU.mult), reads=[bstg[k], bmod], writes=[bwp])
                        if pi_ < 8:
                            cols = [(fb * 128, pi_ * 4 + fb) for fb in range(4)]
                        else:
                            t0 = (pi_ - 8) * 2
                            cols = [(0, 32 + 2 * t0), (64, 33 + 2 * t0), (192, 34 + 2 * t0), (256, 35 + 2 * t0)]
                        for (co, bcol) in cols:
                            for kc in range(8):
                                S.op(PE, "matmul", dict(out=pb[:, bcol, :], lhsT=stg[k][:, kc, co:co + 128], rhs=shT[:, kc, :], start=(kc == 0), stop=(kc == 7)), reads=[bstg[k], bmod], writes=[bpb])
                    S.op(DVE, "tensor_copy", dict(out=biasT[:], in_=pb[:, :, s]), reads=[bpb], writes=[bbias])
                    S.barrier()

                with contextlib.ExitStack() as pp:
                    xt = Ring([sb(pp, nm("xt"), [128, D], F32) for _ in range(3)])
                    junk = sb(pp, nm("junk"), [128, D], F32)
                    bjunk = Buf()
                    xn = Ring([sb(pp, nm("xn"), [128, D], BF16) for _ in range(2)])
                    xnT = Ring([sb(pp, nm("xnT"), [128, 8, 512], BF16) for _ in range(2)])
                    stat = Ring([sb(pp, nm("stat"), [128, 4], F32) for _ in range(4)])
                    ost = Ring([sb(pp, nm("ost"), [128, 4, 512], BF16) for _ in range(3)])
                    cst = Ring([sb(pp, nm("cst"), [128, 2, 512], F32) for _ in range(2)])
                    t12 = Ring([sb(pp, nm("t12"), [128, 2, 512], F32) for _ in range(2)])
                    rot = Ring([sb(pp, nm("rot"), [128, 512], BF16) for _ in range(2)])
                    pT = Ring([ps(pp, nm("pT"), [128, D], BF16) for _ in range(2)])
                    pacc = Ring([ps(pp, nm("pacc"), [128, 512], F32) for _ in range(5)])

                    G = dict(qa=(0, 0), ka=(512, 4), va=(1024, 8), ga=(1536, 12), qb=(2048, 16), kb=(2560, 20), vb=(3072, 24), gb=(3584, 28))
                    RT = dict(qa=0, ka=1, qb=2, kb=3)
                    segs = []
                    full_main = [("qa", "QA", 0), ("ka", "KA", PADK), ("va", "VA", PADK), ("ga", "GA", 0), ("qb", "QB", 0), ("kb", "KB", 0), ("vb", "VB", 0), ("gb", "GB", 0)]
                    full_rot = [("qa", "QAr", 0), ("ka", "KAr", PADK), ("qb", "QBr", 0), ("kb", "KBr", 0)]
                    segs.append((job["xrow"], sq, full_main, full_rot))
                    if s == 0:
                        segs.append((ROW_OTHER, HALF, [("kb", "KB", HALF), ("vb", "VB", HALF)], [("kb", "KBr", HALF)]))
                        segs.append((ROW_HALO, PADK, [("ka", "KA", 0), ("va", "VA", 0)], [("ka", "KAr", 0)]))
                        segs.append((ROW_HALO + PADK, PADK, [("ka", "KA", PADK + HALF), ("va", "VA", PADK + HALF)], [("ka", "KAr", PADK + HALF)]))
                    evi = [0]
                    for (xr0, T, mains, rots) in (segs if lvl >= 2 else []):
                        for ch in range(T // 512):
                            r0 = xr0 + ch * 512
                            xT_t, xT_b = xnT.next()
                            for i in range(4):
                                x_t, x_b = xt.next()
                                S.dma("dma_start", dict(out=x_t[:], in_=X[r0 + i * 128:r0 + (i + 1) * 128, :]), writes=[x_b])
                                st_t, st_b = stat.next()
                                S.op(ACT, "activation", dict(out=junk[:], in_=x_t[:], func=AF.Square, accum_out=st_t[:, 0:1]), reads=[x_b], writes=[bjunk, st_b])
                                S.op(DVE, "tensor_scalar", dict(out=st_t[:, 1:2], in0=st_t[:, 0:1], scalar1=1.0 / D, scalar2=EPS, op0=ALU.mult, op1=ALU.add), reads=[st_b], writes=[st_b])
                                S.op(POOL, "tensor_tensor", dict(out=st_t[:, 2:3], in0=st_t[:, 1:2], in1=mhalf[:], op=ALU.pow), reads=[st_b, bconst], writes=[st_b])
                                xn_t, xn_b = xn.next()
                                S.op(DVE, "tensor_scalar", dict(out=xn_t[:], in0=x_t[:], scalar1=st_t[:, 2:3], scalar2=None, op0=ALU.mult), reads=[x_b, st_b], writes=[xn_b])
                                pT_t, pT_b = pT.next()
                                for kc in range(8):
                                    S.op(PE, "transpose", dict(out=pT_t[:, kc * 128:(kc + 1) * 128], in_=xn_t[:, kc * 128:(kc + 1) * 128], identity=ident[:]), reads=[xn_b, bconst], writes=[pT_b])
                                S.op(DVE, "tensor_copy", dict(out=xT_t[:, :, i * 128:(i + 1) * 128], in_=pT_t[:].rearrange("p (a b) -> p a b", b=128)), reads=[pT_b], writes=[xT_b])
                            c_t, c_b = cst.next()
                            if rots:
                                S.dma("dma_start", dict(out=c_t[:, 0, :], in_=CT[:, r0:r0 + 512]), writes=[c_b])
                                S.dma("dma_start", dict(out=c_t[:, 1, :], in_=ST[:, r0:r0 + 512]), writes=[c_b])
                            for (gname, skey, dcol0) in mains:
                                wcol, bcol = G[gname]
                                o_t, o_b = ost.next()
                                for b in range(4):
                                    pa_t, pa_b = pacc.next()
                                    c0 = wcol + b * 128
                                    bc = bcol + b
                                    for kc in range(8):
                                        S.op(PE, "matmul", dict(out=pa_t[:], lhsT=wp[:, kc, c0:c0 + 128], rhs=xT_t[:, kc, :], start=(kc == 0), stop=(kc == 7)), reads=[bwp, xT_b], writes=[pa_b])
                                    evi[0] += 1
                                    if evi[0] % 4 != 0:
                                        S.op(ACT, "activation", dict(out=o_t[:, b, :], in_=pa_t[:], func=AF.Identity, bias=biasT[:, bc:bc + 1]), reads=[pa_b, bbias], writes=[o_b])
                                    else:
                                        S.op(DVE, "tensor_scalar", dict(out=o_t[:, b, :], in0=pa_t[:], scalar1=biasT[:, bc:bc + 1], scalar2=None, op0=ALU.add), reads=[pa_b, bbias], writes=[o_b])
                                dc = dcol0 + ch * 512
                                S.dma("dma_start", dict(out=SC[skey][:, dc:dc + 512].rearrange("(b p) t -> p b t", p=128), in_=o_t[:]), reads=[o_b], writes=[SCB[skey]])
                            for (gname, skey, dcol0) in rots:
                                ti = RT[gname]
                                wc = 4096 + ti * 192
                                bc = 32 + 2 * ti
                                p1_t, p1_b = pacc.next()
                                p2_t, p2_b = pacc.next()
                                for kc in range(8):
                                    S.op(PE, "matmul", dict(out=p1_t[:], lhsT=wp[:, kc, wc:wc + 128], rhs=xT_t[:, kc, :], start=(kc == 0), stop=(kc == 7)), reads=[bwp, xT_b], writes=[p1_b])
                                for kc in range(8):
                                    S.op(PE, "matmul", dict(out=p2_t[:], lhsT=wp[:, kc, wc + 64:wc + 192], rhs=xT_t[:, kc, :], start=(kc == 0), stop=(kc == 7)), reads=[bwp, xT_b], writes=[p2_b])
                                t_t, t_b = t12.next()
                                S.op(DVE, "scalar_tensor_tensor", dict(out=t_t[:, 0, :], in0=p1_t[:], scalar=biasT[:, bc:bc + 1], in1=c_t[:, 0, :], op0=ALU.add, op1=ALU.mult), reads=[p1_b, c_b, bbias], writes=[t_b])
                                S.op(DVE, "scalar_tensor_tensor", dict(out=t_t[:, 1, :], in0=p2_t[:], scalar=biasT[:, bc + 1:bc + 2], in1=c_t[:, 1, :], op0=ALU.add, op1=ALU.mult), reads=[p2_b, c_b, bbias], writes=[t_b])
                                r_t, r_b = rot.next()
                                S.op(POOL, "tensor_tensor", dict(out=r_t[:], in0=t_t[:, 0, :], in1=t_t[:, 1, :], op=ALU.add), reads=[t_b], writes=[r_b])
                                dc = dcol0 + ch * 512
                                S.dma("dma_start", dict(out=SC[skey][:, dc:dc + 512], in_=r_t[:]), reads=[r_b], writes=[SCB[skey]])
                    S.barrier()

            with contextlib.ExitStack() as pB:
                nkt = skv // 128
                kbT = sb(pB, nm("kbT"), [128, skv], BF16)
                vbT = sb(pB, nm("vbT"), [128, skv], BF16)
                qbT = sb(pB, nm("qbT"), [128, sq], BF16)
                gbT = sb(pB, nm("gbT"), [128, sq], BF16)
                vtok = sb(pB, nm("vtok"), [128, nkt, 128], BF16)
                bk, bv, bq, bg, bvt = Buf(), Buf(), Buf(), Buf(), Buf()
                Pr = Ring([sb(pB, nm("P"), [128, 1024], BF16) for _ in range(4)])
                asum = [sb(pB, nm("asum"), [128, 1024], F32) for _ in range(2)]
                basum = [Buf() for _ in range(2)]
                oS = [sb(pB, nm("oS"), [128, 512], F32) for _ in range(2)]
                boS = [Buf() for _ in range(2)]
                thr = Ring([sb(pB, nm("thb"), [128, 512], F32) for _ in range(2)])
                sgr = Ring([sb(pB, nm("sgb"), [128, 512], BF16) for _ in range(2)])
                yTr = Ring([sb(pB, nm("yTb"), [128, 512], BF16) for _ in range(2)])
                rsr = Ring([sb(pB, nm("rs"), [128, 16], F32) for _ in range(2)])
                rr = Ring([sb(pB, nm("rr"), [128, 8], F32) for _ in range(4)])
                t0r = Ring([sb(pB, nm("t0"), [128, 128], F32) for _ in range(2)])
                orr = Ring([sb(pB, nm("o"), [128, 128], F32) for _ in range(2)])
                onr = Ring([sb(pB, nm("on"), [128, 128], BF16) for _ in range(2)])
                junkb = sb(pB, nm("junkb"), [128, 128], F32)
                bjb = Buf()
                scr_ = Ring([ps(pB, nm("sc"), [128, 1024], F32) for _ in range(2)])
                oT = [ps(pB, nm("oT"), [128, 512], F32) for _ in range(2)]
                boT = [Buf() for _ in range(2)]
                tpb = ps(pB, nm("tpb"), [128, 2, 2, 128], F32)
                btp = [Buf() for _ in range(2)]
                pTv = ps(pB, nm("pTv"), [128, 4, 128], BF16)
                bpv = Buf()

                def load_qk(dst, dbuf, main, rotk, g, ncols):
                    for m in range(2):
                        h = 2 * g + m
                        S.dma("dma_start", dict(out=dst[m * 64:m * 64 + 48, 0:ncols], in_=SC[main][g * 128 + m * 64 + 16:g * 128 + m * 64 + 64, 0:ncols]), reads=[SCB[main]], writes=[dbuf])
                        S.dma("dma_start", dict(out=dst[m * 64 + 48:m * 64 + 56, 0:ncols], in_=SC[rotk][h * 8:h * 8 + 8, 0:ncols]), reads=[SCB[rotk]], writes=[dbuf])
                        S.dma("dma_start", dict(out=dst[m * 64 + 56:m * 64 + 64, 0:ncols], in_=SC[rotk][64 + h * 8:64 + h * 8 + 8, 0:ncols]), reads=[SCB[rotk]], writes=[dbuf])

                for g in (range(4) if lvl >= 3 else []):
                    load_qk(kbT, bk, "KB", "KBr", g, skv)
                    load_qk(qbT, bq, "QB", "QBr", g, sq)
                    S.dma("dma_start", dict(out=vbT[:], in_=SC["VB"][g * 128:(g + 1) * 128, 0:skv]), reads=[SCB["VB"]], writes=[bv])
                    S.dma("dma_start", dict(out=gbT[:], in_=SC["GB"][g * 128:(g + 1) * 128, 0:sq]), reads=[SCB["GB"]], writes=[bg])
                    for k4 in range(nkt // 4):
                        for j in range(4):
                            kt = k4 * 4 + j
                            S.op(PE, "transpose", dict(out=pTv[:, j, :], in_=vbT[:, kt * 128:(kt + 1) * 128], identity=ident[:]), reads=[bv, bconst], writes=[bpv])
                        S.op(DVE, "tensor_copy", dict(out=vtok[:, k4 * 4:k4 * 4 + 4, :], in_=pTv[:]), reads=[bpv], writes=[bvt])
                    for qc in range(sq // 512):
                        q0 = qc * 512
                        th_t, th_b = thr.next()
                        sg_t, sg_b = sgr.next()
                        S.op(ACT, "activation", dict(out=th_t[:], in_=gbT[:, q0:q0 + 512], func=AF.Tanh, scale=0.5), reads=[bg], writes=[th_b])
                        S.op(DVE, "scalar_tensor_tensor", dict(out=sg_t[:], in0=th_t[:], scalar=1.0, in1=gbT[:, q0:q0 + 512], op0=ALU.add, op1=ALU.mult), reads=[th_b, bg], writes=[sg_b])

                        def scores(kt):
                            sc_t, sc_b = scr_.next()
                            for m in range(2):
                                S.op(PE, "matmul", dict(out=sc_t[:, m * 512:(m + 1) * 512], lhsT=kbT[m * 64:(m + 1) * 64, kt * 128:(kt + 1) * 128], rhs=qbT[m * 64:(m + 1) * 64, q0:q0 + 512], start=True, stop=True), reads=[bk, bq], writes=[sc_b])
                            p_t, p_b = Pr.next()
                            S.op(ACT, "activation", dict(out=p_t[:], in_=sc_t[:], func=AF.Exp, scale=0.125), reads=[sc_b], writes=[p_b])
                            return p_t, p_b

                        def av(kt, p_t, p_b):
                            for m in range(2):
                                S.op(PE, "matmul", dict(out=oT[m][:], lhsT=vtok[:, kt, :], rhs=p_t[:, m * 512:(m + 1) * 512], start=(kt == 0), stop=(kt == nkt - 1)), reads=[p_b, bvt], writes=[boT[m]])
                            j = 0 if kt % 3 != 2 else 1
                            eng = DVE if j == 0 else POOL
                            first = kt == 0 or kt == 2
                            if first:
                                S.op(eng, "tensor_copy", dict(out=asum[j][:], in_=p_t[:]), reads=[p_b], writes=[basum[j]])
                            else:
                                S.op(eng, "tensor_tensor", dict(out=asum[j][:], in0=asum[j][:], in1=p_t[:], op=ALU.add), reads=[p_b, basum[j]], writes=[basum[j]])

                        prev = scores(0)
                        for kt in range(nkt):
                            nxt = scores(kt + 1) if kt + 1 < nkt else None
                            av(kt, *prev)
                            prev = nxt
                        for m in range(2):
                            S.op(ACT, "activation", dict(out=oS[m][:], in_=oT[m][:], func=AF.Copy), reads=[boT[m]], writes=[boS[m]])
                        e1, be1 = scr_.next()
                        for m in range(2):
                            for qb in range(4):
                                c = m * 4 + qb
                                for j in range(2):
                                    S.op(PE, "matmul", dict(out=e1[:, 2 * c:2 * c + 2], lhsT=asum[j][:, m * 512 + qb * 128:m * 512 + (qb + 1) * 128], rhs=ones_f[:, 0:2], start=(j == 0), stop=(j == 1)), reads=[basum[j], bconst], writes=[be1])
                        rs_t, rs_b = rsr.next()
                        S.op(DVE, "reciprocal", dict(out=rs_t[:, 0:8], in_=e1[:, 0:16:2]), reads=[be1], writes=[rs_b])
                        S.op(DVE, "tensor_scalar", dict(out=rs_t[:, 8:12], in0=rs_t[:, 4:8], scalar1=neglam[:, 0:1], scalar2=None, op0=ALU.mult), reads=[rs_b, bconst], writes=[rs_b])
                        y_t, y_b = yTr.next()
                        for qb in range(4):
                            sl = qb % 2
                            for m in range(2):
                                S.op(PE, "transpose", dict(out=tpb[:, sl, m, :], in_=oS[m][:, qb * 128:(qb + 1) * 128], identity=ident_f[:]), reads=[boS[m], bconst], writes=[btp[sl]])
                            r_t, r_b = rr.next()
                            t0_t, t0_b = t0r.next()
                            S.op(DVE, "tensor_scalar", dict(out=t0_t[:], in0=tpb[:, sl, 0, :], scalar1=rs_t[:, qb:qb + 1], scalar2=None, op0=ALU.mult), reads=[btp[sl], rs_b], writes=[t0_b])
                            o_t, o_b = orr.next()
                            S.op(DVE, "scalar_tensor_tensor", dict(out=o_t[:], in0=tpb[:, sl, 1, :], scalar=rs_t[:, 8 + qb:9 + qb], in1=t0_t[:], op0=ALU.mult, op1=ALU.add), reads=[btp[sl], rs_b, t0_b], writes=[o_b])
                            S.op(ACT, "activation", dict(out=junkb[:], in_=o_t[:], func=AF.Square, accum_out=r_t[:, 3:4]), reads=[o_b], writes=[bjb, r_b])
                            S.op(DVE, "tensor_scalar", dict(out=r_t[:, 4:5], in0=r_t[:, 3:4], scalar1=1.0 / 128, scalar2=EPS, op0=ALU.mult, op1=ALU.add), reads=[r_b], writes=[r_b])
                            S.op(POOL, "tensor_tensor", dict(out=r_t[:, 5:6], in0=r_t[:, 4:5], in1=mhalf[:], op=ALU.pow), reads=[r_b, bconst], writes=[r_b])
                            on_t, on_b = onr.next()
                            S.op(POOL, "tensor_scalar", dict(out=on_t[:], in0=o_t[:], scalar1=r_t[:, 5:6], scalar2=None, op0=ALU.mult), reads=[o_b, r_b], writes=[on_b])
                            S.op(PE, "transpose", dict(out=pTv[:, 0, :], in_=on_t[:], identity=ident[:]), reads=[on_b, bconst], writes=[bpv])
                            S.op(DVE, "scalar_tensor_tensor", dict(out=y_t[:, qb * 128:(qb + 1) * 128], in0=pTv[:, 0, :], scalar=gsub2[:, 0:1], in1=sg_t[:, qb * 128:(qb + 1) * 128], op0=ALU.mult, op1=ALU.mult), reads=[bpv, sg_b, bconst], writes=[y_b])
                        S.dma("dma_start", dict(out=SC["YM"][512 + g * 128:512 + (g + 1) * 128, q0:q0 + 512], in_=y_t[:]), reads=[y_b], writes=[SCB["YM"]])
                S.barrier()

            with contextlib.ExitStack() as pA:
                kaT = sb(pA, nm("kaT"), [128, sext], BF16)
                vaT = sb(pA, nm("vaT"), [128, sext], BF16)
                qaT = sb(pA, nm("qaT"), [128, sq], BF16)
                gaT = sb(pA, nm("gaT"), [128, sq], BF16)
                acc = sb(pA, nm("accA"), [65, 2, sq], F32)
                bk, bv, bq, bg, bacc = Buf(), Buf(), Buf(), Buf(), Buf()
                ones3 = sb(pA, nm("ones3"), [128, 2, 1], F32)
                bo3 = Buf()
                Vt = Ring([sb(pA, nm("Vt"), [128, 2, 65], BF16) for _ in range(3)])
                Pt = Ring([sb(pA, nm("Pt"), [128, 2, 256], BF16) for _ in range(3)])
                Pm = Ring([sb(pA, nm("Pm"), [128, 2, 256], BF16) for _ in range(3)])
                recr = Ring([sb(pA, nm("rec"), [128, 512], F32) for _ in range(2)])
                thr = Ring([sb(pA, nm("tha"), [128, 512], F32) for _ in range(2)])
                sgr = Ring([sb(pA, nm("sga"), [128, 512], F32) for _ in range(2)])
                tnr = Ring([sb(pA, nm("tn"), [128, 512], F32) for _ in range(2)])
                yTr = Ring([sb(pA, nm("yTa"), [128, 512], BF16) for _ in range(2)])
                scA = Ring([ps(pA, nm("scA"), [128, 2, 512], F32) for _ in range(2)])
                opsr = Ring([ps(pA, nm("ops"), [65, 2, 256], F32) for _ in range(2)])
                pTa = Ring([ps(pA, nm("pTa"), [128, 128], BF16) for _ in range(2)])
                S.op(POOL, "memset", dict(ap=ones3[:], constant=1.0), writes=[bo3])
                tiles = amask_tiles(sq)
                for hp in (range(4) if lvl >= 4 else []):
                    if s != 0:
                        for tl, bb in ((kaT, bk), (vaT, bv)):
                            S.op(POOL, "memset", dict(ap=tl[:, 0:PADK], constant=0.0), writes=[bb])
                            S.op(POOL, "memset", dict(ap=tl[:, PADK + sq:sext], constant=0.0), writes=[bb])
                        kc0, kn = PADK, sq
                    else:
                        kc0, kn = 0, sext
                    for hl in range(2):
                        h = 2 * hp + hl
                        S.dma("dma_start", dict(out=kaT[hl * 64:hl * 64 + 48, kc0:kc0 + kn], in_=SC["KA"][hp * 128 + hl * 64 + 16:hp * 128 + hl * 64 + 64, kc0:kc0 + kn]), reads=[SCB["KA"]], writes=[bk])
                        S.dma("dma_start", dict(out=kaT[hl * 64 + 48:hl * 64 + 56, kc0:kc0 + kn], in_=SC["KAr"][h * 8:h * 8 + 8, kc0:kc0 + kn]), reads=[SCB["KAr"]], writes=[bk])
                        S.dma("dma_start", dict(out=kaT[hl * 64 + 56:hl * 64 + 64, kc0:kc0 + kn], in_=SC["KAr"][64 + h * 8:64 + h * 8 + 8, kc0:kc0 + kn]), reads=[SCB["KAr"]], writes=[bk])
                        S.dma("dma_start", dict(out=qaT[hl * 64:hl * 64 + 48, :], in_=SC["QA"][hp * 128 + hl * 64 + 16:hp * 128 + hl * 64 + 64, 0:sq]), reads=[SCB["QA"]], writes=[bq])
                        S.dma("dma_start", dict(out=qaT[hl * 64 + 48:hl * 64 + 56, :], in_=SC["QAr"][h * 8:h * 8 + 8, 0:sq]), reads=[SCB["QAr"]], writes=[bq])
                        S.dma("dma_start", dict(out=qaT[hl * 64 + 56:hl * 64 + 64, :], in_=SC["QAr"][64 + h * 8:64 + h * 8 + 8, 0:sq]), reads=[SCB["QAr"]], writes=[bq])
                    S.dma("dma_start", dict(out=vaT[:, kc0:kc0 + kn], in_=SC["VA"][hp * 128:(hp + 1) * 128, kc0:kc0 + kn]), reads=[SCB["VA"]], writes=[bv])
                    S.dma("dma_start", dict(out=gaT[:], in_=SC["GA"][hp * 128:(hp + 1) * 128, 0:sq]), reads=[SCB["GA"]], writes=[bg])
                    S.op(POOL, "memset", dict(ap=acc[:], constant=0.0), writes=[bacc])
                    for ti, (pi, dil, r, t, nblk, je0) in enumerate(tiles):
                        e0 = r + dil * (je0 + 128 * t)
                        ksl = slice(e0, e0 + dil * 127 + 1, dil)
                        mlo, mhi = max(t - 1, 0), min(t, nblk - 1)
                        nq = (mhi - mlo + 1) * 128
                        qs = r + dil * 128 * mlo
                        qsl = slice(qs, qs + dil * (nq - 1) + 1, dil)
                        boff = 0 if t >= 1 else 128
                        pa_t, pa_b = pTa.next()
                        S.op(PE, "transpose", dict(out=pa_t[:], in_=vaT[:, ksl], identity=ident[:]), reads=[bv, bconst], writes=[pa_b])
                        v_t, v_b = Vt.next()
                        S.op(DVE, "tensor_scalar", dict(out=v_t[:, :, 0:64], in0=pa_t[:].rearrange("p (a b) -> p a b", b=64), scalar1=am[:, ti:ti + 1], scalar2=None, op0=ALU.mult), reads=[pa_b, bconst], writes=[v_b])
                        S.op(POOL, "tensor_scalar", dict(out=v_t[:, :, 64:65], in0=ones3[:], scalar1=am[:, ti:ti + 1], scalar2=None, op0=ALU.mult), reads=[bo3, bconst], writes=[v_b])
                        sc_t, sc_b = scA.next()
                        for hl in range(2):
                            S.op(PE, "matmul", dict(out=sc_t[:, hl, 0:nq], lhsT=kaT[hl * 64:(hl + 1) * 64, ksl], rhs=qaT[hl * 64:(hl + 1) * 64, qsl], start=True, stop=True), reads=[bk, bq], writes=[sc_b])
                        p_t, p_b = Pt.next()
                        S.op(ACT, "activation", dict(out=p_t[:, :, 0:nq], in_=sc_t[:, :, 0:nq], func=AF.Exp, scale=0.125), reads=[sc_b], writes=[p_b])
                        m_t, m_b = Pm.next()
                        S.op(POOL, "tensor_tensor", dict(out=m_t[:, :, 0:nq], in0=p_t[:, :, 0:nq], in1=band2[:, :, boff:boff + nq], op=ALU.mult), reads=[p_b, bconst], writes=[m_b])
                        o_t, o_b = opsr.next()
                        for hl in range(2):
                            for bi in range(mhi - mlo + 1):
                                S.op(PE, "matmul", dict(out=o_t[:, hl, bi * 128:(bi + 1) * 128], lhsT=v_t[:, hl, 0:65], rhs=m_t[:, hl, bi * 128:(bi + 1) * 128], start=True, stop=True), reads=[v_b, m_b], writes=[o_b])
                        for hl in range(2):
                            S.op(DVE, "tensor_tensor", dict(out=acc[:, hl, qsl], in0=o_t[:, hl, 0:nq], in1=acc[:, hl, qsl], op=ALU.add), reads=[o_b, bacc], writes=[bacc])
                    for qc in range(sq // 512):
                        q0 = qc * 512
                        d_t, d_b = scA.next()
                        n_b = d_b
                        dv = d_t[:, 0, :]
                        nv = d_t[:, 1, :]
                        for h2 in range(2):
                            qa_, qb_ = q0 + h2 * 256, q0 + (h2 + 1) * 256
                            for hl in range(2):
                                S.op(PE, "matmul", dict(out=dv[:, h2 * 256:(h2 + 1) * 256], lhsT=sels[:, hl, :], rhs=acc[:, hl, qa_:qb_], start=(hl == 0), stop=(hl == 1)), reads=[bacc, bconst], writes=[d_b])
                        for h2 in range(2):
                            qa_, qb_ = q0 + h2 * 256, q0 + (h2 + 1) * 256
                            for hl in range(2):
                                S.op(PE, "matmul", dict(out=nv[:, h2 * 256:(h2 + 1) * 256], lhsT=sels[:, 2 + hl, :], rhs=acc[:, hl, qa_:qb_], start=(hl == 0), stop=(hl == 1)), reads=[bacc, bconst], writes=[n_b])
                        rc_t, rc_b = recr.next()
                        S.op(DVE, "reciprocal", dict(out=rc_t[:], in_=dv), reads=[d_b], writes=[rc_b])
                        th_t, th_b = thr.next()
                        S.op(ACT, "activation", dict(out=th_t[:], in_=gaT[:, q0:q0 + 512], func=AF.Tanh, scale=0.5), reads=[bg], writes=[th_b])
                        sg_t, sg_b = sgr.next()
                        S.op(POOL, "tensor_scalar", dict(out=sg_t[:], in0=th_t[:], scalar1=1.0, scalar2=None, op0=ALU.add), reads=[th_b], writes=[sg_b])
                        S.op(POOL, "tensor_tensor", dict(out=sg_t[:], in0=sg_t[:], in1=gaT[:, q0:q0 + 512], op=ALU.mult), reads=[sg_b, bg], writes=[sg_b])
                        tn_t, tn_b = tnr.next()
                        S.op(DVE, "tensor_tensor", dict(out=tn_t[:], in0=nv, in1=rc_t[:], op=ALU.mult), reads=[n_b, rc_b], writes=[tn_b])
                        y_t, y_b = yTr.next()
                        S.op(POOL, "tensor_tensor", dict(out=y_t[:], in0=tn_t[:], in1=sg_t[:], op=ALU.mult), reads=[tn_b, sg_b], writes=[y_b])
                        S.dma("dma_start", dict(out=SC["YM"][hp * 128:(hp + 1) * 128, q0:q0 + 512], in_=y_t[:]), reads=[y_b], writes=[SCB["YM"]])
                S.barrier()

            with contextlib.ExitStack() as p3:
                ymr = Ring([sb(p3, nm("ym"), [128, 8, 512], BF16) for _ in range(2)])
                xr = Ring([sb(p3, nm("x3"), [128, D], F32) for _ in range(2)])
                tmr = Ring([sb(p3, nm("tm3"), [128, D], F32) for _ in range(2)])
                outr = Ring([sb(p3, nm("o3"), [128, D], F32) for _ in range(2)])
                st3 = Ring([sb(p3, nm("st3"), [128, 6], F32) for _ in range(4)])
                junk3 = sb(p3, nm("junk3"), [128, 512], F32)
                bj3 = Buf()
                po = Ring([ps(p3, nm("po"), [128, 2, 512], F32) for _ in range(2)])
                for qc in (range(sq // 512) if lvl >= 5 else []):
                    q0 = qc * 512
                    ym_t, ym_b = ymr.next()
                    S.dma("dma_start", dict(out=ym_t[:], in_=SC["YM"][:, q0:q0 + 512].rearrange("(kc p) t -> p kc t", p=128)), reads=[SCB["YM"]], writes=[ym_b])
                    for i in range(4):
                        row = q0 + i * 128
                        x_t, x_b = xr.next()
                        S.dma("dma_start", dict(out=x_t[:], in_=X[job["xrow"] + row:job["xrow"] + row + 128, :]), writes=[x_b])
                        po_t, po_b = po.next()
                        for hf in range(2):
                            for kc in range(8):
                                S.op(PE, "matmul", dict(out=po_t[:, hf, :], lhsT=ym_t[:, kc, i * 128:(i + 1) * 128], rhs=wout[:, kc, hf * 512:(hf + 1) * 512], start=(kc == 0), stop=(kc == 7)), reads=[ym_b, bwout], writes=[po_b])
                        s_t, s_b = st3.next()
                        for hf in range(2):
                            S.op(ACT, "activation", dict(out=junk3[:], in_=po_t[:, hf, :], func=AF.Square, accum_out=s_t[:, hf:hf + 1]), reads=[po_b], writes=[bj3, s_b])
                        S.op(DVE, "tensor_tensor", dict(out=s_t[:, 2:3], in0=s_t[:, 0:1], in1=s_t[:, 1:2], op=ALU.add), reads=[s_b], writes=[s_b])
                        S.op(DVE, "tensor_scalar", dict(out=s_t[:, 3:4], in0=s_t[:, 2:3], scalar1=1.0 / D, scalar2=EPS, op0=ALU.mult, op1=ALU.add), reads=[s_b], writes=[s_b])
                        S.op(POOL, "tensor_tensor", dict(out=s_t[:, 4:5], in0=s_t[:, 3:4], in1=mhalf[:], op=ALU.pow), reads=[s_b, bconst], writes=[s_b])
                        tm_t, tm_b = tmr.next()
                        for hf in range(2):
                            S.op(DVE, "scalar_tensor_tensor", dict(out=tm_t[:, hf * 512:(hf + 1) * 512], in0=po_t[:, hf, :], scalar=s_t[:, 4:5], in1=gg[s][:, hf * 512:(hf + 1) * 512], op0=ALU.mult, op1=ALU.mult), reads=[po_b, s_b, bgg[s]], writes=[tm_b])
                        o_t, o_b = outr.next()
                        S.op(POOL, "tensor_tensor", dict(out=o_t[:], in0=tm_t[:], in1=x_t[:], op=ALU.add), reads=[tm_b, x_b], writes=[o_b])
                        yrow = job["yrow"] + row
                        out_dmas.append(S.dma("dma_start", dict(out=Y[yrow:yrow + 128, :], in_=o_t[:]), reads=[o_b], writes=[Buf()]))
                S.barrier()

        S.run(final_wait=out_dmas)
    return nc


_NC_CACHE = {}


def _rope_tables(pos):
    half = 8
    inv = (np.float32(500000.0) ** (-np.arange(half, dtype=np.float32) / np.float32(half))).astype(np.float32)
    ang = pos.astype(np.float32)[None, :] * inv[:, None]
    cos = np.cos(ang).astype(np.float32)
    sin = np.sin(ang).astype(np.float32)
    idx = np.arange(128) % 8
    C = cos[idx]
    Sg = sin[idx].copy()
    Sg[:64] *= -1.0
    return np.ascontiguousarray(C), np.ascontiguousarray(Sg)


def _amask(valid_ext, sq):
    tiles = amask_tiles(sq)
    m = np.zeros((128, len(tiles)), np.float32)
    p = np.arange(128)
    for ti, (pi, dil, r, t, nblk, je0) in enumerate(tiles):
        e = r + dil * (je0 + 128 * t + p)
        m[:, ti] = valid_ext[e]
    return m


def prep_inputs(x_prompt, x_sample, c_prompt, c_sample, w_in, w_out, g_pre, g_post,
                w_ada, b_ada, lam_q1, lam_k1, lam_q2, lam_k2, g_sub):
    f32 = np.float32
    x_prompt = np.asarray(x_prompt, f32)
    x_sample = np.asarray(x_sample, f32)
    c_prompt = np.asarray(c_prompt, f32)
    c_sample = np.asarray(c_sample, f32)
    w_in0 = np.ascontiguousarray(np.asarray(w_in, f32)[0])
    w_out0 = np.ascontiguousarray(np.asarray(w_out, f32)[0])
    w_ada0 = np.ascontiguousarray(np.asarray(w_ada, f32)[0])
    b_ada0 = np.asarray(b_ada, f32)[0]
    g_pre0 = np.asarray(g_pre, f32)[0]
    g_post0 = np.asarray(g_post, f32)[0]
    g_sub0 = np.asarray(g_sub, f32)[0]

    offs = dict(qa=0, ka=512, qb=2048, kb=2560)
    w_rot = np.zeros((D, 4, 192), f32)
    for ti, nme in enumerate(("qa", "ka", "qb", "kb")):
        x1 = np.array([offs[nme] + h * 64 + i for h in range(8) for i in range(8)])
        x2 = x1 + 8
        w_rot[:, ti, 0:64] = w_in0[:, x1]
        w_rot[:, ti, 64:128] = w_in0[:, x2]
        w_rot[:, ti, 128:192] = w_in0[:, x1]
    w_rot = np.ascontiguousarray(w_rot.reshape(D, 768))

    g_pre_t = np.ascontiguousarray(g_pre0.reshape(8, 128).T)
    b_ada_t = np.ascontiguousarray(b_ada0[:2048].reshape(16, 128).T)
    b_gate = np.ascontiguousarray(b_ada0[2048:3072].reshape(1, D))
    g_post_r = np.ascontiguousarray(g_post0.reshape(1, D))
    g_sub_t = np.ascontiguousarray(g_sub0.reshape(128, 1))
    lam_v = np.ascontiguousarray(np.stack([np.asarray(v, f32)[0] for v in (lam_q1, lam_k1, lam_q2, lam_k2)], 0))
    ident = np.eye(128, dtype=f32)
    p = np.arange(128)[:, None]
    f = np.arange(128)[None, :]
    band = np.concatenate([(p <= f), (p >= f)], axis=1).astype(f32)
    band = np.ascontiguousarray(np.concatenate([band, band], axis=1))
    sels = np.zeros((65, 4, 128), f32)
    sels[64, 0, 0:64] = 2.0
    sels[64, 1, 64:128] = 2.0
    sels[np.arange(64), 2, np.arange(64)] = 1.0
    sels[np.arange(64), 3, 64 + np.arange(64)] = 1.0
    sels = np.ascontiguousarray(sels.reshape(65, 512))
    valid1 = np.zeros(DSEQ + 2 * PADK, f32)
    valid1[PADK:PADK + DSEQ] = 1.0
    am1 = _amask(valid1, DSEQ)

    in_maps = []
    for c in range(8):
        psq, hf = c // 2, c % 2
        q0 = hf * HALF
        o0 = (1 - hf) * HALF
        xa = np.zeros((NROWS, D), f32)
        pos = np.zeros(NROWS, np.int64)
        xa[ROW_OWN:ROW_OWN + HALF] = x_prompt[psq, q0:q0 + HALF]
        pos[ROW_OWN:ROW_OWN + HALF] = np.arange(q0, q0 + HALF)
        xa[ROW_OTHER:ROW_OTHER + HALF] = x_prompt[psq, o0:o0 + HALF]
        pos[ROW_OTHER:ROW_OTHER + HALF] = np.arange(o0, o0 + HALF)
        hpos = np.concatenate([np.arange(q0 - PADK, q0), np.arange(q0 + HALF, q0 + HALF + PADK)])
        hval = (hpos >= 0) & (hpos < SEQ)
        xa[ROW_HALO:ROW_HALO + 2 * PADK][hval] = x_prompt[psq, hpos[hval]]
        pos[ROW_HALO:ROW_HALO + 2 * PADK] = np.clip(hpos, 0, SEQ - 1)
        xa[ROW_S0:ROW_S0 + DSEQ] = x_sample[2 * c]
        pos[ROW_S0:ROW_S0 + DSEQ] = np.arange(DSEQ)
        xa[ROW_S1:ROW_S1 + DSEQ] = x_sample[2 * c + 1]
        pos[ROW_S1:ROW_S1 + DSEQ] = np.arange(DSEQ)
        C, Sg = _rope_tables(pos)
        cs = np.stack([c_prompt[psq], c_sample[2 * c], c_sample[2 * c + 1]], axis=-1)
        c_t = np.ascontiguousarray(cs.reshape(8, 128, 3).transpose(1, 0, 2).reshape(128, 24))
        valid0 = np.ones(HALF + 2 * PADK, f32)
        valid0[0:PADK] = hval[:PADK]
        valid0[PADK + HALF:] = hval[PADK:]
        am0 = _amask(valid0, HALF)
        in_maps.append({
            "x_all": xa, "rope_c": C, "rope_s": Sg, "c_t": c_t, "w_in": w_in0, "w_rot": w_rot,
            "w_out": w_out0, "w_ada": w_ada0, "g_pre_t": g_pre_t, "b_ada_t": b_ada_t, "b_gate": b_gate,
            "g_post": g_post_r, "g_sub_t": g_sub_t, "lam_v": lam_v, "ident": ident, "band": band,
            "sels": sels, "amask0": am0, "amask1": am1,
        })

    return in_maps


def kernel(**inputs):
    f32 = np.float32
    in_maps = prep_inputs(**inputs)
    if "nc" not in _NC_CACHE:
        _NC_CACHE["nc"] = build_program()
    nc = _NC_CACHE["nc"]
    res = run_bass_kernel_spmd(nc, in_maps, core_ids=list(range(8)))
    y_prompt = np.zeros((4, SEQ, D), f32)
    y_sample = np.zeros((16, DSEQ, D), f32)
    for c in range(8):
        y = np.asarray(res.results[c]["y_all"], f32)
        psq, hf = c // 2, c % 2
        y_prompt[psq, hf * HALF:(hf + 1) * HALF] = y[0:HALF]
        y_sample[2 * c] = y[HALF:HALF + DSEQ]
        y_sample[2 * c + 1] = y[HALF + DSEQ:HALF + 2 * DSEQ]
    return (y_prompt, y_sample)
```

```python
import contextlib
import numpy as np
import concourse.bass as bass
import concourse.mybir as mybir
from concourse.bass_utils import run_bass_kernel_spmd

F32 = mybir.dt.float32
BF16 = mybir.dt.bfloat16
AF = mybir.ActivationFunctionType
ALU = mybir.AluOpType

PE, ACT, DVE, POOL, SP = "tensor", "scalar", "vector", "gpsimd", "sync"
ENGINES = (PE, ACT, DVE, POOL, SP)

D = 1024
SEQ = 8192
DSEQ = 2048
HALF = 4096
PADK = 1024
PATTERNS = ((128, 1), (512, 4), (2048, 16))
EPS = 1e-6
LAM_INIT = 0.2
NROWS = HALF + HALF + 2 * PADK + 2 * DSEQ
ROW_OWN, ROW_OTHER, ROW_HALO, ROW_S0, ROW_S1 = 0, 4096, 8192, 10240, 12288
NOUT = HALF + 2 * DSEQ
WCOLS = 4096 + 4 * 192


class Buf:
    __slots__ = ("last_write", "reads")

    def __init__(self):
        self.last_write = None
        self.reads = []


class Rec:
    __slots__ = ("eng", "fn", "deps", "is_dma", "signal", "count", "dsem", "dval", "epoch")

    def __init__(self, eng, fn, is_dma):
        self.eng, self.fn, self.is_dma = eng, fn, is_dma
        self.deps = []
        self.signal = False
        self.count = 0
        self.dsem = None
        self.dval = 0
        self.epoch = 0


class Sched:
    EPOCH_MAX = 30000

    def __init__(self, nc, n_dma_sems=20):
        self.nc = nc
        self.q = {e: [] for e in ENGINES}
        self.n_dma_sems = n_dma_sems
        self.dma_rr = {e: 0 for e in ENGINES}
        self.dma_last = {}
        self.last_compute = {}

    def op(self, eng, meth, kw, reads=(), writes=(), dma=False, extra=()):
        r = Rec(eng, (meth, kw), dma)
        deps = list(extra)
        for b in reads:
            if b.last_write is not None:
                deps.append(b.last_write)
        for b in writes:
            if b.last_write is not None:
                deps.append(b.last_write)
            deps.extend(b.reads)
        if dma:
            slot = self.dma_rr[eng] % self.n_dma_sems
            self.dma_rr[eng] += 1
            prev = self.dma_last.get((eng, slot))
            if prev is not None:
                deps.append(prev)
            self.dma_last[(eng, slot)] = r
            r.dsem = (eng, slot)
        seen = set()
        for d in deps:
            if d is r or id(d) in seen:
                continue
            if (not d.is_dma) and (not dma) and d.eng == PE and eng == PE:
                continue
            seen.add(id(d))
            r.deps.append(d)
        for b in reads:
            b.reads.append(r)
        for b in writes:
            b.last_write = r
            b.reads = []
        self.q[eng].append(r)
        if not dma:
            self.last_compute[eng] = r
        return r

    def dma(self, meth, kw, reads=(), writes=(), eng=SP):
        return self.op(eng, meth, kw, reads, writes, dma=True)

    def barrier(self):
        deps = list(self.last_compute.values()) + list(self.dma_last.values())
        for e in ENGINES:
            r = Rec(e, None, False)
            for d in deps:
                if (not d.is_dma) and d.eng == e:
                    continue
                r.deps.append(d)
            self.q[e].append(r)

    def run(self, final_wait=()):
        nc = self.nc
        for e in ENGINES:
            for r in self.q[e]:
                for d in r.deps:
                    if not d.is_dma:
                        d.signal = True
        n_epochs = {}
        for e in ENGINES:
            cnt, ep = 0, 0
            for r in self.q[e]:
                if r.is_dma or r.fn is None:
                    continue
                if r.signal:
                    if cnt >= self.EPOCH_MAX:
                        ep += 1
                        cnt = 0
                    cnt += 1
                    r.count = cnt
                    r.epoch = ep
            n_epochs[e] = ep + 1
        dcount = {}
        for e in ENGINES:
            for r in self.q[e]:
                if r.is_dma:
                    dcount[r.dsem] = dcount.get(r.dsem, 0) + 16
                    r.dval = dcount[r.dsem]
        with contextlib.ExitStack() as st:
            csem = {}
            for e in ENGINES:
                for ep in range(n_epochs[e]):
                    if any((not r.is_dma) and r.signal and r.epoch == ep for r in self.q[e]):
                        csem[(e, ep)] = st.enter_context(nc.semaphore(f"c_{e}_{ep}"))
            dsem = {}
            for key in dcount:
                dsem[key] = st.enter_context(nc.semaphore(f"d_{key[0]}_{key[1]}"))
            block = st.enter_context(nc.Block())

            def emit(e, engine):
                waited = {}

                def do_wait(d):
                    if d.is_dma:
                        s, v, k = dsem[d.dsem], d.dval, ("d",) + d.dsem
                    else:
                        s, v, k = csem[(d.eng, d.epoch)], d.count, ("c", d.eng, d.epoch)
                    if waited.get(k, 0) >= v:
                        return
                    waited[k] = v
                    engine.wait_ge(s, v)

                for r in self.q[e]:
                    for d in r.deps:
                        do_wait(d)
                    if r.fn is None:
                        continue
                    ins = getattr(engine, r.fn[0])(**r.fn[1])
                    if r.is_dma:
                        ins.then_inc(dsem[r.dsem], 16)
                    elif r.signal:
                        ins.then_inc(csem[(e, r.epoch)], 1)
                if e == SP:
                    for d in final_wait:
                        do_wait(d)

            @block.tensor
            def _(eng):
                emit(PE, eng)

            @block.scalar
            def _(eng):
                emit(ACT, eng)

            @block.vector
            def _(eng):
                emit(DVE, eng)

            @block.gpsimd
            def _(eng):
                emit(POOL, eng)

            @block.sync
            def _(eng):
                emit(SP, eng)


class Ring:
    def __init__(self, tiles):
        self.tiles = tiles
        self.bufs = [Buf() for _ in tiles]
        self.i = 0

    def next(self):
        k = self.i % len(self.tiles)
        self.i += 1
        return self.tiles[k], self.bufs[k]


def amask_tiles(sq):
    out = []
    for pi, (w, dil) in enumerate(PATTERNS):
        nblk = sq // dil // 128
        je0 = PADK // dil - 64
        for r in range(dil):
            for t in range(nblk + 1):
                out.append((pi, dil, r, t, nblk, je0))
    return out


def build_program(stop_after=None, jobs=(0, 1, 2)):
    nc = bass.Bass("TRN2", target_bir_lowering=False)
    lvl = {'p0': 0, 'w': 1, 'p1': 2, '2b': 3, '2a': 4, None: 5}[stop_after]

    def din(name, shape, dt=F32):
        return nc.dram_tensor(name, list(shape), dt, kind="ExternalInput").ap()

    X = din("x_all", [NROWS, D])
    CT = din("rope_c", [128, NROWS])
    ST = din("rope_s", [128, NROWS])
    CTT = din("c_t", [128, 24])
    W_IN = din("w_in", [D, 4096])
    W_ROT = din("w_rot", [D, 768])
    W_OUT = din("w_out", [D, D])
    W_ADA = din("w_ada", [D, 3 * D])
    GPRE = din("g_pre_t", [128, 8])
    BADA = din("b_ada_t", [128, 16])
    BGATE = din("b_gate", [1, D])
    GPOST = din("g_post", [1, D])
    GSUB = din("g_sub_t", [128, 1])
    LAMV = din("lam_v", [4, 64])
    IDENT = din("ident", [128, 128])
    BAND = din("band", [128, 512])
    SELS = din("sels", [65, 512])
    AM0 = din("amask0", [128, 117])
    AM1 = din("amask1", [128, 69])
    Y = nc.dram_tensor("y_all", [NOUT, D], F32, kind="ExternalOutput").ap()

    JOBS = [
        dict(s=0, sq=HALF, skv=SEQ, sext=HALF + 2 * PADK, xrow=ROW_OWN, yrow=0, am=AM0),
        dict(s=1, sq=DSEQ, skv=DSEQ, sext=DSEQ + 2 * PADK, xrow=ROW_S0, yrow=HALF, am=AM1),
        dict(s=2, sq=DSEQ, skv=DSEQ, sext=DSEQ + 2 * PADK, xrow=ROW_S1, yrow=HALF + DSEQ, am=AM1),
    ]
    def scr(name, rows, cols):
        return nc.dram_tensor(name, [rows, cols], BF16, kind="Internal").ap()

    SC = dict(
        QA=scr("s_qa", 512, HALF), QAr=scr("s_qar", 128, HALF), GA=scr("s_ga", 512, HALF),
        KA=scr("s_ka", 512, HALF + 2 * PADK), KAr=scr("s_kar", 128, HALF + 2 * PADK),
        VA=scr("s_va", 512, HALF + 2 * PADK),
        QB=scr("s_qb", 512, HALF), QBr=scr("s_qbr", 128, HALF), GB=scr("s_gb", 512, HALF),
        KB=scr("s_kb", 512, SEQ), KBr=scr("s_kbr", 128, SEQ), VB=scr("s_vb", 512, SEQ),
        YM=scr("s_ym", 1024, HALF),
    )
    SCB = {k: Buf() for k in SC}

    S = Sched(nc)
    out_dmas = []
    top = contextlib.ExitStack()
    with top:
        def sb(st, name, shape, dt):
            return st.enter_context(nc.sbuf_tensor("sb_" + name, list(shape), dt))

        def ps(st, name, shape, dt):
            return st.enter_context(nc.psum_tensor("ps_" + name, list(shape), dt))

        uid = [0]

        def nm(p):
            uid[0] += 1
            return f"{p}{uid[0]}"

        ident_f = sb(top, "ident_f", [128, 128], F32)
        ident = sb(top, "ident", [128, 128], BF16)
        band_f = sb(top, "band_f", [128, 512], F32)
        band2 = sb(top, "band2", [128, 2, 256], BF16)
        sels = sb(top, "sels", [65, 4, 128], F32)
        am0 = sb(top, "am0", [128, 117], F32)
        am1 = sb(top, "am1", [128, 69], F32)
        ones_f = sb(top, "ones_f", [128, 128], F32)
        mhalf = sb(top, "mhalf", [128, 1], F32)
        zer = sb(top, "zer", [1, 512], BF16)
        gsub2 = sb(top, "gsub2", [128, 1], F32)
        neglam = sb(top, "neglam", [128, 1], F32)
        lamt = sb(top, "lamt", [128, 4, 64], F32)
        lamj = sb(top, "lamj", [128, 64], F32)
        lams = sb(top, "lams", [128, 4], F32)
        gg = [sb(top, f"gg{s}", [128, D], F32) for s in range(3)]
        gsT = sb(top, "gsT", [128, 8, 4], F32)
        shT = sb(top, "shT", [128, 8, 4], F32)
        wout = sb(top, "wout", [128, 8, D], BF16)
        bconst = Buf()
        bgg = [Buf() for _ in range(3)]
        bmod = Buf()
        bwout = Buf()

        S.dma("dma_start", dict(out=ident_f[:], in_=IDENT[:, :]), writes=[bconst])
        S.dma("dma_start", dict(out=band_f[:], in_=BAND[:, :]), writes=[bconst])
        S.dma("dma_start", dict(out=sels[:].rearrange("p a b -> p (a b)"), in_=SELS[:, :]), writes=[bconst])
        S.dma("dma_start", dict(out=am0[:], in_=AM0[:, :]), writes=[bconst])
        S.dma("dma_start", dict(out=am1[:], in_=AM1[:, :]), writes=[bconst])
        S.dma("dma_start", dict(out=gsub2[:], in_=GSUB[:, :]), writes=[bconst])
        for i in range(4):
            S.dma("dma_start", dict(out=lamt[:, i, :], in_=LAMV[i:i + 1, :].broadcast_to([128, 64])), writes=[bconst])
        S.op(DVE, "tensor_copy", dict(out=ident[:], in_=ident_f[:]), reads=[bconst], writes=[bconst])
        S.op(DVE, "tensor_copy", dict(out=band2[:].rearrange("p a b -> p (a b)"), in_=band_f[:]), reads=[bconst], writes=[bconst])
        S.op(POOL, "memset", dict(ap=ones_f[:], constant=1.0), writes=[bconst])
        S.op(POOL, "memset", dict(ap=mhalf[:], constant=-0.5), writes=[bconst])
        S.op(POOL, "memset", dict(ap=zer[:], constant=0.0), writes=[bconst])
        S.op(DVE, "tensor_scalar", dict(out=gsub2[:], in0=gsub2[:], scalar1=(1.0 - LAM_INIT) * 0.5, scalar2=None, op0=ALU.mult), reads=[bconst], writes=[bconst])
        for i in range(2):
            S.op(DVE, "tensor_tensor", dict(out=lamj[:], in0=lamt[:, 2 * i, :], in1=lamt[:, 2 * i + 1, :], op=ALU.mult), reads=[bconst], writes=[bconst])
            S.op(ACT, "activation", dict(out=lamj[:], in_=lamj[:], func=AF.Identity, accum_out=lams[:, i:i + 1]), reads=[bconst], writes=[bconst])
        S.op(ACT, "activation", dict(out=lams[:, 2:4], in_=lams[:, 0:2], func=AF.Exp), reads=[bconst], writes=[bconst])
        S.op(DVE, "tensor_tensor", dict(out=neglam[:], in0=lams[:, 3:4], in1=lams[:, 2:3], op=ALU.subtract), reads=[bconst], writes=[bconst])
        S.op(DVE, "tensor_scalar", dict(out=neglam[:], in0=neglam[:], scalar1=-LAM_INIT, scalar2=None, op0=ALU.add), reads=[bconst], writes=[bconst])

        with contextlib.ExitStack() as p0:
            stg = [sb(p0, f"stg0_{i}", [128, 8, 512], F32) for i in range(2)]
            bstg = [Buf() for _ in range(2)]
            ct = sb(p0, "ct", [128, 24], F32)
            th = sb(p0, "th0", [128, 24], F32)
            scT = sb(p0, "scT", [128, 8, 4], F32)
            screp = sb(p0, "screp", [128, 24, 128], F32)
            gpre = sb(p0, "gpre", [128, 8], F32)
            bada = sb(p0, "bada", [128, 16], F32)
            bgate = sb(p0, "bgate", [128, D], F32)
            gpost = sb(p0, "gpost", [128, D], F32)
            tmpm = sb(p0, "tmpm", [128, 8], F32)
            tmpg = sb(p0, "tmpg", [128, 512], F32)
            pmod = ps(p0, "pmod", [128, 16, 4], F32)
            pg = [ps(p0, f"pg{i}", [128, 512], F32) for i in range(3)]
            b0 = Buf()
            bpm = Buf()
            bpg = [Buf() for _ in range(3)]
            S.dma("dma_start", dict(out=ct[:], in_=CTT[:, :]), writes=[b0])
            S.dma("dma_start", dict(out=gpre[:], in_=GPRE[:, :]), writes=[b0])
            S.dma("dma_start", dict(out=bada[:], in_=BADA[:, :]), writes=[b0])
            S.dma("dma_start", dict(out=bgate[:], in_=BGATE[0:1, :].broadcast_to([128, D])), writes=[b0])
            S.dma("dma_start", dict(out=gpost[:], in_=GPOST[0:1, :].broadcast_to([128, D])), writes=[b0])
            S.op(ACT, "activation", dict(out=th[:], in_=ct[:], func=AF.Tanh, scale=0.5), reads=[b0], writes=[b0])
            S.op(DVE, "scalar_tensor_tensor", dict(out=th[:], in0=th[:], scalar=1.0, in1=ct[:], op0=ALU.add, op1=ALU.mult), reads=[b0], writes=[b0])
            S.op(POOL, "memset", dict(ap=scT[:], constant=0.0), writes=[b0])
            S.op(POOL, "memset", dict(ap=shT[:], constant=0.0), writes=[bmod])
            S.op(POOL, "memset", dict(ap=gsT[:], constant=0.0), writes=[bmod])
            S.op(DVE, "tensor_scalar", dict(out=scT[:, :, 0:3], in0=th[:].rearrange("p (a b) -> p a b", b=3), scalar1=0.5, scalar2=None, op0=ALU.mult), reads=[b0], writes=[b0])
            for i in range(24):
                kc, s = divmod(i, 3)
                S.op(DVE, "tensor_scalar", dict(out=screp[:, i, :], in0=ones_f[:], scalar1=scT[:, kc, s:s + 1], scalar2=None, op0=ALU.mult), reads=[b0, bconst], writes=[b0])
            for pc in range(6):
                k = pc % 2
                S.dma("dma_start", dict(out=stg[k][:], in_=W_ADA[:, pc * 512:(pc + 1) * 512].rearrange("(kc p) n -> p kc n", p=128)), writes=[bstg[k]])
                if pc < 4:
                    for fb in range(4):
                        blk = pc * 4 + fb
                        for kc in range(8):
                            S.op(PE, "matmul", dict(out=pmod[:, blk, :], lhsT=stg[k][:, kc, fb * 128:(fb + 1) * 128], rhs=scT[:, kc, :], start=(kc == 0), stop=(kc == 7)), reads=[bstg[k], b0], writes=[bpm])
                else:
                    hf = pc - 4
                    for s in range(3):
                        for h2 in range(2):
                            for kc in range(8):
                                S.op(PE, "matmul", dict(out=pg[s][:, h2 * 256:(h2 + 1) * 256], lhsT=screp[:, kc * 3 + s, :], rhs=stg[k][:, kc, h2 * 256:(h2 + 1) * 256], start=(kc == 0), stop=(kc == 7)), reads=[bstg[k], b0], writes=[bpg[s]])
                        S.op(DVE, "tensor_tensor", dict(out=tmpg[:], in0=pg[s][:], in1=bgate[:, hf * 512:(hf + 1) * 512], op=ALU.add), reads=[bpg[s], b0], writes=[b0])
                        S.op(DVE, "tensor_tensor", dict(out=gg[s][:, hf * 512:(hf + 1) * 512], in0=tmpg[:], in1=gpost[:, hf * 512:(hf + 1) * 512], op=ALU.mult), reads=[b0], writes=[bgg[s]])
            for s in range(3):
                S.op(DVE, "tensor_tensor", dict(out=shT[:, :, s], in0=pmod[:, 0:8, s], in1=bada[:, 0:8], op=ALU.add), reads=[bpm, b0], writes=[bmod])
                S.op(DVE, "scalar_tensor_tensor", dict(out=tmpm[:], in0=pmod[:, 8:16, s], scalar=1.0, in1=bada[:, 8:16], op0=ALU.add, op1=ALU.add), reads=[bpm, b0], writes=[b0])
                S.op(DVE, "tensor_tensor", dict(out=gsT[:, :, s], in0=tmpm[:], in1=gpre[:], op=ALU.mult), reads=[b0], writes=[bmod])
            for pc in range(2):
                k = pc % 2
                S.dma("dma_start", dict(out=stg[k][:], in_=W_OUT[:, pc * 512:(pc + 1) * 512].rearrange("(kc p) n -> p kc n", p=128)), writes=[bstg[k]])
                S.op(DVE, "tensor_copy", dict(out=wout[:, :, pc * 512:(pc + 1) * 512], in_=stg[k][:]), reads=[bstg[k]], writes=[bwout])
            S.barrier()

        for job in [JOBS[j_] for j_ in jobs] if stop_after != 'p0' else []:
            s, sq, skv, sext = job["s"], job["sq"], job["skv"], job["sext"]
            am = am0 if job["am"] is AM0 else am1
            with contextlib.ExitStack() as p1:
                wp = sb(p1, nm("wp"), [128, 8, WCOLS], BF16)
                biasT = sb(p1, nm("biasT"), [128, 40], F32)
                bwp = Buf()
                bbias = Buf()
                with contextlib.ExitStack() as pw:
                    stg = [sb(pw, nm("stgw"), [128, 8, 512], F32) for _ in range(2)]
                    bstg = [Buf() for _ in range(2)]
                    pb = ps(pw, nm("pb"), [128, 40, 4], F32)
                    bpb = Buf()
                    pieces = [(W_IN, pc * 512, 512, pc * 512) for pc in range(8)] + [(W_ROT, 0, 384, 4096), (W_ROT, 384, 384, 4096 + 384)]
                    for pi_, (src, c0, ncol, dcol) in enumerate(pieces):
                        k = pi_ % 2
                        S.dma("dma_start", dict(out=stg[k][:, :, 0:ncol], in_=src[:, c0:c0 + ncol].rearrange("(kc p) n -> p kc n", p=128)), writes=[bstg[k]])
                        for kc in range(8):
                            eng = DVE if kc % 2 == 0 else POOL
                            S.op(eng, "tensor_scalar", dict(out=wp[:, kc, dcol:dcol + ncol], in0=stg[k][:, kc, 0:ncol], scalar1=gsT[:, kc, s:s + 1], scalar2=None, op0=ALU.mult), reads=[bstg[k], bmod], writes=[bwp])
                        if pi_ < 8:
                            cols = [(fb * 128, pi_ * 4 + fb) for fb in range(4)]
                        else:
                            t0 = (pi_ - 8) * 2
                            cols = [(0, 32 + 2 * t0), (64, 33 + 2 * t0), (192, 34 + 2 * t0), (256, 35 + 2 * t0)]
                        for (co, bcol) in cols:
                            for kc in range(8):
                                S.op(PE, "matmul", dict(out=pb[:, bcol, :], lhsT=stg[k][:, kc, co:co + 128], rhs=shT[:, kc, :], start=(kc == 0), stop=(kc == 7)), reads=[bstg[k], bmod], writes=[bpb])
                    S.op(DVE, "tensor_copy", dict(out=biasT[:], in_=pb[:, :, s]), reads=[bpb], writes=[bbias])
                    S.barrier()

                with contextlib.ExitStack() as pp:
                    xt = Ring([sb(pp, nm("xt"), [128, D], F32) for _ in range(3)])
                    junk = sb(pp, nm("junk"), [128, D], F32)
                    bjunk = Buf()
                    xn = Ring([sb(pp, nm("xn"), [128, D], BF16) for _ in range(2)])
                    xnT = Ring([sb(pp, nm("xnT"), [128, 8, 512], BF16) for _ in range(2)])
                    stat = Ring([sb(pp, nm("stat"), [128, 4], F32) for _ in range(4)])
                    ost = Ring([sb(pp, nm("ost"), [128, 4, 512], BF16) for _ in range(3)])
                    cst = Ring([sb(pp, nm("cst"), [128, 2, 512], F32) for _ in range(2)])
                    t12 = Ring([sb(pp, nm("t12"), [128, 2, 512], F32) for _ in range(2)])
                    rot = Ring([sb(pp, nm("rot"), [128, 512], BF16) for _ in range(2)])
                    pT = Ring([ps(pp, nm("pT"), [128, D], BF16) for _ in range(2)])
                    pacc = Ring([ps(pp, nm("pacc"), [128, 512], F32) for _ in range(5)])

                    G = dict(qa=(0, 0), ka=(512, 4), va=(1024, 8), ga=(1536, 12), qb=(2048, 16), kb=(2560, 20), vb=(3072, 24), gb=(3584, 28))
                    RT = dict(qa=0, ka=1, qb=2, kb=3)
                    segs = []
                    full_main = [("qa", "QA", 0), ("ka", "KA", PADK), ("va", "VA", PADK), ("ga", "GA", 0), ("qb", "QB", 0), ("kb", "KB", 0), ("vb", "VB", 0), ("gb", "GB", 0)]
                    full_rot = [("qa", "QAr", 0), ("ka", "KAr", PADK), ("qb", "QBr", 0), ("kb", "KBr", 0)]
                    segs.append((job["xrow"], sq, full_main, full_rot))
                    if s == 0:
                        segs.append((ROW_OTHER, HALF, [("kb", "KB", HALF), ("vb", "VB", HALF)], [("kb", "KBr", HALF)]))
                        segs.append((ROW_HALO, PADK, [("ka", "KA", 0), ("va", "VA", 0)], [("ka", "KAr", 0)]))
                        segs.append((ROW_HALO + PADK, PADK, [("ka", "KA", PADK + HALF), ("va", "VA", PADK + HALF)], [("ka", "KAr", PADK + HALF)]))
                    evi = [0]
                    for (xr0, T, mains, rots) in (segs if lvl >= 2 else []):
                        for ch in range(T // 512):
                            r0 = xr0 + ch * 512
                            xT_t, xT_b = xnT.next()
                            for i in range(4):
                                x_t, x_b = xt.next()
                                S.dma("dma_start", dict(out=x_t[:], in_=X[r0 + i * 128:r0 + (i + 1) * 128, :]), writes=[x_b])
                                st_t, st_b = stat.next()
                                S.op(ACT, "activation", dict(out=junk[:], in_=x_t[:], func=AF.Square, accum_out=st_t[:, 0:1]), reads=[x_b], writes=[bjunk, st_b])
                                S.op(DVE, "tensor_scalar", dict(out=st_t[:, 1:2], in0=st_t[:, 0:1], scalar1=1.0 / D, scalar2=EPS, op0=ALU.mult, op1=ALU.add), reads=[st_b], writes=[st_b])
                                S.op(POOL, "tensor_tensor", dict(out=st_t[:, 2:3], in0=st_t[:, 1:2], in1=mhalf[:], op=ALU.pow), reads=[st_b, bconst], writes=[st_b])
                                xn_t, xn_b = xn.next()
                                S.op(DVE, "tensor_scalar", dict(out=xn_t[:], in0=x_t[:], scalar1=st_t[:, 2:3], scalar2=None, op0=ALU.mult), reads=[x_b, st_b], writes=[xn_b])
                                pT_t, pT_b = pT.next()
                                for kc in range(8):
                                    S.op(PE, "transpose", dict(out=pT_t[:, kc * 128:(kc + 1) * 128], in_=xn_t[:, kc * 128:(kc + 1) * 128], identity=ident[:]), reads=[xn_b, bconst], writes=[pT_b])
                                S.op(DVE, "tensor_copy", dict(out=xT_t[:, :, i * 128:(i + 1) * 128], in_=pT_t[:].rearrange("p (a b) -> p a b", b=128)), reads=[pT_b], writes=[xT_b])
                            c_t, c_b = cst.next()
                            if rots:
                                S.dma("dma_start", dict(out=c_t[:, 0, :], in_=CT[:, r0:r0 + 512]), writes=[c_b])
                                S.dma("dma_start", dict(out=c_t[:, 1, :], in_=ST[:, r0:r0 + 512]), writes=[c_b])
                            for (gname, skey, dcol0) in mains:
                                wcol, bcol = G[gname]
                                o_t, o_b = ost.next()
                                for b in range(4):
                                    pa_t, pa_b = pacc.next()
                                    c0 = wcol + b * 128
                                    bc = bcol + b
                                    for kc in range(8):
                                        S.op(PE, "matmul", dict(out=pa_t[:], lhsT=wp[:, kc, c0:c0 + 128], rhs=xT_t[:, kc, :], start=(kc == 0), stop=(kc == 7)), reads=[bwp, xT_b], writes=[pa_b])
                                    evi[0] += 1
                                    if evi[0] % 4 != 0:
                                        S.op(ACT, "activation", dict(out=o_t[:, b, :], in_=pa_t[:], func=AF.Identity, bias=biasT[:, bc:bc + 1]), reads=[pa_b, bbias], writes=[o_b])
                                    else:
                                        S.op(DVE, "tensor_scalar", dict(out=o_t[:, b, :], in0=pa_t[:], scalar1=biasT[:, bc:bc + 1], scalar2=None, op0=ALU.add), reads=[pa_b, bbias], writes=[o_b])
                                dc = dcol0 + ch * 512
                                S.dma("dma_start", dict(out=SC[skey][:, dc:dc + 512].rearrange("(b p) t -> p b t", p=128), in_=o_t[:]), reads=[o_b], writes=[SCB[skey]])
                            for (gname, skey, dcol0) in rots:
                                ti = RT[gname]
                                wc = 4096 + ti * 192
                                bc = 32 + 2 * ti
                                p1_t, p1_b = pacc.next()
                                p2_t, p2_b = pacc.next()
                                for kc in range(8):
                                    S.op(PE, "matmul", dict(out=p1_t[:], lhsT=wp[:, kc, wc:wc + 128], rhs=xT_t[:, kc, :], start=(kc == 0), stop=(kc == 7)), reads=[bwp, xT_b], writes=[p1_b])
                                for kc in range(8):
                                    S.op(PE, "matmul", dict(out=p2_t[:], lhsT=wp[:, kc, wc + 64:wc + 192], rhs=xT_t[:, kc, :], start=(kc == 0), stop=(kc == 7)), reads=[bwp, xT_b], writes=[p2_b])
                                t_t, t_b = t12.next()
                                S.op(DVE, "scalar_tensor_tensor", dict(out=t_t[:, 0, :], in0=p1_t[:], scalar=biasT[:, bc:bc + 1], in1=c_t[:, 0, :], op0=ALU.add, op1=ALU.mult), reads=[p1_b, c_b, bbias], writes=[t_b])
                                S.op(DVE, "scalar_tensor_tensor", dict(out=t_t[:, 1, :], in0=p2_t[:], scalar=biasT[:, bc + 1:bc + 2], in1=c_t[:, 1, :], op0=ALU.add, op1=ALU.mult), reads=[p2_b, c_b, bbias], writes=[t_b])
                                r_t, r_b = rot.next()
                                S.op(POOL, "tensor_tensor", dict(out=r_t[:], in0=t_t[:, 0, :], in1=t_t[:, 1, :], op=ALU.add), reads=[t_b], writes=[r_b])
                                dc = dcol0 + ch * 512
                                S.dma("dma_start", dict(out=SC[skey][:, dc:dc + 512], in_=r_t[:]), reads=[r_b], writes=[SCB[skey]])
                    S.barrier()

            with contextlib.ExitStack() as pB:
                nkt = skv // 128
                nqc = sq // 512
                kbT = [sb(pB, nm("kbT"), [128, skv], BF16) for _ in range(2)]
                vbT = [sb(pB, nm("vbT"), [128, skv], BF16) for _ in range(2)]
                qbT = [sb(pB, nm("qbT"), [128, sq], BF16) for _ in range(2)]
                gbT = [sb(pB, nm("gbT"), [128, sq], BF16) for _ in range(2)]
                bk, bv, bq, bg = ([Buf(), Buf()] for _ in range(4))
                vtok = sb(pB, nm("vtok"), [128, nkt, 128], BF16)
                bvt = Buf()
                Pr = Ring([sb(pB, nm("P"), [128, 1024], BF16) for _ in range(6)])
                asum = [sb(pB, nm("asum"), [128, 1024], F32) for _ in range(2)]
                basum = [Buf() for _ in range(2)]
                oS = [[sb(pB, nm("oS"), [128, 512], F32) for _ in range(2)] for _ in range(2)]
                boS = [[Buf() for _ in range(2)] for _ in range(2)]
                thr = Ring([sb(pB, nm("thb"), [128, 512], F32) for _ in range(2)])
                sgr = Ring([sb(pB, nm("sgb"), [128, 512], BF16) for _ in range(3)])
                yTr = Ring([sb(pB, nm("yTb"), [128, 512], BF16) for _ in range(3)])
                rsr = Ring([sb(pB, nm("rs"), [128, 16], F32) for _ in range(3)])
                rr = Ring([sb(pB, nm("rr"), [128, 8], F32) for _ in range(6)])
                t0r = Ring([sb(pB, nm("t0"), [128, 128], F32) for _ in range(3)])
                orr = Ring([sb(pB, nm("o"), [128, 128], F32) for _ in range(5)])
                onr = Ring([sb(pB, nm("on"), [128, 128], BF16) for _ in range(5)])
                junkb = sb(pB, nm("junkb"), [128, 128], F32)
                bjb = Buf()
                scr_ = Ring([ps(pB, nm("sc"), [128, 1024], F32) for _ in range(2)])
                oT = [ps(pB, nm("oT"), [128, 512], F32) for _ in range(2)]
                boT = [Buf() for _ in range(2)]
                tpb = ps(pB, nm("tpb"), [128, 512], F32)
                btp = Buf()
                pTv = ps(pB, nm("pTv"), [128, 4, 128], BF16)
                bpv = Buf()

                def load_head(g):
                    hb = g % 2
                    for (dst, dbuf, main, rotk, ncols) in ((kbT[hb], bk[hb], "KB", "KBr", skv), (qbT[hb], bq[hb], "QB", "QBr", sq)):
                        for m in range(2):
                            h = 2 * g + m
                            S.dma("dma_start", dict(out=dst[m * 64:m * 64 + 48, 0:ncols], in_=SC[main][g * 128 + m * 64 + 16:g * 128 + m * 64 + 64, 0:ncols]), reads=[SCB[main]], writes=[dbuf])
                            S.dma("dma_start", dict(out=dst[m * 64 + 48:m * 64 + 56, 0:ncols], in_=SC[rotk][h * 8:h * 8 + 8, 0:ncols]), reads=[SCB[rotk]], writes=[dbuf])
                            S.dma("dma_start", dict(out=dst[m * 64 + 56:m * 64 + 64, 0:ncols], in_=SC[rotk][64 + h * 8:64 + h * 8 + 8, 0:ncols]), reads=[SCB[rotk]], writes=[dbuf])
                    S.dma("dma_start", dict(out=vbT[hb][:], in_=SC["VB"][g * 128:(g + 1) * 128, 0:skv]), reads=[SCB["VB"]], writes=[bv[hb]])
                    S.dma("dma_start", dict(out=gbT[hb][:], in_=SC["GB"][g * 128:(g + 1) * 128, 0:sq]), reads=[SCB["GB"]], writes=[bg[hb]])

                def make_epilogue(g, q0, par, sg_t, sg_b):
                    steps = []
                    rs_t, rs_b = rsr.next()
                    y_t, y_b = yTr.next()
                    state = {}

                    def rowsums():
                        for m in range(2):
                            for qb in range(4):
                                c = 256 + 2 * (m * 4 + qb)
                                S.op(PE, "matmul", dict(out=tpb[:, c:c + 2], lhsT=asum[par][:, m * 512 + qb * 128:m * 512 + (qb + 1) * 128], rhs=ones_f[:, 0:2], start=True, stop=True), reads=[basum[par], bconst], writes=[btp])
                        S.op(DVE, "reciprocal", dict(out=rs_t[:, 0:8], in_=tpb[:, 256:272:2]), reads=[btp], writes=[rs_b])
                        S.op(DVE, "tensor_scalar", dict(out=rs_t[:, 8:12], in0=rs_t[:, 4:8], scalar1=neglam[:, 0:1], scalar2=None, op0=ALU.mult), reads=[rs_b, bconst], writes=[rs_b])
                    steps.append(rowsums)

                    def block_a(qb):
                        def f():
                            e_t, e_b = tpb, btp
                            for m in range(2):
                                S.op(PE, "transpose", dict(out=e_t[:, m * 128:(m + 1) * 128], in_=oS[par][m][:, qb * 128:(qb + 1) * 128], identity=ident_f[:]), reads=[boS[par][m], bconst], writes=[e_b])
                            r_t, r_b = rr.next()
                            t0_t, t0_b = t0r.next()
                            S.op(DVE, "tensor_scalar", dict(out=t0_t[:], in0=e_t[:, 0:128], scalar1=rs_t[:, qb:qb + 1], scalar2=None, op0=ALU.mult), reads=[e_b, rs_b], writes=[t0_b])
                            o_t, o_b = orr.next()
                            S.op(DVE, "scalar_tensor_tensor", dict(out=o_t[:], in0=e_t[:, 128:256], scalar=rs_t[:, 8 + qb:9 + qb], in1=t0_t[:], op0=ALU.mult, op1=ALU.add), reads=[e_b, rs_b, t0_b], writes=[o_b])
                            S.op(ACT, "activation", dict(out=junkb[:], in_=o_t[:], func=AF.Square, accum_out=r_t[:, 3:4]), reads=[o_b], writes=[bjb, r_b])
                            S.op(DVE, "tensor_scalar", dict(out=r_t[:, 4:5], in0=r_t[:, 3:4], scalar1=1.0 / 128, scalar2=EPS, op0=ALU.mult, op1=ALU.add), reads=[r_b], writes=[r_b])
                            S.op(POOL, "tensor_tensor", dict(out=r_t[:, 5:6], in0=r_t[:, 4:5], in1=mhalf[:], op=ALU.pow), reads=[r_b, bconst], writes=[r_b])
                            on_t, on_b = onr.next()
                            S.op(ACT, "activation", dict(out=on_t[:], in_=o_t[:], func=AF.Identity, scale=r_t[:, 5:6]), reads=[o_b, r_b], writes=[on_b])
                            state[qb] = (on_t, on_b)
                        return f

                    def block_b(qb, last):
                        def f():
                            on_t, on_b = state[qb]
                            ev = pTv[:, 0, :]
                            S.op(PE, "transpose", dict(out=ev, in_=on_t[:], identity=ident[:]), reads=[on_b, bconst], writes=[bpv])
                            S.op(DVE, "scalar_tensor_tensor", dict(out=y_t[:, qb * 128:(qb + 1) * 128], in0=ev, scalar=gsub2[:, 0:1], in1=sg_t[:, qb * 128:(qb + 1) * 128], op0=ALU.mult, op1=ALU.mult), reads=[bpv, sg_b, bconst], writes=[y_b])
                            if last:
                                S.dma("dma_start", dict(out=SC["YM"][512 + g * 128:512 + (g + 1) * 128, q0:q0 + 512], in_=y_t[:]), reads=[y_b], writes=[SCB["YM"]])
                        return f

                    for kind, qb in [("a", 0), ("a", 1), ("a", 2), ("b", 0), ("a", 3), ("b", 1), ("b", 2), ("b", 3)]:
                        steps.append(block_a(qb) if kind == "a" else block_b(qb, qb == 3))
                    return steps

                pending = []
                heads = list(range(4)) if lvl >= 3 else []
                if heads:
                    load_head(0)
                for g in heads:
                    hb = g % 2
                    if g + 1 < 4:
                        load_head(g + 1)
                    for k4 in range(nkt // 4):
                        for j in range(4):
                            kt = k4 * 4 + j
                            S.op(PE, "transpose", dict(out=pTv[:, j, :], in_=vbT[hb][:, kt * 128:(kt + 1) * 128], identity=ident[:]), reads=[bv[hb], bconst], writes=[bpv])
                        S.op(ACT, "activation", dict(out=vtok[:, k4 * 4:k4 * 4 + 4, :], in_=pTv[:], func=AF.Copy), reads=[bpv], writes=[bvt])
                    for qc in range(nqc):
                        q0 = qc * 512
                        par = qc % 2
                        th_t, th_b = thr.next()
                        sg_t, sg_b = sgr.next()
                        S.op(ACT, "activation", dict(out=th_t[:], in_=gbT[hb][:, q0:q0 + 512], func=AF.Tanh, scale=0.5), reads=[bg[hb]], writes=[th_b])
                        S.op(DVE, "scalar_tensor_tensor", dict(out=sg_t[:], in0=th_t[:], scalar=1.0, in1=gbT[hb][:, q0:q0 + 512], op0=ALU.add, op1=ALU.mult), reads=[th_b, bg[hb]], writes=[sg_b])

                        def scores(kt):
                            sc_t, sc_b = scr_.next()
                            for m in range(2):
                                S.op(PE, "matmul", dict(out=sc_t[:, m * 512:(m + 1) * 512], lhsT=kbT[hb][m * 64:(m + 1) * 64, kt * 128:(kt + 1) * 128], rhs=qbT[hb][m * 64:(m + 1) * 64, q0:q0 + 512], start=True, stop=True), reads=[bk[hb], bq[hb]], writes=[sc_b])
                            p_t, p_b = Pr.next()
                            S.op(ACT, "activation", dict(out=p_t[:], in_=sc_t[:], func=AF.Exp, scale=0.125), reads=[sc_b], writes=[p_b])
                            return p_t, p_b

                        def av(kt, p_t, p_b):
                            for m in range(2):
                                S.op(PE, "matmul", dict(out=oT[m][:], lhsT=vtok[:, kt, :], rhs=p_t[:, m * 512:(m + 1) * 512], start=(kt == 0), stop=(kt == nkt - 1)), reads=[p_b, bvt], writes=[boT[m]])
                            if kt == 0:
                                S.op(DVE, "tensor_copy", dict(out=asum[par][:], in_=p_t[:]), reads=[p_b], writes=[basum[par]])
                            else:
                                S.op(DVE, "tensor_tensor", dict(out=asum[par][:], in0=asum[par][:], in1=p_t[:], op=ALU.add), reads=[p_b, basum[par]], writes=[basum[par]])

                        ptiles = {0: scores(0)}
                        for kt in range(nkt):
                            if kt + 1 < nkt:
                                ptiles[kt + 1] = scores(kt + 1)
                            if kt >= 1:
                                av(kt - 1, *ptiles.pop(kt - 1))
                            if pending and kt >= 2 and kt % 2 == 0:
                                pending.pop(0)()
                        av(nkt - 1, *ptiles.pop(nkt - 1))
                        for m in range(2):
                            S.op(ACT, "activation", dict(out=oS[par][m][:], in_=oT[m][:], func=AF.Copy), reads=[boT[m]], writes=[boS[par][m]])
                        while pending:
                            pending.pop(0)()
                        pending.extend(make_epilogue(g, q0, par, sg_t, sg_b))
                while pending:
                    pending.pop(0)()
                S.barrier()

            with contextlib.ExitStack() as pA:
                kaT = sb(pA, nm("kaT"), [128, sext], BF16)
                vaT = sb(pA, nm("vaT"), [128, sext], BF16)
                qaT = sb(pA, nm("qaT"), [128, sq], BF16)
                gaT = sb(pA, nm("gaT"), [128, sq], BF16)
                acc = sb(pA, nm("accA"), [65, 2, sq], F32)
                bk, bv, bq, bg, bacc = Buf(), Buf(), Buf(), Buf(), Buf()
                ones3 = sb(pA, nm("ones3"), [128, 2, 1], F32)
                bo3 = Buf()
                Vt = Ring([sb(pA, nm("Vt"), [128, 2, 65], BF16) for _ in range(3)])
                Pt = Ring([sb(pA, nm("Pt"), [128, 2, 256], BF16) for _ in range(3)])
                Pm = Ring([sb(pA, nm("Pm"), [128, 2, 256], BF16) for _ in range(3)])
                recr = Ring([sb(pA, nm("rec"), [128, 512], F32) for _ in range(2)])
                thr = Ring([sb(pA, nm("tha"), [128, 512], F32) for _ in range(2)])
                sgr = Ring([sb(pA, nm("sga"), [128, 512], F32) for _ in range(2)])
                tnr = Ring([sb(pA, nm("tn"), [128, 512], F32) for _ in range(2)])
                yTr = Ring([sb(pA, nm("yTa"), [128, 512], BF16) for _ in range(2)])
                scA = Ring([ps(pA, nm("scA"), [128, 2, 512], F32) for _ in range(2)])
                opsr = Ring([ps(pA, nm("ops"), [65, 2, 256], F32) for _ in range(2)])
                pTa = Ring([ps(pA, nm("pTa"), [128, 128], BF16) for _ in range(2)])
                S.op(POOL, "memset", dict(ap=ones3[:], constant=1.0), writes=[bo3])
                tiles = amask_tiles(sq)
                for hp in (range(4) if lvl >= 4 else []):
                    if s != 0:
                        for tl, bb in ((kaT, bk), (vaT, bv)):
                            S.op(POOL, "memset", dict(ap=tl[:, 0:PADK], constant=0.0), writes=[bb])
                            S.op(POOL, "memset", dict(ap=tl[:, PADK + sq:sext], constant=0.0), writes=[bb])
                        kc0, kn = PADK, sq
                    else:
                        kc0, kn = 0, sext
                    for hl in range(2):
                        h = 2 * hp + hl
                        S.dma("dma_start", dict(out=kaT[hl * 64:hl * 64 + 48, kc0:kc0 + kn], in_=SC["KA"][hp * 128 + hl * 64 + 16:hp * 128 + hl * 64 + 64, kc0:kc0 + kn]), reads=[SCB["KA"]], writes=[bk])
                        S.dma("dma_start", dict(out=kaT[hl * 64 + 48:hl * 64 + 56, kc0:kc0 + kn], in_=SC["KAr"][h * 8:h * 8 + 8, kc0:kc0 + kn]), reads=[SCB["KAr"]], writes=[bk])
                        S.dma("dma_start", dict(out=kaT[hl * 64 + 56:hl * 64 + 64, kc0:kc0 + kn], in_=SC["KAr"][64 + h * 8:64 + h * 8 + 8, kc0:kc0 + kn]), reads=[SCB["KAr"]], writes=[bk])
                        S.dma("dma_start", dict(out=qaT[hl * 64:hl * 64 + 48, :], in_=SC["QA"][hp * 128 + hl * 64 + 16:hp * 128 + hl * 64 + 64, 0:sq]), reads=[SCB["QA"]], writes=[bq])
                        S.dma("dma_start", dict(out=qaT[hl * 64 + 48:hl * 64 + 56, :], in_=SC["QAr"][h * 8:h * 8 + 8, 0:sq]), reads=[SCB["QAr"]], writes=[bq])
                        S.dma("dma_start", dict(out=qaT[hl * 64 + 56:hl * 64 + 64, :], in_=SC["QAr"][64 + h * 8:64 + h * 8 + 8, 0:sq]), reads=[SCB["QAr"]], writes=[bq])
                    S.dma("dma_start", dict(out=vaT[:, kc0:kc0 + kn], in_=SC["VA"][hp * 128:(hp + 1) * 128, kc0:kc0 + kn]), reads=[SCB["VA"]], writes=[bv])
                    S.dma("dma_start", dict(out=gaT[:], in_=SC["GA"][hp * 128:(hp + 1) * 128, 0:sq]), reads=[SCB["GA"]], writes=[bg])
                    S.op(POOL, "memset", dict(ap=acc[:], constant=0.0), writes=[bacc])
                    for ti, (pi, dil, r, t, nblk, je0) in enumerate(tiles):
                        e0 = r + dil * (je0 + 128 * t)
                        ksl = slice(e0, e0 + dil * 127 + 1, dil)
                        mlo, mhi = max(t - 1, 0), min(t, nblk - 1)
                        nq = (mhi - mlo + 1) * 128
                        qs = r + dil * 128 * mlo
                        qsl = slice(qs, qs + dil * (nq - 1) + 1, dil)
                        boff = 0 if t >= 1 else 128
                        pa_t, pa_b = pTa.next()
                        S.op(PE, "transpose", dict(out=pa_t[:], in_=vaT[:, ksl], identity=ident[:]), reads=[bv, bconst], writes=[pa_b])
                        v_t, v_b = Vt.next()
                        S.op(DVE, "tensor_scalar", dict(out=v_t[:, :, 0:64], in0=pa_t[:].rearrange("p (a b) -> p a b", b=64), scalar1=am[:, ti:ti + 1], scalar2=None, op0=ALU.mult), reads=[pa_b, bconst], writes=[v_b])
                        S.op(POOL, "tensor_scalar", dict(out=v_t[:, :, 64:65], in0=ones3[:], scalar1=am[:, ti:ti + 1], scalar2=None, op0=ALU.mult), reads=[bo3, bconst], writes=[v_b])
                        sc_t, sc_b = scA.next()
                        for hl in range(2):
                            S.op(PE, "matmul", dict(out=sc_t[:, hl, 0:nq], lhsT=kaT[hl * 64:(hl + 1) * 64, ksl], rhs=qaT[hl * 64:(hl + 1) * 64, qsl], start=True, stop=True), reads=[bk, bq], writes=[sc_b])
                        p_t, p_b = Pt.next()
                        S.op(ACT, "activation", dict(out=p_t[:, :, 0:nq], in_=sc_t[:, :, 0:nq], func=AF.Exp, scale=0.125), reads=[sc_b], writes=[p_b])
                        m_t, m_b = Pm.next()
                        S.op(POOL, "tensor_tensor", dict(out=m_t[:, :, 0:nq], in0=p_t[:, :, 0:nq], in1=band2[:, :, boff:boff + nq], op=ALU.mult), reads=[p_b, bconst], writes=[m_b])
                        o_t, o_b = opsr.next()
                        for hl in range(2):
                            for bi in range(mhi - mlo + 1):
                                S.op(PE, "matmul", dict(out=o_t[:, hl, bi * 128:(bi + 1) * 128], lhsT=v_t[:, hl, 0:65], rhs=m_t[:, hl, bi * 128:(bi + 1) * 128], start=True, stop=True), reads=[v_b, m_b], writes=[o_b])
                        for hl in range(2):
                            S.op(DVE, "tensor_tensor", dict(out=acc[:, hl, qsl], in0=o_t[:, hl, 0:nq], in1=acc[:, hl, qsl], op=ALU.add), reads=[o_b, bacc], writes=[bacc])
                    for qc in range(sq // 512):
                        q0 = qc * 512
                        d_t, d_b = scA.next()
                        n_b = d_b
                        dv = d_t[:, 0, :]
                        nv = d_t[:, 1, :]
                        for h2 in range(2):
                            qa_, qb_ = q0 + h2 * 256, q0 + (h2 + 1) * 256
                            for hl in range(2):
                                S.op(PE, "matmul", dict(out=dv[:, h2 * 256:(h2 + 1) * 256], lhsT=sels[:, hl, :], rhs=acc[:, hl, qa_:qb_], start=(hl == 0), stop=(hl == 1)), reads=[bacc, bconst], writes=[d_b])
                        for h2 in range(2):
                            qa_, qb_ = q0 + h2 * 256, q0 + (h2 + 1) * 256
                            for hl in range(2):
                                S.op(PE, "matmul", dict(out=nv[:, h2 * 256:(h2 + 1) * 256], lhsT=sels[:, 2 + hl, :], rhs=acc[:, hl, qa_:qb_], start=(hl == 0), stop=(hl == 1)), reads=[bacc, bconst], writes=[n_b])
                        rc_t, rc_b = recr.next()
                        S.op(DVE, "reciprocal", dict(out=rc_t[:], in_=dv), reads=[d_b], writes=[rc_b])
                        th_t, th_b = thr.next()
                        S.op(ACT, "activation", dict(out=th_t[:], in_=gaT[:, q0:q0 + 512], func=AF.Tanh, scale=0.5), reads=[bg], writes=[th_b])
                        sg_t, sg_b = sgr.next()
                        S.op(POOL, "tensor_scalar", dict(out=sg_t[:], in0=th_t[:], scalar1=1.0, scalar2=None, op0=ALU.add), reads=[th_b], writes=[sg_b])
                        S.op(POOL, "tensor_tensor", dict(out=sg_t[:], in0=sg_t[:], in1=gaT[:, q0:q0 + 512], op=ALU.mult), reads=[sg_b, bg], writes=[sg_b])
                        tn_t, tn_b = tnr.next()
                        S.op(DVE, "tensor_tensor", dict(out=tn_t[:], in0=nv, in1=rc_t[:], op=ALU.mult), reads=[n_b, rc_b], writes=[tn_b])
                        y_t, y_b = yTr.next()
                        S.op(POOL, "tensor_tensor", dict(out=y_t[:], in0=tn_t[:], in1=sg_t[:], op=ALU.mult), reads=[tn_b, sg_b], writes=[y_b])
                        S.dma("dma_start", dict(out=SC["YM"][hp * 128:(hp + 1) * 128, q0:q0 + 512], in_=y_t[:]), reads=[y_b], writes=[SCB["YM"]])
                S.barrier()

            with contextlib.ExitStack() as p3:
                ymr = Ring([sb(p3, nm("ym"), [128, 8, 512], BF16) for _ in range(2)])
                xr = Ring([sb(p3, nm("x3"), [128, D], F32) for _ in range(2)])
                tmr = Ring([sb(p3, nm("tm3"), [128, D], F32) for _ in range(2)])
                outr = Ring([sb(p3, nm("o3"), [128, D], F32) for _ in range(2)])
                st3 = Ring([sb(p3, nm("st3"), [128, 6], F32) for _ in range(4)])
                junk3 = sb(p3, nm("junk3"), [128, 512], F32)
                bj3 = Buf()
                po = Ring([ps(p3, nm("po"), [128, 2, 512], F32) for _ in range(2)])
                for qc in (range(sq // 512) if lvl >= 5 else []):
                    q0 = qc * 512
                    ym_t, ym_b = ymr.next()
                    S.dma("dma_start", dict(out=ym_t[:], in_=SC["YM"][:, q0:q0 + 512].rearrange("(kc p) t -> p kc t", p=128)), reads=[SCB["YM"]], writes=[ym_b])
                    for i in range(4):
                        row = q0 + i * 128
                        x_t, x_b = xr.next()
                        S.dma("dma_start", dict(out=x_t[:], in_=X[job["xrow"] + row:job["xrow"] + row + 128, :]), writes=[x_b])
                        po_t, po_b = po.next()
                        for hf in range(2):
                            for kc in range(8):
                                S.op(PE, "matmul", dict(out=po_t[:, hf, :], lhsT=ym_t[:, kc, i * 128:(i + 1) * 128], rhs=wout[:, kc, hf * 512:(hf + 1) * 512], start=(kc == 0), stop=(kc == 7)), reads=[ym_b, bwout], writes=[po_b])
                        s_t, s_b = st3.next()
                        for hf in range(2):
                            S.op(ACT, "activation", dict(out=junk3[:], in_=po_t[:, hf, :], func=AF.Square, accum_out=s_t[:, hf:hf + 1]), reads=[po_b], writes=[bj3, s_b])
                        S.op(DVE, "tensor_tensor", dict(out=s_t[:, 2:3], in0=s_t[:, 0:1], in1=s_t[:, 1:2], op=ALU.add), reads=[s_b], writes=[s_b])
                        S.op(DVE, "tensor_scalar", dict(out=s_t[:, 3:4], in0=s_t[:, 2:3], scalar1=1.0 / D, scalar2=EPS, op0=ALU.mult, op1=ALU.add), reads=[s_b], writes=[s_b])
                        S.op(POOL, "tensor_tensor", dict(out=s_t[:, 4:5], in0=s_t[:, 3:4], in1=mhalf[:], op=ALU.pow), reads=[s_b, bconst], writes=[s_b])
                        tm_t, tm_b = tmr.next()
                        for hf in range(2):
                            S.op(DVE, "scalar_tensor_tensor", dict(out=tm_t[:, hf * 512:(hf + 1) * 512], in0=po_t[:, hf, :], scalar=s_t[:, 4:5], in1=gg[s][:, hf * 512:(hf + 1) * 512], op0=ALU.mult, op1=ALU.mult), reads=[po_b, s_b, bgg[s]], writes=[tm_b])
                        o_t, o_b = outr.next()
                        S.op(POOL, "tensor_tensor", dict(out=o_t[:], in0=tm_t[:], in1=x_t[:], op=ALU.add), reads=[tm_b, x_b], writes=[o_b])
                        yrow = job["yrow"] + row
                        out_dmas.append(S.dma("dma_start", dict(out=Y[yrow:yrow + 128, :], in_=o_t[:]), reads=[o_b], writes=[Buf()]))
                S.barrier()

        S.run(final_wait=out_dmas)
    return nc


_NC_CACHE = {}


def _rope_tables(pos):
    half = 8
    inv = (np.float32(500000.0) ** (-np.arange(half, dtype=np.float32) / np.float32(half))).astype(np.float32)
    ang = pos.astype(np.float32)[None, :] * inv[:, None]
    cos = np.cos(ang).astype(np.float32)
    sin = np.sin(ang).astype(np.float32)
    idx = np.arange(128) % 8
    C = cos[idx]
    Sg = sin[idx].copy()
    Sg[:64] *= -1.0
    return np.ascontiguousarray(C), np.ascontiguousarray(Sg)


def _amask(valid_ext, sq):
    tiles = amask_tiles(sq)
    m = np.zeros((128, len(tiles)), np.float32)
    p = np.arange(128)
    for ti, (pi, dil, r, t, nblk, je0) in enumerate(tiles):
        e = r + dil * (je0 + 128 * t + p)
        m[:, ti] = valid_ext[e]
    return m


def prep_inputs(x_prompt, x_sample, c_prompt, c_sample, w_in, w_out, g_pre, g_post,
                w_ada, b_ada, lam_q1, lam_k1, lam_q2, lam_k2, g_sub):
    f32 = np.float32
    x_prompt = np.asarray(x_prompt, f32)
    x_sample = np.asarray(x_sample, f32)
    c_prompt = np.asarray(c_prompt, f32)
    c_sample = np.asarray(c_sample, f32)
    w_in0 = np.ascontiguousarray(np.asarray(w_in, f32)[0])
    w_out0 = np.ascontiguousarray(np.asarray(w_out, f32)[0])
    w_ada0 = np.ascontiguousarray(np.asarray(w_ada, f32)[0])
    b_ada0 = np.asarray(b_ada, f32)[0]
    g_pre0 = np.asarray(g_pre, f32)[0]
    g_post0 = np.asarray(g_post, f32)[0]
    g_sub0 = np.asarray(g_sub, f32)[0]

    offs = dict(qa=0, ka=512, qb=2048, kb=2560)
    w_rot = np.zeros((D, 4, 192), f32)
    for ti, nme in enumerate(("qa", "ka", "qb", "kb")):
        x1 = np.array([offs[nme] + h * 64 + i for h in range(8) for i in range(8)])
        x2 = x1 + 8
        w_rot[:, ti, 0:64] = w_in0[:, x1]
        w_rot[:, ti, 64:128] = w_in0[:, x2]
        w_rot[:, ti, 128:192] = w_in0[:, x1]
    w_rot = np.ascontiguousarray(w_rot.reshape(D, 768))

    g_pre_t = np.ascontiguousarray(g_pre0.reshape(8, 128).T)
    b_ada_t = np.ascontiguousarray(b_ada0[:2048].reshape(16, 128).T)
    b_gate = np.ascontiguousarray(b_ada0[2048:3072].reshape(1, D))
    g_post_r = np.ascontiguousarray(g_post0.reshape(1, D))
    g_sub_t = np.ascontiguousarray(g_sub0.reshape(128, 1))
    lam_v = np.ascontiguousarray(np.stack([np.asarray(v, f32)[0] for v in (lam_q1, lam_k1, lam_q2, lam_k2)], 0))
    ident = np.eye(128, dtype=f32)
    p = np.arange(128)[:, None]
    f = np.arange(128)[None, :]
    band = np.concatenate([(p <= f), (p >= f)], axis=1).astype(f32)
    band = np.ascontiguousarray(np.concatenate([band, band], axis=1))
    sels = np.zeros((65, 4, 128), f32)
    sels[64, 0, 0:64] = 2.0
    sels[64, 1, 64:128] = 2.0
    sels[np.arange(64), 2, np.arange(64)] = 1.0
    sels[np.arange(64), 3, 64 + np.arange(64)] = 1.0
    sels = np.ascontiguousarray(sels.reshape(65, 512))
    valid1 = np.zeros(DSEQ + 2 * PADK, f32)
    valid1[PADK:PADK + DSEQ] = 1.0
    am1 = _amask(valid1, DSEQ)

    in_maps = []
    for c in range(8):
        psq, hf = c // 2, c % 2
        q0 = hf * HALF
        o0 = (1 - hf) * HALF
        xa = np.zeros((NROWS, D), f32)
        pos = np.zeros(NROWS, np.int64)
        xa[ROW_OWN:ROW_OWN + HALF] = x_prompt[psq, q0:q0 + HALF]
        pos[ROW_OWN:ROW_OWN + HALF] = np.arange(q0, q0 + HALF)
        xa[ROW_OTHER:ROW_OTHER + HALF] = x_prompt[psq, o0:o0 + HALF]
        pos[ROW_OTHER:ROW_OTHER + HALF] = np.arange(o0, o0 + HALF)
        hpos = np.concatenate([np.arange(q0 - PADK, q0), np.arange(q0 + HALF, q0 + HALF + PADK)])
        hval = (hpos >= 0) & (hpos < SEQ)
        xa[ROW_HALO:ROW_HALO + 2 * PADK][hval] = x_prompt[psq, hpos[hval]]
        pos[ROW_HALO:ROW_HALO + 2 * PADK] = np.clip(hpos, 0, SEQ - 1)
        xa[ROW_S0:ROW_S0 + DSEQ] = x_sample[2 * c]
        pos[ROW_S0:ROW_S0 + DSEQ] = np.arange(DSEQ)
        xa[ROW_S1:ROW_S1 + DSEQ] = x_sample[2 * c + 1]
        pos[ROW_S1:ROW_S1 + DSEQ] = np.arange(DSEQ)
        C, Sg = _rope_tables(pos)
        cs = np.stack([c_prompt[psq], c_sample[2 * c], c_sample[2 * c + 1]], axis=-1)
        c_t = np.ascontiguousarray(cs.reshape(8, 128, 3).transpose(1, 0, 2).reshape(128, 24))
        valid0 = np.ones(HALF + 2 * PADK, f32)
        valid0[0:PADK] = hval[:PADK]
        valid0[PADK + HALF:] = hval[PADK:]
        am0 = _amask(valid0, HALF)
        in_maps.append({
            "x_all": xa, "rope_c": C, "rope_s": Sg, "c_t": c_t, "w_in": w_in0, "w_rot": w_rot,
            "w_out": w_out0, "w_ada": w_ada0, "g_pre_t": g_pre_t, "b_ada_t": b_ada_t, "b_gate": b_gate,
            "g_post": g_post_r, "g_sub_t": g_sub_t, "lam_v": lam_v, "ident": ident, "band": band,
            "sels": sels, "amask0": am0, "amask1": am1,
        })

    return in_maps


def kernel(**inputs):
    f32 = np.float32
    in_maps = prep_inputs(**inputs)
    if "nc" not in _NC_CACHE:
        _NC_CACHE["nc"] = build_program()
    nc = _NC_CACHE["nc"]
    res = run_bass_kernel_spmd(nc, in_maps, core_ids=list(range(8)))
    y_prompt = np.zeros((4, SEQ, D), f32)
    y_sample = np.zeros((16, DSEQ, D), f32)
    for c in range(8):
        y = np.asarray(res.results[c]["y_all"], f32)
        psq, hf = c // 2, c % 2
        y_prompt[psq, hf * HALF:(hf + 1) * HALF] = y[0:HALF]
        y_sample[2 * c] = y[HALF:HALF + DSEQ]
        y_sample[2 * c + 1] = y[HALF + DSEQ:HALF + 2 * DSEQ]
    return (y_prompt, y_sample)
```

```python
import contextlib
import numpy as np
import concourse.bass as bass
import concourse.mybir as mybir
from concourse.bass_utils import run_bass_kernel_spmd

F32 = mybir.dt.float32
BF16 = mybir.dt.bfloat16
AF = mybir.ActivationFunctionType
ALU = mybir.AluOpType

PE, ACT, DVE, POOL, SP = "tensor", "scalar", "vector", "gpsimd", "sync"
ENGINES = (PE, ACT, DVE, POOL, SP)

D = 1024
SEQ = 8192
DSEQ = 2048
HALF = 4096
PADK = 1024
PATTERNS = ((128, 1), (512, 4), (2048, 16))
EPS = 1e-6
LAM_INIT = 0.2
NROWS = HALF + HALF + 2 * PADK + 2 * DSEQ
ROW_OWN, ROW_OTHER, ROW_HALO, ROW_S0, ROW_S1 = 0, 4096, 8192, 10240, 12288
NOUT = HALF + 2 * DSEQ
WCOLS = 4096 + 4 * 192


class Buf:
    __slots__ = ("last_write", "reads")

    def __init__(self):
        self.last_write = None
        self.reads = []


class Rec:
    __slots__ = ("eng", "fn", "deps", "is_dma", "signal", "count", "dsem", "dval", "epoch")

    def __init__(self, eng, fn, is_dma):
        self.eng, self.fn, self.is_dma = eng, fn, is_dma
        self.deps = []
        self.signal = False
        self.count = 0
        self.dsem = None
        self.dval = 0
        self.epoch = 0


class Sched:
    EPOCH_MAX = 30000

    def __init__(self, nc, n_dma_sems=20):
        self.nc = nc
        self.q = {e: [] for e in ENGINES}
        self.n_dma_sems = n_dma_sems
        self.dma_rr = {e: 0 for e in ENGINES}
        self.dma_last = {}
        self.last_compute = {}

    def op(self, eng, meth, kw, reads=(), writes=(), dma=False, extra=()):
        r = Rec(eng, (meth, kw), dma)
        deps = list(extra)
        for b in reads:
            if b.last_write is not None:
                deps.append(b.last_write)
        for b in writes:
            if b.last_write is not None:
                deps.append(b.last_write)
            deps.extend(b.reads)
        if dma:
            slot = self.dma_rr[eng] % self.n_dma_sems
            self.dma_rr[eng] += 1
            prev = self.dma_last.get((eng, slot))
            if prev is not None:
                deps.append(prev)
            self.dma_last[(eng, slot)] = r
            r.dsem = (eng, slot)
        seen = set()
        for d in deps:
            if d is r or id(d) in seen:
                continue
            if (not d.is_dma) and (not dma) and d.eng == PE and eng == PE:
                continue
            seen.add(id(d))
            r.deps.append(d)
        for b in reads:
            b.reads.append(r)
        for b in writes:
            b.last_write = r
            b.reads = []
        self.q[eng].append(r)
        if not dma:
            self.last_compute[eng] = r
        return r

    def dma(self, meth, kw, reads=(), writes=(), eng=SP):
        return self.op(eng, meth, kw, reads, writes, dma=True)

    def barrier(self):
        deps = list(self.last_compute.values()) + list(self.dma_last.values())
        for e in ENGINES:
            r = Rec(e, None, False)
            for d in deps:
                if (not d.is_dma) and d.eng == e:
                    continue
                r.deps.append(d)
            self.q[e].append(r)

    def run(self, final_wait=()):
        nc = self.nc
        for e in ENGINES:
            for r in self.q[e]:
                for d in r.deps:
                    if not d.is_dma:
                        d.signal = True
        n_epochs = {}
        for e in ENGINES:
            cnt, ep = 0, 0
            for r in self.q[e]:
                if r.is_dma or r.fn is None:
                    continue
                if r.signal:
                    if cnt >= self.EPOCH_MAX:
                        ep += 1
                        cnt = 0
                    cnt += 1
                    r.count = cnt
                    r.epoch = ep
            n_epochs[e] = ep + 1
        dcount = {}
        for e in ENGINES:
            for r in self.q[e]:
                if r.is_dma:
                    dcount[r.dsem] = dcount.get(r.dsem, 0) + 16
                    r.dval = dcount[r.dsem]
        with contextlib.ExitStack() as st:
            csem = {}
            for e in ENGINES:
                for ep in range(n_epochs[e]):
                    if any((not r.is_dma) and r.signal and r.epoch == ep for r in self.q[e]):
                        csem[(e, ep)] = st.enter_context(nc.semaphore(f"c_{e}_{ep}"))
            dsem = {}
            for key in dcount:
                dsem[key] = st.enter_context(nc.semaphore(f"d_{key[0]}_{key[1]}"))
            block = st.enter_context(nc.Block())

            def emit(e, engine):
                waited = {}

                def do_wait(d):
                    if d.is_dma:
                        s, v, k = dsem[d.dsem], d.dval, ("d",) + d.dsem
                    else:
                        s, v, k = csem[(d.eng, d.epoch)], d.count, ("c", d.eng, d.epoch)
                    if waited.get(k, 0) >= v:
                        return
                    waited[k] = v
                    engine.wait_ge(s, v)

                for r in self.q[e]:
                    for d in r.deps:
                        do_wait(d)
                    if r.fn is None:
                        continue
                    ins = getattr(engine, r.fn[0])(**r.fn[1])
                    if r.is_dma:
                        ins.then_inc(dsem[r.dsem], 16)
                    elif r.signal:
                        ins.then_inc(csem[(e, r.epoch)], 1)
                if e == SP:
                    for d in final_wait:
                        do_wait(d)

            @block.tensor
            def _(eng):
                emit(PE, eng)

            @block.scalar
            def _(eng):
                emit(ACT, eng)

            @block.vector
            def _(eng):
                emit(DVE, eng)

            @block.gpsimd
            def _(eng):
                emit(POOL, eng)

            @block.sync
            def _(eng):
                emit(SP, eng)


class Ring:
    def __init__(self, tiles):
        self.tiles = tiles
        self.bufs = [Buf() for _ in tiles]
        self.i = 0

    def next(self):
        k = self.i % len(self.tiles)
        self.i += 1
        return self.tiles[k], self.bufs[k]


def amask_tiles(sq):
    out = []
    for pi, (w, dil) in enumerate(PATTERNS):
        nblk = sq // dil // 128
        je0 = PADK // dil - 64
        for r in range(dil):
            for t in range(nblk + 1):
                out.append((pi, dil, r, t, nblk, je0))
    return out


def build_program(stop_after=None, jobs=(0, 1, 2)):
    nc = bass.Bass("TRN2", target_bir_lowering=False)
    lvl = {'p0': 0, 'w': 1, 'p1': 2, '2b': 3, '2a': 4, None: 5}[stop_after]

    def din(name, shape, dt=F32):
        return nc.dram_tensor(name, list(shape), dt, kind="ExternalInput").ap()

    X = din("x_all", [NROWS, D])
    CT = din("rope_c", [128, NROWS])
    ST = din("rope_s", [128, NROWS])
    CTT = din("c_t", [128, 24])
    W_IN = din("w_in", [D, 4096])
    W_ROT = din("w_rot", [D, 768])
    W_OUT = din("w_out", [D, D])
    W_ADA = din("w_ada", [D, 3 * D])
    GPRE = din("g_pre_t", [128, 8])
    BADA = din("b_ada_t", [128, 16])
    BGATE = din("b_gate", [1, D])
    GPOST = din("g_post", [1, D])
    GSUB = din("g_sub_t", [128, 1])
    LAMV = din("lam_v", [4, 64])
    IDENT = din("ident", [128, 128])
    BAND = din("band", [128, 512])
    SELS = din("sels", [65, 512])
    AM0 = din("amask0", [128, 117])
    VROW = din("vrow", [128, 2 * PADK])
    AM1 = din("amask1", [128, 69])
    Y = nc.dram_tensor("y_all", [NOUT, D], F32, kind="ExternalOutput").ap()

    JOBS = [
        dict(s=0, sq=HALF, skv=SEQ, sext=HALF + 2 * PADK, xrow=ROW_OWN, yrow=0, am=AM0),
        dict(s=1, sq=DSEQ, skv=DSEQ, sext=DSEQ + 2 * PADK, xrow=ROW_S0, yrow=HALF, am=AM1),
        dict(s=2, sq=DSEQ, skv=DSEQ, sext=DSEQ + 2 * PADK, xrow=ROW_S1, yrow=HALF + DSEQ, am=AM1),
    ]
    def scr(name, rows, cols):
        return nc.dram_tensor(name, [rows, cols], BF16, kind="Internal").ap()

    SC = dict(
        QA=scr("s_qa", 512, HALF), QAr=scr("s_qar", 128, HALF), GA=scr("s_ga", 512, HALF),
        KA=scr("s_ka", 512, HALF + 2 * PADK), KAr=scr("s_kar", 128, HALF + 2 * PADK),
        VA=scr("s_va", 512, HALF + 2 * PADK),
        QB=scr("s_qb", 512, HALF), QBr=scr("s_qbr", 128, HALF), GB=scr("s_gb", 512, HALF),
        KB=scr("s_kb", 512, SEQ), KBr=scr("s_kbr", 128, SEQ), VB=scr("s_vb", 512, SEQ),
        YM=scr("s_ym", 1024, HALF),
    )
    SCB = {k: Buf() for k in SC}

    S = Sched(nc)
    out_dmas = []
    top = contextlib.ExitStack()
    with top:
        def sb(st, name, shape, dt):
            return st.enter_context(nc.sbuf_tensor("sb_" + name, list(shape), dt))

        def ps(st, name, shape, dt):
            return st.enter_context(nc.psum_tensor("ps_" + name, list(shape), dt))

        uid = [0]

        def nm(p):
            uid[0] += 1
            return f"{p}{uid[0]}"

        ident_f = sb(top, "ident_f", [128, 128], F32)
        ident = sb(top, "ident", [128, 128], BF16)
        band_f = sb(top, "band_f", [128, 512], F32)
        bandneg = sb(top, "bandneg", [128, 256], BF16)
        sels = sb(top, "sels", [65, 4, 128], F32)
        am0 = sb(top, "am0", [128, 117], F32)
        am1 = sb(top, "am1", [128, 69], F32)
        ones_f = sb(top, "ones_f", [128, 128], F32)
        mhalf = sb(top, "mhalf", [128, 1], F32)
        zer = sb(top, "zer", [1, 512], BF16)
        gsub2 = sb(top, "gsub2", [128, 1], F32)
        neglam = sb(top, "neglam", [128, 1], F32)
        lamt = sb(top, "lamt", [128, 4, 64], F32)
        lamj = sb(top, "lamj", [128, 64], F32)
        lams = sb(top, "lams", [128, 4], F32)
        gg = [sb(top, f"gg{s}", [128, D], F32) for s in range(3)]
        gsT = sb(top, "gsT", [128, 8, 4], F32)
        shT = sb(top, "shT", [128, 8, 4], F32)
        wout = sb(top, "wout", [128, 8, D], BF16)
        bconst = Buf()
        bgg = [Buf() for _ in range(3)]
        bmod = Buf()
        bwout = Buf()

        S.dma("dma_start", dict(out=ident_f[:], in_=IDENT[:, :]), writes=[bconst])
        S.dma("dma_start", dict(out=band_f[:], in_=BAND[:, :]), writes=[bconst])
        S.dma("dma_start", dict(out=sels[:].rearrange("p a b -> p (a b)"), in_=SELS[:, :]), writes=[bconst])
        S.dma("dma_start", dict(out=am0[:], in_=AM0[:, :]), writes=[bconst])
        S.dma("dma_start", dict(out=am1[:], in_=AM1[:, :]), writes=[bconst])
        S.dma("dma_start", dict(out=gsub2[:], in_=GSUB[:, :]), writes=[bconst])
        for i in range(4):
            S.dma("dma_start", dict(out=lamt[:, i, :], in_=LAMV[i:i + 1, :].broadcast_to([128, 64])), writes=[bconst])
        S.op(DVE, "tensor_copy", dict(out=ident[:], in_=ident_f[:]), reads=[bconst], writes=[bconst])
        S.op(DVE, "tensor_scalar", dict(out=bandneg[:], in0=band_f[:, 0:256], scalar1=30000.0, scalar2=-30000.0, op0=ALU.mult, op1=ALU.add), reads=[bconst], writes=[bconst])
        S.op(POOL, "memset", dict(ap=ones_f[:], constant=1.0), writes=[bconst])
        S.op(POOL, "memset", dict(ap=mhalf[:], constant=-0.5), writes=[bconst])
        S.op(POOL, "memset", dict(ap=zer[:], constant=0.0), writes=[bconst])
        S.op(DVE, "tensor_scalar", dict(out=gsub2[:], in0=gsub2[:], scalar1=(1.0 - LAM_INIT) * 0.5, scalar2=None, op0=ALU.mult), reads=[bconst], writes=[bconst])
        for i in range(2):
            S.op(DVE, "tensor_tensor", dict(out=lamj[:], in0=lamt[:, 2 * i, :], in1=lamt[:, 2 * i + 1, :], op=ALU.mult), reads=[bconst], writes=[bconst])
            S.op(ACT, "activation", dict(out=lamj[:], in_=lamj[:], func=AF.Identity, accum_out=lams[:, i:i + 1]), reads=[bconst], writes=[bconst])
        S.op(ACT, "activation", dict(out=lams[:, 2:4], in_=lams[:, 0:2], func=AF.Exp), reads=[bconst], writes=[bconst])
        S.op(DVE, "tensor_tensor", dict(out=neglam[:], in0=lams[:, 3:4], in1=lams[:, 2:3], op=ALU.subtract), reads=[bconst], writes=[bconst])
        S.op(DVE, "tensor_scalar", dict(out=neglam[:], in0=neglam[:], scalar1=-LAM_INIT, scalar2=None, op0=ALU.add), reads=[bconst], writes=[bconst])

        with contextlib.ExitStack() as p0:
            stg = [sb(p0, f"stg0_{i}", [128, 8, 512], F32) for i in range(2)]
            bstg = [Buf() for _ in range(2)]
            ct = sb(p0, "ct", [128, 24], F32)
            th = sb(p0, "th0", [128, 24], F32)
            scT = sb(p0, "scT", [128, 8, 4], F32)
            screp = sb(p0, "screp", [128, 24, 128], F32)
            gpre = sb(p0, "gpre", [128, 8], F32)
            bada = sb(p0, "bada", [128, 16], F32)
            bgate = sb(p0, "bgate", [128, D], F32)
            gpost = sb(p0, "gpost", [128, D], F32)
            tmpm = sb(p0, "tmpm", [128, 8], F32)
            tmpg = sb(p0, "tmpg", [128, 512], F32)
            pmod = ps(p0, "pmod", [128, 16, 4], F32)
            pg = [ps(p0, f"pg{i}", [128, 512], F32) for i in range(3)]
            b0 = Buf()
            bpm = Buf()
            bpg = [Buf() for _ in range(3)]
            S.dma("dma_start", dict(out=ct[:], in_=CTT[:, :]), writes=[b0])
            S.dma("dma_start", dict(out=gpre[:], in_=GPRE[:, :]), writes=[b0])
            S.dma("dma_start", dict(out=bada[:], in_=BADA[:, :]), writes=[b0])
            S.dma("dma_start", dict(out=bgate[:], in_=BGATE[0:1, :].broadcast_to([128, D])), writes=[b0])
            S.dma("dma_start", dict(out=gpost[:], in_=GPOST[0:1, :].broadcast_to([128, D])), writes=[b0])
            S.op(ACT, "activation", dict(out=th[:], in_=ct[:], func=AF.Tanh, scale=0.5), reads=[b0], writes=[b0])
            S.op(DVE, "scalar_tensor_tensor", dict(out=th[:], in0=th[:], scalar=1.0, in1=ct[:], op0=ALU.add, op1=ALU.mult), reads=[b0], writes=[b0])
            S.op(POOL, "memset", dict(ap=scT[:], constant=0.0), writes=[b0])
            S.op(POOL, "memset", dict(ap=shT[:], constant=0.0), writes=[bmod])
            S.op(POOL, "memset", dict(ap=gsT[:], constant=0.0), writes=[bmod])
            S.op(DVE, "tensor_scalar", dict(out=scT[:, :, 0:3], in0=th[:].rearrange("p (a b) -> p a b", b=3), scalar1=0.5, scalar2=None, op0=ALU.mult), reads=[b0], writes=[b0])
            for i in range(24):
                kc, s = divmod(i, 3)
                S.op(DVE, "tensor_scalar", dict(out=screp[:, i, :], in0=ones_f[:], scalar1=scT[:, kc, s:s + 1], scalar2=None, op0=ALU.mult), reads=[b0, bconst], writes=[b0])
            for pc in range(6):
                k = pc % 2
                S.dma("dma_start", dict(out=stg[k][:], in_=W_ADA[:, pc * 512:(pc + 1) * 512].rearrange("(kc p) n -> p kc n", p=128)), writes=[bstg[k]])
                if pc < 4:
                    for fb in range(4):
                        blk = pc * 4 + fb
                        for kc in range(8):
                            S.op(PE, "matmul", dict(out=pmod[:, blk, :], lhsT=stg[k][:, kc, fb * 128:(fb + 1) * 128], rhs=scT[:, kc, :], start=(kc == 0), stop=(kc == 7)), reads=[bstg[k], b0], writes=[bpm])
                else:
                    hf = pc - 4
                    for s in range(3):
                        for h2 in range(2):
                            for kc in range(8):
                                S.op(PE, "matmul", dict(out=pg[s][:, h2 * 256:(h2 + 1) * 256], lhsT=screp[:, kc * 3 + s, :], rhs=stg[k][:, kc, h2 * 256:(h2 + 1) * 256], start=(kc == 0), stop=(kc == 7)), reads=[bstg[k], b0], writes=[bpg[s]])
                        S.op(DVE, "tensor_tensor", dict(out=tmpg[:], in0=pg[s][:], in1=bgate[:, hf * 512:(hf + 1) * 512], op=ALU.add), reads=[bpg[s], b0], writes=[b0])
                        S.op(DVE, "tensor_tensor", dict(out=gg[s][:, hf * 512:(hf + 1) * 512], in0=tmpg[:], in1=gpost[:, hf * 512:(hf + 1) * 512], op=ALU.mult), reads=[b0], writes=[bgg[s]])
            for s in range(3):
                S.op(DVE, "tensor_tensor", dict(out=shT[:, :, s], in0=pmod[:, 0:8, s], in1=bada[:, 0:8], op=ALU.add), reads=[bpm, b0], writes=[bmod])
                S.op(DVE, "scalar_tensor_tensor", dict(out=tmpm[:], in0=pmod[:, 8:16, s], scalar=1.0, in1=bada[:, 8:16], op0=ALU.add, op1=ALU.add), reads=[bpm, b0], writes=[b0])
                S.op(DVE, "tensor_tensor", dict(out=gsT[:, :, s], in0=tmpm[:], in1=gpre[:], op=ALU.mult), reads=[b0], writes=[bmod])
            for pc in range(2):
                k = pc % 2
                S.dma("dma_start", dict(out=stg[k][:], in_=W_OUT[:, pc * 512:(pc + 1) * 512].rearrange("(kc p) n -> p kc n", p=128)), writes=[bstg[k]])
                S.op(DVE, "tensor_copy", dict(out=wout[:, :, pc * 512:(pc + 1) * 512], in_=stg[k][:]), reads=[bstg[k]], writes=[bwout])
            S.barrier()

        for job in [JOBS[j_] for j_ in jobs] if stop_after != 'p0' else []:
            s, sq, skv, sext = job["s"], job["sq"], job["skv"], job["sext"]
            am = am0 if job["am"] is AM0 else am1
            with contextlib.ExitStack() as p1:
                wp = sb(p1, nm("wp"), [128, 8, WCOLS], BF16)
                biasT = sb(p1, nm("biasT"), [128, 40], F32)
                bwp = Buf()
                bbias = Buf()
                with contextlib.ExitStack() as pw:
                    stg = [sb(pw, nm("stgw"), [128, 8, 512], F32) for _ in range(2)]
                    bstg = [Buf() for _ in range(2)]
                    pb = ps(pw, nm("pb"), [128, 40, 4], F32)
                    bpb = Buf()
                    pieces = [(W_IN, pc * 512, 512, pc * 512) for pc in range(8)] + [(W_ROT, 0, 384, 4096), (W_ROT, 384, 384, 4096 + 384)]
                    for pi_, (src, c0, ncol, dcol) in enumerate(pieces):
                        k = pi_ % 2
                        S.dma("dma_start", dict(out=stg[k][:, :, 0:ncol], in_=src[:, c0:c0 + ncol].rearrange("(kc p) n -> p kc n", p=128)), writes=[bstg[k]])
                        for kc in range(8):
                            eng = DVE if kc % 2 == 0 else POOL
                            S.op(eng, "tensor_scalar", dict(out=wp[:, kc, dcol:dcol + ncol], in0=stg[k][:, kc, 0:ncol], scalar1=gsT[:, kc, s:s + 1], scalar2=None, op0=ALU.mult), reads=[bstg[k], bmod], writes=[bwp])
                        if pi_ < 8:
                            cols = [(fb * 128, pi_ * 4 + fb) for fb in range(4)]
                        else:
                            t0 = (pi_ - 8) * 2
                            cols = [(0, 32 + 2 * t0), (64, 33 + 2 * t0), (192, 34 + 2 * t0), (256, 35 + 2 * t0)]
                        for (co, bcol) in cols:
                            for kc in range(8):
                                S.op(PE, "matmul", dict(out=pb[:, bcol, :], lhsT=stg[k][:, kc, co:co + 128], rhs=shT[:, kc, :], start=(kc == 0), stop=(kc == 7)), reads=[bstg[k], bmod], writes=[bpb])
                    S.op(DVE, "tensor_copy", dict(out=biasT[:], in_=pb[:, :, s]), reads=[bpb], writes=[bbias])
                    S.barrier()

                with contextlib.ExitStack() as pp:
                    xt = Ring([sb(pp, nm("xt"), [128, D], F32) for _ in range(3)])
                    junk = sb(pp, nm("junk"), [128, D], F32)
                    bjunk = Buf()
                    xn = Ring([sb(pp, nm("xn"), [128, D], BF16) for _ in range(2)])
                    xnT = Ring([sb(pp, nm("xnT"), [128, 8, 512], BF16) for _ in range(2)])
                    stat = Ring([sb(pp, nm("stat"), [128, 4], F32) for _ in range(4)])
                    ost = Ring([sb(pp, nm("ost"), [128, 4, 512], BF16) for _ in range(3)])
                    cst = Ring([sb(pp, nm("cst"), [128, 2, 512], F32) for _ in range(2)])
                    t12 = Ring([sb(pp, nm("t12"), [128, 2, 512], F32) for _ in range(2)])
                    rot = Ring([sb(pp, nm("rot"), [128, 512], BF16) for _ in range(2)])
                    pT = Ring([ps(pp, nm("pT"), [128, D], BF16) for _ in range(2)])
                    pacc = Ring([ps(pp, nm("pacc"), [128, 512], F32) for _ in range(5)])

                    G = dict(qa=(0, 0), ka=(512, 4), va=(1024, 8), ga=(1536, 12), qb=(2048, 16), kb=(2560, 20), vb=(3072, 24), gb=(3584, 28))
                    RT = dict(qa=0, ka=1, qb=2, kb=3)
                    segs = []
                    full_main = [("qa", "QA", 0), ("ka", "KA", PADK), ("va", "VA", PADK), ("ga", "GA", 0), ("qb", "QB", 0), ("kb", "KB", 0), ("vb", "VB", 0), ("gb", "GB", 0)]
                    full_rot = [("qa", "QAr", 0), ("ka", "KAr", PADK), ("qb", "QBr", 0), ("kb", "KBr", 0)]
                    segs.append((job["xrow"], sq, full_main, full_rot))
                    if s == 0:
                        segs.append((ROW_OTHER, HALF, [("kb", "KB", HALF), ("vb", "VB", HALF)], [("kb", "KBr", HALF)]))
                        segs.append((ROW_HALO, PADK, [("ka", "KA", 0), ("va", "VA", 0)], [("ka", "KAr", 0)]))
                        segs.append((ROW_HALO + PADK, PADK, [("ka", "KA", PADK + HALF), ("va", "VA", PADK + HALF)], [("ka", "KAr", PADK + HALF)]))
                    evi = [0]
                    for (xr0, T, mains, rots) in (segs if lvl >= 2 else []):
                        for ch in range(T // 512):
                            r0 = xr0 + ch * 512
                            xT_t, xT_b = xnT.next()
                            for i in range(4):
                                x_t, x_b = xt.next()
                                S.dma("dma_start", dict(out=x_t[:], in_=X[r0 + i * 128:r0 + (i + 1) * 128, :]), writes=[x_b])
                                st_t, st_b = stat.next()
                                S.op(ACT, "activation", dict(out=junk[:], in_=x_t[:], func=AF.Square, accum_out=st_t[:, 0:1]), reads=[x_b], writes=[bjunk, st_b])
                                S.op(DVE, "tensor_scalar", dict(out=st_t[:, 1:2], in0=st_t[:, 0:1], scalar1=1.0 / D, scalar2=EPS, op0=ALU.mult, op1=ALU.add), reads=[st_b], writes=[st_b])
                                S.op(POOL, "tensor_tensor", dict(out=st_t[:, 2:3], in0=st_t[:, 1:2], in1=mhalf[:], op=ALU.pow), reads=[st_b, bconst], writes=[st_b])
                                xn_t, xn_b = xn.next()
                                S.op(DVE, "tensor_scalar", dict(out=xn_t[:], in0=x_t[:], scalar1=st_t[:, 2:3], scalar2=None, op0=ALU.mult), reads=[x_b, st_b], writes=[xn_b])
                                pT_t, pT_b = pT.next()
                                for kc in range(8):
                                    S.op(PE, "transpose", dict(out=pT_t[:, kc * 128:(kc + 1) * 128], in_=xn_t[:, kc * 128:(kc + 1) * 128], identity=ident[:]), reads=[xn_b, bconst], writes=[pT_b])
                                S.op(DVE, "tensor_copy", dict(out=xT_t[:, :, i * 128:(i + 1) * 128], in_=pT_t[:].rearrange("p (a b) -> p a b", b=128)), reads=[pT_b], writes=[xT_b])
                            c_t, c_b = cst.next()
                            if rots:
                                S.dma("dma_start", dict(out=c_t[:, 0, :], in_=CT[:, r0:r0 + 512]), writes=[c_b])
                                S.dma("dma_start", dict(out=c_t[:, 1, :], in_=ST[:, r0:r0 + 512]), writes=[c_b])
                            for (gname, skey, dcol0) in mains:
                                wcol, bcol = G[gname]
                                o_t, o_b = ost.next()
                                for b in range(4):
                                    pa_t, pa_b = pacc.next()
                                    c0 = wcol + b * 128
                                    bc = bcol + b
                                    for kc in range(8):
                                        S.op(PE, "matmul", dict(out=pa_t[:], lhsT=wp[:, kc, c0:c0 + 128], rhs=xT_t[:, kc, :], start=(kc == 0), stop=(kc == 7)), reads=[bwp, xT_b], writes=[pa_b])
                                    evi[0] += 1
                                    if evi[0] % 4 != 0:
                                        S.op(ACT, "activation", dict(out=o_t[:, b, :], in_=pa_t[:], func=AF.Identity, bias=biasT[:, bc:bc + 1]), reads=[pa_b, bbias], writes=[o_b])
                                    else:
                                        S.op(DVE, "tensor_scalar", dict(out=o_t[:, b, :], in0=pa_t[:], scalar1=biasT[:, bc:bc + 1], scalar2=None, op0=ALU.add), reads=[pa_b, bbias], writes=[o_b])
                                dc = dcol0 + ch * 512
                                S.dma("dma_start", dict(out=SC[skey][:, dc:dc + 512].rearrange("(b p) t -> p b t", p=128), in_=o_t[:]), reads=[o_b], writes=[SCB[skey]])
                            for (gname, skey, dcol0) in rots:
                                ti = RT[gname]
                                wc = 4096 + ti * 192
                                bc = 32 + 2 * ti
                                p1_t, p1_b = pacc.next()
                                p2_t, p2_b = pacc.next()
                                for kc in range(8):
                                    S.op(PE, "matmul", dict(out=p1_t[:], lhsT=wp[:, kc, wc:wc + 128], rhs=xT_t[:, kc, :], start=(kc == 0), stop=(kc == 7)), reads=[bwp, xT_b], writes=[p1_b])
                                for kc in range(8):
                                    S.op(PE, "matmul", dict(out=p2_t[:], lhsT=wp[:, kc, wc + 64:wc + 192], rhs=xT_t[:, kc, :], start=(kc == 0), stop=(kc == 7)), reads=[bwp, xT_b], writes=[p2_b])
                                t_t, t_b = t12.next()
                                S.op(DVE, "scalar_tensor_tensor", dict(out=t_t[:, 0, :], in0=p1_t[:], scalar=biasT[:, bc:bc + 1], in1=c_t[:, 0, :], op0=ALU.add, op1=ALU.mult), reads=[p1_b, c_b, bbias], writes=[t_b])
                                S.op(DVE, "scalar_tensor_tensor", dict(out=t_t[:, 1, :], in0=p2_t[:], scalar=biasT[:, bc + 1:bc + 2], in1=c_t[:, 1, :], op0=ALU.add, op1=ALU.mult), reads=[p2_b, c_b, bbias], writes=[t_b])
                                r_t, r_b = rot.next()
                                S.op(POOL, "tensor_tensor", dict(out=r_t[:], in0=t_t[:, 0, :], in1=t_t[:, 1, :], op=ALU.add), reads=[t_b], writes=[r_b])
                                dc = dcol0 + ch * 512
                                S.dma("dma_start", dict(out=SC[skey][:, dc:dc + 512], in_=r_t[:]), reads=[r_b], writes=[SCB[skey]])
                    S.barrier()

            with contextlib.ExitStack() as pB:
                nkt = skv // 128
                nqc = sq // 512
                kbT = [sb(pB, nm("kbT"), [128, skv], BF16) for _ in range(2)]
                vbT = [sb(pB, nm("vbT"), [128, skv], BF16) for _ in range(2)]
                qbT = [sb(pB, nm("qbT"), [128, sq], BF16) for _ in range(2)]
                gbT = [sb(pB, nm("gbT"), [128, sq], BF16) for _ in range(2)]
                bk, bv, bq, bg = ([Buf(), Buf()] for _ in range(4))
                vtok = sb(pB, nm("vtok"), [128, nkt, 128], BF16)
                bvt = Buf()
                Pr = Ring([sb(pB, nm("P"), [128, 1024], BF16) for _ in range(6)])
                asum = [sb(pB, nm("asum"), [128, 1024], F32) for _ in range(2)]
                basum = [Buf() for _ in range(2)]
                oS = [[sb(pB, nm("oS"), [128, 512], F32) for _ in range(2)] for _ in range(2)]
                boS = [[Buf() for _ in range(2)] for _ in range(2)]
                thr = Ring([sb(pB, nm("thb"), [128, 512], F32) for _ in range(2)])
                sgr = Ring([sb(pB, nm("sgb"), [128, 512], BF16) for _ in range(3)])
                yTr = Ring([sb(pB, nm("yTb"), [128, 512], BF16) for _ in range(3)])
                rsr = Ring([sb(pB, nm("rs"), [128, 16], F32) for _ in range(3)])
                rr = Ring([sb(pB, nm("rr"), [128, 8], F32) for _ in range(6)])
                t0r = Ring([sb(pB, nm("t0"), [128, 128], F32) for _ in range(3)])
                orr = Ring([sb(pB, nm("o"), [128, 128], F32) for _ in range(5)])
                onr = Ring([sb(pB, nm("on"), [128, 128], BF16) for _ in range(5)])
                junkb = sb(pB, nm("junkb"), [128, 128], F32)
                bjb = Buf()
                scr_ = Ring([ps(pB, nm("sc"), [128, 1024], F32) for _ in range(2)])
                oT = [ps(pB, nm("oT"), [128, 512], F32) for _ in range(2)]
                boT = [Buf() for _ in range(2)]
                tpb = ps(pB, nm("tpb"), [128, 512], F32)
                btp = Buf()
                pTv = ps(pB, nm("pTv"), [128, 4, 128], BF16)
                bpv = Buf()

                def load_head(g):
                    hb = g % 2
                    for (dst, dbuf, main, rotk, ncols) in ((kbT[hb], bk[hb], "KB", "KBr", skv), (qbT[hb], bq[hb], "QB", "QBr", sq)):
                        for m in range(2):
                            h = 2 * g + m
                            S.dma("dma_start", dict(out=dst[m * 64:m * 64 + 48, 0:ncols], in_=SC[main][g * 128 + m * 64 + 16:g * 128 + m * 64 + 64, 0:ncols]), reads=[SCB[main]], writes=[dbuf])
                            S.dma("dma_start", dict(out=dst[m * 64 + 48:m * 64 + 56, 0:ncols], in_=SC[rotk][h * 8:h * 8 + 8, 0:ncols]), reads=[SCB[rotk]], writes=[dbuf])
                            S.dma("dma_start", dict(out=dst[m * 64 + 56:m * 64 + 64, 0:ncols], in_=SC[rotk][64 + h * 8:64 + h * 8 + 8, 0:ncols]), reads=[SCB[rotk]], writes=[dbuf])
                    S.dma("dma_start", dict(out=vbT[hb][:], in_=SC["VB"][g * 128:(g + 1) * 128, 0:skv]), reads=[SCB["VB"]], writes=[bv[hb]])
                    S.dma("dma_start", dict(out=gbT[hb][:], in_=SC["GB"][g * 128:(g + 1) * 128, 0:sq]), reads=[SCB["GB"]], writes=[bg[hb]])

                def make_epilogue(g, q0, par, sg_t, sg_b):
                    steps = []
                    rs_t, rs_b = rsr.next()
                    y_t, y_b = yTr.next()
                    state = {}

                    def rowsums():
                        for m in range(2):
                            for qb in range(4):
                                c = 256 + 2 * (m * 4 + qb)
                                S.op(PE, "matmul", dict(out=tpb[:, c:c + 2], lhsT=asum[par][:, m * 512 + qb * 128:m * 512 + (qb + 1) * 128], rhs=ones_f[:, 0:2], start=True, stop=True), reads=[basum[par], bconst], writes=[btp])
                        S.op(DVE, "reciprocal", dict(out=rs_t[:, 0:8], in_=tpb[:, 256:272:2]), reads=[btp], writes=[rs_b])
                        S.op(DVE, "tensor_scalar", dict(out=rs_t[:, 8:12], in0=rs_t[:, 4:8], scalar1=neglam[:, 0:1], scalar2=None, op0=ALU.mult), reads=[rs_b, bconst], writes=[rs_b])
                    steps.append(rowsums)

                    def block_a(qb):
                        def f():
                            e_t, e_b = tpb, btp
                            for m in range(2):
                                S.op(PE, "transpose", dict(out=e_t[:, m * 128:(m + 1) * 128], in_=oS[par][m][:, qb * 128:(qb + 1) * 128], identity=ident_f[:]), reads=[boS[par][m], bconst], writes=[e_b])
                            r_t, r_b = rr.next()
                            t0_t, t0_b = t0r.next()
                            S.op(DVE, "tensor_scalar", dict(out=t0_t[:], in0=e_t[:, 0:128], scalar1=rs_t[:, qb:qb + 1], scalar2=None, op0=ALU.mult), reads=[e_b, rs_b], writes=[t0_b])
                            o_t, o_b = orr.next()
                            S.op(DVE, "scalar_tensor_tensor", dict(out=o_t[:], in0=e_t[:, 128:256], scalar=rs_t[:, 8 + qb:9 + qb], in1=t0_t[:], op0=ALU.mult, op1=ALU.add), reads=[e_b, rs_b, t0_b], writes=[o_b])
                            S.op(ACT, "activation", dict(out=junkb[:], in_=o_t[:], func=AF.Square, accum_out=r_t[:, 3:4]), reads=[o_b], writes=[bjb, r_b])
                            S.op(DVE, "tensor_scalar", dict(out=r_t[:, 4:5], in0=r_t[:, 3:4], scalar1=1.0 / 128, scalar2=EPS, op0=ALU.mult, op1=ALU.add), reads=[r_b], writes=[r_b])
                            S.op(POOL, "tensor_tensor", dict(out=r_t[:, 5:6], in0=r_t[:, 4:5], in1=mhalf[:], op=ALU.pow), reads=[r_b, bconst], writes=[r_b])
                            on_t, on_b = onr.next()
                            S.op(ACT, "activation", dict(out=on_t[:], in_=o_t[:], func=AF.Identity, scale=r_t[:, 5:6]), reads=[o_b, r_b], writes=[on_b])
                            state[qb] = (on_t, on_b)
                        return f

                    def block_b(qb, last):
                        def f():
                            on_t, on_b = state[qb]
                            ev = pTv[:, 0, :]
                            S.op(PE, "transpose", dict(out=ev, in_=on_t[:], identity=ident[:]), reads=[on_b, bconst], writes=[bpv])
                            S.op(DVE, "scalar_tensor_tensor", dict(out=y_t[:, qb * 128:(qb + 1) * 128], in0=ev, scalar=gsub2[:, 0:1], in1=sg_t[:, qb * 128:(qb + 1) * 128], op0=ALU.mult, op1=ALU.mult), reads=[bpv, sg_b, bconst], writes=[y_b])
                            if last:
                                S.dma("dma_start", dict(out=SC["YM"][512 + g * 128:512 + (g + 1) * 128, q0:q0 + 512], in_=y_t[:]), reads=[y_b], writes=[SCB["YM"]])
                        return f

                    for kind, qb in [("a", 0), ("a", 1), ("a", 2), ("b", 0), ("a", 3), ("b", 1), ("b", 2), ("b", 3)]:
                        steps.append(block_a(qb) if kind == "a" else block_b(qb, qb == 3))
                    return steps

                pending = []
                heads = list(range(4)) if lvl >= 3 else []
                if heads:
                    load_head(0)
                for g in heads:
                    hb = g % 2
                    if g + 1 < 4:
                        load_head(g + 1)
                    for k4 in range(nkt // 4):
                        for j in range(4):
                            kt = k4 * 4 + j
                            S.op(PE, "transpose", dict(out=pTv[:, j, :], in_=vbT[hb][:, kt * 128:(kt + 1) * 128], identity=ident[:]), reads=[bv[hb], bconst], writes=[bpv])
                        S.op(ACT, "activation", dict(out=vtok[:, k4 * 4:k4 * 4 + 4, :], in_=pTv[:], func=AF.Copy), reads=[bpv], writes=[bvt])
                    for qc in range(nqc):
                        q0 = qc * 512
                        par = qc % 2
                        th_t, th_b = thr.next()
                        sg_t, sg_b = sgr.next()
                        S.op(ACT, "activation", dict(out=th_t[:], in_=gbT[hb][:, q0:q0 + 512], func=AF.Tanh, scale=0.5), reads=[bg[hb]], writes=[th_b])
                        S.op(DVE, "scalar_tensor_tensor", dict(out=sg_t[:], in0=th_t[:], scalar=1.0, in1=gbT[hb][:, q0:q0 + 512], op0=ALU.add, op1=ALU.mult), reads=[th_b, bg[hb]], writes=[sg_b])

                        def scores(kt):
                            sc_t, sc_b = scr_.next()
                            for m in range(2):
                                S.op(PE, "matmul", dict(out=sc_t[:, m * 512:(m + 1) * 512], lhsT=kbT[hb][m * 64:(m + 1) * 64, kt * 128:(kt + 1) * 128], rhs=qbT[hb][m * 64:(m + 1) * 64, q0:q0 + 512], start=True, stop=True), reads=[bk[hb], bq[hb]], writes=[sc_b])
                            p_t, p_b = Pr.next()
                            S.op(ACT, "activation", dict(out=p_t[:], in_=sc_t[:], func=AF.Exp, scale=0.125), reads=[sc_b], writes=[p_b])
                            return p_t, p_b

                        def av(kt, p_t, p_b):
                            for m in range(2):
                                S.op(PE, "matmul", dict(out=oT[m][:], lhsT=vtok[:, kt, :], rhs=p_t[:, m * 512:(m + 1) * 512], start=(kt == 0), stop=(kt == nkt - 1)), reads=[p_b, bvt], writes=[boT[m]])
                            if kt == 0:
                                S.op(DVE, "tensor_copy", dict(out=asum[par][:], in_=p_t[:]), reads=[p_b], writes=[basum[par]])
                            else:
                                S.op(DVE, "tensor_tensor", dict(out=asum[par][:], in0=asum[par][:], in1=p_t[:], op=ALU.add), reads=[p_b, basum[par]], writes=[basum[par]])

                        ptiles = {0: scores(0)}
                        for kt in range(nkt):
                            if kt + 1 < nkt:
                                ptiles[kt + 1] = scores(kt + 1)
                            if kt >= 1:
                                av(kt - 1, *ptiles.pop(kt - 1))
                            if pending and kt >= 2 and kt % 2 == 0:
                                pending.pop(0)()
                        av(nkt - 1, *ptiles.pop(nkt - 1))
                        for m in range(2):
                            S.op(ACT, "activation", dict(out=oS[par][m][:], in_=oT[m][:], func=AF.Copy), reads=[boT[m]], writes=[boS[par][m]])
                        while pending:
                            pending.pop(0)()
                        pending.extend(make_epilogue(g, q0, par, sg_t, sg_b))
                while pending:
                    pending.pop(0)()
                S.barrier()

            with contextlib.ExitStack() as pA:
                tiles = amask_tiles(sq)
                nt = len(tiles)
                kaT = [sb(pA, nm("kaT"), [128, sext], BF16) for _ in range(2)]
                vaT1 = sb(pA, nm("vaT"), [128, sext], BF16)
                vaT = [vaT1, vaT1]
                bv1 = Buf()
                qaT = [sb(pA, nm("qaT"), [128, sq], BF16) for _ in range(2)]
                gaT = [sb(pA, nm("gaT"), [128, sq], BF16) for _ in range(2)]
                bk, bv, bq, bg = ([Buf(), Buf()] for _ in range(4))
                bv = [bv1, bv1]
                acc = sb(pA, nm("accA"), [65, 2, sq], F32)
                bacc = Buf()
                vt_all = sb(pA, nm("vtall"), [128, nt, 2, 65], BF16)
                bvta = Buf()
                vrow = sb(pA, nm("vrow"), [128, 2 * PADK], F32)
                bvrow = Buf()
                Pt = Ring([sb(pA, nm("Pt"), [128, 2, 512], BF16) for _ in range(4)])
                recr = Ring([sb(pA, nm("rec"), [128, 512], F32) for _ in range(2)])
                thr = Ring([sb(pA, nm("tha"), [128, 512], F32) for _ in range(2)])
                sgr = Ring([sb(pA, nm("sga"), [128, 512], F32) for _ in range(2)])
                tnr = Ring([sb(pA, nm("tn"), [128, 512], F32) for _ in range(2)])
                yTr = Ring([sb(pA, nm("yTa"), [128, 512], BF16) for _ in range(2)])
                scA = Ring([ps(pA, nm("scA"), [128, 2, 512], F32) for _ in range(2)])
                opsr = Ring([ps(pA, nm("ops"), [128, 2, 512], F32) for _ in range(2)])
                if s == 0:
                    S.dma("dma_start", dict(out=vrow[:], in_=VROW[:, :]), writes=[bvrow])

                kc0, kn = (PADK, sq) if s != 0 else (0, sext)

                def load_v(hp):
                    hb = hp % 2
                    if s != 0:
                        S.op(POOL, "memset", dict(ap=vaT[hb][:, 0:PADK], constant=0.0), writes=[bv[hb]])
                        S.op(POOL, "memset", dict(ap=vaT[hb][:, PADK + sq:sext], constant=0.0), writes=[bv[hb]])
                    S.dma("dma_start", dict(out=vaT[hb][:, kc0:kc0 + kn], in_=SC["VA"][hp * 128:(hp + 1) * 128, kc0:kc0 + kn]), reads=[SCB["VA"]], writes=[bv[hb]])
                    if s == 0:
                        S.op(DVE, "tensor_tensor", dict(out=vaT[hb][:, 0:PADK], in0=vaT[hb][:, 0:PADK], in1=vrow[:, 0:PADK], op=ALU.mult), reads=[bvrow, bv[hb]], writes=[bv[hb]])
                        S.op(DVE, "tensor_tensor", dict(out=vaT[hb][:, PADK + sq:sext], in0=vaT[hb][:, PADK + sq:sext], in1=vrow[:, PADK:2 * PADK], op=ALU.mult), reads=[bvrow, bv[hb]], writes=[bv[hb]])

                def load_pair(hp):
                    hb = hp % 2
                    if s != 0:
                        S.op(POOL, "memset", dict(ap=kaT[hb][:, 0:PADK], constant=0.0), writes=[bk[hb]])
                        S.op(POOL, "memset", dict(ap=kaT[hb][:, PADK + sq:sext], constant=0.0), writes=[bk[hb]])
                    for hl in range(2):
                        h = 2 * hp + hl
                        S.dma("dma_start", dict(out=kaT[hb][hl * 64:hl * 64 + 48, kc0:kc0 + kn], in_=SC["KA"][hp * 128 + hl * 64 + 16:hp * 128 + hl * 64 + 64, kc0:kc0 + kn]), reads=[SCB["KA"]], writes=[bk[hb]])
                        S.dma("dma_start", dict(out=kaT[hb][hl * 64 + 48:hl * 64 + 56, kc0:kc0 + kn], in_=SC["KAr"][h * 8:h * 8 + 8, kc0:kc0 + kn]), reads=[SCB["KAr"]], writes=[bk[hb]])
                        S.dma("dma_start", dict(out=kaT[hb][hl * 64 + 56:hl * 64 + 64, kc0:kc0 + kn], in_=SC["KAr"][64 + h * 8:64 + h * 8 + 8, kc0:kc0 + kn]), reads=[SCB["KAr"]], writes=[bk[hb]])
                        S.dma("dma_start", dict(out=qaT[hb][hl * 64:hl * 64 + 48, :], in_=SC["QA"][hp * 128 + hl * 64 + 16:hp * 128 + hl * 64 + 64, 0:sq]), reads=[SCB["QA"]], writes=[bq[hb]])
                        S.dma("dma_start", dict(out=qaT[hb][hl * 64 + 48:hl * 64 + 56, :], in_=SC["QAr"][h * 8:h * 8 + 8, 0:sq]), reads=[SCB["QAr"]], writes=[bq[hb]])
                        S.dma("dma_start", dict(out=qaT[hb][hl * 64 + 56:hl * 64 + 64, :], in_=SC["QAr"][64 + h * 8:64 + h * 8 + 8, 0:sq]), reads=[SCB["QAr"]], writes=[bq[hb]])
                    S.dma("dma_start", dict(out=gaT[hb][:], in_=SC["GA"][hp * 128:(hp + 1) * 128, 0:sq]), reads=[SCB["GA"]], writes=[bg[hb]])

                def tile_geom(tile):
                    (pi, dil, r, t, nblk, je0) = tile
                    e0 = r + dil * (je0 + 128 * t)
                    ksl = slice(e0, e0 + dil * 127 + 1, dil)
                    mlo, mhi = max(t - 1, 0), min(t, nblk - 1)
                    nq = (mhi - mlo + 1) * 128
                    qs = r + dil * 128 * mlo
                    qsl = slice(qs, qs + dil * (nq - 1) + 1, dil)
                    boff = 0 if t >= 1 else 128
                    return ksl, qsl, nq, boff

                pairs = list(range(4)) if lvl >= 4 else []
                if pairs:
                    load_pair(0)
                    load_v(0)
                for hp in pairs:
                    hb = hp % 2
                    if hp + 1 < 4:
                        load_pair(hp + 1)
                    S.op(POOL, "memset", dict(ap=acc[:], constant=0.0), writes=[bacc])
                    for t4 in range(0, nt, 4):
                        n4 = min(4, nt - t4)
                        o_t, o_b = opsr.next()
                        pv = o_t[:, 0, 0:256].bitcast(BF16).rearrange("p (a b) -> p a b", b=128)
                        for j in range(n4):
                            ksl = tile_geom(tiles[t4 + j])[0]
                            S.op(PE, "transpose", dict(out=pv[:, j, :], in_=vaT[hb][:, ksl], identity=ident[:]), reads=[bv[hb], bconst], writes=[o_b])
                        eng = ACT if (t4 // 4) % 2 == 0 else DVE
                        if eng == ACT:
                            S.op(ACT, "activation", dict(out=vt_all[:, t4:t4 + n4, :, 0:64], in_=pv[:, 0:n4, :].rearrange("p a (h d) -> p a h d", h=2), func=AF.Copy), reads=[o_b], writes=[bvta])
                        else:
                            S.op(DVE, "tensor_copy", dict(out=vt_all[:, t4:t4 + n4, :, 0:64], in_=pv[:, 0:n4, :].rearrange("p a (h d) -> p a h d", h=2)), reads=[o_b], writes=[bvta])
                    for hl in range(2):
                        S.op(DVE, "tensor_copy", dict(out=vt_all[:, :, hl, 64], in_=am[:, 0:nt]), reads=[bconst], writes=[bvta])
                    if hp + 1 < 4:
                        load_v(hp + 1)
                    units = [list(range(u, min(u + 2, nt))) for u in range(0, nt, 2)]

                    def scores(unit):
                        sc_t, sc_b = scA.next()
                        geo = []
                        for j, ti in enumerate(unit):
                            ksl, qsl, nq, boff = tile_geom(tiles[ti])
                            geo.append((ti, qsl, nq))
                            for hl in range(2):
                                S.op(PE, "matmul", dict(out=sc_t[:, hl, j * 256:j * 256 + nq], lhsT=kaT[hb][hl * 64:(hl + 1) * 64, ksl], rhs=qaT[hb][hl * 64:(hl + 1) * 64, qsl], start=True, stop=False), reads=[bk[hb], bq[hb]], writes=[sc_b])
                                S.op(PE, "matmul", dict(out=sc_t[:, hl, j * 256:j * 256 + nq], lhsT=ident[:], rhs=bandneg[:, boff:boff + nq], start=False, stop=True), reads=[bconst], writes=[sc_b])
                        p_t, p_b = Pt.next()
                        if len(unit) == 2 and all(g_[2] == 256 for g_ in geo):
                            S.op(ACT, "activation", dict(out=p_t[:], in_=sc_t[:], func=AF.Exp, scale=0.125), reads=[sc_b], writes=[p_b])
                        else:
                            for j, (ti, qsl, nq) in enumerate(geo):
                                S.op(ACT, "activation", dict(out=p_t[:, :, j * 256:j * 256 + nq], in_=sc_t[:, :, j * 256:j * 256 + nq], func=AF.Exp, scale=0.125), reads=[sc_b], writes=[p_b])
                        return geo, p_t, p_b

                    def av(geo, p_t, p_b):
                        o_t, o_b = opsr.next()
                        for j, (ti, qsl, nq) in enumerate(geo):
                            for hl in range(2):
                                for bi in range(nq // 128):
                                    c = j * 256 + bi * 128
                                    S.op(PE, "matmul", dict(out=o_t[0:65, hl, c:c + 128], lhsT=vt_all[:, ti, hl, :], rhs=p_t[:, hl, c:c + 128], start=True, stop=True), reads=[bvta, p_b], writes=[o_b])
                        for j, (ti, qsl, nq) in enumerate(geo):
                            S.op(DVE, "tensor_tensor", dict(out=acc[:, :, qsl], in0=o_t[0:65, :, j * 256:j * 256 + nq], in1=acc[:, :, qsl], op=ALU.add), reads=[o_b, bacc], writes=[bacc])

                    inflight = {0: scores(units[0])}
                    for u in range(len(units)):
                        if u + 1 < len(units):
                            inflight[u + 1] = scores(units[u + 1])
                        if u >= 1:
                            av(*inflight.pop(u - 1))
                    av(*inflight.pop(len(units) - 1))
                    for qc in range(sq // 512):
                        q0 = qc * 512
                        d_t, d_b = opsr.next()
                        dv = d_t[:, 0, :]
                        nv = d_t[:, 1, :]
                        for h2 in range(2):
                            qa_, qb_ = q0 + h2 * 256, q0 + (h2 + 1) * 256
                            for hl in range(2):
                                S.op(PE, "matmul", dict(out=dv[:, h2 * 256:(h2 + 1) * 256], lhsT=sels[:, hl, :], rhs=acc[:, hl, qa_:qb_], start=(hl == 0), stop=(hl == 1)), reads=[bacc, bconst], writes=[d_b])
                        for h2 in range(2):
                            qa_, qb_ = q0 + h2 * 256, q0 + (h2 + 1) * 256
                            for hl in range(2):
                                S.op(PE, "matmul", dict(out=nv[:, h2 * 256:(h2 + 1) * 256], lhsT=sels[:, 2 + hl, :], rhs=acc[:, hl, qa_:qb_], start=(hl == 0), stop=(hl == 1)), reads=[bacc, bconst], writes=[d_b])
                        rc_t, rc_b = recr.next()
                        S.op(DVE, "reciprocal", dict(out=rc_t[:], in_=dv), reads=[d_b], writes=[rc_b])
                        th_t, th_b = thr.next()
                        S.op(ACT, "activation", dict(out=th_t[:], in_=gaT[hb][:, q0:q0 + 512], func=AF.Tanh, scale=0.5), reads=[bg[hb]], writes=[th_b])
                        sg_t, sg_b = sgr.next()
                        S.op(DVE, "scalar_tensor_tensor", dict(out=sg_t[:], in0=th_t[:], scalar=1.0, in1=gaT[hb][:, q0:q0 + 512], op0=ALU.add, op1=ALU.mult), reads=[th_b, bg[hb]], writes=[sg_b])
                        tn_t, tn_b = tnr.next()
                        S.op(DVE, "tensor_tensor", dict(out=tn_t[:], in0=nv, in1=rc_t[:], op=ALU.mult), reads=[d_b, rc_b], writes=[tn_b])
                        y_t, y_b = yTr.next()
                        S.op(DVE, "tensor_tensor", dict(out=y_t[:], in0=tn_t[:], in1=sg_t[:], op=ALU.mult), reads=[tn_b, sg_b], writes=[y_b])
                        S.dma("dma_start", dict(out=SC["YM"][hp * 128:(hp + 1) * 128, q0:q0 + 512], in_=y_t[:]), reads=[y_b], writes=[SCB["YM"]])
                S.barrier()

            with contextlib.ExitStack() as p3:
                ymr = Ring([sb(p3, nm("ym"), [128, 8, 512], BF16) for _ in range(2)])
                xr = Ring([sb(p3, nm("x3"), [128, D], F32) for _ in range(2)])
                tmr = Ring([sb(p3, nm("tm3"), [128, D], F32) for _ in range(2)])
                outr = Ring([sb(p3, nm("o3"), [128, D], F32) for _ in range(2)])
                st3 = Ring([sb(p3, nm("st3"), [128, 6], F32) for _ in range(4)])
                junk3 = sb(p3, nm("junk3"), [128, 512], F32)
                bj3 = Buf()
                po = Ring([ps(p3, nm("po"), [128, 2, 512], F32) for _ in range(2)])
                for qc in (range(sq // 512) if lvl >= 5 else []):
                    q0 = qc * 512
                    ym_t, ym_b = ymr.next()
                    S.dma("dma_start", dict(out=ym_t[:], in_=SC["YM"][:, q0:q0 + 512].rearrange("(kc p) t -> p kc t", p=128)), reads=[SCB["YM"]], writes=[ym_b])
                    for i in range(4):
                        row = q0 + i * 128
                        x_t, x_b = xr.next()
                        S.dma("dma_start", dict(out=x_t[:], in_=X[job["xrow"] + row:job["xrow"] + row + 128, :]), writes=[x_b])
                        po_t, po_b = po.next()
                        for hf in range(2):
                            for kc in range(8):
                                S.op(PE, "matmul", dict(out=po_t[:, hf, :], lhsT=ym_t[:, kc, i * 128:(i + 1) * 128], rhs=wout[:, kc, hf * 512:(hf + 1) * 512], start=(kc == 0), stop=(kc == 7)), reads=[ym_b, bwout], writes=[po_b])
                        s_t, s_b = st3.next()
                        for hf in range(2):
                            S.op(ACT, "activation", dict(out=junk3[:], in_=po_t[:, hf, :], func=AF.Square, accum_out=s_t[:, hf:hf + 1]), reads=[po_b], writes=[bj3, s_b])
                        S.op(DVE, "tensor_tensor", dict(out=s_t[:, 2:3], in0=s_t[:, 0:1], in1=s_t[:, 1:2], op=ALU.add), reads=[s_b], writes=[s_b])
                        S.op(DVE, "tensor_scalar", dict(out=s_t[:, 3:4], in0=s_t[:, 2:3], scalar1=1.0 / D, scalar2=EPS, op0=ALU.mult, op1=ALU.add), reads=[s_b], writes=[s_b])
                        S.op(POOL, "tensor_tensor", dict(out=s_t[:, 4:5], in0=s_t[:, 3:4], in1=mhalf[:], op=ALU.pow), reads=[s_b, bconst], writes=[s_b])
                        tm_t, tm_b = tmr.next()
                        for hf in range(2):
                            S.op(DVE, "scalar_tensor_tensor", dict(out=tm_t[:, hf * 512:(hf + 1) * 512], in0=po_t[:, hf, :], scalar=s_t[:, 4:5], in1=gg[s][:, hf * 512:(hf + 1) * 512], op0=ALU.mult, op1=ALU.mult), reads=[po_b, s_b, bgg[s]], writes=[tm_b])
                        o_t, o_b = outr.next()
                        S.op(POOL, "tensor_tensor", dict(out=o_t[:], in0=tm_t[:], in1=x_t[:], op=ALU.add), reads=[tm_b, x_b], writes=[o_b])
                        yrow = job["yrow"] + row
                        out_dmas.append(S.dma("dma_start", dict(out=Y[yrow:yrow + 128, :], in_=o_t[:]), reads=[o_b], writes=[Buf()]))
                S.barrier()

        S.run(final_wait=out_dmas)
    return nc


_NC_CACHE = {}


def _rope_tables(pos):
    half = 8
    inv = (np.float32(500000.0) ** (-np.arange(half, dtype=np.float32) / np.float32(half))).astype(np.float32)
    ang = pos.astype(np.float32)[None, :] * inv[:, None]
    cos = np.cos(ang).astype(np.float32)
    sin = np.sin(ang).astype(np.float32)
    idx = np.arange(128) % 8
    C = cos[idx]
    Sg = sin[idx].copy()
    Sg[:64] *= -1.0
    return np.ascontiguousarray(C), np.ascontiguousarray(Sg)


def _amask(valid_ext, sq):
    tiles = amask_tiles(sq)
    m = np.zeros((128, len(tiles)), np.float32)
    p = np.arange(128)
    for ti, (pi, dil, r, t, nblk, je0) in enumerate(tiles):
        e = r + dil * (je0 + 128 * t + p)
        m[:, ti] = valid_ext[e]
    return m


def prep_inputs(x_prompt, x_sample, c_prompt, c_sample, w_in, w_out, g_pre, g_post,
                w_ada, b_ada, lam_q1, lam_k1, lam_q2, lam_k2, g_sub):
    f32 = np.float32
    x_prompt = np.asarray(x_prompt, f32)
    x_sample = np.asarray(x_sample, f32)
    c_prompt = np.asarray(c_prompt, f32)
    c_sample = np.asarray(c_sample, f32)
    w_in0 = np.ascontiguousarray(np.asarray(w_in, f32)[0])
    w_out0 = np.ascontiguousarray(np.asarray(w_out, f32)[0])
    w_ada0 = np.ascontiguousarray(np.asarray(w_ada, f32)[0])
    b_ada0 = np.asarray(b_ada, f32)[0]
    g_pre0 = np.asarray(g_pre, f32)[0]
    g_post0 = np.asarray(g_post, f32)[0]
    g_sub0 = np.asarray(g_sub, f32)[0]

    offs = dict(qa=0, ka=512, qb=2048, kb=2560)
    w_rot = np.zeros((D, 4, 192), f32)
    for ti, nme in enumerate(("qa", "ka", "qb", "kb")):
        x1 = np.array([offs[nme] + h * 64 + i for h in range(8) for i in range(8)])
        x2 = x1 + 8
        w_rot[:, ti, 0:64] = w_in0[:, x1]
        w_rot[:, ti, 64:128] = w_in0[:, x2]
        w_rot[:, ti, 128:192] = w_in0[:, x1]
    w_rot = np.ascontiguousarray(w_rot.reshape(D, 768))

    g_pre_t = np.ascontiguousarray(g_pre0.reshape(8, 128).T)
    b_ada_t = np.ascontiguousarray(b_ada0[:2048].reshape(16, 128).T)
    b_gate = np.ascontiguousarray(b_ada0[2048:3072].reshape(1, D))
    g_post_r = np.ascontiguousarray(g_post0.reshape(1, D))
    g_sub_t = np.ascontiguousarray(g_sub0.reshape(128, 1))
    lam_v = np.ascontiguousarray(np.stack([np.asarray(v, f32)[0] for v in (lam_q1, lam_k1, lam_q2, lam_k2)], 0))
    ident = np.eye(128, dtype=f32)
    p = np.arange(128)[:, None]
    f = np.arange(128)[None, :]
    band = np.concatenate([(p <= f), (p >= f)], axis=1).astype(f32)
    band = np.ascontiguousarray(np.concatenate([band, band], axis=1))
    sels = np.zeros((65, 4, 128), f32)
    sels[64, 0, 0:64] = 2.0
    sels[64, 1, 64:128] = 2.0
    sels[np.arange(64), 2, np.arange(64)] = 1.0
    sels[np.arange(64), 3, 64 + np.arange(64)] = 1.0
    sels = np.ascontiguousarray(sels.reshape(65, 512))
    valid1 = np.zeros(DSEQ + 2 * PADK, f32)
    valid1[PADK:PADK + DSEQ] = 1.0
    am1 = _amask(valid1, DSEQ)

    in_maps = []
    for c in range(8):
        psq, hf = c // 2, c % 2
        q0 = hf * HALF
        o0 = (1 - hf) * HALF
        xa = np.zeros((NROWS, D), f32)
        pos = np.zeros(NROWS, np.int64)
        xa[ROW_OWN:ROW_OWN + HALF] = x_prompt[psq, q0:q0 + HALF]
        pos[ROW_OWN:ROW_OWN + HALF] = np.arange(q0, q0 + HALF)
        xa[ROW_OTHER:ROW_OTHER + HALF] = x_prompt[psq, o0:o0 + HALF]
        pos[ROW_OTHER:ROW_OTHER + HALF] = np.arange(o0, o0 + HALF)
        hpos = np.concatenate([np.arange(q0 - PADK, q0), np.arange(q0 + HALF, q0 + HALF + PADK)])
        hval = (hpos >= 0) & (hpos < SEQ)
        xa[ROW_HALO:ROW_HALO + 2 * PADK][hval] = x_prompt[psq, hpos[hval]]
        pos[ROW_HALO:ROW_HALO + 2 * PADK] = np.clip(hpos, 0, SEQ - 1)
        xa[ROW_S0:ROW_S0 + DSEQ] = x_sample[2 * c]
        pos[ROW_S0:ROW_S0 + DSEQ] = np.arange(DSEQ)
        xa[ROW_S1:ROW_S1 + DSEQ] = x_sample[2 * c + 1]
        pos[ROW_S1:ROW_S1 + DSEQ] = np.arange(DSEQ)
        C, Sg = _rope_tables(pos)
        cs = np.stack([c_prompt[psq], c_sample[2 * c], c_sample[2 * c + 1]], axis=-1)
        c_t = np.ascontiguousarray(cs.reshape(8, 128, 3).transpose(1, 0, 2).reshape(128, 24))
        valid0 = np.ones(HALF + 2 * PADK, f32)
        valid0[0:PADK] = hval[:PADK]
        valid0[PADK + HALF:] = hval[PADK:]
        am0 = _amask(valid0, HALF)
        vrow = np.ascontiguousarray(np.broadcast_to(hval.astype(f32)[None, :], (128, 2 * PADK)))
        in_maps.append({
            "x_all": xa, "rope_c": C, "rope_s": Sg, "c_t": c_t, "w_in": w_in0, "w_rot": w_rot,
            "w_out": w_out0, "w_ada": w_ada0, "g_pre_t": g_pre_t, "b_ada_t": b_ada_t, "b_gate": b_gate,
            "g_post": g_post_r, "g_sub_t": g_sub_t, "lam_v": lam_v, "ident": ident, "band": band,
            "sels": sels, "amask0": am0, "amask1": am1, "vrow": vrow,
        })

    return in_maps


def kernel(**inputs):
    f32 = np.float32
    in_maps = prep_inputs(**inputs)
    if "nc" not in _NC_CACHE:
        _NC_CACHE["nc"] = build_program()
    nc = _NC_CACHE["nc"]
    res = run_bass_kernel_spmd(nc, in_maps, core_ids=list(range(8)))
    y_prompt = np.zeros((4, SEQ, D), f32)
    y_sample = np.zeros((16, DSEQ, D), f32)
    for c in range(8):
        y = np.asarray(res.results[c]["y_all"], f32)
        psq, hf = c // 2, c % 2
        y_prompt[psq, hf * HALF:(hf + 1) * HALF] = y[0:HALF]
        y_sample[2 * c] = y[HALF:HALF + DSEQ]
        y_sample[2 * c + 1] = y[HALF + DSEQ:HALF + 2 * DSEQ]
    return (y_prompt, y_sample)
```

```python
import contextlib
import numpy as np
import concourse.bass as bass
import concourse.mybir as mybir
from concourse.bass_utils import run_bass_kernel_spmd

F32 = mybir.dt.float32
BF16 = mybir.dt.bfloat16
AF = mybir.ActivationFunctionType
ALU = mybir.AluOpType

PE, ACT, DVE, POOL, SP = "tensor", "scalar", "vector", "gpsimd", "sync"
ENGINES = (PE, ACT, DVE, POOL, SP)

D = 1024
SEQ = 8192
DSEQ = 2048
HALF = 4096
PADK = 1024
PATTERNS = ((128, 1), (512, 4), (2048, 16))
EPS = 1e-6
LAM_INIT = 0.2
NROWS = HALF + HALF + 2 * PADK + 2 * DSEQ
ROW_OWN, ROW_OTHER, ROW_HALO, ROW_S0, ROW_S1 = 0, 4096, 8192, 10240, 12288
NOUT = HALF + 2 * DSEQ
WCOLS = 4096 + 4 * 192


class Buf:
    __slots__ = ("last_write", "reads")

    def __init__(self):
        self.last_write = None
        self.reads = []


class Rec:
    __slots__ = ("eng", "fn", "deps", "is_dma", "signal", "count", "dsem", "dval", "epoch")

    def __init__(self, eng, fn, is_dma):
        self.eng, self.fn, self.is_dma = eng, fn, is_dma
        self.deps = []
        self.signal = False
        self.count = 0
        self.dsem = None
        self.dval = 0
        self.epoch = 0


class Sched:
    EPOCH_MAX = 30000

    def __init__(self, nc, n_dma_sems=20):
        self.nc = nc
        self.q = {e: [] for e in ENGINES}
        self.n_dma_sems = n_dma_sems
        self.dma_rr = {e: 0 for e in ENGINES}
        self.dma_last = {}
        self.last_compute = {}

    def op(self, eng, meth, kw, reads=(), writes=(), dma=False, extra=()):
        r = Rec(eng, (meth, kw), dma)
        deps = list(extra)
        for b in reads:
            if b.last_write is not None:
                deps.append(b.last_write)
        for b in writes:
            if b.last_write is not None:
                deps.append(b.last_write)
            deps.extend(b.reads)
        if dma:
            slot = self.dma_rr[eng] % self.n_dma_sems
            self.dma_rr[eng] += 1
            prev = self.dma_last.get((eng, slot))
            if prev is not None:
                deps.append(prev)
            self.dma_last[(eng, slot)] = r
            r.dsem = (eng, slot)
        seen = set()
        for d in deps:
            if d is r or id(d) in seen:
                continue
            if (not d.is_dma) and (not dma) and d.eng == PE and eng == PE:
                continue
            seen.add(id(d))
            r.deps.append(d)
        for b in reads:
            b.reads.append(r)
        for b in writes:
            b.last_write = r
            b.reads = []
        self.q[eng].append(r)
        if not dma:
            self.last_compute[eng] = r
        return r

    def dma(self, meth, kw, reads=(), writes=(), eng=SP):
        return self.op(eng, meth, kw, reads, writes, dma=True)

    def barrier(self):
        deps = list(self.last_compute.values()) + list(self.dma_last.values())
        for e in ENGINES:
            r = Rec(e, None, False)
            for d in deps:
                if (not d.is_dma) and d.eng == e:
                    continue
                r.deps.append(d)
            self.q[e].append(r)

    def run(self, final_wait=()):
        nc = self.nc
        for e in ENGINES:
            for r in self.q[e]:
                for d in r.deps:
                    if not d.is_dma:
                        d.signal = True
        n_epochs = {}
        for e in ENGINES:
            cnt, ep = 0, 0
            for r in self.q[e]:
                if r.is_dma or r.fn is None:
                    continue
                if r.signal:
                    if cnt >= self.EPOCH_MAX:
                        ep += 1
                        cnt = 0
                    cnt += 1
                    r.count = cnt
                    r.epoch = ep
            n_epochs[e] = ep + 1
        dcount = {}
        for e in ENGINES:
            for r in self.q[e]:
                if r.is_dma:
                    dcount[r.dsem] = dcount.get(r.dsem, 0) + 16
                    r.dval = dcount[r.dsem]
        with contextlib.ExitStack() as st:
            csem = {}
            for e in ENGINES:
                for ep in range(n_epochs[e]):
                    if any((not r.is_dma) and r.signal and r.epoch == ep for r in self.q[e]):
                        csem[(e, ep)] = st.enter_context(nc.semaphore(f"c_{e}_{ep}"))
            dsem = {}
            for key in dcount:
                dsem[key] = st.enter_context(nc.semaphore(f"d_{key[0]}_{key[1]}"))
            block = st.enter_context(nc.Block())

            def emit(e, engine):
                waited = {}

                def do_wait(d):
                    if d.is_dma:
                        s, v, k = dsem[d.dsem], d.dval, ("d",) + d.dsem
                    else:
                        s, v, k = csem[(d.eng, d.epoch)], d.count, ("c", d.eng, d.epoch)
                    if waited.get(k, 0) >= v:
                        return
                    waited[k] = v
                    engine.wait_ge(s, v)

                for r in self.q[e]:
                    for d in r.deps:
                        do_wait(d)
                    if r.fn is None:
                        continue
                    ins = getattr(engine, r.fn[0])(**r.fn[1])
                    if r.is_dma:
                        ins.then_inc(dsem[r.dsem], 16)
                    elif r.signal:
                        ins.then_inc(csem[(e, r.epoch)], 1)
                if e == SP:
                    for d in final_wait:
                        do_wait(d)

            @block.tensor
            def _(eng):
                emit(PE, eng)

            @block.scalar
            def _(eng):
                emit(ACT, eng)

            @block.vector
            def _(eng):
                emit(DVE, eng)

            @block.gpsimd
            def _(eng):
                emit(POOL, eng)

            @block.sync
            def _(eng):
                emit(SP, eng)


class Ring:
    def __init__(self, tiles):
        self.tiles = tiles
        self.bufs = [Buf() for _ in tiles]
        self.i = 0

    def next(self):
        k = self.i % len(self.tiles)
        self.i += 1
        return self.tiles[k], self.bufs[k]


def amask_tiles(sq):
    out = []
    for pi, (w, dil) in enumerate(PATTERNS):
        nblk = sq // dil // 128
        je0 = PADK // dil - 64
        for r in range(dil):
            for t in range(nblk + 1):
                out.append((pi, dil, r, t, nblk, je0))
    return out


def build_program(stop_after=None, jobs=(0, 1, 2)):
    nc = bass.Bass("TRN2", target_bir_lowering=False)
    lvl = {'p0': 0, 'w': 1, 'p1': 2, '2b': 3, '2a': 4, None: 5}[stop_after]

    def din(name, shape, dt=F32):
        return nc.dram_tensor(name, list(shape), dt, kind="ExternalInput").ap()

    X = din("x_all", [NROWS, D])
    CT = din("rope_c", [128, NROWS])
    ST = din("rope_s", [128, NROWS])
    CTT = din("c_t", [128, 24])
    W_IN = din("w_in", [D, 4096])
    W_ROT = din("w_rot", [D, 768])
    W_OUT = din("w_out", [D, D])
    W_ADA = din("w_ada", [D, 3 * D])
    GPRE = din("g_pre_t", [128, 8])
    BADA = din("b_ada_t", [128, 16])
    BGATE = din("b_gate", [1, D])
    GPOST = din("g_post", [1, D])
    GSUB = din("g_sub_t", [128, 1])
    LAMV = din("lam_v", [4, 64])
    IDENT = din("ident", [128, 128])
    BAND = din("band", [128, 512])
    SELS = din("sels", [65, 512])
    AM0 = din("amask0", [128, 117])
    VROW = din("vrow", [128, 2 * PADK])
    AM1 = din("amask1", [128, 69])
    Y = nc.dram_tensor("y_all", [NOUT, D], F32, kind="ExternalOutput").ap()

    JOBS = [
        dict(s=0, sq=HALF, skv=SEQ, sext=HALF + 2 * PADK, xrow=ROW_OWN, yrow=0, am=AM0),
        dict(s=1, sq=DSEQ, skv=DSEQ, sext=DSEQ + 2 * PADK, xrow=ROW_S0, yrow=HALF, am=AM1),
        dict(s=2, sq=DSEQ, skv=DSEQ, sext=DSEQ + 2 * PADK, xrow=ROW_S1, yrow=HALF + DSEQ, am=AM1),
    ]
    def scr(name, rows, cols):
        return nc.dram_tensor(name, [rows, cols], BF16, kind="Internal").ap()

    SC = dict(
        QA=scr("s_qa", 512, HALF), QAr=scr("s_qar", 128, HALF), GA=scr("s_ga", 512, HALF),
        KA=scr("s_ka", 512, HALF + 2 * PADK), KAr=scr("s_kar", 128, HALF + 2 * PADK),
        VA=scr("s_va", 512, HALF + 2 * PADK),
        QB=scr("s_qb", 512, HALF), QBr=scr("s_qbr", 128, HALF), GB=scr("s_gb", 512, HALF),
        KB=scr("s_kb", 512, SEQ), KBr=scr("s_kbr", 128, SEQ), VB=scr("s_vb", 512, SEQ),
        YM=scr("s_ym", 1024, HALF),
    )
    SCB = {k: Buf() for k in SC}
    WS = scr("s_w", D, WCOLS)
    bWS = Buf()

    S = Sched(nc)
    out_dmas = []
    top = contextlib.ExitStack()
    with top:
        def sb(st, name, shape, dt):
            return st.enter_context(nc.sbuf_tensor("sb_" + name, list(shape), dt))

        def ps(st, name, shape, dt):
            return st.enter_context(nc.psum_tensor("ps_" + name, list(shape), dt))

        uid = [0]

        def nm(p):
            uid[0] += 1
            return f"{p}{uid[0]}"

        ident_f = sb(top, "ident_f", [128, 128], F32)
        ident = sb(top, "ident", [128, 128], BF16)
        band_f = sb(top, "band_f", [128, 512], F32)
        bandneg = sb(top, "bandneg", [128, 256], BF16)
        sels = sb(top, "sels", [65, 4, 128], F32)
        am0 = sb(top, "am0", [128, 117], F32)
        am1 = sb(top, "am1", [128, 69], F32)
        ones_f = sb(top, "ones_f", [128, 128], F32)
        mhalf = sb(top, "mhalf", [128, 1], F32)
        zer = sb(top, "zer", [1, 512], BF16)
        gsub2 = sb(top, "gsub2", [128, 1], F32)
        neglam = sb(top, "neglam", [128, 1], F32)
        lamt = sb(top, "lamt", [128, 4, 64], F32)
        lamj = sb(top, "lamj", [128, 64], F32)
        lams = sb(top, "lams", [128, 4], F32)
        gg = [sb(top, f"gg{s}", [128, D], F32) for s in range(3)]
        gsT = sb(top, "gsT", [128, 8, 4], F32)
        shT = sb(top, "shT", [128, 8, 4], F32)
        wout = sb(top, "wout", [128, 8, D], BF16)
        bconst = Buf()
        bgg = [Buf() for _ in range(3)]
        bmod = Buf()
        bwout = Buf()

        biasA = sb(top, "biasA", [128, 40, 4], F32)
        bbiasA = Buf()
        S.dma("dma_start", dict(out=ident_f[:], in_=IDENT[:, :]), writes=[bconst])
        S.dma("dma_start", dict(out=band_f[:], in_=BAND[:, :]), writes=[bconst])
        S.dma("dma_start", dict(out=sels[:].rearrange("p a b -> p (a b)"), in_=SELS[:, :]), writes=[bconst])
        S.dma("dma_start", dict(out=am0[:], in_=AM0[:, :]), writes=[bconst])
        S.dma("dma_start", dict(out=am1[:], in_=AM1[:, :]), writes=[bconst])
        S.dma("dma_start", dict(out=gsub2[:], in_=GSUB[:, :]), writes=[bconst])
        for i in range(4):
            S.dma("dma_start", dict(out=lamt[:, i, :], in_=LAMV[i:i + 1, :].broadcast_to([128, 64])), writes=[bconst])
        S.op(DVE, "tensor_copy", dict(out=ident[:], in_=ident_f[:]), reads=[bconst], writes=[bconst])
        S.op(DVE, "tensor_scalar", dict(out=bandneg[:], in0=band_f[:, 0:256], scalar1=30000.0, scalar2=-30000.0, op0=ALU.mult, op1=ALU.add), reads=[bconst], writes=[bconst])
        S.op(POOL, "memset", dict(ap=ones_f[:], constant=1.0), writes=[bconst])
        S.op(POOL, "memset", dict(ap=mhalf[:], constant=-0.5), writes=[bconst])
        S.op(POOL, "memset", dict(ap=zer[:], constant=0.0), writes=[bconst])
        S.op(DVE, "tensor_scalar", dict(out=gsub2[:], in0=gsub2[:], scalar1=(1.0 - LAM_INIT) * 0.5, scalar2=None, op0=ALU.mult), reads=[bconst], writes=[bconst])
        for i in range(2):
            S.op(DVE, "tensor_tensor", dict(out=lamj[:], in0=lamt[:, 2 * i, :], in1=lamt[:, 2 * i + 1, :], op=ALU.mult), reads=[bconst], writes=[bconst])
            S.op(ACT, "activation", dict(out=lamj[:], in_=lamj[:], func=AF.Identity, accum_out=lams[:, i:i + 1]), reads=[bconst], writes=[bconst])
        S.op(ACT, "activation", dict(out=lams[:, 2:4], in_=lams[:, 0:2], func=AF.Exp), reads=[bconst], writes=[bconst])
        S.op(DVE, "tensor_tensor", dict(out=neglam[:], in0=lams[:, 3:4], in1=lams[:, 2:3], op=ALU.subtract), reads=[bconst], writes=[bconst])
        S.op(DVE, "tensor_scalar", dict(out=neglam[:], in0=neglam[:], scalar1=-LAM_INIT, scalar2=None, op0=ALU.add), reads=[bconst], writes=[bconst])

        with contextlib.ExitStack() as p0:
            stg = [sb(p0, f"stg0_{i}", [128, 8, 512], F32) for i in range(2)]
            bstg = [Buf() for _ in range(2)]
            ct = sb(p0, "ct", [128, 24], F32)
            th = sb(p0, "th0", [128, 24], F32)
            scT = sb(p0, "scT", [128, 8, 4], F32)
            screp = sb(p0, "screp", [128, 24, 128], F32)
            gpre = sb(p0, "gpre", [128, 8], F32)
            bada = sb(p0, "bada", [128, 16], F32)
            bgate = sb(p0, "bgate", [128, D], F32)
            gpost = sb(p0, "gpost", [128, D], F32)
            tmpm = sb(p0, "tmpm", [128, 8], F32)
            tmpg = sb(p0, "tmpg", [128, 512], F32)
            pmod = ps(p0, "pmod", [128, 16, 4], F32)
            pg = [ps(p0, f"pg{i}", [128, 512], F32) for i in range(3)]
            b0 = Buf()
            bpm = Buf()
            bpg = [Buf() for _ in range(3)]
            S.dma("dma_start", dict(out=ct[:], in_=CTT[:, :]), writes=[b0])
            S.dma("dma_start", dict(out=gpre[:], in_=GPRE[:, :]), writes=[b0])
            S.dma("dma_start", dict(out=bada[:], in_=BADA[:, :]), writes=[b0])
            S.dma("dma_start", dict(out=bgate[:], in_=BGATE[0:1, :].broadcast_to([128, D])), writes=[b0])
            S.dma("dma_start", dict(out=gpost[:], in_=GPOST[0:1, :].broadcast_to([128, D])), writes=[b0])
            S.op(ACT, "activation", dict(out=th[:], in_=ct[:], func=AF.Tanh, scale=0.5), reads=[b0], writes=[b0])
            S.op(DVE, "scalar_tensor_tensor", dict(out=th[:], in0=th[:], scalar=1.0, in1=ct[:], op0=ALU.add, op1=ALU.mult), reads=[b0], writes=[b0])
            S.op(POOL, "memset", dict(ap=scT[:], constant=0.0), writes=[b0])
            S.op(POOL, "memset", dict(ap=shT[:], constant=0.0), writes=[bmod])
            S.op(POOL, "memset", dict(ap=gsT[:], constant=0.0), writes=[bmod])
            S.op(DVE, "tensor_scalar", dict(out=scT[:, :, 0:3], in0=th[:].rearrange("p (a b) -> p a b", b=3), scalar1=0.5, scalar2=None, op0=ALU.mult), reads=[b0], writes=[b0])
            for i in range(24):
                kc, s = divmod(i, 3)
                S.op(DVE, "tensor_scalar", dict(out=screp[:, i, :], in0=ones_f[:], scalar1=scT[:, kc, s:s + 1], scalar2=None, op0=ALU.mult), reads=[b0, bconst], writes=[b0])
            for pc in range(6):
                k = pc % 2
                S.dma("dma_start", dict(out=stg[k][:], in_=W_ADA[:, pc * 512:(pc + 1) * 512].rearrange("(kc p) n -> p kc n", p=128)), writes=[bstg[k]])
                if pc < 4:
                    for fb in range(4):
                        blk = pc * 4 + fb
                        for kc in range(8):
                            S.op(PE, "matmul", dict(out=pmod[:, blk, :], lhsT=stg[k][:, kc, fb * 128:(fb + 1) * 128], rhs=scT[:, kc, :], start=(kc == 0), stop=(kc == 7)), reads=[bstg[k], b0], writes=[bpm])
                else:
                    hf = pc - 4
                    for s in range(3):
                        for h2 in range(2):
                            for kc in range(8):
                                S.op(PE, "matmul", dict(out=pg[s][:, h2 * 256:(h2 + 1) * 256], lhsT=screp[:, kc * 3 + s, :], rhs=stg[k][:, kc, h2 * 256:(h2 + 1) * 256], start=(kc == 0), stop=(kc == 7)), reads=[bstg[k], b0], writes=[bpg[s]])
                        S.op(DVE, "tensor_tensor", dict(out=tmpg[:], in0=pg[s][:], in1=bgate[:, hf * 512:(hf + 1) * 512], op=ALU.add), reads=[bpg[s], b0], writes=[b0])
                        S.op(DVE, "tensor_tensor", dict(out=gg[s][:, hf * 512:(hf + 1) * 512], in0=tmpg[:], in1=gpost[:, hf * 512:(hf + 1) * 512], op=ALU.mult), reads=[b0], writes=[bgg[s]])
            for s in range(3):
                S.op(DVE, "tensor_tensor", dict(out=shT[:, :, s], in0=pmod[:, 0:8, s], in1=bada[:, 0:8], op=ALU.add), reads=[bpm, b0], writes=[bmod])
                S.op(DVE, "scalar_tensor_tensor", dict(out=tmpm[:], in0=pmod[:, 8:16, s], scalar=1.0, in1=bada[:, 8:16], op0=ALU.add, op1=ALU.add), reads=[bpm, b0], writes=[b0])
                S.op(DVE, "tensor_tensor", dict(out=gsT[:, :, s], in0=tmpm[:], in1=gpre[:], op=ALU.mult), reads=[b0], writes=[bmod])
            for pc in range(2):
                k = pc % 2
                S.dma("dma_start", dict(out=stg[k][:], in_=W_OUT[:, pc * 512:(pc + 1) * 512].rearrange("(kc p) n -> p kc n", p=128)), writes=[bstg[k]])
                S.op(DVE, "tensor_copy", dict(out=wout[:, :, pc * 512:(pc + 1) * 512], in_=stg[k][:]), reads=[bstg[k]], writes=[bwout])
            wbr = Ring([sb(p0, f"wbst{i}", [128, 8, 512], BF16) for i in range(2)])
            pb = ps(p0, "pb0", [128, 40, 4], F32)
            bpb = Buf()
            pieces = [(W_IN, pc * 512, 512, pc * 512) for pc in range(8)] + [(W_ROT, 0, 384, 4096), (W_ROT, 384, 384, 4096 + 384)]
            for pi_, (wsrc, c0, ncol, dcol) in enumerate(pieces):
                k = pi_ % 2
                S.dma("dma_start", dict(out=stg[k][:, :, 0:ncol], in_=wsrc[:, c0:c0 + ncol].rearrange("(kc p) n -> p kc n", p=128)), writes=[bstg[k]])
                wb_t, wb_b = wbr.next()
                S.op(ACT, "activation", dict(out=wb_t[:, 0:4, 0:ncol], in_=stg[k][:, 0:4, 0:ncol], func=AF.Copy), reads=[bstg[k]], writes=[wb_b])
                S.op(DVE, "tensor_copy", dict(out=wb_t[:, 4:8, 0:ncol], in_=stg[k][:, 4:8, 0:ncol]), reads=[bstg[k]], writes=[wb_b])
                S.dma("dma_start", dict(out=WS[:, dcol:dcol + ncol].rearrange("(kc p) n -> p kc n", p=128), in_=wb_t[:, :, 0:ncol]), reads=[wb_b], writes=[bWS])
                if pi_ < 8:
                    cols = [(fb * 128, pi_ * 4 + fb) for fb in range(4)]
                else:
                    t0 = (pi_ - 8) * 2
                    cols = [(0, 32 + 2 * t0), (64, 33 + 2 * t0), (192, 34 + 2 * t0), (256, 35 + 2 * t0)]
                for (co, bcol) in cols:
                    for kc in range(8):
                        S.op(PE, "matmul", dict(out=pb[:, bcol, :], lhsT=stg[k][:, kc, co:co + 128], rhs=shT[:, kc, :], start=(kc == 0), stop=(kc == 7)), reads=[bstg[k], bmod], writes=[bpb])
            S.op(DVE, "tensor_copy", dict(out=biasA[:], in_=pb[:]), reads=[bpb], writes=[bbiasA])
            S.barrier()

        for job in [JOBS[j_] for j_ in jobs] if stop_after != 'p0' else []:
            s, sq, skv, sext = job["s"], job["sq"], job["skv"], job["sext"]
            am = am0 if job["am"] is AM0 else am1
            with contextlib.ExitStack() as p1:
                wp = sb(p1, nm("wp"), [128, 8, WCOLS], BF16)
                bwp = Buf()
                gsrep = sb(p1, nm("gsrep"), [128, 8, 128], F32)
                bgsrep = Buf()
                if lvl >= 1:
                    for c0 in range(0, WCOLS, 1216):
                        c1 = min(c0 + 1216, WCOLS)
                        S.dma("dma_start", dict(out=wp[:, :, c0:c1], in_=WS[:, c0:c1].rearrange("(kc p) n -> p kc n", p=128)), reads=[bWS], writes=[bwp])
                    for kc in range(8):
                        S.op(DVE, "tensor_scalar", dict(out=gsrep[:, kc, :], in0=ones_f[:], scalar1=gsT[:, kc, s:s + 1], scalar2=None, op0=ALU.mult), reads=[bmod, bconst], writes=[bgsrep])
                biasT = biasA[:, :, s]
                bbias = bbiasA

                with contextlib.ExitStack() as pp:
                    xt = Ring([sb(pp, nm("xt"), [128, D], F32) for _ in range(4)])
                    junk = sb(pp, nm("junk"), [128, D], F32)
                    bjunk = Buf()
                    xn = Ring([sb(pp, nm("xn"), [128, D], F32) for _ in range(4)])
                    xnT = Ring([sb(pp, nm("xnT"), [128, 8, 512], BF16) for _ in range(2)])
                    stat = Ring([sb(pp, nm("stat"), [128, 4], F32) for _ in range(8)])
                    ost = Ring([sb(pp, nm("ost"), [128, 4, 512], BF16) for _ in range(3)])
                    cst = Ring([sb(pp, nm("cst"), [128, 2, 512], F32) for _ in range(2)])
                    t12 = Ring([sb(pp, nm("t12"), [128, 2, 512], F32) for _ in range(2)])
                    rot = Ring([sb(pp, nm("rot"), [128, 512], BF16) for _ in range(2)])
                    pT = Ring([ps(pp, nm("pT"), [128, D], F32) for _ in range(2)])
                    pacc = Ring([ps(pp, nm("pacc"), [128, 512], F32) for _ in range(4)])

                    G = dict(qa=(0, 0), ka=(512, 4), va=(1024, 8), ga=(1536, 12), qb=(2048, 16), kb=(2560, 20), vb=(3072, 24), gb=(3584, 28))
                    RT = dict(qa=0, ka=1, qb=2, kb=3)
                    segs = []
                    full_main = [("qa", "QA", 0), ("ka", "KA", PADK), ("va", "VA", PADK), ("ga", "GA", 0), ("qb", "QB", 0), ("kb", "KB", 0), ("vb", "VB", 0), ("gb", "GB", 0)]
                    full_rot = [("qa", "QAr", 0), ("ka", "KAr", PADK), ("qb", "QBr", 0), ("kb", "KBr", 0)]
                    segs.append((job["xrow"], sq, full_main, full_rot))
                    if s == 0:
                        segs.append((ROW_OTHER, HALF, [("kb", "KB", HALF), ("vb", "VB", HALF)], [("kb", "KBr", HALF)]))
                        segs.append((ROW_HALO, PADK, [("ka", "KA", 0), ("va", "VA", 0)], [("ka", "KAr", 0)]))
                        segs.append((ROW_HALO + PADK, PADK, [("ka", "KA", PADK + HALF), ("va", "VA", PADK + HALF)], [("ka", "KAr", PADK + HALF)]))
                    evi = [0]
                    for (xr0, T, mains, rots) in (segs if lvl >= 2 else []):
                        for ch in range(T // 512):
                            r0 = xr0 + ch * 512
                            xT_t, xT_b = xnT.next()
                            for i in range(4):
                                x_t, x_b = xt.next()
                                S.dma("dma_start", dict(out=x_t[:], in_=X[r0 + i * 128:r0 + (i + 1) * 128, :]), writes=[x_b])
                                st_t, st_b = stat.next()
                                S.op(ACT, "activation", dict(out=junk[:], in_=x_t[:], func=AF.Square, accum_out=st_t[:, 0:1]), reads=[x_b], writes=[bjunk, st_b])
                                S.op(DVE, "tensor_scalar", dict(out=st_t[:, 1:2], in0=st_t[:, 0:1], scalar1=1.0 / D, scalar2=EPS, op0=ALU.mult, op1=ALU.add), reads=[st_b], writes=[st_b])
                                S.op(POOL, "tensor_tensor", dict(out=st_t[:, 2:3], in0=st_t[:, 1:2], in1=mhalf[:], op=ALU.pow), reads=[st_b, bconst], writes=[st_b])
                                xn_t, xn_b = xn.next()
                                S.op(DVE, "tensor_scalar", dict(out=xn_t[:], in0=x_t[:], scalar1=st_t[:, 2:3], scalar2=None, op0=ALU.mult), reads=[x_b, st_b], writes=[xn_b])
                                pT_t, pT_b = pT.next()
                                for kc in range(8):
                                    S.op(PE, "transpose", dict(out=pT_t[:, kc * 128:(kc + 1) * 128], in_=xn_t[:, kc * 128:(kc + 1) * 128], identity=ident_f[:]), reads=[xn_b, bconst], writes=[pT_b])
                                S.op(DVE, "tensor_tensor", dict(out=xT_t[:, :, i * 128:(i + 1) * 128], in0=pT_t[:].rearrange("p (a b) -> p a b", b=128), in1=gsrep[:], op=ALU.mult), reads=[pT_b, bgsrep], writes=[xT_b])
                            c_t, c_b = cst.next()
                            if rots:
                                S.dma("dma_start", dict(out=c_t[:, 0, :], in_=CT[:, r0:r0 + 512]), writes=[c_b])
                                S.dma("dma_start", dict(out=c_t[:, 1, :], in_=ST[:, r0:r0 + 512]), writes=[c_b])
                            for (gname, skey, dcol0) in mains:
                                wcol, bcol = G[gname]
                                o_t, o_b = ost.next()
                                for b in range(4):
                                    pa_t, pa_b = pacc.next()
                                    c0 = wcol + b * 128
                                    bc = bcol + b
                                    for kc in range(8):
                                        S.op(PE, "matmul", dict(out=pa_t[:], lhsT=wp[:, kc, c0:c0 + 128], rhs=xT_t[:, kc, :], start=(kc == 0), stop=(kc == 7)), reads=[bwp, xT_b], writes=[pa_b])
                                    evi[0] += 1
                                    if evi[0] % 4 != 0:
                                        S.op(ACT, "activation", dict(out=o_t[:, b, :], in_=pa_t[:], func=AF.Identity, bias=biasT[:, bc:bc + 1]), reads=[pa_b, bbias], writes=[o_b])
                                    else:
                                        S.op(DVE, "tensor_scalar", dict(out=o_t[:, b, :], in0=pa_t[:], scalar1=biasT[:, bc:bc + 1], scalar2=None, op0=ALU.add), reads=[pa_b, bbias], writes=[o_b])
                                dc = dcol0 + ch * 512
                                S.dma("dma_start", dict(out=SC[skey][:, dc:dc + 512].rearrange("(b p) t -> p b t", p=128), in_=o_t[:]), reads=[o_b], writes=[SCB[skey]])
                            for (gname, skey, dcol0) in rots:
                                ti = RT[gname]
                                wc = 4096 + ti * 192
                                bc = 32 + 2 * ti
                                p1_t, p1_b = pacc.next()
                                p2_t, p2_b = pacc.next()
                                for kc in range(8):
                                    S.op(PE, "matmul", dict(out=p1_t[:], lhsT=wp[:, kc, wc:wc + 128], rhs=xT_t[:, kc, :], start=(kc == 0), stop=(kc == 7)), reads=[bwp, xT_b], writes=[p1_b])
                                for kc in range(8):
                                    S.op(PE, "matmul", dict(out=p2_t[:], lhsT=wp[:, kc, wc + 64:wc + 192], rhs=xT_t[:, kc, :], start=(kc == 0), stop=(kc == 7)), reads=[bwp, xT_b], writes=[p2_b])
                                t_t, t_b = t12.next()
                                S.op(DVE, "scalar_tensor_tensor", dict(out=t_t[:, 0, :], in0=p1_t[:], scalar=biasT[:, bc:bc + 1], in1=c_t[:, 0, :], op0=ALU.add, op1=ALU.mult), reads=[p1_b, c_b, bbias], writes=[t_b])
                                S.op(DVE, "scalar_tensor_tensor", dict(out=t_t[:, 1, :], in0=p2_t[:], scalar=biasT[:, bc + 1:bc + 2], in1=c_t[:, 1, :], op0=ALU.add, op1=ALU.mult), reads=[p2_b, c_b, bbias], writes=[t_b])
                                r_t, r_b = rot.next()
                                S.op(POOL, "tensor_tensor", dict(out=r_t[:], in0=t_t[:, 0, :], in1=t_t[:, 1, :], op=ALU.add), reads=[t_b], writes=[r_b])
                                dc = dcol0 + ch * 512
                                S.dma("dma_start", dict(out=SC[skey][:, dc:dc + 512], in_=r_t[:]), reads=[r_b], writes=[SCB[skey]])
                    S.barrier()

            with contextlib.ExitStack() as pB:
                nkt = skv // 128
                nqc = sq // 512
                kbT = [sb(pB, nm("kbT"), [128, skv], BF16) for _ in range(2)]
                vbT = [sb(pB, nm("vbT"), [128, skv], BF16) for _ in range(2)]
                qbT = [sb(pB, nm("qbT"), [128, sq], BF16) for _ in range(2)]
                gbT = [sb(pB, nm("gbT"), [128, sq], BF16) for _ in range(2)]
                bk, bv, bq, bg = ([Buf(), Buf()] for _ in range(4))
                vtok = sb(pB, nm("vtok"), [128, nkt, 128], BF16)
                bvt = Buf()
                Pr = Ring([sb(pB, nm("P"), [128, 1024], BF16) for _ in range(6)])
                asum = [sb(pB, nm("asum"), [128, 1024], F32) for _ in range(2)]
                basum = [Buf() for _ in range(2)]
                oS = [[sb(pB, nm("oS"), [128, 512], F32) for _ in range(2)] for _ in range(2)]
                boS = [[Buf() for _ in range(2)] for _ in range(2)]
                thr = Ring([sb(pB, nm("thb"), [128, 512], F32) for _ in range(2)])
                sgr = Ring([sb(pB, nm("sgb"), [128, 512], BF16) for _ in range(3)])
                yTr = Ring([sb(pB, nm("yTb"), [128, 512], BF16) for _ in range(3)])
                rsr = Ring([sb(pB, nm("rs"), [128, 16], F32) for _ in range(3)])
                rr = Ring([sb(pB, nm("rr"), [128, 8], F32) for _ in range(6)])
                t0r = Ring([sb(pB, nm("t0"), [128, 128], F32) for _ in range(3)])
                orr = Ring([sb(pB, nm("o"), [128, 128], F32) for _ in range(5)])
                onr = Ring([sb(pB, nm("on"), [128, 128], BF16) for _ in range(5)])
                junkb = sb(pB, nm("junkb"), [128, 128], F32)
                bjb = Buf()
                scr_ = Ring([ps(pB, nm("sc"), [128, 1024], F32) for _ in range(2)])
                oT = [ps(pB, nm("oT"), [128, 512], F32) for _ in range(2)]
                boT = [Buf() for _ in range(2)]
                tpb = ps(pB, nm("tpb"), [128, 512], F32)
                btp = Buf()
                pTv = ps(pB, nm("pTv"), [128, 4, 128], BF16)
                bpv = Buf()

                def load_head(g):
                    hb = g % 2
                    for (dst, dbuf, main, rotk, ncols) in ((kbT[hb], bk[hb], "KB", "KBr", skv), (qbT[hb], bq[hb], "QB", "QBr", sq)):
                        for m in range(2):
                            h = 2 * g + m
                            S.dma("dma_start", dict(out=dst[m * 64:m * 64 + 48, 0:ncols], in_=SC[main][g * 128 + m * 64 + 16:g * 128 + m * 64 + 64, 0:ncols]), reads=[SCB[main]], writes=[dbuf])
                            S.dma("dma_start", dict(out=dst[m * 64 + 48:m * 64 + 56, 0:ncols], in_=SC[rotk][h * 8:h * 8 + 8, 0:ncols]), reads=[SCB[rotk]], writes=[dbuf])
                            S.dma("dma_start", dict(out=dst[m * 64 + 56:m * 64 + 64, 0:ncols], in_=SC[rotk][64 + h * 8:64 + h * 8 + 8, 0:ncols]), reads=[SCB[rotk]], writes=[dbuf])
                    S.dma("dma_start", dict(out=vbT[hb][:], in_=SC["VB"][g * 128:(g + 1) * 128, 0:skv]), reads=[SCB["VB"]], writes=[bv[hb]])
                    S.dma("dma_start", dict(out=gbT[hb][:], in_=SC["GB"][g * 128:(g + 1) * 128, 0:sq]), reads=[SCB["GB"]], writes=[bg[hb]])

                def make_epilogue(g, q0, par, sg_t, sg_b):
                    steps = []
                    rs_t, rs_b = rsr.next()
                    y_t, y_b = yTr.next()
                    state = {}

                    def rowsums():
                        for m in range(2):
                            for qb in range(4):
                                c = 256 + 2 * (m * 4 + qb)
                                S.op(PE, "matmul", dict(out=tpb[:, c:c + 2], lhsT=asum[par][:, m * 512 + qb * 128:m * 512 + (qb + 1) * 128], rhs=ones_f[:, 0:2], start=True, stop=True), reads=[basum[par], bconst], writes=[btp])
                        S.op(DVE, "reciprocal", dict(out=rs_t[:, 0:8], in_=tpb[:, 256:272:2]), reads=[btp], writes=[rs_b])
                        S.op(DVE, "tensor_scalar", dict(out=rs_t[:, 8:12], in0=rs_t[:, 4:8], scalar1=neglam[:, 0:1], scalar2=None, op0=ALU.mult), reads=[rs_b, bconst], writes=[rs_b])
                    steps.append(rowsums)

                    def block_a(qb):
                        def f():
                            e_t, e_b = tpb, btp
                            for m in range(2):
                                S.op(PE, "transpose", dict(out=e_t[:, m * 128:(m + 1) * 128], in_=oS[par][m][:, qb * 128:(qb + 1) * 128], identity=ident_f[:]), reads=[boS[par][m], bconst], writes=[e_b])
                            r_t, r_b = rr.next()
                            t0_t, t0_b = t0r.next()
                            S.op(DVE, "tensor_scalar", dict(out=t0_t[:], in0=e_t[:, 0:128], scalar1=rs_t[:, qb:qb + 1], scalar2=None, op0=ALU.mult), reads=[e_b, rs_b], writes=[t0_b])
                            o_t, o_b = orr.next()
                            S.op(DVE, "scalar_tensor_tensor", dict(out=o_t[:], in0=e_t[:, 128:256], scalar=rs_t[:, 8 + qb:9 + qb], in1=t0_t[:], op0=ALU.mult, op1=ALU.add), reads=[e_b, rs_b, t0_b], writes=[o_b])
                            S.op(ACT, "activation", dict(out=junkb[:], in_=o_t[:], func=AF.Square, accum_out=r_t[:, 3:4]), reads=[o_b], writes=[bjb, r_b])
                            S.op(DVE, "tensor_scalar", dict(out=r_t[:, 4:5], in0=r_t[:, 3:4], scalar1=1.0 / 128, scalar2=EPS, op0=ALU.mult, op1=ALU.add), reads=[r_b], writes=[r_b])
                            S.op(POOL, "tensor_tensor", dict(out=r_t[:, 5:6], in0=r_t[:, 4:5], in1=mhalf[:], op=ALU.pow), reads=[r_b, bconst], writes=[r_b])
                            on_t, on_b = onr.next()
                            S.op(ACT, "activation", dict(out=on_t[:], in_=o_t[:], func=AF.Identity, scale=r_t[:, 5:6]), reads=[o_b, r_b], writes=[on_b])
                            state[qb] = (on_t, on_b)
                        return f

                    def block_b(qb, last):
                        def f():
                            on_t, on_b = state[qb]
                            ev = pTv[:, 0, :]
                            S.op(PE, "transpose", dict(out=ev, in_=on_t[:], identity=ident[:]), reads=[on_b, bconst], writes=[bpv])
                            S.op(DVE, "scalar_tensor_tensor", dict(out=y_t[:, qb * 128:(qb + 1) * 128], in0=ev, scalar=gsub2[:, 0:1], in1=sg_t[:, qb * 128:(qb + 1) * 128], op0=ALU.mult, op1=ALU.mult), reads=[bpv, sg_b, bconst], writes=[y_b])
                            if last:
                                S.dma("dma_start", dict(out=SC["YM"][512 + g * 128:512 + (g + 1) * 128, q0:q0 + 512], in_=y_t[:]), reads=[y_b], writes=[SCB["YM"]])
                        return f

                    for kind, qb in [("a", 0), ("a", 1), ("a", 2), ("b", 0), ("a", 3), ("b", 1), ("b", 2), ("b", 3)]:
                        steps.append(block_a(qb) if kind == "a" else block_b(qb, qb == 3))
                    return steps

                pending = []
                heads = list(range(4)) if lvl >= 3 else []
                if heads:
                    load_head(0)
                for g in heads:
                    hb = g % 2
                    if g + 1 < 4:
                        load_head(g + 1)
                    for k4 in range(nkt // 4):
                        for j in range(4):
                            kt = k4 * 4 + j
                            S.op(PE, "transpose", dict(out=pTv[:, j, :], in_=vbT[hb][:, kt * 128:(kt + 1) * 128], identity=ident[:]), reads=[bv[hb], bconst], writes=[bpv])
                        S.op(ACT, "activation", dict(out=vtok[:, k4 * 4:k4 * 4 + 4, :], in_=pTv[:], func=AF.Copy), reads=[bpv], writes=[bvt])
                    for qc in range(nqc):
                        q0 = qc * 512
                        par = qc % 2
                        th_t, th_b = thr.next()
                        sg_t, sg_b = sgr.next()
                        S.op(ACT, "activation", dict(out=th_t[:], in_=gbT[hb][:, q0:q0 + 512], func=AF.Tanh, scale=0.5), reads=[bg[hb]], writes=[th_b])
                        S.op(DVE, "scalar_tensor_tensor", dict(out=sg_t[:], in0=th_t[:], scalar=1.0, in1=gbT[hb][:, q0:q0 + 512], op0=ALU.add, op1=ALU.mult), reads=[th_b, bg[hb]], writes=[sg_b])

                        def scores(kt):
                            sc_t, sc_b = scr_.next()
                            for m in range(2):
                                S.op(PE, "matmul", dict(out=sc_t[:, m * 512:(m + 1) * 512], lhsT=kbT[hb][m * 64:(m + 1) * 64, kt * 128:(kt + 1) * 128], rhs=qbT[hb][m * 64:(m + 1) * 64, q0:q0 + 512], start=True, stop=True), reads=[bk[hb], bq[hb]], writes=[sc_b])
                            p_t, p_b = Pr.next()
                            S.op(ACT, "activation", dict(out=p_t[:], in_=sc_t[:], func=AF.Exp, scale=0.125), reads=[sc_b], writes=[p_b])
                            return p_t, p_b

                        def av(kt, p_t, p_b):
                            for m in range(2):
                                S.op(PE, "matmul", dict(out=oT[m][:], lhsT=vtok[:, kt, :], rhs=p_t[:, m * 512:(m + 1) * 512], start=(kt == 0), stop=(kt == nkt - 1)), reads=[p_b, bvt], writes=[boT[m]])
                            if kt == 0:
                                S.op(DVE, "tensor_copy", dict(out=asum[par][:], in_=p_t[:]), reads=[p_b], writes=[basum[par]])
                            else:
                                S.op(DVE, "tensor_tensor", dict(out=asum[par][:], in0=asum[par][:], in1=p_t[:], op=ALU.add), reads=[p_b, basum[par]], writes=[basum[par]])

                        ptiles = {0: scores(0)}
                        for kt in range(nkt):
                            if kt + 1 < nkt:
                                ptiles[kt + 1] = scores(kt + 1)
                            if kt >= 1:
                                av(kt - 1, *ptiles.pop(kt - 1))
                            if pending and kt >= 2 and kt % 2 == 0:
                                pending.pop(0)()
                        av(nkt - 1, *ptiles.pop(nkt - 1))
                        for m in range(2):
                            S.op(ACT, "activation", dict(out=oS[par][m][:], in_=oT[m][:], func=AF.Copy), reads=[boT[m]], writes=[boS[par][m]])
                        while pending:
                            pending.pop(0)()
                        pending.extend(make_epilogue(g, q0, par, sg_t, sg_b))
                while pending:
                    pending.pop(0)()
                S.barrier()

            with contextlib.ExitStack() as pA:
                tiles = amask_tiles(sq)
                nt = len(tiles)
                kaT = [sb(pA, nm("kaT"), [128, sext], BF16) for _ in range(2)]
                vaT1 = sb(pA, nm("vaT"), [128, sext], BF16)
                vaT = [vaT1, vaT1]
                bv1 = Buf()
                qaT = [sb(pA, nm("qaT"), [128, sq], BF16) for _ in range(2)]
                gaT = [sb(pA, nm("gaT"), [128, sq], BF16) for _ in range(2)]
                bk, bv, bq, bg = ([Buf(), Buf()] for _ in range(4))
                bv = [bv1, bv1]
                acc = sb(pA, nm("accA"), [65, 2, sq], F32)
                bacc = Buf()
                vt_all = sb(pA, nm("vtall"), [128, nt, 2, 65], BF16)
                bvta = Buf()
                vrow = sb(pA, nm("vrow"), [128, 2 * PADK], F32)
                bvrow = Buf()
                Pt = Ring([sb(pA, nm("Pt"), [128, 2, 512], BF16) for _ in range(4)])
                recr = Ring([sb(pA, nm("rec"), [128, 512], F32) for _ in range(2)])
                thr = Ring([sb(pA, nm("tha"), [128, 512], F32) for _ in range(2)])
                sgr = Ring([sb(pA, nm("sga"), [128, 512], F32) for _ in range(2)])
                tnr = Ring([sb(pA, nm("tn"), [128, 512], F32) for _ in range(2)])
                yTr = Ring([sb(pA, nm("yTa"), [128, 512], BF16) for _ in range(2)])
                scA = Ring([ps(pA, nm("scA"), [128, 2, 512], F32) for _ in range(2)])
                opsr = Ring([ps(pA, nm("ops"), [128, 2, 512], F32) for _ in range(2)])
                if s == 0:
                    S.dma("dma_start", dict(out=vrow[:], in_=VROW[:, :]), writes=[bvrow])

                kc0, kn = (PADK, sq) if s != 0 else (0, sext)

                def load_v(hp):
                    hb = hp % 2
                    if s != 0:
                        S.op(POOL, "memset", dict(ap=vaT[hb][:, 0:PADK], constant=0.0), writes=[bv[hb]])
                        S.op(POOL, "memset", dict(ap=vaT[hb][:, PADK + sq:sext], constant=0.0), writes=[bv[hb]])
                    S.dma("dma_start", dict(out=vaT[hb][:, kc0:kc0 + kn], in_=SC["VA"][hp * 128:(hp + 1) * 128, kc0:kc0 + kn]), reads=[SCB["VA"]], writes=[bv[hb]])
                    if s == 0:
                        S.op(DVE, "tensor_tensor", dict(out=vaT[hb][:, 0:PADK], in0=vaT[hb][:, 0:PADK], in1=vrow[:, 0:PADK], op=ALU.mult), reads=[bvrow, bv[hb]], writes=[bv[hb]])
                        S.op(DVE, "tensor_tensor", dict(out=vaT[hb][:, PADK + sq:sext], in0=vaT[hb][:, PADK + sq:sext], in1=vrow[:, PADK:2 * PADK], op=ALU.mult), reads=[bvrow, bv[hb]], writes=[bv[hb]])

                def load_pair(hp):
                    hb = hp % 2
                    if s != 0:
                        S.op(POOL, "memset", dict(ap=kaT[hb][:, 0:PADK], constant=0.0), writes=[bk[hb]])
                        S.op(POOL, "memset", dict(ap=kaT[hb][:, PADK + sq:sext], constant=0.0), writes=[bk[hb]])
                    for hl in range(2):
                        h = 2 * hp + hl
                        S.dma("dma_start", dict(out=kaT[hb][hl * 64:hl * 64 + 48, kc0:kc0 + kn], in_=SC["KA"][hp * 128 + hl * 64 + 16:hp * 128 + hl * 64 + 64, kc0:kc0 + kn]), reads=[SCB["KA"]], writes=[bk[hb]])
                        S.dma("dma_start", dict(out=kaT[hb][hl * 64 + 48:hl * 64 + 56, kc0:kc0 + kn], in_=SC["KAr"][h * 8:h * 8 + 8, kc0:kc0 + kn]), reads=[SCB["KAr"]], writes=[bk[hb]])
                        S.dma("dma_start", dict(out=kaT[hb][hl * 64 + 56:hl * 64 + 64, kc0:kc0 + kn], in_=SC["KAr"][64 + h * 8:64 + h * 8 + 8, kc0:kc0 + kn]), reads=[SCB["KAr"]], writes=[bk[hb]])
                        S.dma("dma_start", dict(out=qaT[hb][hl * 64:hl * 64 + 48, :], in_=SC["QA"][hp * 128 + hl * 64 + 16:hp * 128 + hl * 64 + 64, 0:sq]), reads=[SCB["QA"]], writes=[bq[hb]])
                        S.dma("dma_start", dict(out=qaT[hb][hl * 64 + 48:hl * 64 + 56, :], in_=SC["QAr"][h * 8:h * 8 + 8, 0:sq]), reads=[SCB["QAr"]], writes=[bq[hb]])
                        S.dma("dma_start", dict(out=qaT[hb][hl * 64 + 56:hl * 64 + 64, :], in_=SC["QAr"][64 + h * 8:64 + h * 8 + 8, 0:sq]), reads=[SCB["QAr"]], writes=[bq[hb]])
                    S.dma("dma_start", dict(out=gaT[hb][:], in_=SC["GA"][hp * 128:(hp + 1) * 128, 0:sq]), reads=[SCB["GA"]], writes=[bg[hb]])

                def tile_geom(tile):
                    (pi, dil, r, t, nblk, je0) = tile
                    e0 = r + dil * (je0 + 128 * t)
                    ksl = slice(e0, e0 + dil * 127 + 1, dil)
                    mlo, mhi = max(t - 1, 0), min(t, nblk - 1)
                    nq = (mhi - mlo + 1) * 128
                    qs = r + dil * 128 * mlo
                    qsl = slice(qs, qs + dil * (nq - 1) + 1, dil)
                    boff = 0 if t >= 1 else 128
                    return ksl, qsl, nq, boff

                pairs = list(range(4)) if lvl >= 4 else []
                if pairs:
                    load_pair(0)
                    load_v(0)
                for hp in pairs:
                    hb = hp % 2
                    if hp + 1 < 4:
                        load_pair(hp + 1)
                    S.op(POOL, "memset", dict(ap=acc[:], constant=0.0), writes=[bacc])
                    for t4 in range(0, nt, 4):
                        n4 = min(4, nt - t4)
                        o_t, o_b = opsr.next()
                        pv = o_t[:, 0, 0:256].bitcast(BF16).rearrange("p (a b) -> p a b", b=128)
                        for j in range(n4):
                            ksl = tile_geom(tiles[t4 + j])[0]
                            S.op(PE, "transpose", dict(out=pv[:, j, :], in_=vaT[hb][:, ksl], identity=ident[:]), reads=[bv[hb], bconst], writes=[o_b])
                        eng = ACT if (t4 // 4) % 2 == 0 else DVE
                        if eng == ACT:
                            S.op(ACT, "activation", dict(out=vt_all[:, t4:t4 + n4, :, 0:64], in_=pv[:, 0:n4, :].rearrange("p a (h d) -> p a h d", h=2), func=AF.Copy), reads=[o_b], writes=[bvta])
                        else:
                            S.op(DVE, "tensor_copy", dict(out=vt_all[:, t4:t4 + n4, :, 0:64], in_=pv[:, 0:n4, :].rearrange("p a (h d) -> p a h d", h=2)), reads=[o_b], writes=[bvta])
                    for hl in range(2):
                        S.op(DVE, "tensor_copy", dict(out=vt_all[:, :, hl, 64], in_=am[:, 0:nt]), reads=[bconst], writes=[bvta])
                    if hp + 1 < 4:
                        load_v(hp + 1)
                    units = [list(range(u, min(u + 2, nt))) for u in range(0, nt, 2)]

                    def scores(unit):
                        sc_t, sc_b = scA.next()
                        geo = []
                        for j, ti in enumerate(unit):
                            ksl, qsl, nq, boff = tile_geom(tiles[ti])
                            geo.append((ti, qsl, nq))
                            for hl in range(2):
                                S.op(PE, "matmul", dict(out=sc_t[:, hl, j * 256:j * 256 + nq], lhsT=kaT[hb][hl * 64:(hl + 1) * 64, ksl], rhs=qaT[hb][hl * 64:(hl + 1) * 64, qsl], start=True, stop=False), reads=[bk[hb], bq[hb]], writes=[sc_b])
                                S.op(PE, "matmul", dict(out=sc_t[:, hl, j * 256:j * 256 + nq], lhsT=ident[:], rhs=bandneg[:, boff:boff + nq], start=False, stop=True), reads=[bconst], writes=[sc_b])
                        p_t, p_b = Pt.next()
                        if len(unit) == 2 and all(g_[2] == 256 for g_ in geo):
                            S.op(ACT, "activation", dict(out=p_t[:], in_=sc_t[:], func=AF.Exp, scale=0.125), reads=[sc_b], writes=[p_b])
                        else:
                            for j, (ti, qsl, nq) in enumerate(geo):
                                S.op(ACT, "activation", dict(out=p_t[:, :, j * 256:j * 256 + nq], in_=sc_t[:, :, j * 256:j * 256 + nq], func=AF.Exp, scale=0.125), reads=[sc_b], writes=[p_b])
                        return geo, p_t, p_b

                    def av(geo, p_t, p_b):
                        o_t, o_b = opsr.next()
                        for j, (ti, qsl, nq) in enumerate(geo):
                            for hl in range(2):
                                for bi in range(nq // 128):
                                    c = j * 256 + bi * 128
                                    S.op(PE, "matmul", dict(out=o_t[0:65, hl, c:c + 128], lhsT=vt_all[:, ti, hl, :], rhs=p_t[:, hl, c:c + 128], start=True, stop=True), reads=[bvta, p_b], writes=[o_b])
                        for j, (ti, qsl, nq) in enumerate(geo):
                            S.op(DVE, "tensor_tensor", dict(out=acc[:, :, qsl], in0=o_t[0:65, :, j * 256:j * 256 + nq], in1=acc[:, :, qsl], op=ALU.add), reads=[o_b, bacc], writes=[bacc])

                    inflight = {0: scores(units[0])}
                    for u in range(len(units)):
                        if u + 1 < len(units):
                            inflight[u + 1] = scores(units[u + 1])
                        if u >= 1:
                            av(*inflight.pop(u - 1))
                    av(*inflight.pop(len(units) - 1))
                    for qc in range(sq // 512):
                        q0 = qc * 512
                        d_t, d_b = opsr.next()
                        dv = d_t[:, 0, :]
                        nv = d_t[:, 1, :]
                        for h2 in range(2):
                            qa_, qb_ = q0 + h2 * 256, q0 + (h2 + 1) * 256
                            for hl in range(2):
                                S.op(PE, "matmul", dict(out=dv[:, h2 * 256:(h2 + 1) * 256], lhsT=sels[:, hl, :], rhs=acc[:, hl, qa_:qb_], start=(hl == 0), stop=(hl == 1)), reads=[bacc, bconst], writes=[d_b])
                        for h2 in range(2):
                            qa_, qb_ = q0 + h2 * 256, q0 + (h2 + 1) * 256
                            for hl in range(2):
                                S.op(PE, "matmul", dict(out=nv[:, h2 * 256:(h2 + 1) * 256], lhsT=sels[:, 2 + hl, :], rhs=acc[:, hl, qa_:qb_], start=(hl == 0), stop=(hl == 1)), reads=[bacc, bconst], writes=[d_b])
                        rc_t, rc_b = recr.next()
                        S.op(DVE, "reciprocal", dict(out=rc_t[:], in_=dv), reads=[d_b], writes=[rc_b])
                        th_t, th_b = thr.next()
                        S.op(ACT, "activation", dict(out=th_t[:], in_=gaT[hb][:, q0:q0 + 512], func=AF.Tanh, scale=0.5), reads=[bg[hb]], writes=[th_b])
                        sg_t, sg_b = sgr.next()
                        S.op(DVE, "scalar_tensor_tensor", dict(out=sg_t[:], in0=th_t[:], scalar=1.0, in1=gaT[hb][:, q0:q0 + 512], op0=ALU.add, op1=ALU.mult), reads=[th_b, bg[hb]], writes=[sg_b])
                        tn_t, tn_b = tnr.next()
                        S.op(DVE, "tensor_tensor", dict(out=tn_t[:], in0=nv, in1=rc_t[:], op=ALU.mult), reads=[d_b, rc_b], writes=[tn_b])
                        y_t, y_b = yTr.next()
                        S.op(DVE, "tensor_tensor", dict(out=y_t[:], in0=tn_t[:], in1=sg_t[:], op=ALU.mult), reads=[tn_b, sg_b], writes=[y_b])
                        S.dma("dma_start", dict(out=SC["YM"][hp * 128:(hp + 1) * 128, q0:q0 + 512], in_=y_t[:]), reads=[y_b], writes=[SCB["YM"]])
                S.barrier()

            with contextlib.ExitStack() as p3:
                ymr = Ring([sb(p3, nm("ym"), [128, 8, 512], BF16) for _ in range(2)])
                xr = Ring([sb(p3, nm("x3"), [128, D], F32) for _ in range(4)])
                tmr = Ring([sb(p3, nm("tm3"), [128, D], F32) for _ in range(3)])
                outr = Ring([sb(p3, nm("o3"), [128, D], F32) for _ in range(4)])
                st3 = Ring([sb(p3, nm("st3"), [128, 6], F32) for _ in range(8)])
                junk3 = sb(p3, nm("junk3"), [128, 512], F32)
                bj3 = Buf()
                po = Ring([ps(p3, nm("po"), [128, 2, 512], F32) for _ in range(4)])
                for qc in (range(sq // 512) if lvl >= 5 else []):
                    q0 = qc * 512
                    ym_t, ym_b = ymr.next()
                    S.dma("dma_start", dict(out=ym_t[:], in_=SC["YM"][:, q0:q0 + 512].rearrange("(kc p) t -> p kc t", p=128)), reads=[SCB["YM"]], writes=[ym_b])
                    for i in range(4):
                        row = q0 + i * 128
                        x_t, x_b = xr.next()
                        S.dma("dma_start", dict(out=x_t[:], in_=X[job["xrow"] + row:job["xrow"] + row + 128, :]), writes=[x_b])
                        po_t, po_b = po.next()
                        for hf in range(2):
                            for kc in range(8):
                                S.op(PE, "matmul", dict(out=po_t[:, hf, :], lhsT=ym_t[:, kc, i * 128:(i + 1) * 128], rhs=wout[:, kc, hf * 512:(hf + 1) * 512], start=(kc == 0), stop=(kc == 7)), reads=[ym_b, bwout], writes=[po_b])
                        s_t, s_b = st3.next()
                        for hf in range(2):
                            S.op(ACT, "activation", dict(out=junk3[:], in_=po_t[:, hf, :], func=AF.Square, accum_out=s_t[:, hf:hf + 1]), reads=[po_b], writes=[bj3, s_b])
                        S.op(DVE, "tensor_tensor", dict(out=s_t[:, 2:3], in0=s_t[:, 0:1], in1=s_t[:, 1:2], op=ALU.add), reads=[s_b], writes=[s_b])
                        S.op(DVE, "tensor_scalar", dict(out=s_t[:, 3:4], in0=s_t[:, 2:3], scalar1=1.0 / D, scalar2=EPS, op0=ALU.mult, op1=ALU.add), reads=[s_b], writes=[s_b])
                        S.op(POOL, "tensor_tensor", dict(out=s_t[:, 4:5], in0=s_t[:, 3:4], in1=mhalf[:], op=ALU.pow), reads=[s_b, bconst], writes=[s_b])
                        tm_t, tm_b = tmr.next()
                        for hf in range(2):
                            S.op(DVE, "scalar_tensor_tensor", dict(out=tm_t[:, hf * 512:(hf + 1) * 512], in0=po_t[:, hf, :], scalar=s_t[:, 4:5], in1=gg[s][:, hf * 512:(hf + 1) * 512], op0=ALU.mult, op1=ALU.mult), reads=[po_b, s_b, bgg[s]], writes=[tm_b])
                        o_t, o_b = outr.next()
                        S.op(DVE, "tensor_tensor", dict(out=o_t[:], in0=tm_t[:], in1=x_t[:], op=ALU.add), reads=[tm_b, x_b], writes=[o_b])
                        yrow = job["yrow"] + row
                        out_dmas.append(S.dma("dma_start", dict(out=Y[yrow:yrow + 128, :], in_=o_t[:]), reads=[o_b], writes=[Buf()]))
                S.barrier()

        S.run(final_wait=out_dmas)
    return nc


_NC_CACHE = {}


def _rope_tables(pos):
    half = 8
    inv = (np.float32(500000.0) ** (-np.arange(half, dtype=np.float32) / np.float32(half))).astype(np.float32)
    ang = pos.astype(np.float32)[None, :] * inv[:, None]
    cos = np.cos(ang).astype(np.float32)
    sin = np.sin(ang).astype(np.float32)
    idx = np.arange(128) % 8
    C = cos[idx]
    Sg = sin[idx].copy()
    Sg[:64] *= -1.0
    return np.ascontiguousarray(C), np.ascontiguousarray(Sg)


def _amask(valid_ext, sq):
    tiles = amask_tiles(sq)
    m = np.zeros((128, len(tiles)), np.float32)
    p = np.arange(128)
    for ti, (pi, dil, r, t, nblk, je0) in enumerate(tiles):
        e = r + dil * (je0 + 128 * t + p)
        m[:, ti] = valid_ext[e]
    return m


def prep_inputs(x_prompt, x_sample, c_prompt, c_sample, w_in, w_out, g_pre, g_post,
                w_ada, b_ada, lam_q1, lam_k1, lam_q2, lam_k2, g_sub):
    f32 = np.float32
    x_prompt = np.asarray(x_prompt, f32)
    x_sample = np.asarray(x_sample, f32)
    c_prompt = np.asarray(c_prompt, f32)
    c_sample = np.asarray(c_sample, f32)
    w_in0 = np.ascontiguousarray(np.asarray(w_in, f32)[0])
    w_out0 = np.ascontiguousarray(np.asarray(w_out, f32)[0])
    w_ada0 = np.ascontiguousarray(np.asarray(w_ada, f32)[0])
    b_ada0 = np.asarray(b_ada, f32)[0]
    g_pre0 = np.asarray(g_pre, f32)[0]
    g_post0 = np.asarray(g_post, f32)[0]
    g_sub0 = np.asarray(g_sub, f32)[0]

    offs = dict(qa=0, ka=512, qb=2048, kb=2560)
    w_rot = np.zeros((D, 4, 192), f32)
    for ti, nme in enumerate(("qa", "ka", "qb", "kb")):
        x1 = np.array([offs[nme] + h * 64 + i for h in range(8) for i in range(8)])
        x2 = x1 + 8
        w_rot[:, ti, 0:64] = w_in0[:, x1]
        w_rot[:, ti, 64:128] = w_in0[:, x2]
        w_rot[:, ti, 128:192] = w_in0[:, x1]
    w_rot = np.ascontiguousarray(w_rot.reshape(D, 768))

    g_pre_t = np.ascontiguousarray(g_pre0.reshape(8, 128).T)
    b_ada_t = np.ascontiguousarray(b_ada0[:2048].reshape(16, 128).T)
    b_gate = np.ascontiguousarray(b_ada0[2048:3072].reshape(1, D))
    g_post_r = np.ascontiguousarray(g_post0.reshape(1, D))
    g_sub_t = np.ascontiguousarray(g_sub0.reshape(128, 1))
    lam_v = np.ascontiguousarray(np.stack([np.asarray(v, f32)[0] for v in (lam_q1, lam_k1, lam_q2, lam_k2)], 0))
    ident = np.eye(128, dtype=f32)
    p = np.arange(128)[:, None]
    f = np.arange(128)[None, :]
    band = np.concatenate([(p <= f), (p >= f)], axis=1).astype(f32)
    band = np.ascontiguousarray(np.concatenate([band, band], axis=1))
    sels = np.zeros((65, 4, 128), f32)
    sels[64, 0, 0:64] = 2.0
    sels[64, 1, 64:128] = 2.0
    sels[np.arange(64), 2, np.arange(64)] = 1.0
    sels[np.arange(64), 3, 64 + np.arange(64)] = 1.0
    sels = np.ascontiguousarray(sels.reshape(65, 512))
    valid1 = np.zeros(DSEQ + 2 * PADK, f32)
    valid1[PADK:PADK + DSEQ] = 1.0
    am1 = _amask(valid1, DSEQ)

    in_maps = []
    for c in range(8):
        psq, hf = c // 2, c % 2
        q0 = hf * HALF
        o0 = (1 - hf) * HALF
        xa = np.zeros((NROWS, D), f32)
        pos = np.zeros(NROWS, np.int64)
        xa[ROW_OWN:ROW_OWN + HALF] = x_prompt[psq, q0:q0 + HALF]
        pos[ROW_OWN:ROW_OWN + HALF] = np.arange(q0, q0 + HALF)
        xa[ROW_OTHER:ROW_OTHER + HALF] = x_prompt[psq, o0:o0 + HALF]
        pos[ROW_OTHER:ROW_OTHER + HALF] = np.arange(o0, o0 + HALF)
        hpos = np.concatenate([np.arange(q0 - PADK, q0), np.arange(q0 + HALF, q0 + HALF + PADK)])
        hval = (hpos >= 0) & (hpos < SEQ)
        xa[ROW_HALO:ROW_HALO + 2 * PADK][hval] = x_prompt[psq, hpos[hval]]
        pos[ROW_HALO:ROW_HALO + 2 * PADK] = np.clip(hpos, 0, SEQ - 1)
        xa[ROW_S0:ROW_S0 + DSEQ] = x_sample[2 * c]
        pos[ROW_S0:ROW_S0 + DSEQ] = np.arange(DSEQ)
        xa[ROW_S1:ROW_S1 + DSEQ] = x_sample[2 * c + 1]
        pos[ROW_S1:ROW_S1 + DSEQ] = np.arange(DSEQ)
        C, Sg = _rope_tables(pos)
        cs = np.stack([c_prompt[psq], c_sample[2 * c], c_sample[2 * c + 1]], axis=-1)
        c_t = np.ascontiguousarray(cs.reshape(8, 128, 3).transpose(1, 0, 2).reshape(128, 24))
        valid0 = np.ones(HALF + 2 * PADK, f32)
        valid0[0:PADK] = hval[:PADK]
        valid0[PADK + HALF:] = hval[PADK:]
        am0 = _amask(valid0, HALF)
        vrow = np.ascontiguousarray(np.broadcast_to(hval.astype(f32)[None, :], (128, 2 * PADK)))
        in_maps.append({
            "x_all": xa, "rope_c": C, "rope_s": Sg, "c_t": c_t, "w_in": w_in0, "w_rot": w_rot,
            "w_out": w_out0, "w_ada": w_ada0, "g_pre_t": g_pre_t, "b_ada_t": b_ada_t, "b_gate": b_gate,
            "g_post": g_post_r, "g_sub_t": g_sub_t, "lam_v": lam_v, "ident": ident, "band": band,
            "sels": sels, "amask0": am0, "amask1": am1, "vrow": vrow,
        })

    return in_maps


def kernel(**inputs):
    f32 = np.float32
    in_maps = prep_inputs(**inputs)
    if "nc" not in _NC_CACHE:
        _NC_CACHE["nc"] = build_program()
    nc = _NC_CACHE["nc"]
    res = run_bass_kernel_spmd(nc, in_maps, core_ids=list(range(8)))
    y_prompt = np.zeros((4, SEQ, D), f32)
    y_sample = np.zeros((16, DSEQ, D), f32)
    for c in range(8):
        y = np.asarray(res.results[c]["y_all"], f32)
        psq, hf = c // 2, c % 2
        y_prompt[psq, hf * HALF:(hf + 1) * HALF] = y[0:HALF]
        y_sample[2 * c] = y[HALF:HALF + DSEQ]
        y_sample[2 * c + 1] = y[HALF + DSEQ:HALF + 2 * DSEQ]
    return (y_prompt, y_sample)
```

```python
import contextlib
import numpy as np
import concourse.bass as bass
import concourse.mybir as mybir
from concourse.bass_utils import run_bass_kernel_spmd

F32 = mybir.dt.float32
BF16 = mybir.dt.bfloat16
AF = mybir.ActivationFunctionType
ALU = mybir.AluOpType

PE, ACT, DVE, POOL, SP = "tensor", "scalar", "vector", "gpsimd", "sync"
ENGINES = (PE, ACT, DVE, POOL, SP)

D = 1024
SEQ = 8192
DSEQ = 2048
HALF = 4096
PADK = 1024
PATTERNS = ((128, 1), (512, 4), (2048, 16))
EPS = 1e-6
LAM_INIT = 0.2
NROWS = HALF + HALF + 2 * PADK + 2 * DSEQ
ROW_OWN, ROW_OTHER, ROW_HALO, ROW_S0, ROW_S1 = 0, 4096, 8192, 10240, 12288
NOUT = HALF + 2 * DSEQ
WCOLS = 4096 + 4 * 192


class Buf:
    __slots__ = ("last_write", "reads")

    def __init__(self):
        self.last_write = None
        self.reads = []


class Rec:
    __slots__ = ("eng", "fn", "deps", "is_dma", "signal", "count", "dsem", "dval", "epoch")

    def __init__(self, eng, fn, is_dma):
        self.eng, self.fn, self.is_dma = eng, fn, is_dma
        self.deps = []
        self.signal = False
        self.count = 0
        self.dsem = None
        self.dval = 0
        self.epoch = 0


class Sched:
    EPOCH_MAX = 30000

    def __init__(self, nc, n_dma_sems=20):
        self.nc = nc
        self.q = {e: [] for e in ENGINES}
        self.n_dma_sems = n_dma_sems
        self.dma_rr = {e: 0 for e in ENGINES}
        self.dma_last = {}
        self.last_compute = {}

    def op(self, eng, meth, kw, reads=(), writes=(), dma=False, extra=()):
        r = Rec(eng, (meth, kw), dma)
        deps = list(extra)
        for b in reads:
            if b.last_write is not None:
                deps.append(b.last_write)
        for b in writes:
            if b.last_write is not None:
                deps.append(b.last_write)
            deps.extend(b.reads)
        if dma:
            slot = self.dma_rr[eng] % self.n_dma_sems
            self.dma_rr[eng] += 1
            prev = self.dma_last.get((eng, slot))
            if prev is not None:
                deps.append(prev)
            self.dma_last[(eng, slot)] = r
            r.dsem = (eng, slot)
        seen = set()
        for d in deps:
            if d is r or id(d) in seen:
                continue
            if (not d.is_dma) and (not dma) and d.eng == PE and eng == PE:
                continue
            seen.add(id(d))
            r.deps.append(d)
        for b in reads:
            b.reads.append(r)
        for b in writes:
            b.last_write = r
            b.reads = []
        self.q[eng].append(r)
        if not dma:
            self.last_compute[eng] = r
        return r

    def dma(self, meth, kw, reads=(), writes=(), eng=SP):
        return self.op(eng, meth, kw, reads, writes, dma=True)

    def barrier(self):
        deps = list(self.last_compute.values()) + list(self.dma_last.values())
        for e in ENGINES:
            r = Rec(e, None, False)
            for d in deps:
                if (not d.is_dma) and d.eng == e:
                    continue
                r.deps.append(d)
            self.q[e].append(r)

    def run(self, final_wait=()):
        nc = self.nc
        for e in ENGINES:
            for r in self.q[e]:
                for d in r.deps:
                    if not d.is_dma:
                        d.signal = True
        n_epochs = {}
        for e in ENGINES:
            cnt, ep = 0, 0
            for r in self.q[e]:
                if r.is_dma or r.fn is None:
                    continue
                if r.signal:
                    if cnt >= self.EPOCH_MAX:
                        ep += 1
                        cnt = 0
                    cnt += 1
                    r.count = cnt
                    r.epoch = ep
            n_epochs[e] = ep + 1
        dcount = {}
        for e in ENGINES:
            for r in self.q[e]:
                if r.is_dma:
                    dcount[r.dsem] = dcount.get(r.dsem, 0) + 16
                    r.dval = dcount[r.dsem]
        with contextlib.ExitStack() as st:
            csem = {}
            for e in ENGINES:
                for ep in range(n_epochs[e]):
                    if any((not r.is_dma) and r.signal and r.epoch == ep for r in self.q[e]):
                        csem[(e, ep)] = st.enter_context(nc.semaphore(f"c_{e}_{ep}"))
            dsem = {}
            for key in dcount:
                dsem[key] = st.enter_context(nc.semaphore(f"d_{key[0]}_{key[1]}"))
            block = st.enter_context(nc.Block())

            def emit(e, engine):
                waited = {}

                def do_wait(d):
                    if d.is_dma:
                        s, v, k = dsem[d.dsem], d.dval, ("d",) + d.dsem
                    else:
                        s, v, k = csem[(d.eng, d.epoch)], d.count, ("c", d.eng, d.epoch)
                    if waited.get(k, 0) >= v:
                        return
                    waited[k] = v
                    engine.wait_ge(s, v)

                for r in self.q[e]:
                    for d in r.deps:
                        do_wait(d)
                    if r.fn is None:
                        continue
                    ins = getattr(engine, r.fn[0])(**r.fn[1])
                    if r.is_dma:
                        ins.then_inc(dsem[r.dsem], 16)
                    elif r.signal:
                        ins.then_inc(csem[(e, r.epoch)], 1)
                if e == SP:
                    for d in final_wait:
                        do_wait(d)

            @block.tensor
            def _(eng):
                emit(PE, eng)

            @block.scalar
            def _(eng):
                emit(ACT, eng)

            @block.vector
            def _(eng):
                emit(DVE, eng)

            @block.gpsimd
            def _(eng):
                emit(POOL, eng)

            @block.sync
            def _(eng):
                emit(SP, eng)


class Ring:
    def __init__(self, tiles):
        self.tiles = tiles
        self.bufs = [Buf() for _ in tiles]
        self.i = 0

    def next(self):
        k = self.i % len(self.tiles)
        self.i += 1
        return self.tiles[k], self.bufs[k]


def amask_tiles(sq):
    out = []
    for pi, (w, dil) in enumerate(PATTERNS):
        nblk = sq // dil // 128
        je0 = PADK // dil - 64
        for r in range(dil):
            for t in range(nblk + 1):
                out.append((pi, dil, r, t, nblk, je0))
    return out


def build_program(stop_after=None, jobs=(0, 1, 2)):
    nc = bass.Bass("TRN2", target_bir_lowering=False)
    lvl = {'p0': 0, 'w': 1, 'p1': 2, '2b': 3, '2a': 4, None: 5}[stop_after]

    def din(name, shape, dt=F32):
        return nc.dram_tensor(name, list(shape), dt, kind="ExternalInput").ap()

    X = din("x_all", [NROWS, D])
    CT = din("rope_c", [128, NROWS])
    ST = din("rope_s", [128, NROWS])
    CTT = din("c_t", [128, 24])
    W_IN = din("w_in", [D, 4096])
    W_ROT = din("w_rot", [D, 768])
    W_OUT = din("w_out", [D, D])
    W_ADA = din("w_ada", [D, 3 * D])
    GPRE = din("g_pre_t", [128, 8])
    BADA = din("b_ada_t", [128, 16])
    BGATE = din("b_gate", [1, D])
    GPOST = din("g_post", [1, D])
    GSUB = din("g_sub_t", [128, 1])
    LAMV = din("lam_v", [4, 64])
    IDENT = din("ident", [128, 128])
    BAND = din("band", [128, 512])
    SELS = din("sels", [65, 512])
    AM0 = din("amask0", [128, 117])
    VROW = din("vrow", [128, 2 * PADK])
    AM1 = din("amask1", [128, 69])
    Y = nc.dram_tensor("y_all", [NOUT, D], F32, kind="ExternalOutput").ap()

    JOBS = [
        dict(s=0, sq=HALF, skv=SEQ, sext=HALF + 2 * PADK, xrow=ROW_OWN, yrow=0, am=AM0),
        dict(s=1, sq=DSEQ, skv=DSEQ, sext=DSEQ + 2 * PADK, xrow=ROW_S0, yrow=HALF, am=AM1),
        dict(s=2, sq=DSEQ, skv=DSEQ, sext=DSEQ + 2 * PADK, xrow=ROW_S1, yrow=HALF + DSEQ, am=AM1),
    ]
    def scr(name, rows, cols):
        return nc.dram_tensor(name, [rows, cols], BF16, kind="Internal").ap()

    SC = dict(
        QA=scr("s_qa", 512, HALF), QAr=scr("s_qar", 128, HALF), GA=scr("s_ga", 512, HALF),
        KA=scr("s_ka", 512, HALF + 2 * PADK), KAr=scr("s_kar", 128, HALF + 2 * PADK),
        VA=scr("s_va", 512, HALF + 2 * PADK),
        QB=scr("s_qb", 512, HALF), QBr=scr("s_qbr", 128, HALF), GB=scr("s_gb", 512, HALF),
        KB=scr("s_kb", 512, SEQ), KBr=scr("s_kbr", 128, SEQ), VB=scr("s_vb", 512, SEQ),
        YM=scr("s_ym", 1024, HALF),
    )
    SCB = {k: Buf() for k in SC}
    WS = scr("s_w", D, WCOLS)
    bWS = Buf()

    S = Sched(nc)
    out_dmas = []
    top = contextlib.ExitStack()
    with top:
        def sb(st, name, shape, dt):
            return st.enter_context(nc.sbuf_tensor("sb_" + name, list(shape), dt))

        def ps(st, name, shape, dt):
            return st.enter_context(nc.psum_tensor("ps_" + name, list(shape), dt))

        uid = [0]

        def nm(p):
            uid[0] += 1
            return f"{p}{uid[0]}"

        ident_f = sb(top, "ident_f", [128, 128], F32)
        ident = sb(top, "ident", [128, 128], BF16)
        band_f = sb(top, "band_f", [128, 512], F32)
        bandneg = sb(top, "bandneg", [128, 256], BF16)
        sels = sb(top, "sels", [65, 4, 128], F32)
        am0 = sb(top, "am0", [128, 117], F32)
        am1 = sb(top, "am1", [128, 69], F32)
        ones_f = sb(top, "ones_f", [128, 128], F32)
        mhalf = sb(top, "mhalf", [128, 1], F32)
        zer = sb(top, "zer", [1, 512], BF16)
        gsub2 = sb(top, "gsub2", [128, 1], F32)
        neglam = sb(top, "neglam", [128, 1], F32)
        lamt = sb(top, "lamt", [128, 4, 64], F32)
        lamj = sb(top, "lamj", [128, 64], F32)
        lams = sb(top, "lams", [128, 4], F32)
        gg = [sb(top, f"gg{s}", [128, D], F32) for s in range(3)]
        gsT = sb(top, "gsT", [128, 8, 4], F32)
        shT = sb(top, "shT", [128, 8, 4], F32)
        wout = sb(top, "wout", [128, 8, D], BF16)
        bconst = Buf()
        bgg = [Buf() for _ in range(3)]
        bmod = Buf()
        bwout = Buf()

        biasA = sb(top, "biasA", [128, 40, 4], F32)
        bbiasA = Buf()
        S.dma("dma_start", dict(out=ident_f[:], in_=IDENT[:, :]), writes=[bconst])
        S.dma("dma_start", dict(out=band_f[:], in_=BAND[:, :]), writes=[bconst])
        S.dma("dma_start", dict(out=sels[:].rearrange("p a b -> p (a b)"), in_=SELS[:, :]), writes=[bconst])
        S.dma("dma_start", dict(out=am0[:], in_=AM0[:, :]), writes=[bconst])
        S.dma("dma_start", dict(out=am1[:], in_=AM1[:, :]), writes=[bconst])
        S.dma("dma_start", dict(out=gsub2[:], in_=GSUB[:, :]), writes=[bconst])
        for i in range(4):
            S.dma("dma_start", dict(out=lamt[:, i, :], in_=LAMV[i:i + 1, :].broadcast_to([128, 64])), writes=[bconst])
        S.op(DVE, "tensor_copy", dict(out=ident[:], in_=ident_f[:]), reads=[bconst], writes=[bconst])
        S.op(DVE, "tensor_scalar", dict(out=bandneg[:], in0=band_f[:, 0:256], scalar1=30000.0, scalar2=-30000.0, op0=ALU.mult, op1=ALU.add), reads=[bconst], writes=[bconst])
        S.op(POOL, "memset", dict(ap=ones_f[:], constant=1.0), writes=[bconst])
        S.op(POOL, "memset", dict(ap=mhalf[:], constant=-0.5), writes=[bconst])
        S.op(POOL, "memset", dict(ap=zer[:], constant=0.0), writes=[bconst])
        S.op(DVE, "tensor_scalar", dict(out=gsub2[:], in0=gsub2[:], scalar1=(1.0 - LAM_INIT) * 0.5, scalar2=None, op0=ALU.mult), reads=[bconst], writes=[bconst])
        for i in range(2):
            S.op(DVE, "tensor_tensor", dict(out=lamj[:], in0=lamt[:, 2 * i, :], in1=lamt[:, 2 * i + 1, :], op=ALU.mult), reads=[bconst], writes=[bconst])
            S.op(ACT, "activation", dict(out=lamj[:], in_=lamj[:], func=AF.Identity, accum_out=lams[:, i:i + 1]), reads=[bconst], writes=[bconst])
        S.op(ACT, "activation", dict(out=lams[:, 2:4], in_=lams[:, 0:2], func=AF.Exp), reads=[bconst], writes=[bconst])
        S.op(DVE, "tensor_tensor", dict(out=neglam[:], in0=lams[:, 3:4], in1=lams[:, 2:3], op=ALU.subtract), reads=[bconst], writes=[bconst])
        S.op(DVE, "tensor_scalar", dict(out=neglam[:], in0=neglam[:], scalar1=-LAM_INIT, scalar2=None, op0=ALU.add), reads=[bconst], writes=[bconst])

        with contextlib.ExitStack() as p0:
            stg = [sb(p0, f"stg0_{i}", [128, 8, 512], F32) for i in range(2)]
            bstg = [Buf() for _ in range(2)]
            ct = sb(p0, "ct", [128, 24], F32)
            th = sb(p0, "th0", [128, 24], F32)
            scT = sb(p0, "scT", [128, 8, 4], F32)
            screp = sb(p0, "screp", [128, 24, 128], F32)
            gpre = sb(p0, "gpre", [128, 8], F32)
            bada = sb(p0, "bada", [128, 16], F32)
            bgate = sb(p0, "bgate", [128, D], F32)
            gpost = sb(p0, "gpost", [128, D], F32)
            tmpm = sb(p0, "tmpm", [128, 8], F32)
            tmpg = sb(p0, "tmpg", [128, 512], F32)
            pmod = ps(p0, "pmod", [128, 16, 4], F32)
            pg = [ps(p0, f"pg{i}", [128, 512], F32) for i in range(3)]
            b0 = Buf()
            bpm = Buf()
            bpg = [Buf() for _ in range(3)]
            S.dma("dma_start", dict(out=ct[:], in_=CTT[:, :]), writes=[b0])
            S.dma("dma_start", dict(out=gpre[:], in_=GPRE[:, :]), writes=[b0])
            S.dma("dma_start", dict(out=bada[:], in_=BADA[:, :]), writes=[b0])
            S.dma("dma_start", dict(out=bgate[:], in_=BGATE[0:1, :].broadcast_to([128, D])), writes=[b0])
            S.dma("dma_start", dict(out=gpost[:], in_=GPOST[0:1, :].broadcast_to([128, D])), writes=[b0])
            S.op(ACT, "activation", dict(out=th[:], in_=ct[:], func=AF.Tanh, scale=0.5), reads=[b0], writes=[b0])
            S.op(DVE, "scalar_tensor_tensor", dict(out=th[:], in0=th[:], scalar=1.0, in1=ct[:], op0=ALU.add, op1=ALU.mult), reads=[b0], writes=[b0])
            S.op(POOL, "memset", dict(ap=scT[:], constant=0.0), writes=[b0])
            S.op(POOL, "memset", dict(ap=shT[:], constant=0.0), writes=[bmod])
            S.op(POOL, "memset", dict(ap=gsT[:], constant=0.0), writes=[bmod])
            S.op(DVE, "tensor_scalar", dict(out=scT[:, :, 0:3], in0=th[:].rearrange("p (a b) -> p a b", b=3), scalar1=0.5, scalar2=None, op0=ALU.mult), reads=[b0], writes=[b0])
            for i in range(24):
                kc, s = divmod(i, 3)
                S.op(DVE, "tensor_scalar", dict(out=screp[:, i, :], in0=ones_f[:], scalar1=scT[:, kc, s:s + 1], scalar2=None, op0=ALU.mult), reads=[b0, bconst], writes=[b0])
            for pc in range(6):
                k = pc % 2
                S.dma("dma_start", dict(out=stg[k][:], in_=W_ADA[:, pc * 512:(pc + 1) * 512].rearrange("(kc p) n -> p kc n", p=128)), writes=[bstg[k]])
                if pc < 4:
                    for fb in range(4):
                        blk = pc * 4 + fb
                        for kc in range(8):
                            S.op(PE, "matmul", dict(out=pmod[:, blk, :], lhsT=stg[k][:, kc, fb * 128:(fb + 1) * 128], rhs=scT[:, kc, :], start=(kc == 0), stop=(kc == 7)), reads=[bstg[k], b0], writes=[bpm])
                else:
                    hf = pc - 4
                    for s in range(3):
                        for h2 in range(2):
                            for kc in range(8):
                                S.op(PE, "matmul", dict(out=pg[s][:, h2 * 256:(h2 + 1) * 256], lhsT=screp[:, kc * 3 + s, :], rhs=stg[k][:, kc, h2 * 256:(h2 + 1) * 256], start=(kc == 0), stop=(kc == 7)), reads=[bstg[k], b0], writes=[bpg[s]])
                        S.op(DVE, "tensor_tensor", dict(out=tmpg[:], in0=pg[s][:], in1=bgate[:, hf * 512:(hf + 1) * 512], op=ALU.add), reads=[bpg[s], b0], writes=[b0])
                        S.op(DVE, "tensor_tensor", dict(out=gg[s][:, hf * 512:(hf + 1) * 512], in0=tmpg[:], in1=gpost[:, hf * 512:(hf + 1) * 512], op=ALU.mult), reads=[b0], writes=[bgg[s]])
            for s in range(3):
                S.op(DVE, "tensor_tensor", dict(out=shT[:, :, s], in0=pmod[:, 0:8, s], in1=bada[:, 0:8], op=ALU.add), reads=[bpm, b0], writes=[bmod])
                S.op(DVE, "scalar_tensor_tensor", dict(out=tmpm[:], in0=pmod[:, 8:16, s], scalar=1.0, in1=bada[:, 8:16], op0=ALU.add, op1=ALU.add), reads=[bpm, b0], writes=[b0])
                S.op(DVE, "tensor_tensor", dict(out=gsT[:, :, s], in0=tmpm[:], in1=gpre[:], op=ALU.mult), reads=[b0], writes=[bmod])
            for pc in range(2):
                k = pc % 2
                S.dma("dma_start", dict(out=stg[k][:], in_=W_OUT[:, pc * 512:(pc + 1) * 512].rearrange("(kc p) n -> p kc n", p=128)), writes=[bstg[k]])
                S.op(DVE, "tensor_copy", dict(out=wout[:, :, pc * 512:(pc + 1) * 512], in_=stg[k][:]), reads=[bstg[k]], writes=[bwout])
            wbr = Ring([sb(p0, f"wbst{i}", [128, 8, 512], BF16) for i in range(2)])
            pb = ps(p0, "pb0", [128, 40, 4], F32)
            bpb = Buf()
            pieces = [(W_IN, pc * 512, 512, pc * 512) for pc in range(8)] + [(W_ROT, 0, 384, 4096), (W_ROT, 384, 384, 4096 + 384)]
            for pi_, (wsrc, c0, ncol, dcol) in enumerate(pieces):
                k = pi_ % 2
                S.dma("dma_start", dict(out=stg[k][:, :, 0:ncol], in_=wsrc[:, c0:c0 + ncol].rearrange("(kc p) n -> p kc n", p=128)), writes=[bstg[k]])
                wb_t, wb_b = wbr.next()
                S.op(ACT, "activation", dict(out=wb_t[:, 0:4, 0:ncol], in_=stg[k][:, 0:4, 0:ncol], func=AF.Copy), reads=[bstg[k]], writes=[wb_b])
                S.op(DVE, "tensor_copy", dict(out=wb_t[:, 4:8, 0:ncol], in_=stg[k][:, 4:8, 0:ncol]), reads=[bstg[k]], writes=[wb_b])
                S.dma("dma_start", dict(out=WS[:, dcol:dcol + ncol].rearrange("(kc p) n -> p kc n", p=128), in_=wb_t[:, :, 0:ncol]), reads=[wb_b], writes=[bWS])
                if pi_ < 8:
                    cols = [(fb * 128, pi_ * 4 + fb) for fb in range(4)]
                else:
                    t0 = (pi_ - 8) * 2
                    cols = [(0, 32 + 2 * t0), (64, 33 + 2 * t0), (192, 34 + 2 * t0), (256, 35 + 2 * t0)]
                for (co, bcol) in cols:
                    for kc in range(8):
                        S.op(PE, "matmul", dict(out=pb[:, bcol, :], lhsT=stg[k][:, kc, co:co + 128], rhs=shT[:, kc, :], start=(kc == 0), stop=(kc == 7)), reads=[bstg[k], bmod], writes=[bpb])
            S.op(DVE, "tensor_copy", dict(out=biasA[:], in_=pb[:]), reads=[bpb], writes=[bbiasA])
            S.barrier()

        for job in [JOBS[j_] for j_ in jobs] if stop_after != 'p0' else []:
            s, sq, skv, sext = job["s"], job["sq"], job["skv"], job["sext"]
            am = am0 if job["am"] is AM0 else am1
            with contextlib.ExitStack() as p1:
                wp = sb(p1, nm("wp"), [128, 8, WCOLS], BF16)
                bwp = Buf()
                gsrep = sb(p1, nm("gsrep"), [128, 8, 128], F32)
                bgsrep = Buf()
                if lvl >= 1:
                    for c0 in range(0, WCOLS, 1216):
                        c1 = min(c0 + 1216, WCOLS)
                        S.dma("dma_start", dict(out=wp[:, :, c0:c1], in_=WS[:, c0:c1].rearrange("(kc p) n -> p kc n", p=128)), reads=[bWS], writes=[bwp])
                    for kc in range(8):
                        S.op(DVE, "tensor_scalar", dict(out=gsrep[:, kc, :], in0=ones_f[:], scalar1=gsT[:, kc, s:s + 1], scalar2=None, op0=ALU.mult), reads=[bmod, bconst], writes=[bgsrep])
                biasT = biasA[:, :, s]
                bbias = bbiasA

                with contextlib.ExitStack() as pp:
                    xt = Ring([sb(pp, nm("xt"), [128, D], F32) for _ in range(4)])
                    junk = sb(pp, nm("junk"), [128, D], F32)
                    bjunk = Buf()
                    xn = Ring([sb(pp, nm("xn"), [128, D], F32) for _ in range(4)])
                    xnT = Ring([sb(pp, nm("xnT"), [128, 8, 512], BF16) for _ in range(2)])
                    stat = Ring([sb(pp, nm("stat"), [128, 4], F32) for _ in range(8)])
                    ost = Ring([sb(pp, nm("ost"), [128, 4, 512], BF16) for _ in range(3)])
                    cst = Ring([sb(pp, nm("cst"), [128, 2, 512], F32) for _ in range(2)])
                    t12 = Ring([sb(pp, nm("t12"), [128, 2, 512], F32) for _ in range(2)])
                    rot = Ring([sb(pp, nm("rot"), [128, 512], BF16) for _ in range(2)])
                    pT = Ring([ps(pp, nm("pT"), [128, D], F32) for _ in range(2)])
                    pacc = Ring([ps(pp, nm("pacc"), [128, 512], F32) for _ in range(4)])

                    G = dict(qa=(0, 0), ka=(512, 4), va=(1024, 8), ga=(1536, 12), qb=(2048, 16), kb=(2560, 20), vb=(3072, 24), gb=(3584, 28))
                    RT = dict(qa=0, ka=1, qb=2, kb=3)
                    segs = []
                    full_main = [("qa", "QA", 0), ("ka", "KA", PADK), ("va", "VA", PADK), ("ga", "GA", 0), ("qb", "QB", 0), ("kb", "KB", 0), ("vb", "VB", 0), ("gb", "GB", 0)]
                    full_rot = [("qa", "QAr", 0), ("ka", "KAr", PADK), ("qb", "QBr", 0), ("kb", "KBr", 0)]
                    segs.append((job["xrow"], sq, full_main, full_rot))
                    if s == 0:
                        segs.append((ROW_OTHER, HALF, [("kb", "KB", HALF), ("vb", "VB", HALF)], [("kb", "KBr", HALF)]))
                        segs.append((ROW_HALO, PADK, [("ka", "KA", 0), ("va", "VA", 0)], [("ka", "KAr", 0)]))
                        segs.append((ROW_HALO + PADK, PADK, [("ka", "KA", PADK + HALF), ("va", "VA", PADK + HALF)], [("ka", "KAr", PADK + HALF)]))
                    evi = [0]
                    chunks = []
                    for (xr0, T, mains, rots) in (segs if lvl >= 2 else []):
                        for ch in range(T // 512):
                            chunks.append((xr0 + ch * 512, ch, mains, rots))

                    def emit_loads(cidx):
                        r0, ch, mains, rots = chunks[cidx]
                        xs = []
                        for i in range(4):
                            x_t, x_b = xt.next()
                            S.dma("dma_start", dict(out=x_t[:], in_=X[r0 + i * 128:r0 + (i + 1) * 128, :]), writes=[x_b])
                            xs.append((x_t, x_b))
                        c_t, c_b = cst.next()
                        if rots:
                            S.dma("dma_start", dict(out=c_t[:, 0, :], in_=CT[:, r0:r0 + 512]), writes=[c_b])
                            S.dma("dma_start", dict(out=c_t[:, 1, :], in_=ST[:, r0:r0 + 512]), writes=[c_b])
                        return xs, c_t, c_b

                    def emit_prologue(cidx, xs):
                        xT_t, xT_b = xnT.next()
                        for i, (x_t, x_b) in enumerate(xs):
                            st_t, st_b = stat.next()
                            S.op(ACT, "activation", dict(out=junk[:], in_=x_t[:], func=AF.Square, accum_out=st_t[:, 0:1]), reads=[x_b], writes=[bjunk, st_b])
                            S.op(DVE, "tensor_scalar", dict(out=st_t[:, 1:2], in0=st_t[:, 0:1], scalar1=1.0 / D, scalar2=EPS, op0=ALU.mult, op1=ALU.add), reads=[st_b], writes=[st_b])
                            S.op(POOL, "tensor_tensor", dict(out=st_t[:, 2:3], in0=st_t[:, 1:2], in1=mhalf[:], op=ALU.pow), reads=[st_b, bconst], writes=[st_b])
                            xn_t, xn_b = xn.next()
                            S.op(DVE, "tensor_scalar", dict(out=xn_t[:], in0=x_t[:], scalar1=st_t[:, 2:3], scalar2=None, op0=ALU.mult), reads=[x_b, st_b], writes=[xn_b])
                            pT_t, pT_b = pT.next()
                            for kc in range(8):
                                S.op(PE, "transpose", dict(out=pT_t[:, kc * 128:(kc + 1) * 128], in_=xn_t[:, kc * 128:(kc + 1) * 128], identity=ident_f[:]), reads=[xn_b, bconst], writes=[pT_b])
                            S.op(DVE, "tensor_tensor", dict(out=xT_t[:, :, i * 128:(i + 1) * 128], in0=pT_t[:].rearrange("p (a b) -> p a b", b=128), in1=gsrep[:], op=ALU.mult), reads=[pT_b, bgsrep], writes=[xT_b])
                        return xT_t, xT_b

                    def emit_main(cidx, xT_t, xT_b, c_t, c_b):
                        r0, ch, mains, rots = chunks[cidx]
                        for (gname, skey, dcol0) in mains:
                            wcol, bcol = G[gname]
                            o_t, o_b = ost.next()
                            for b in range(4):
                                pa_t, pa_b = pacc.next()
                                c0 = wcol + b * 128
                                bc = bcol + b
                                for kc in range(8):
                                    S.op(PE, "matmul", dict(out=pa_t[:], lhsT=wp[:, kc, c0:c0 + 128], rhs=xT_t[:, kc, :], start=(kc == 0), stop=(kc == 7)), reads=[bwp, xT_b], writes=[pa_b])
                                evi[0] += 1
                                if evi[0] % 4 != 0:
                                    S.op(ACT, "activation", dict(out=o_t[:, b, :], in_=pa_t[:], func=AF.Identity, bias=biasT[:, bc:bc + 1]), reads=[pa_b, bbias], writes=[o_b])
                                else:
                                    S.op(DVE, "tensor_scalar", dict(out=o_t[:, b, :], in0=pa_t[:], scalar1=biasT[:, bc:bc + 1], scalar2=None, op0=ALU.add), reads=[pa_b, bbias], writes=[o_b])
                            dc = dcol0 + ch * 512
                            S.dma("dma_start", dict(out=SC[skey][:, dc:dc + 512].rearrange("(b p) t -> p b t", p=128), in_=o_t[:]), reads=[o_b], writes=[SCB[skey]])
                        for (gname, skey, dcol0) in rots:
                            ti = RT[gname]
                            wc = 4096 + ti * 192
                            bc = 32 + 2 * ti
                            p1_t, p1_b = pacc.next()
                            p2_t, p2_b = pacc.next()
                            for kc in range(8):
                                S.op(PE, "matmul", dict(out=p1_t[:], lhsT=wp[:, kc, wc:wc + 128], rhs=xT_t[:, kc, :], start=(kc == 0), stop=(kc == 7)), reads=[bwp, xT_b], writes=[p1_b])
                            for kc in range(8):
                                S.op(PE, "matmul", dict(out=p2_t[:], lhsT=wp[:, kc, wc + 64:wc + 192], rhs=xT_t[:, kc, :], start=(kc == 0), stop=(kc == 7)), reads=[bwp, xT_b], writes=[p2_b])
                            t_t, t_b = t12.next()
                            S.op(DVE, "scalar_tensor_tensor", dict(out=t_t[:, 0, :], in0=p1_t[:], scalar=biasT[:, bc:bc + 1], in1=c_t[:, 0, :], op0=ALU.add, op1=ALU.mult), reads=[p1_b, c_b, bbias], writes=[t_b])
                            S.op(DVE, "scalar_tensor_tensor", dict(out=t_t[:, 1, :], in0=p2_t[:], scalar=biasT[:, bc + 1:bc + 2], in1=c_t[:, 1, :], op0=ALU.add, op1=ALU.mult), reads=[p2_b, c_b, bbias], writes=[t_b])
                            r_t, r_b = rot.next()
                            S.op(DVE, "tensor_tensor", dict(out=r_t[:], in0=t_t[:, 0, :], in1=t_t[:, 1, :], op=ALU.add), reads=[t_b], writes=[r_b])
                            dc = dcol0 + ch * 512
                            S.dma("dma_start", dict(out=SC[skey][:, dc:dc + 512], in_=r_t[:]), reads=[r_b], writes=[SCB[skey]])

                    loaded = {}
                    if chunks:
                        loaded[0] = emit_loads(0)
                    for cidx in range(len(chunks)):
                        xs, c_t, c_b = loaded.pop(cidx)
                        xT_t, xT_b = emit_prologue(cidx, xs)
                        if cidx + 1 < len(chunks):
                            loaded[cidx + 1] = emit_loads(cidx + 1)
                        emit_main(cidx, xT_t, xT_b, c_t, c_b)
                    S.barrier()

            with contextlib.ExitStack() as pB:
                nkt = skv // 128
                nqc = sq // 512
                kbT = [sb(pB, nm("kbT"), [128, skv], BF16) for _ in range(2)]
                vbT = [sb(pB, nm("vbT"), [128, skv], BF16) for _ in range(2)]
                qbT = [sb(pB, nm("qbT"), [128, sq], BF16) for _ in range(2)]
                gbT = [sb(pB, nm("gbT"), [128, sq], BF16) for _ in range(2)]
                bk, bv, bq, bg = ([Buf(), Buf()] for _ in range(4))
                vtok = sb(pB, nm("vtok"), [128, nkt, 128], BF16)
                bvt = Buf()
                Pr = Ring([sb(pB, nm("P"), [128, 1024], BF16) for _ in range(6)])
                asum = [sb(pB, nm("asum"), [128, 1024], F32) for _ in range(2)]
                basum = [Buf() for _ in range(2)]
                oS = [[sb(pB, nm("oS"), [128, 512], F32) for _ in range(2)] for _ in range(2)]
                boS = [[Buf() for _ in range(2)] for _ in range(2)]
                thr = Ring([sb(pB, nm("thb"), [128, 512], F32) for _ in range(2)])
                sgr = Ring([sb(pB, nm("sgb"), [128, 512], BF16) for _ in range(3)])
                yTr = Ring([sb(pB, nm("yTb"), [128, 512], BF16) for _ in range(3)])
                rsr = Ring([sb(pB, nm("rs"), [128, 16], F32) for _ in range(3)])
                rr = Ring([sb(pB, nm("rr"), [128, 8], F32) for _ in range(6)])
                t0r = Ring([sb(pB, nm("t0"), [128, 128], F32) for _ in range(3)])
                orr = Ring([sb(pB, nm("o"), [128, 128], F32) for _ in range(5)])
                onr = Ring([sb(pB, nm("on"), [128, 128], BF16) for _ in range(5)])
                junkb = sb(pB, nm("junkb"), [128, 128], F32)
                bjb = Buf()
                scr_ = Ring([ps(pB, nm("sc"), [128, 1024], F32) for _ in range(2)])
                oT = [ps(pB, nm("oT"), [128, 512], F32) for _ in range(2)]
                boT = [Buf() for _ in range(2)]
                tpb = ps(pB, nm("tpb"), [128, 512], F32)
                btp = Buf()
                pTv = ps(pB, nm("pTv"), [128, 4, 128], BF16)
                bpv = Buf()

                def load_head(g):
                    hb = g % 2
                    for (dst, dbuf, main, rotk, ncols) in ((kbT[hb], bk[hb], "KB", "KBr", skv), (qbT[hb], bq[hb], "QB", "QBr", sq)):
                        for m in range(2):
                            h = 2 * g + m
                            S.dma("dma_start", dict(out=dst[m * 64:m * 64 + 48, 0:ncols], in_=SC[main][g * 128 + m * 64 + 16:g * 128 + m * 64 + 64, 0:ncols]), reads=[SCB[main]], writes=[dbuf])
                            S.dma("dma_start", dict(out=dst[m * 64 + 48:m * 64 + 56, 0:ncols], in_=SC[rotk][h * 8:h * 8 + 8, 0:ncols]), reads=[SCB[rotk]], writes=[dbuf])
                            S.dma("dma_start", dict(out=dst[m * 64 + 56:m * 64 + 64, 0:ncols], in_=SC[rotk][64 + h * 8:64 + h * 8 + 8, 0:ncols]), reads=[SCB[rotk]], writes=[dbuf])
                    S.dma("dma_start", dict(out=vbT[hb][:], in_=SC["VB"][g * 128:(g + 1) * 128, 0:skv]), reads=[SCB["VB"]], writes=[bv[hb]])
                    S.dma("dma_start", dict(out=gbT[hb][:], in_=SC["GB"][g * 128:(g + 1) * 128, 0:sq]), reads=[SCB["GB"]], writes=[bg[hb]])

                def make_epilogue(g, q0, par, sg_t, sg_b):
                    steps = []
                    rs_t, rs_b = rsr.next()
                    y_t, y_b = yTr.next()
                    state = {}

                    def rowsums():
                        for m in range(2):
                            for qb in range(4):
                                c = 256 + 2 * (m * 4 + qb)
                                S.op(PE, "matmul", dict(out=tpb[:, c:c + 2], lhsT=asum[par][:, m * 512 + qb * 128:m * 512 + (qb + 1) * 128], rhs=ones_f[:, 0:2], start=True, stop=True), reads=[basum[par], bconst], writes=[btp])
                        S.op(DVE, "reciprocal", dict(out=rs_t[:, 0:8], in_=tpb[:, 256:272:2]), reads=[btp], writes=[rs_b])
                        S.op(DVE, "tensor_scalar", dict(out=rs_t[:, 8:12], in0=rs_t[:, 4:8], scalar1=neglam[:, 0:1], scalar2=None, op0=ALU.mult), reads=[rs_b, bconst], writes=[rs_b])
                    steps.append(rowsums)

                    def block_a(qb):
                        def f():
                            e_t, e_b = tpb, btp
                            for m in range(2):
                                S.op(PE, "transpose", dict(out=e_t[:, m * 128:(m + 1) * 128], in_=oS[par][m][:, qb * 128:(qb + 1) * 128], identity=ident_f[:]), reads=[boS[par][m], bconst], writes=[e_b])
                            r_t, r_b = rr.next()
                            t0_t, t0_b = t0r.next()
                            S.op(DVE, "tensor_scalar", dict(out=t0_t[:], in0=e_t[:, 0:128], scalar1=rs_t[:, qb:qb + 1], scalar2=None, op0=ALU.mult), reads=[e_b, rs_b], writes=[t0_b])
                            o_t, o_b = orr.next()
                            S.op(DVE, "scalar_tensor_tensor", dict(out=o_t[:], in0=e_t[:, 128:256], scalar=rs_t[:, 8 + qb:9 + qb], in1=t0_t[:], op0=ALU.mult, op1=ALU.add), reads=[e_b, rs_b, t0_b], writes=[o_b])
                            S.op(ACT, "activation", dict(out=junkb[:], in_=o_t[:], func=AF.Square, accum_out=r_t[:, 3:4]), reads=[o_b], writes=[bjb, r_b])
                            S.op(DVE, "tensor_scalar", dict(out=r_t[:, 4:5], in0=r_t[:, 3:4], scalar1=1.0 / 128, scalar2=EPS, op0=ALU.mult, op1=ALU.add), reads=[r_b], writes=[r_b])
                            S.op(POOL, "tensor_tensor", dict(out=r_t[:, 5:6], in0=r_t[:, 4:5], in1=mhalf[:], op=ALU.pow), reads=[r_b, bconst], writes=[r_b])
                            on_t, on_b = onr.next()
                            S.op(ACT, "activation", dict(out=on_t[:], in_=o_t[:], func=AF.Identity, scale=r_t[:, 5:6]), reads=[o_b, r_b], writes=[on_b])
                            state[qb] = (on_t, on_b)
                        return f

                    def block_b(qb, last):
                        def f():
                            on_t, on_b = state[qb]
                            ev = pTv[:, 0, :]
                            S.op(PE, "transpose", dict(out=ev, in_=on_t[:], identity=ident[:]), reads=[on_b, bconst], writes=[bpv])
                            S.op(DVE, "scalar_tensor_tensor", dict(out=y_t[:, qb * 128:(qb + 1) * 128], in0=ev, scalar=gsub2[:, 0:1], in1=sg_t[:, qb * 128:(qb + 1) * 128], op0=ALU.mult, op1=ALU.mult), reads=[bpv, sg_b, bconst], writes=[y_b])
                            if last:
                                S.dma("dma_start", dict(out=SC["YM"][512 + g * 128:512 + (g + 1) * 128, q0:q0 + 512], in_=y_t[:]), reads=[y_b], writes=[SCB["YM"]])
                        return f

                    for kind, qb in [("a", 0), ("a", 1), ("a", 2), ("b", 0), ("a", 3), ("b", 1), ("b", 2), ("b", 3)]:
                        steps.append(block_a(qb) if kind == "a" else block_b(qb, qb == 3))
                    return steps

                pending = []
                heads = list(range(4)) if lvl >= 3 else []
                if heads:
                    load_head(0)
                for g in heads:
                    hb = g % 2
                    if g + 1 < 4:
                        load_head(g + 1)
                    for k4 in range(nkt // 4):
                        for j in range(4):
                            kt = k4 * 4 + j
                            S.op(PE, "transpose", dict(out=pTv[:, j, :], in_=vbT[hb][:, kt * 128:(kt + 1) * 128], identity=ident[:]), reads=[bv[hb], bconst], writes=[bpv])
                        S.op(ACT, "activation", dict(out=vtok[:, k4 * 4:k4 * 4 + 4, :], in_=pTv[:], func=AF.Copy), reads=[bpv], writes=[bvt])
                    for qc in range(nqc):
                        q0 = qc * 512
                        par = qc % 2
                        th_t, th_b = thr.next()
                        sg_t, sg_b = sgr.next()
                        S.op(ACT, "activation", dict(out=th_t[:], in_=gbT[hb][:, q0:q0 + 512], func=AF.Tanh, scale=0.5), reads=[bg[hb]], writes=[th_b])
                        S.op(DVE, "scalar_tensor_tensor", dict(out=sg_t[:], in0=th_t[:], scalar=1.0, in1=gbT[hb][:, q0:q0 + 512], op0=ALU.add, op1=ALU.mult), reads=[th_b, bg[hb]], writes=[sg_b])

                        def scores(kt):
                            sc_t, sc_b = scr_.next()
                            for m in range(2):
                                S.op(PE, "matmul", dict(out=sc_t[:, m * 512:(m + 1) * 512], lhsT=kbT[hb][m * 64:(m + 1) * 64, kt * 128:(kt + 1) * 128], rhs=qbT[hb][m * 64:(m + 1) * 64, q0:q0 + 512], start=True, stop=True), reads=[bk[hb], bq[hb]], writes=[sc_b])
                            p_t, p_b = Pr.next()
                            S.op(ACT, "activation", dict(out=p_t[:], in_=sc_t[:], func=AF.Exp, scale=0.125), reads=[sc_b], writes=[p_b])
                            return p_t, p_b

                        def av(kt, p_t, p_b):
                            for m in range(2):
                                S.op(PE, "matmul", dict(out=oT[m][:], lhsT=vtok[:, kt, :], rhs=p_t[:, m * 512:(m + 1) * 512], start=(kt == 0), stop=(kt == nkt - 1)), reads=[p_b, bvt], writes=[boT[m]])
                            if kt == 0:
                                S.op(DVE, "tensor_copy", dict(out=asum[par][:], in_=p_t[:]), reads=[p_b], writes=[basum[par]])
                            else:
                                S.op(DVE, "tensor_tensor", dict(out=asum[par][:], in0=asum[par][:], in1=p_t[:], op=ALU.add), reads=[p_b, basum[par]], writes=[basum[par]])

                        ptiles = {0: scores(0)}
                        for kt in range(nkt):
                            if kt + 1 < nkt:
                                ptiles[kt + 1] = scores(kt + 1)
                            if kt >= 1:
                                av(kt - 1, *ptiles.pop(kt - 1))
                            if pending and kt >= 2 and kt % 2 == 0:
                                pending.pop(0)()
                        av(nkt - 1, *ptiles.pop(nkt - 1))
                        for m in range(2):
                            S.op(ACT, "activation", dict(out=oS[par][m][:], in_=oT[m][:], func=AF.Copy), reads=[boT[m]], writes=[boS[par][m]])
                        while pending:
                            pending.pop(0)()
                        pending.extend(make_epilogue(g, q0, par, sg_t, sg_b))
                while pending:
                    pending.pop(0)()
                S.barrier()

            with contextlib.ExitStack() as pA:
                tiles = amask_tiles(sq)
                nt = len(tiles)
                kaT = [sb(pA, nm("kaT"), [128, sext], BF16) for _ in range(2)]
                vaT1 = sb(pA, nm("vaT"), [128, sext], BF16)
                vaT = [vaT1, vaT1]
                bv1 = Buf()
                qaT = [sb(pA, nm("qaT"), [128, sq], BF16) for _ in range(2)]
                gaT = [sb(pA, nm("gaT"), [128, sq], BF16) for _ in range(2)]
                bk, bv, bq, bg = ([Buf(), Buf()] for _ in range(4))
                bv = [bv1, bv1]
                acc = sb(pA, nm("accA"), [65, 2, sq], F32)
                bacc = Buf()
                vt_all = sb(pA, nm("vtall"), [128, nt, 2, 65], BF16)
                bvta = Buf()
                vrow = sb(pA, nm("vrow"), [128, 2 * PADK], F32)
                bvrow = Buf()
                Pt = Ring([sb(pA, nm("Pt"), [128, 2, 512], BF16) for _ in range(4)])
                recr = Ring([sb(pA, nm("rec"), [128, 512], F32) for _ in range(2)])
                thr = Ring([sb(pA, nm("tha"), [128, 512], F32) for _ in range(2)])
                sgr = Ring([sb(pA, nm("sga"), [128, 512], F32) for _ in range(2)])
                tnr = Ring([sb(pA, nm("tn"), [128, 512], F32) for _ in range(2)])
                yTr = Ring([sb(pA, nm("yTa"), [128, 512], BF16) for _ in range(2)])
                scA = Ring([ps(pA, nm("scA"), [128, 2, 512], F32) for _ in range(2)])
                opsr = Ring([ps(pA, nm("ops"), [128, 2, 512], F32) for _ in range(2)])
                if s == 0:
                    S.dma("dma_start", dict(out=vrow[:], in_=VROW[:, :]), writes=[bvrow])

                kc0, kn = (PADK, sq) if s != 0 else (0, sext)

                def load_v(hp):
                    hb = hp % 2
                    if s != 0:
                        S.op(POOL, "memset", dict(ap=vaT[hb][:, 0:PADK], constant=0.0), writes=[bv[hb]])
                        S.op(POOL, "memset", dict(ap=vaT[hb][:, PADK + sq:sext], constant=0.0), writes=[bv[hb]])
                    S.dma("dma_start", dict(out=vaT[hb][:, kc0:kc0 + kn], in_=SC["VA"][hp * 128:(hp + 1) * 128, kc0:kc0 + kn]), reads=[SCB["VA"]], writes=[bv[hb]])
                    if s == 0:
                        S.op(DVE, "tensor_tensor", dict(out=vaT[hb][:, 0:PADK], in0=vaT[hb][:, 0:PADK], in1=vrow[:, 0:PADK], op=ALU.mult), reads=[bvrow, bv[hb]], writes=[bv[hb]])
                        S.op(DVE, "tensor_tensor", dict(out=vaT[hb][:, PADK + sq:sext], in0=vaT[hb][:, PADK + sq:sext], in1=vrow[:, PADK:2 * PADK], op=ALU.mult), reads=[bvrow, bv[hb]], writes=[bv[hb]])

                def load_pair(hp):
                    hb = hp % 2
                    if s != 0:
                        S.op(POOL, "memset", dict(ap=kaT[hb][:, 0:PADK], constant=0.0), writes=[bk[hb]])
                        S.op(POOL, "memset", dict(ap=kaT[hb][:, PADK + sq:sext], constant=0.0), writes=[bk[hb]])
                    for hl in range(2):
                        h = 2 * hp + hl
                        S.dma("dma_start", dict(out=kaT[hb][hl * 64:hl * 64 + 48, kc0:kc0 + kn], in_=SC["KA"][hp * 128 + hl * 64 + 16:hp * 128 + hl * 64 + 64, kc0:kc0 + kn]), reads=[SCB["KA"]], writes=[bk[hb]])
                        S.dma("dma_start", dict(out=kaT[hb][hl * 64 + 48:hl * 64 + 56, kc0:kc0 + kn], in_=SC["KAr"][h * 8:h * 8 + 8, kc0:kc0 + kn]), reads=[SCB["KAr"]], writes=[bk[hb]])
                        S.dma("dma_start", dict(out=kaT[hb][hl * 64 + 56:hl * 64 + 64, kc0:kc0 + kn], in_=SC["KAr"][64 + h * 8:64 + h * 8 + 8, kc0:kc0 + kn]), reads=[SCB["KAr"]], writes=[bk[hb]])
                        S.dma("dma_start", dict(out=qaT[hb][hl * 64:hl * 64 + 48, :], in_=SC["QA"][hp * 128 + hl * 64 + 16:hp * 128 + hl * 64 + 64, 0:sq]), reads=[SCB["QA"]], writes=[bq[hb]])
                        S.dma("dma_start", dict(out=qaT[hb][hl * 64 + 48:hl * 64 + 56, :], in_=SC["QAr"][h * 8:h * 8 + 8, 0:sq]), reads=[SCB["QAr"]], writes=[bq[hb]])
                        S.dma("dma_start", dict(out=qaT[hb][hl * 64 + 56:hl * 64 + 64, :], in_=SC["QAr"][64 + h * 8:64 + h * 8 + 8, 0:sq]), reads=[SCB["QAr"]], writes=[bq[hb]])
                    S.dma("dma_start", dict(out=gaT[hb][:], in_=SC["GA"][hp * 128:(hp + 1) * 128, 0:sq]), reads=[SCB["GA"]], writes=[bg[hb]])

                def tile_geom(tile):
                    (pi, dil, r, t, nblk, je0) = tile
                    e0 = r + dil * (je0 + 128 * t)
                    ksl = slice(e0, e0 + dil * 127 + 1, dil)
                    mlo, mhi = max(t - 1, 0), min(t, nblk - 1)
                    nq = (mhi - mlo + 1) * 128
                    qs = r + dil * 128 * mlo
                    qsl = slice(qs, qs + dil * (nq - 1) + 1, dil)
                    boff = 0 if t >= 1 else 128
                    return ksl, qsl, nq, boff

                pairs = list(range(4)) if lvl >= 4 else []
                if pairs:
                    load_pair(0)
                    load_v(0)
                for hp in pairs:
                    hb = hp % 2
                    if hp + 1 < 4:
                        load_pair(hp + 1)
                    S.op(POOL, "memset", dict(ap=acc[:], constant=0.0), writes=[bacc])
                    for t4 in range(0, nt, 4):
                        n4 = min(4, nt - t4)
                        o_t, o_b = opsr.next()
                        pv = o_t[:, 0, 0:256].bitcast(BF16).rearrange("p (a b) -> p a b", b=128)
                        for j in range(n4):
                            ksl = tile_geom(tiles[t4 + j])[0]
                            S.op(PE, "transpose", dict(out=pv[:, j, :], in_=vaT[hb][:, ksl], identity=ident[:]), reads=[bv[hb], bconst], writes=[o_b])
                        eng = ACT if (t4 // 4) % 2 == 0 else DVE
                        if eng == ACT:
                            S.op(ACT, "activation", dict(out=vt_all[:, t4:t4 + n4, :, 0:64], in_=pv[:, 0:n4, :].rearrange("p a (h d) -> p a h d", h=2), func=AF.Copy), reads=[o_b], writes=[bvta])
                        else:
                            S.op(DVE, "tensor_copy", dict(out=vt_all[:, t4:t4 + n4, :, 0:64], in_=pv[:, 0:n4, :].rearrange("p a (h d) -> p a h d", h=2)), reads=[o_b], writes=[bvta])
                    for hl in range(2):
                        S.op(DVE, "tensor_copy", dict(out=vt_all[:, :, hl, 64], in_=am[:, 0:nt]), reads=[bconst], writes=[bvta])
                    if hp + 1 < 4:
                        load_v(hp + 1)
                    units = [list(range(u, min(u + 2, nt))) for u in range(0, nt, 2)]

                    def scores(unit):
                        sc_t, sc_b = scA.next()
                        geo = []
                        for j, ti in enumerate(unit):
                            ksl, qsl, nq, boff = tile_geom(tiles[ti])
                            geo.append((ti, qsl, nq))
                            for hl in range(2):
                                S.op(PE, "matmul", dict(out=sc_t[:, hl, j * 256:j * 256 + nq], lhsT=kaT[hb][hl * 64:(hl + 1) * 64, ksl], rhs=qaT[hb][hl * 64:(hl + 1) * 64, qsl], start=True, stop=False), reads=[bk[hb], bq[hb]], writes=[sc_b])
                                S.op(PE, "matmul", dict(out=sc_t[:, hl, j * 256:j * 256 + nq], lhsT=ident[:], rhs=bandneg[:, boff:boff + nq], start=False, stop=True), reads=[bconst], writes=[sc_b])
                        p_t, p_b = Pt.next()
                        if len(unit) == 2 and all(g_[2] == 256 for g_ in geo):
                            S.op(ACT, "activation", dict(out=p_t[:], in_=sc_t[:], func=AF.Exp, scale=0.125), reads=[sc_b], writes=[p_b])
                        else:
                            for j, (ti, qsl, nq) in enumerate(geo):
                                S.op(ACT, "activation", dict(out=p_t[:, :, j * 256:j * 256 + nq], in_=sc_t[:, :, j * 256:j * 256 + nq], func=AF.Exp, scale=0.125), reads=[sc_b], writes=[p_b])
                        return geo, p_t, p_b

                    def av(geo, p_t, p_b):
                        o_t, o_b = opsr.next()
                        for j, (ti, qsl, nq) in enumerate(geo):
                            for hl in range(2):
                                for bi in range(nq // 128):
                                    c = j * 256 + bi * 128
                                    S.op(PE, "matmul", dict(out=o_t[0:65, hl, c:c + 128], lhsT=vt_all[:, ti, hl, :], rhs=p_t[:, hl, c:c + 128], start=True, stop=True), reads=[bvta, p_b], writes=[o_b])
                        for j, (ti, qsl, nq) in enumerate(geo):
                            S.op(DVE, "tensor_tensor", dict(out=acc[:, :, qsl], in0=o_t[0:65, :, j * 256:j * 256 + nq], in1=acc[:, :, qsl], op=ALU.add), reads=[o_b, bacc], writes=[bacc])

                    inflight = {0: scores(units[0])}
                    for u in range(len(units)):
                        if u + 1 < len(units):
                            inflight[u + 1] = scores(units[u + 1])
                        if u >= 1:
                            av(*inflight.pop(u - 1))
                    av(*inflight.pop(len(units) - 1))
                    for qc in range(sq // 512):
                        q0 = qc * 512
                        d_t, d_b = opsr.next()
                        dv = d_t[:, 0, :]
                        nv = d_t[:, 1, :]
                        for h2 in range(2):
                            qa_, qb_ = q0 + h2 * 256, q0 + (h2 + 1) * 256
                            for hl in range(2):
                                S.op(PE, "matmul", dict(out=dv[:, h2 * 256:(h2 + 1) * 256], lhsT=sels[:, hl, :], rhs=acc[:, hl, qa_:qb_], start=(hl == 0), stop=(hl == 1)), reads=[bacc, bconst], writes=[d_b])
                        for h2 in range(2):
                            qa_, qb_ = q0 + h2 * 256, q0 + (h2 + 1) * 256
                            for hl in range(2):
                                S.op(PE, "matmul", dict(out=nv[:, h2 * 256:(h2 + 1) * 256], lhsT=sels[:, 2 + hl, :], rhs=acc[:, hl, qa_:qb_], start=(hl == 0), stop=(hl == 1)), reads=[bacc, bconst], writes=[d_b])
                        rc_t, rc_b = recr.next()
                        S.op(DVE, "reciprocal", dict(out=rc_t[:], in_=dv), reads=[d_b], writes=[rc_b])
                        th_t, th_b = thr.next()
                        S.op(ACT, "activation", dict(out=th_t[:], in_=gaT[hb][:, q0:q0 + 512], func=AF.Tanh, scale=0.5), reads=[bg[hb]], writes=[th_b])
                        sg_t, sg_b = sgr.next()
                        S.op(DVE, "scalar_tensor_tensor", dict(out=sg_t[:], in0=th_t[:], scalar=1.0, in1=gaT[hb][:, q0:q0 + 512], op0=ALU.add, op1=ALU.mult), reads=[th_b, bg[hb]], writes=[sg_b])
                        tn_t, tn_b = tnr.next()
                        S.op(DVE, "tensor_tensor", dict(out=tn_t[:], in0=nv, in1=rc_t[:], op=ALU.mult), reads=[d_b, rc_b], writes=[tn_b])
                        y_t, y_b = yTr.next()
                        S.op(DVE, "tensor_tensor", dict(out=y_t[:], in0=tn_t[:], in1=sg_t[:], op=ALU.mult), reads=[tn_b, sg_b], writes=[y_b])
                        S.dma("dma_start", dict(out=SC["YM"][hp * 128:(hp + 1) * 128, q0:q0 + 512], in_=y_t[:]), reads=[y_b], writes=[SCB["YM"]])
                S.barrier()

            with contextlib.ExitStack() as p3:
                ymr = Ring([sb(p3, nm("ym"), [128, 8, 512], BF16) for _ in range(2)])
                xr = Ring([sb(p3, nm("x3"), [128, D], F32) for _ in range(8)])
                tmr = Ring([sb(p3, nm("tm3"), [128, D], F32) for _ in range(3)])
                outr = Ring([sb(p3, nm("o3"), [128, D], F32) for _ in range(4)])
                st3 = Ring([sb(p3, nm("st3"), [128, 6], F32) for _ in range(8)])
                junk3 = sb(p3, nm("junk3"), [128, 512], F32)
                bj3 = Buf()
                po = Ring([ps(p3, nm("po"), [128, 2, 512], F32) for _ in range(4)])
                nq3 = (sq // 512) if lvl >= 5 else 0

                def p3_loads(qc):
                    q0 = qc * 512
                    ym_t, ym_b = ymr.next()
                    S.dma("dma_start", dict(out=ym_t[:], in_=SC["YM"][:, q0:q0 + 512].rearrange("(kc p) t -> p kc t", p=128)), reads=[SCB["YM"]], writes=[ym_b])
                    xs = []
                    for i in range(4):
                        row = q0 + i * 128
                        x_t, x_b = xr.next()
                        S.dma("dma_start", dict(out=x_t[:], in_=X[job["xrow"] + row:job["xrow"] + row + 128, :]), writes=[x_b])
                        xs.append((x_t, x_b))
                    return ym_t, ym_b, xs

                ld3 = {}
                if nq3:
                    ld3[0] = p3_loads(0)
                for qc in range(nq3):
                    q0 = qc * 512
                    ym_t, ym_b, xs = ld3.pop(qc)
                    if qc + 1 < nq3:
                        ld3[qc + 1] = p3_loads(qc + 1)
                    for i in range(4):
                        row = q0 + i * 128
                        x_t, x_b = xs[i]
                        po_t, po_b = po.next()
                        for hf in range(2):
                            for kc in range(8):
                                S.op(PE, "matmul", dict(out=po_t[:, hf, :], lhsT=ym_t[:, kc, i * 128:(i + 1) * 128], rhs=wout[:, kc, hf * 512:(hf + 1) * 512], start=(kc == 0), stop=(kc == 7)), reads=[ym_b, bwout], writes=[po_b])
                        s_t, s_b = st3.next()
                        for hf in range(2):
                            S.op(ACT, "activation", dict(out=junk3[:], in_=po_t[:, hf, :], func=AF.Square, accum_out=s_t[:, hf:hf + 1]), reads=[po_b], writes=[bj3, s_b])
                        S.op(DVE, "tensor_tensor", dict(out=s_t[:, 2:3], in0=s_t[:, 0:1], in1=s_t[:, 1:2], op=ALU.add), reads=[s_b], writes=[s_b])
                        S.op(DVE, "tensor_scalar", dict(out=s_t[:, 3:4], in0=s_t[:, 2:3], scalar1=1.0 / D, scalar2=EPS, op0=ALU.mult, op1=ALU.add), reads=[s_b], writes=[s_b])
                        S.op(POOL, "tensor_tensor", dict(out=s_t[:, 4:5], in0=s_t[:, 3:4], in1=mhalf[:], op=ALU.pow), reads=[s_b, bconst], writes=[s_b])
                        tm_t, tm_b = tmr.next()
                        for hf in range(2):
                            S.op(DVE, "scalar_tensor_tensor", dict(out=tm_t[:, hf * 512:(hf + 1) * 512], in0=po_t[:, hf, :], scalar=s_t[:, 4:5], in1=gg[s][:, hf * 512:(hf + 1) * 512], op0=ALU.mult, op1=ALU.mult), reads=[po_b, s_b, bgg[s]], writes=[tm_b])
                        o_t, o_b = outr.next()
                        S.op(DVE, "tensor_tensor", dict(out=o_t[:], in0=tm_t[:], in1=x_t[:], op=ALU.add), reads=[tm_b, x_b], writes=[o_b])
                        yrow = job["yrow"] + row
                        out_dmas.append(S.dma("dma_start", dict(out=Y[yrow:yrow + 128, :], in_=o_t[:]), reads=[o_b], writes=[Buf()]))
                S.barrier()

        S.run(final_wait=out_dmas)
    return nc


_NC_CACHE = {}


def _rope_tables(pos):
    half = 8
    inv = (np.float32(500000.0) ** (-np.arange(half, dtype=np.float32) / np.float32(half))).astype(np.float32)
    ang = pos.astype(np.float32)[None, :] * inv[:, None]
    cos = np.cos(ang).astype(np.float32)
    sin = np.sin(ang).astype(np.float32)
    idx = np.arange(128) % 8
    C = cos[idx]
    Sg = sin[idx].copy()
    Sg[:64] *= -1.0
    return np.ascontiguousarray(C), np.ascontiguousarray(Sg)


def _amask(valid_ext, sq):
    tiles = amask_tiles(sq)
    m = np.zeros((128, len(tiles)), np.float32)
    p = np.arange(128)
    for ti, (pi, dil, r, t, nblk, je0) in enumerate(tiles):
        e = r + dil * (je0 + 128 * t + p)
        m[:, ti] = valid_ext[e]
    return m


def prep_inputs(x_prompt, x_sample, c_prompt, c_sample, w_in, w_out, g_pre, g_post,
                w_ada, b_ada, lam_q1, lam_k1, lam_q2, lam_k2, g_sub):
    f32 = np.float32
    x_prompt = np.asarray(x_prompt, f32)
    x_sample = np.asarray(x_sample, f32)
    c_prompt = np.asarray(c_prompt, f32)
    c_sample = np.asarray(c_sample, f32)
    w_in0 = np.ascontiguousarray(np.asarray(w_in, f32)[0])
    w_out0 = np.ascontiguousarray(np.asarray(w_out, f32)[0])
    w_ada0 = np.ascontiguousarray(np.asarray(w_ada, f32)[0])
    b_ada0 = np.asarray(b_ada, f32)[0]
    g_pre0 = np.asarray(g_pre, f32)[0]
    g_post0 = np.asarray(g_post, f32)[0]
    g_sub0 = np.asarray(g_sub, f32)[0]

    offs = dict(qa=0, ka=512, qb=2048, kb=2560)
    w_rot = np.zeros((D, 4, 192), f32)
    for ti, nme in enumerate(("qa", "ka", "qb", "kb")):
        x1 = np.array([offs[nme] + h * 64 + i for h in range(8) for i in range(8)])
        x2 = x1 + 8
        w_rot[:, ti, 0:64] = w_in0[:, x1]
        w_rot[:, ti, 64:128] = w_in0[:, x2]
        w_rot[:, ti, 128:192] = w_in0[:, x1]
    w_rot = np.ascontiguousarray(w_rot.reshape(D, 768))

    g_pre_t = np.ascontiguousarray(g_pre0.reshape(8, 128).T)
    b_ada_t = np.ascontiguousarray(b_ada0[:2048].reshape(16, 128).T)
    b_gate = np.ascontiguousarray(b_ada0[2048:3072].reshape(1, D))
    g_post_r = np.ascontiguousarray(g_post0.reshape(1, D))
    g_sub_t = np.ascontiguousarray(g_sub0.reshape(128, 1))
    lam_v = np.ascontiguousarray(np.stack([np.asarray(v, f32)[0] for v in (lam_q1, lam_k1, lam_q2, lam_k2)], 0))
    ident = np.eye(128, dtype=f32)
    p = np.arange(128)[:, None]
    f = np.arange(128)[None, :]
    band = np.concatenate([(p <= f), (p >= f)], axis=1).astype(f32)
    band = np.ascontiguousarray(np.concatenate([band, band], axis=1))
    sels = np.zeros((65, 4, 128), f32)
    sels[64, 0, 0:64] = 2.0
    sels[64, 1, 64:128] = 2.0
    sels[np.arange(64), 2, np.arange(64)] = 1.0
    sels[np.arange(64), 3, 64 + np.arange(64)] = 1.0
    sels = np.ascontiguousarray(sels.reshape(65, 512))
    valid1 = np.zeros(DSEQ + 2 * PADK, f32)
    valid1[PADK:PADK + DSEQ] = 1.0
    am1 = _amask(valid1, DSEQ)

    in_maps = []
    for c in range(8):
        psq, hf = c // 2, c % 2
        q0 = hf * HALF
        o0 = (1 - hf) * HALF
        xa = np.zeros((NROWS, D), f32)
        pos = np.zeros(NROWS, np.int64)
        xa[ROW_OWN:ROW_OWN + HALF] = x_prompt[psq, q0:q0 + HALF]
        pos[ROW_OWN:ROW_OWN + HALF] = np.arange(q0, q0 + HALF)
        xa[ROW_OTHER:ROW_OTHER + HALF] = x_prompt[psq, o0:o0 + HALF]
        pos[ROW_OTHER:ROW_OTHER + HALF] = np.arange(o0, o0 + HALF)
        hpos = np.concatenate([np.arange(q0 - PADK, q0), np.arange(q0 + HALF, q0 + HALF + PADK)])
        hval = (hpos >= 0) & (hpos < SEQ)
        xa[ROW_HALO:ROW_HALO + 2 * PADK][hval] = x_prompt[psq, hpos[hval]]
        pos[ROW_HALO:ROW_HALO + 2 * PADK] = np.clip(hpos, 0, SEQ - 1)
        xa[ROW_S0:ROW_S0 + DSEQ] = x_sample[2 * c]
        pos[ROW_S0:ROW_S0 + DSEQ] = np.arange(DSEQ)
        xa[ROW_S1:ROW_S1 + DSEQ] = x_sample[2 * c + 1]
        pos[ROW_S1:ROW_S1 + DSEQ] = np.arange(DSEQ)
        C, Sg = _rope_tables(pos)
        cs = np.stack([c_prompt[psq], c_sample[2 * c], c_sample[2 * c + 1]], axis=-1)
        c_t = np.ascontiguousarray(cs.reshape(8, 128, 3).transpose(1, 0, 2).reshape(128, 24))
        valid0 = np.ones(HALF + 2 * PADK, f32)
        valid0[0:PADK] = hval[:PADK]
        valid0[PADK + HALF:] = hval[PADK:]
        am0 = _amask(valid0, HALF)
        vrow = np.ascontiguousarray(np.broadcast_to(hval.astype(f32)[None, :], (128, 2 * PADK)))
        in_maps.append({
            "x_all": xa, "rope_c": C, "rope_s": Sg, "c_t": c_t, "w_in": w_in0, "w_rot": w_rot,
            "w_out": w_out0, "w_ada": w_ada0, "g_pre_t": g_pre_t, "b_ada_t": b_ada_t, "b_gate": b_gate,
            "g_post": g_post_r, "g_sub_t": g_sub_t, "lam_v": lam_v, "ident": ident, "band": band,
            "sels": sels, "amask0": am0, "amask1": am1, "vrow": vrow,
        })

    return in_maps


def kernel(**inputs):
    f32 = np.float32
    in_maps = prep_inputs(**inputs)
    if "nc" not in _NC_CACHE:
        _NC_CACHE["nc"] = build_program()
    nc = _NC_CACHE["nc"]
    res = run_bass_kernel_spmd(nc, in_maps, core_ids=list(range(8)))
    y_prompt = np.zeros((4, SEQ, D), f32)
    y_sample = np.zeros((16, DSEQ, D), f32)
    for c in range(8):
        y = np.asarray(res.results[c]["y_all"], f32)
        psq, hf = c // 2, c % 2
        y_prompt[psq, hf * HALF:(hf + 1) * HALF] = y[0:HALF]
        y_sample[2 * c] = y[HALF:HALF + DSEQ]
        y_sample[2 * c + 1] = y[HALF + DSEQ:HALF + 2 * DSEQ]
    return (y_prompt, y_sample)
```

```python
import contextlib
import numpy as np
import concourse.bass as bass
import concourse.mybir as mybir
from concourse.bass_utils import run_bass_kernel_spmd

F32 = mybir.dt.float32
BF16 = mybir.dt.bfloat16
AF = mybir.ActivationFunctionType
ALU = mybir.AluOpType

PE, ACT, DVE, POOL, SP = "tensor", "scalar", "vector", "gpsimd", "sync"
ENGINES = (PE, ACT, DVE, POOL, SP)

D = 1024
SEQ = 8192
DSEQ = 2048
HALF = 4096
PADK = 1024
PATTERNS = ((128, 1), (512, 4), (2048, 16))
EPS = 1e-6
LAM_INIT = 0.2
NROWS = HALF + HALF + 2 * PADK + 2 * DSEQ
ROW_OWN, ROW_OTHER, ROW_HALO, ROW_S0, ROW_S1 = 0, 4096, 8192, 10240, 12288
NOUT = HALF + 2 * DSEQ
WCOLS = 4096 + 4 * 192


class Buf:
    __slots__ = ("last_write", "reads")

    def __init__(self):
        self.last_write = None
        self.reads = []


class Rec:
    __slots__ = ("eng", "fn", "deps", "is_dma", "signal", "count", "dsem", "dval", "epoch")

    def __init__(self, eng, fn, is_dma):
        self.eng, self.fn, self.is_dma = eng, fn, is_dma
        self.deps = []
        self.signal = False
        self.count = 0
        self.dsem = None
        self.dval = 0
        self.epoch = 0


class Sched:
    EPOCH_MAX = 30000

    def __init__(self, nc, n_dma_sems=20):
        self.nc = nc
        self.q = {e: [] for e in ENGINES}
        self.n_dma_sems = n_dma_sems
        self.dma_rr = {e: 0 for e in ENGINES}
        self.dma_last = {}
        self.last_compute = {}

    def op(self, eng, meth, kw, reads=(), writes=(), dma=False, extra=()):
        r = Rec(eng, (meth, kw), dma)
        deps = list(extra)
        for b in reads:
            if b.last_write is not None:
                deps.append(b.last_write)
        for b in writes:
            if b.last_write is not None:
                deps.append(b.last_write)
            deps.extend(b.reads)
        if dma:
            slot = self.dma_rr[eng] % self.n_dma_sems
            self.dma_rr[eng] += 1
            prev = self.dma_last.get((eng, slot))
            if prev is not None:
                deps.append(prev)
            self.dma_last[(eng, slot)] = r
            r.dsem = (eng, slot)
        seen = set()
        for d in deps:
            if d is r or id(d) in seen:
                continue
            if (not d.is_dma) and (not dma) and d.eng == PE and eng == PE:
                continue
            seen.add(id(d))
            r.deps.append(d)
        for b in reads:
            b.reads.append(r)
        for b in writes:
            b.last_write = r
            b.reads = []
        self.q[eng].append(r)
        if not dma:
            self.last_compute[eng] = r
        return r

    def dma(self, meth, kw, reads=(), writes=(), eng=SP):
        return self.op(eng, meth, kw, reads, writes, dma=True)

    def barrier(self):
        deps = list(self.last_compute.values()) + list(self.dma_last.values())
        for e in ENGINES:
            r = Rec(e, None, False)
            for d in deps:
                if (not d.is_dma) and d.eng == e:
                    continue
                r.deps.append(d)
            self.q[e].append(r)

    def run(self, final_wait=()):
        nc = self.nc
        for e in ENGINES:
            for r in self.q[e]:
                for d in r.deps:
                    if not d.is_dma:
                        d.signal = True
        n_epochs = {}
        for e in ENGINES:
            cnt, ep = 0, 0
            for r in self.q[e]:
                if r.is_dma or r.fn is None:
                    continue
                if r.signal:
                    if cnt >= self.EPOCH_MAX:
                        ep += 1
                        cnt = 0
                    cnt += 1
                    r.count = cnt
                    r.epoch = ep
            n_epochs[e] = ep + 1
        dcount = {}
        for e in ENGINES:
            for r in self.q[e]:
                if r.is_dma:
                    dcount[r.dsem] = dcount.get(r.dsem, 0) + 16
                    r.dval = dcount[r.dsem]
        with contextlib.ExitStack() as st:
            csem = {}
            for e in ENGINES:
                for ep in range(n_epochs[e]):
                    if any((not r.is_dma) and r.signal and r.epoch == ep for r in self.q[e]):
                        csem[(e, ep)] = st.enter_context(nc.semaphore(f"c_{e}_{ep}"))
            dsem = {}
            for key in dcount:
                dsem[key] = st.enter_context(nc.semaphore(f"d_{key[0]}_{key[1]}"))
            block = st.enter_context(nc.Block())

            def emit(e, engine):
                waited = {}

                def do_wait(d):
                    if d.is_dma:
                        s, v, k = dsem[d.dsem], d.dval, ("d",) + d.dsem
                    else:
                        s, v, k = csem[(d.eng, d.epoch)], d.count, ("c", d.eng, d.epoch)
                    if waited.get(k, 0) >= v:
                        return
                    waited[k] = v
                    engine.wait_ge(s, v)

                for r in self.q[e]:
                    for d in r.deps:
                        do_wait(d)
                    if r.fn is None:
                        continue
                    ins = getattr(engine, r.fn[0])(**r.fn[1])
                    if r.is_dma:
                        ins.then_inc(dsem[r.dsem], 16)
                    elif r.signal:
                        ins.then_inc(csem[(e, r.epoch)], 1)
                if e == SP:
                    for d in final_wait:
                        do_wait(d)

            @block.tensor
            def _(eng):
                emit(PE, eng)

            @block.scalar
            def _(eng):
                emit(ACT, eng)

            @block.vector
            def _(eng):
                emit(DVE, eng)

            @block.gpsimd
            def _(eng):
                emit(POOL, eng)

            @block.sync
            def _(eng):
                emit(SP, eng)


class Ring:
    def __init__(self, tiles):
        self.tiles = tiles
        self.bufs = [Buf() for _ in tiles]
        self.i = 0

    def next(self):
        k = self.i % len(self.tiles)
        self.i += 1
        return self.tiles[k], self.bufs[k]


def amask_tiles(sq):
    out = []
    for pi, (w, dil) in enumerate(PATTERNS):
        nblk = sq // dil // 128
        je0 = PADK // dil - 64
        for r in range(dil):
            for t in range(nblk + 1):
                out.append((pi, dil, r, t, nblk, je0))
    return out


def build_program(stop_after=None, jobs=(0, 1, 2)):
    nc = bass.Bass("TRN2", target_bir_lowering=False)
    lvl = {'p0': 0, 'w': 1, 'p1': 2, '2b': 3, '2a': 4, None: 5}[stop_after]

    def din(name, shape, dt=F32):
        return nc.dram_tensor(name, list(shape), dt, kind="ExternalInput").ap()

    X = din("x_all", [NROWS, D])
    CT = din("rope_c", [128, NROWS])
    ST = din("rope_s", [128, NROWS])
    CTT = din("c_t", [128, 24])
    W_IN = din("w_in", [D, 4096])
    W_ROT = din("w_rot", [D, 768])
    W_OUT = din("w_out", [D, D])
    W_ADA = din("w_ada", [D, 3 * D])
    GPRE = din("g_pre_t", [128, 8])
    BADA = din("b_ada_t", [128, 16])
    BGATE = din("b_gate", [1, D])
    GPOST = din("g_post", [1, D])
    GSUB = din("g_sub_t", [128, 1])
    LAMV = din("lam_v", [4, 64])
    IDENT = din("ident", [128, 128])
    BAND = din("band", [128, 512])
    SELS = din("sels", [65, 512])
    AM0 = din("amask0", [128, 117])
    VROW = din("vrow", [128, 2 * PADK])
    AM1 = din("amask1", [128, 69])
    Y = nc.dram_tensor("y_all", [NOUT, D], F32, kind="ExternalOutput").ap()

    JOBS = [
        dict(s=0, sq=HALF, skv=SEQ, sext=HALF + 2 * PADK, xrow=ROW_OWN, yrow=0, am=AM0),
        dict(s=1, sq=DSEQ, skv=DSEQ, sext=DSEQ + 2 * PADK, xrow=ROW_S0, yrow=HALF, am=AM1),
        dict(s=2, sq=DSEQ, skv=DSEQ, sext=DSEQ + 2 * PADK, xrow=ROW_S1, yrow=HALF + DSEQ, am=AM1),
    ]
    def scr(name, rows, cols):
        return nc.dram_tensor(name, [rows, cols], BF16, kind="Internal").ap()

    SC = dict(
        QA=scr("s_qa", 512, HALF), QAr=scr("s_qar", 128, HALF), GA=scr("s_ga", 512, HALF),
        KA=scr("s_ka", 512, HALF + 2 * PADK), KAr=scr("s_kar", 128, HALF + 2 * PADK),
        VA=scr("s_va", 512, HALF + 2 * PADK),
        QB=scr("s_qb", 512, HALF), QBr=scr("s_qbr", 128, HALF), GB=scr("s_gb", 512, HALF),
        KB=scr("s_kb", 512, SEQ), KBr=scr("s_kbr", 128, SEQ), VB=scr("s_vb", 512, SEQ),
        YM=scr("s_ym", 1024, HALF),
    )
    SCB = {k: Buf() for k in SC}
    WS = scr("s_w", D, WCOLS)
    bWS = Buf()

    S = Sched(nc)
    out_dmas = []
    top = contextlib.ExitStack()
    with top:
        def sb(st, name, shape, dt):
            return st.enter_context(nc.sbuf_tensor("sb_" + name, list(shape), dt))

        def ps(st, name, shape, dt):
            return st.enter_context(nc.psum_tensor("ps_" + name, list(shape), dt))

        uid = [0]

        def nm(p):
            uid[0] += 1
            return f"{p}{uid[0]}"

        ident_f = sb(top, "ident_f", [128, 128], F32)
        ident = sb(top, "ident", [128, 128], BF16)
        band_f = sb(top, "band_f", [128, 512], F32)
        bandneg = sb(top, "bandneg", [128, 256], BF16)
        sels = sb(top, "sels", [65, 4, 128], F32)
        am0 = sb(top, "am0", [128, 117], F32)
        am1 = sb(top, "am1", [128, 69], F32)
        ones_f = sb(top, "ones_f", [128, 128], F32)
        mhalf = sb(top, "mhalf", [128, 1], F32)
        zer = sb(top, "zer", [1, 512], BF16)
        gsub2 = sb(top, "gsub2", [128, 1], F32)
        neglam = sb(top, "neglam", [128, 1], F32)
        lamt = sb(top, "lamt", [128, 4, 64], F32)
        lamj = sb(top, "lamj", [128, 64], F32)
        lams = sb(top, "lams", [128, 4], F32)
        gg = [sb(top, f"gg{s}", [128, D], F32) for s in range(3)]
        gsT = sb(top, "gsT", [128, 8, 4], F32)
        shT = sb(top, "shT", [128, 8, 4], F32)
        wout = sb(top, "wout", [128, 8, D], BF16)
        bconst = Buf()
        bgg = [Buf() for _ in range(3)]
        bmod = Buf()
        bwout = Buf()

        biasA = sb(top, "biasA", [128, 40, 4], F32)
        bbiasA = Buf()
        S.dma("dma_start", dict(out=ident_f[:], in_=IDENT[:, :]), writes=[bconst])
        S.dma("dma_start", dict(out=band_f[:], in_=BAND[:, :]), writes=[bconst])
        S.dma("dma_start", dict(out=sels[:].rearrange("p a b -> p (a b)"), in_=SELS[:, :]), writes=[bconst])
        S.dma("dma_start", dict(out=am0[:], in_=AM0[:, :]), writes=[bconst])
        S.dma("dma_start", dict(out=am1[:], in_=AM1[:, :]), writes=[bconst])
        S.dma("dma_start", dict(out=gsub2[:], in_=GSUB[:, :]), writes=[bconst])
        for i in range(4):
            S.dma("dma_start", dict(out=lamt[:, i, :], in_=LAMV[i:i + 1, :].broadcast_to([128, 64])), writes=[bconst])
        S.op(DVE, "tensor_copy", dict(out=ident[:], in_=ident_f[:]), reads=[bconst], writes=[bconst])
        S.op(DVE, "tensor_scalar", dict(out=bandneg[:], in0=band_f[:, 0:256], scalar1=30000.0, scalar2=-30000.0, op0=ALU.mult, op1=ALU.add), reads=[bconst], writes=[bconst])
        S.op(POOL, "memset", dict(ap=ones_f[:], constant=1.0), writes=[bconst])
        S.op(POOL, "memset", dict(ap=mhalf[:], constant=-0.5), writes=[bconst])
        S.op(POOL, "memset", dict(ap=zer[:], constant=0.0), writes=[bconst])
        S.op(DVE, "tensor_scalar", dict(out=gsub2[:], in0=gsub2[:], scalar1=(1.0 - LAM_INIT) * 0.5, scalar2=None, op0=ALU.mult), reads=[bconst], writes=[bconst])
        for i in range(2):
            S.op(DVE, "tensor_tensor", dict(out=lamj[:], in0=lamt[:, 2 * i, :], in1=lamt[:, 2 * i + 1, :], op=ALU.mult), reads=[bconst], writes=[bconst])
            S.op(ACT, "activation", dict(out=lamj[:], in_=lamj[:], func=AF.Identity, accum_out=lams[:, i:i + 1]), reads=[bconst], writes=[bconst])
        S.op(ACT, "activation", dict(out=lams[:, 2:4], in_=lams[:, 0:2], func=AF.Exp), reads=[bconst], writes=[bconst])
        S.op(DVE, "tensor_tensor", dict(out=neglam[:], in0=lams[:, 3:4], in1=lams[:, 2:3], op=ALU.subtract), reads=[bconst], writes=[bconst])
        S.op(DVE, "tensor_scalar", dict(out=neglam[:], in0=neglam[:], scalar1=-LAM_INIT, scalar2=None, op0=ALU.add), reads=[bconst], writes=[bconst])

        with contextlib.ExitStack() as p0:
            stg = [sb(p0, f"stg0_{i}", [128, 8, 512], F32) for i in range(2)]
            bstg = [Buf() for _ in range(2)]
            ct = sb(p0, "ct", [128, 24], F32)
            th = sb(p0, "th0", [128, 24], F32)
            scT = sb(p0, "scT", [128, 8, 4], F32)
            screp = sb(p0, "screp", [128, 24, 128], F32)
            gpre = sb(p0, "gpre", [128, 8], F32)
            bada = sb(p0, "bada", [128, 16], F32)
            bgate = sb(p0, "bgate", [128, D], F32)
            gpost = sb(p0, "gpost", [128, D], F32)
            tmpm = sb(p0, "tmpm", [128, 8], F32)
            tmpg = sb(p0, "tmpg", [128, 512], F32)
            pmod = ps(p0, "pmod", [128, 16, 4], F32)
            pg = [ps(p0, f"pg{i}", [128, 512], F32) for i in range(3)]
            b0 = Buf()
            bpm = Buf()
            bpg = [Buf() for _ in range(3)]
            S.dma("dma_start", dict(out=ct[:], in_=CTT[:, :]), writes=[b0])
            S.dma("dma_start", dict(out=gpre[:], in_=GPRE[:, :]), writes=[b0])
            S.dma("dma_start", dict(out=bada[:], in_=BADA[:, :]), writes=[b0])
            S.dma("dma_start", dict(out=bgate[:], in_=BGATE[0:1, :].broadcast_to([128, D])), writes=[b0])
            S.dma("dma_start", dict(out=gpost[:], in_=GPOST[0:1, :].broadcast_to([128, D])), writes=[b0])
            S.op(ACT, "activation", dict(out=th[:], in_=ct[:], func=AF.Tanh, scale=0.5), reads=[b0], writes=[b0])
            S.op(DVE, "scalar_tensor_tensor", dict(out=th[:], in0=th[:], scalar=1.0, in1=ct[:], op0=ALU.add, op1=ALU.mult), reads=[b0], writes=[b0])
            S.op(POOL, "memset", dict(ap=scT[:], constant=0.0), writes=[b0])
            S.op(POOL, "memset", dict(ap=shT[:], constant=0.0), writes=[bmod])
            S.op(POOL, "memset", dict(ap=gsT[:], constant=0.0), writes=[bmod])
            S.op(DVE, "tensor_scalar", dict(out=scT[:, :, 0:3], in0=th[:].rearrange("p (a b) -> p a b", b=3), scalar1=0.5, scalar2=None, op0=ALU.mult), reads=[b0], writes=[b0])
            for i in range(24):
                kc, s = divmod(i, 3)
                S.op(DVE, "tensor_scalar", dict(out=screp[:, i, :], in0=ones_f[:], scalar1=scT[:, kc, s:s + 1], scalar2=None, op0=ALU.mult), reads=[b0, bconst], writes=[b0])
            for pc in range(6):
                k = pc % 2
                S.dma("dma_start", dict(out=stg[k][:], in_=W_ADA[:, pc * 512:(pc + 1) * 512].rearrange("(kc p) n -> p kc n", p=128)), writes=[bstg[k]])
                if pc < 4:
                    for fb in range(4):
                        blk = pc * 4 + fb
                        for kc in range(8):
                            S.op(PE, "matmul", dict(out=pmod[:, blk, :], lhsT=stg[k][:, kc, fb * 128:(fb + 1) * 128], rhs=scT[:, kc, :], start=(kc == 0), stop=(kc == 7)), reads=[bstg[k], b0], writes=[bpm])
                else:
                    hf = pc - 4
                    for s in range(3):
                        for h2 in range(2):
                            for kc in range(8):
                                S.op(PE, "matmul", dict(out=pg[s][:, h2 * 256:(h2 + 1) * 256], lhsT=screp[:, kc * 3 + s, :], rhs=stg[k][:, kc, h2 * 256:(h2 + 1) * 256], start=(kc == 0), stop=(kc == 7)), reads=[bstg[k], b0], writes=[bpg[s]])
                        S.op(DVE, "tensor_tensor", dict(out=tmpg[:], in0=pg[s][:], in1=bgate[:, hf * 512:(hf + 1) * 512], op=ALU.add), reads=[bpg[s], b0], writes=[b0])
                        S.op(DVE, "tensor_tensor", dict(out=gg[s][:, hf * 512:(hf + 1) * 512], in0=tmpg[:], in1=gpost[:, hf * 512:(hf + 1) * 512], op=ALU.mult), reads=[b0], writes=[bgg[s]])
            for s in range(3):
                S.op(DVE, "tensor_tensor", dict(out=shT[:, :, s], in0=pmod[:, 0:8, s], in1=bada[:, 0:8], op=ALU.add), reads=[bpm, b0], writes=[bmod])
                S.op(DVE, "scalar_tensor_tensor", dict(out=tmpm[:], in0=pmod[:, 8:16, s], scalar=1.0, in1=bada[:, 8:16], op0=ALU.add, op1=ALU.add), reads=[bpm, b0], writes=[b0])
                S.op(DVE, "tensor_tensor", dict(out=gsT[:, :, s], in0=tmpm[:], in1=gpre[:], op=ALU.mult), reads=[b0], writes=[bmod])
            for pc in range(2):
                k = pc % 2
                S.dma("dma_start", dict(out=stg[k][:], in_=W_OUT[:, pc * 512:(pc + 1) * 512].rearrange("(kc p) n -> p kc n", p=128)), writes=[bstg[k]])
                S.op(DVE, "tensor_copy", dict(out=wout[:, :, pc * 512:(pc + 1) * 512], in_=stg[k][:]), reads=[bstg[k]], writes=[bwout])
            wbr = Ring([sb(p0, f"wbst{i}", [128, 8, 512], BF16) for i in range(2)])
            pb = ps(p0, "pb0", [128, 40, 4], F32)
            bpb = Buf()
            pieces = [(W_IN, pc * 512, 512, pc * 512) for pc in range(8)] + [(W_ROT, 0, 384, 4096), (W_ROT, 384, 384, 4096 + 384)]
            for pi_, (wsrc, c0, ncol, dcol) in enumerate(pieces):
                k = pi_ % 2
                S.dma("dma_start", dict(out=stg[k][:, :, 0:ncol], in_=wsrc[:, c0:c0 + ncol].rearrange("(kc p) n -> p kc n", p=128)), writes=[bstg[k]])
                wb_t, wb_b = wbr.next()
                S.op(ACT, "activation", dict(out=wb_t[:, 0:4, 0:ncol], in_=stg[k][:, 0:4, 0:ncol], func=AF.Copy), reads=[bstg[k]], writes=[wb_b])
                S.op(DVE, "tensor_copy", dict(out=wb_t[:, 4:8, 0:ncol], in_=stg[k][:, 4:8, 0:ncol]), reads=[bstg[k]], writes=[wb_b])
                S.dma("dma_start", dict(out=WS[:, dcol:dcol + ncol].rearrange("(kc p) n -> p kc n", p=128), in_=wb_t[:, :, 0:ncol]), reads=[wb_b], writes=[bWS])
                if pi_ < 8:
                    cols = [(fb * 128, pi_ * 4 + fb) for fb in range(4)]
                else:
                    t0 = (pi_ - 8) * 2
                    cols = [(0, 32 + 2 * t0), (64, 33 + 2 * t0), (192, 34 + 2 * t0), (256, 35 + 2 * t0)]
                for (co, bcol) in cols:
                    for kc in range(8):
                        S.op(PE, "matmul", dict(out=pb[:, bcol, :], lhsT=stg[k][:, kc, co:co + 128], rhs=shT[:, kc, :], start=(kc == 0), stop=(kc == 7)), reads=[bstg[k], bmod], writes=[bpb])
            S.op(DVE, "tensor_copy", dict(out=biasA[:], in_=pb[:]), reads=[bpb], writes=[bbiasA])
            S.barrier()

        for job in [JOBS[j_] for j_ in jobs] if stop_after != 'p0' else []:
            s, sq, skv, sext = job["s"], job["sq"], job["skv"], job["sext"]
            am = am0 if job["am"] is AM0 else am1
            with contextlib.ExitStack() as p1:
                wp = sb(p1, nm("wp"), [128, 8, WCOLS], BF16)
                bwp = Buf()
                gsrep = sb(p1, nm("gsrep"), [128, 8, 128], F32)
                bgsrep = Buf()
                if lvl >= 1:
                    for c0 in range(0, WCOLS, 1216):
                        c1 = min(c0 + 1216, WCOLS)
                        S.dma("dma_start", dict(out=wp[:, :, c0:c1], in_=WS[:, c0:c1].rearrange("(kc p) n -> p kc n", p=128)), reads=[bWS], writes=[bwp])
                    for kc in range(8):
                        S.op(DVE, "tensor_scalar", dict(out=gsrep[:, kc, :], in0=ones_f[:], scalar1=gsT[:, kc, s:s + 1], scalar2=None, op0=ALU.mult), reads=[bmod, bconst], writes=[bgsrep])
                biasT = biasA[:, :, s]
                bbias = bbiasA

                with contextlib.ExitStack() as pp:
                    xt = Ring([sb(pp, nm("xt"), [128, D], F32) for _ in range(4)])
                    junk = sb(pp, nm("junk"), [128, D], F32)
                    bjunk = Buf()
                    xn = Ring([sb(pp, nm("xn"), [128, D], F32) for _ in range(4)])
                    xnT = Ring([sb(pp, nm("xnT"), [128, 8, 512], BF16) for _ in range(2)])
                    stat = Ring([sb(pp, nm("stat"), [128, 4], F32) for _ in range(8)])
                    ost = Ring([sb(pp, nm("ost"), [128, 4, 512], BF16) for _ in range(3)])
                    cst = Ring([sb(pp, nm("cst"), [128, 2, 512], F32) for _ in range(2)])
                    t12 = Ring([sb(pp, nm("t12"), [128, 2, 512], F32) for _ in range(2)])
                    rot = Ring([sb(pp, nm("rot"), [128, 512], BF16) for _ in range(2)])
                    pT = Ring([ps(pp, nm("pT"), [128, D], F32) for _ in range(2)])
                    pacc = Ring([ps(pp, nm("pacc"), [128, 512], F32) for _ in range(4)])

                    G = dict(qa=(0, 0), ka=(512, 4), va=(1024, 8), ga=(1536, 12), qb=(2048, 16), kb=(2560, 20), vb=(3072, 24), gb=(3584, 28))
                    RT = dict(qa=0, ka=1, qb=2, kb=3)
                    segs = []
                    full_main = [("qa", "QA", 0), ("ka", "KA", PADK), ("va", "VA", PADK), ("ga", "GA", 0), ("qb", "QB", 0), ("kb", "KB", 0), ("vb", "VB", 0), ("gb", "GB", 0)]
                    full_rot = [("qa", "QAr", 0), ("ka", "KAr", PADK), ("qb", "QBr", 0), ("kb", "KBr", 0)]
                    segs.append((job["xrow"], sq, full_main, full_rot))
                    if s == 0:
                        segs.append((ROW_OTHER, HALF, [("kb", "KB", HALF), ("vb", "VB", HALF)], [("kb", "KBr", HALF)]))
                        segs.append((ROW_HALO, PADK, [("ka", "KA", 0), ("va", "VA", 0)], [("ka", "KAr", 0)]))
                        segs.append((ROW_HALO + PADK, PADK, [("ka", "KA", PADK + HALF), ("va", "VA", PADK + HALF)], [("ka", "KAr", PADK + HALF)]))
                    evi = [0]
                    chunks = []
                    for (xr0, T, mains, rots) in (segs if lvl >= 2 else []):
                        for ch in range(T // 512):
                            chunks.append((xr0 + ch * 512, ch, mains, rots))

                    def emit_loads(cidx):
                        r0, ch, mains, rots = chunks[cidx]
                        xs = []
                        for i in range(4):
                            x_t, x_b = xt.next()
                            S.dma("dma_start", dict(out=x_t[:], in_=X[r0 + i * 128:r0 + (i + 1) * 128, :]), writes=[x_b])
                            xs.append((x_t, x_b))
                        c_t, c_b = cst.next()
                        if rots:
                            S.dma("dma_start", dict(out=c_t[:, 0, :], in_=CT[:, r0:r0 + 512]), writes=[c_b])
                            S.dma("dma_start", dict(out=c_t[:, 1, :], in_=ST[:, r0:r0 + 512]), writes=[c_b])
                        return xs, c_t, c_b

                    def emit_prologue(cidx, xs):
                        xT_t, xT_b = xnT.next()
                        for i, (x_t, x_b) in enumerate(xs):
                            st_t, st_b = stat.next()
                            S.op(ACT, "activation", dict(out=junk[:], in_=x_t[:], func=AF.Square, accum_out=st_t[:, 0:1]), reads=[x_b], writes=[bjunk, st_b])
                            S.op(DVE, "tensor_scalar", dict(out=st_t[:, 1:2], in0=st_t[:, 0:1], scalar1=1.0 / D, scalar2=EPS, op0=ALU.mult, op1=ALU.add), reads=[st_b], writes=[st_b])
                            S.op(POOL, "tensor_tensor", dict(out=st_t[:, 2:3], in0=st_t[:, 1:2], in1=mhalf[:], op=ALU.pow), reads=[st_b, bconst], writes=[st_b])
                            xn_t, xn_b = xn.next()
                            S.op(DVE, "tensor_scalar", dict(out=xn_t[:], in0=x_t[:], scalar1=st_t[:, 2:3], scalar2=None, op0=ALU.mult), reads=[x_b, st_b], writes=[xn_b])
                            pT_t, pT_b = pT.next()
                            for kc in range(8):
                                S.op(PE, "transpose", dict(out=pT_t[:, kc * 128:(kc + 1) * 128], in_=xn_t[:, kc * 128:(kc + 1) * 128], identity=ident_f[:]), reads=[xn_b, bconst], writes=[pT_b])
                            S.op(DVE, "tensor_tensor", dict(out=xT_t[:, :, i * 128:(i + 1) * 128], in0=pT_t[:].rearrange("p (a b) -> p a b", b=128), in1=gsrep[:], op=ALU.mult), reads=[pT_b, bgsrep], writes=[xT_b])
                        return xT_t, xT_b

                    def emit_main(cidx, xT_t, xT_b, c_t, c_b):
                        r0, ch, mains, rots = chunks[cidx]
                        for (gname, skey, dcol0) in mains:
                            wcol, bcol = G[gname]
                            o_t, o_b = ost.next()
                            for b in range(4):
                                pa_t, pa_b = pacc.next()
                                c0 = wcol + b * 128
                                bc = bcol + b
                                for kc in range(8):
                                    S.op(PE, "matmul", dict(out=pa_t[:], lhsT=wp[:, kc, c0:c0 + 128], rhs=xT_t[:, kc, :], start=(kc == 0), stop=(kc == 7)), reads=[bwp, xT_b], writes=[pa_b])
                                evi[0] += 1
                                if evi[0] % 4 != 0:
                                    S.op(ACT, "activation", dict(out=o_t[:, b, :], in_=pa_t[:], func=AF.Identity, bias=biasT[:, bc:bc + 1]), reads=[pa_b, bbias], writes=[o_b])
                                else:
                                    S.op(DVE, "tensor_scalar", dict(out=o_t[:, b, :], in0=pa_t[:], scalar1=biasT[:, bc:bc + 1], scalar2=None, op0=ALU.add), reads=[pa_b, bbias], writes=[o_b])
                            dc = dcol0 + ch * 512
                            S.dma("dma_start", dict(out=SC[skey][:, dc:dc + 512].rearrange("(b p) t -> p b t", p=128), in_=o_t[:]), reads=[o_b], writes=[SCB[skey]])
                        for (gname, skey, dcol0) in rots:
                            ti = RT[gname]
                            wc = 4096 + ti * 192
                            bc = 32 + 2 * ti
                            p1_t, p1_b = pacc.next()
                            p2_t, p2_b = pacc.next()
                            for kc in range(8):
                                S.op(PE, "matmul", dict(out=p1_t[:], lhsT=wp[:, kc, wc:wc + 128], rhs=xT_t[:, kc, :], start=(kc == 0), stop=(kc == 7)), reads=[bwp, xT_b], writes=[p1_b])
                            for kc in range(8):
                                S.op(PE, "matmul", dict(out=p2_t[:], lhsT=wp[:, kc, wc + 64:wc + 192], rhs=xT_t[:, kc, :], start=(kc == 0), stop=(kc == 7)), reads=[bwp, xT_b], writes=[p2_b])
                            t_t, t_b = t12.next()
                            S.op(DVE, "scalar_tensor_tensor", dict(out=t_t[:, 0, :], in0=p1_t[:], scalar=biasT[:, bc:bc + 1], in1=c_t[:, 0, :], op0=ALU.add, op1=ALU.mult), reads=[p1_b, c_b, bbias], writes=[t_b])
                            S.op(DVE, "scalar_tensor_tensor", dict(out=t_t[:, 1, :], in0=p2_t[:], scalar=biasT[:, bc + 1:bc + 2], in1=c_t[:, 1, :], op0=ALU.add, op1=ALU.mult), reads=[p2_b, c_b, bbias], writes=[t_b])
                            r_t, r_b = rot.next()
                            S.op(DVE, "tensor_tensor", dict(out=r_t[:], in0=t_t[:, 0, :], in1=t_t[:, 1, :], op=ALU.add), reads=[t_b], writes=[r_b])
                            dc = dcol0 + ch * 512
                            S.dma("dma_start", dict(out=SC[skey][:, dc:dc + 512], in_=r_t[:]), reads=[r_b], writes=[SCB[skey]])

                    loaded = {}
                    if chunks:
                        loaded[0] = emit_loads(0)
                    for cidx in range(len(chunks)):
                        xs, c_t, c_b = loaded.pop(cidx)
                        xT_t, xT_b = emit_prologue(cidx, xs)
                        if cidx + 1 < len(chunks):
                            loaded[cidx + 1] = emit_loads(cidx + 1)
                        emit_main(cidx, xT_t, xT_b, c_t, c_b)
                    S.barrier()

            with contextlib.ExitStack() as pB:
                nkt = skv // 128
                nqc = sq // 512
                kbT = [sb(pB, nm("kbT"), [128, skv], BF16) for _ in range(2)]
                vbT = [sb(pB, nm("vbT"), [128, skv], BF16) for _ in range(2)]
                qbT = [sb(pB, nm("qbT"), [128, sq], BF16) for _ in range(2)]
                gbT = [sb(pB, nm("gbT"), [128, sq], BF16) for _ in range(2)]
                bk, bv, bq, bg = ([Buf(), Buf()] for _ in range(4))
                vtok = sb(pB, nm("vtok"), [128, nkt, 128], BF16)
                bvt = Buf()
                Pr = Ring([sb(pB, nm("P"), [128, 1024], BF16) for _ in range(6)])
                asum = [[sb(pB, nm("asum"), [128, 1024], F32) for _ in range(2)] for _ in range(2)]
                basum = [[Buf() for _ in range(2)] for _ in range(2)]
                oS = [[sb(pB, nm("oS"), [128, 512], F32) for _ in range(2)] for _ in range(2)]
                boS = [[Buf() for _ in range(2)] for _ in range(2)]
                thr = Ring([sb(pB, nm("thb"), [128, 512], F32) for _ in range(2)])
                sgr = Ring([sb(pB, nm("sgb"), [128, 512], BF16) for _ in range(3)])
                yTr = Ring([sb(pB, nm("yTb"), [128, 512], BF16) for _ in range(3)])
                rsr = Ring([sb(pB, nm("rs"), [128, 16], F32) for _ in range(3)])
                rr = Ring([sb(pB, nm("rr"), [128, 8], F32) for _ in range(8)])
                t0r = Ring([sb(pB, nm("t0"), [128, 128], F32) for _ in range(3)])
                orr = Ring([sb(pB, nm("o"), [128, 128], F32) for _ in range(8)])
                onr = Ring([sb(pB, nm("on"), [128, 128], BF16) for _ in range(8)])
                junkb = sb(pB, nm("junkb"), [128, 128], F32)
                bjb = Buf()
                scr_ = Ring([ps(pB, nm("sc"), [128, 1024], F32) for _ in range(2)])
                oT = [ps(pB, nm("oT"), [128, 512], F32) for _ in range(2)]
                boT = [Buf() for _ in range(2)]
                tpb = ps(pB, nm("tpb"), [128, 512], F32)
                btp = Buf()
                pTv = ps(pB, nm("pTv"), [128, 4, 128], BF16)
                bpv = Buf()

                def load_head(g):
                    hb = g % 2
                    for (dst, dbuf, main, rotk, ncols) in ((kbT[hb], bk[hb], "KB", "KBr", skv), (qbT[hb], bq[hb], "QB", "QBr", sq)):
                        for m in range(2):
                            h = 2 * g + m
                            S.dma("dma_start", dict(out=dst[m * 64:m * 64 + 48, 0:ncols], in_=SC[main][g * 128 + m * 64 + 16:g * 128 + m * 64 + 64, 0:ncols]), reads=[SCB[main]], writes=[dbuf])
                            S.dma("dma_start", dict(out=dst[m * 64 + 48:m * 64 + 56, 0:ncols], in_=SC[rotk][h * 8:h * 8 + 8, 0:ncols]), reads=[SCB[rotk]], writes=[dbuf])
                            S.dma("dma_start", dict(out=dst[m * 64 + 56:m * 64 + 64, 0:ncols], in_=SC[rotk][64 + h * 8:64 + h * 8 + 8, 0:ncols]), reads=[SCB[rotk]], writes=[dbuf])
                    S.dma("dma_start", dict(out=vbT[hb][:], in_=SC["VB"][g * 128:(g + 1) * 128, 0:skv]), reads=[SCB["VB"]], writes=[bv[hb]])
                    S.dma("dma_start", dict(out=gbT[hb][:], in_=SC["GB"][g * 128:(g + 1) * 128, 0:sq]), reads=[SCB["GB"]], writes=[bg[hb]])

                def make_epilogue(g, q0, par, sg_t, sg_b):
                    steps = []
                    rs_t, rs_b = rsr.next()
                    y_t, y_b = yTr.next()
                    state = {}

                    def rowsums():
                        for m in range(2):
                            for qb in range(4):
                                c = 256 + 2 * (m * 4 + qb)
                                for j in range(2):
                                    S.op(PE, "matmul", dict(out=tpb[:, c:c + 2], lhsT=asum[par][j][:, m * 512 + qb * 128:m * 512 + (qb + 1) * 128], rhs=ones_f[:, 0:2], start=(j == 0), stop=(j == 1)), reads=[basum[par][j], bconst], writes=[btp])
                        S.op(DVE, "reciprocal", dict(out=rs_t[:, 0:8], in_=tpb[:, 256:272:2]), reads=[btp], writes=[rs_b])
                        S.op(DVE, "tensor_scalar", dict(out=rs_t[:, 8:12], in0=rs_t[:, 4:8], scalar1=neglam[:, 0:1], scalar2=None, op0=ALU.mult), reads=[rs_b, bconst], writes=[rs_b])
                    steps.append(rowsums)

                    def st1(qb):
                        def f():
                            for m in range(2):
                                S.op(PE, "transpose", dict(out=tpb[:, m * 128:(m + 1) * 128], in_=oS[par][m][:, qb * 128:(qb + 1) * 128], identity=ident_f[:]), reads=[boS[par][m], bconst], writes=[btp])
                            r_t, r_b = rr.next()
                            t0_t, t0_b = t0r.next()
                            S.op(DVE, "tensor_scalar", dict(out=t0_t[:], in0=tpb[:, 0:128], scalar1=rs_t[:, qb:qb + 1], scalar2=None, op0=ALU.mult), reads=[btp, rs_b], writes=[t0_b])
                            o_t, o_b = orr.next()
                            S.op(DVE, "scalar_tensor_tensor", dict(out=o_t[:], in0=tpb[:, 128:256], scalar=rs_t[:, 8 + qb:9 + qb], in1=t0_t[:], op0=ALU.mult, op1=ALU.add), reads=[btp, rs_b, t0_b], writes=[o_b])
                            state[qb] = dict(r=(r_t, r_b), o=(o_t, o_b))
                        return f

                    def st2(qb):
                        def f():
                            (r_t, r_b), (o_t, o_b) = state[qb]["r"], state[qb]["o"]
                            S.op(ACT, "activation", dict(out=junkb[:], in_=o_t[:], func=AF.Square, accum_out=r_t[:, 3:4]), reads=[o_b], writes=[bjb, r_b])
                        return f

                    def st3(qb):
                        def f():
                            r_t, r_b = state[qb]["r"]
                            S.op(DVE, "tensor_scalar", dict(out=r_t[:, 4:5], in0=r_t[:, 3:4], scalar1=1.0 / 128, scalar2=EPS, op0=ALU.mult, op1=ALU.add), reads=[r_b], writes=[r_b])
                            S.op(POOL, "tensor_tensor", dict(out=r_t[:, 5:6], in0=r_t[:, 4:5], in1=mhalf[:], op=ALU.pow), reads=[r_b, bconst], writes=[r_b])
                        return f

                    def st4(qb):
                        def f():
                            (r_t, r_b), (o_t, o_b) = state[qb]["r"], state[qb]["o"]
                            on_t, on_b = onr.next()
                            S.op(ACT, "activation", dict(out=on_t[:], in_=o_t[:], func=AF.Identity, scale=r_t[:, 5:6]), reads=[o_b, r_b], writes=[on_b])
                            state[qb]["on"] = (on_t, on_b)
                        return f

                    def st5(qb, last):
                        def f():
                            on_t, on_b = state[qb]["on"]
                            ev = pTv[:, 0, :]
                            S.op(PE, "transpose", dict(out=ev, in_=on_t[:], identity=ident[:]), reads=[on_b, bconst], writes=[bpv])
                            S.op(DVE, "scalar_tensor_tensor", dict(out=y_t[:, qb * 128:(qb + 1) * 128], in0=ev, scalar=gsub2[:, 0:1], in1=sg_t[:, qb * 128:(qb + 1) * 128], op0=ALU.mult, op1=ALU.mult), reads=[bpv, sg_b, bconst], writes=[y_b])
                            if last:
                                S.dma("dma_start", dict(out=SC["YM"][512 + g * 128:512 + (g + 1) * 128, q0:q0 + 512], in_=y_t[:]), reads=[y_b], writes=[SCB["YM"]])
                        return f

                    for stage in (st1, st2, st3, st4):
                        for qb in range(4):
                            steps.append(stage(qb))
                    for qb in range(4):
                        steps.append(st5(qb, qb == 3))
                    return steps

                pending = []
                steps_per_iter = -(-21 // max(nkt - 3, 1))
                heads = list(range(4)) if lvl >= 3 else []
                if heads:
                    load_head(0)
                for g in heads:
                    hb = g % 2
                    if g + 1 < 4:
                        load_head(g + 1)
                    for k4 in range(nkt // 4):
                        for j in range(4):
                            kt = k4 * 4 + j
                            S.op(PE, "transpose", dict(out=pTv[:, j, :], in_=vbT[hb][:, kt * 128:(kt + 1) * 128], identity=ident[:]), reads=[bv[hb], bconst], writes=[bpv])
                        S.op(ACT, "activation", dict(out=vtok[:, k4 * 4:k4 * 4 + 4, :], in_=pTv[:], func=AF.Copy), reads=[bpv], writes=[bvt])
                    for qc in range(nqc):
                        q0 = qc * 512
                        par = qc % 2
                        th_t, th_b = thr.next()
                        sg_t, sg_b = sgr.next()
                        S.op(ACT, "activation", dict(out=th_t[:], in_=gbT[hb][:, q0:q0 + 512], func=AF.Tanh, scale=0.5), reads=[bg[hb]], writes=[th_b])
                        S.op(DVE, "scalar_tensor_tensor", dict(out=sg_t[:], in0=th_t[:], scalar=1.0, in1=gbT[hb][:, q0:q0 + 512], op0=ALU.add, op1=ALU.mult), reads=[th_b, bg[hb]], writes=[sg_b])

                        def scores(kt):
                            sc_t, sc_b = scr_.next()
                            for m in range(2):
                                S.op(PE, "matmul", dict(out=sc_t[:, m * 512:(m + 1) * 512], lhsT=kbT[hb][m * 64:(m + 1) * 64, kt * 128:(kt + 1) * 128], rhs=qbT[hb][m * 64:(m + 1) * 64, q0:q0 + 512], start=True, stop=True), reads=[bk[hb], bq[hb]], writes=[sc_b])
                            p_t, p_b = Pr.next()
                            S.op(ACT, "activation", dict(out=p_t[:], in_=sc_t[:], func=AF.Exp, scale=0.125), reads=[sc_b], writes=[p_b])
                            return p_t, p_b

                        def av(kt, p_t, p_b):
                            for m in range(2):
                                S.op(PE, "matmul", dict(out=oT[m][:], lhsT=vtok[:, kt, :], rhs=p_t[:, m * 512:(m + 1) * 512], start=(kt == 0), stop=(kt == nkt - 1)), reads=[p_b, bvt], writes=[boT[m]])
                            j = kt % 2
                            if kt < 2:
                                S.op(DVE, "tensor_copy", dict(out=asum[par][j][:], in_=p_t[:]), reads=[p_b], writes=[basum[par][j]])
                            else:
                                S.op(DVE, "tensor_tensor", dict(out=asum[par][j][:], in0=asum[par][j][:], in1=p_t[:], op=ALU.add), reads=[p_b, basum[par][j]], writes=[basum[par][j]])

                        ptiles = {0: scores(0)}
                        for kt in range(nkt):
                            if kt + 1 < nkt:
                                ptiles[kt + 1] = scores(kt + 1)
                            if kt >= 1:
                                av(kt - 1, *ptiles.pop(kt - 1))
                            if kt >= 1:
                                for _ in range(steps_per_iter):
                                    if pending:
                                        pending.pop(0)()
                        av(nkt - 1, *ptiles.pop(nkt - 1))
                        for m in range(2):
                            S.op(ACT, "activation", dict(out=oS[par][m][:], in_=oT[m][:], func=AF.Copy), reads=[boT[m]], writes=[boS[par][m]])
                        while pending:
                            pending.pop(0)()
                        pending.extend(make_epilogue(g, q0, par, sg_t, sg_b))
                while pending:
                    pending.pop(0)()
                S.barrier()

            with contextlib.ExitStack() as pA:
                tiles = amask_tiles(sq)
                nt = len(tiles)
                kaT = [sb(pA, nm("kaT"), [128, sext], BF16) for _ in range(2)]
                vaT1 = sb(pA, nm("vaT"), [128, sext], BF16)
                vaT = [vaT1, vaT1]
                bv1 = Buf()
                qz = [[sb(pA, nm("qz"), [128, sq], BF16) for _ in range(2)] for _ in range(2)]
                gaT1 = sb(pA, nm("gaT"), [128, sq], BF16)
                gaT = [gaT1, gaT1]
                bg1 = Buf()
                bk, bv, bq, bg = ([Buf(), Buf()] for _ in range(4))
                bv = [bv1, bv1]
                bg = [bg1, bg1]
                for hb_ in range(2):
                    S.op(POOL, "memset", dict(ap=qz[hb_][0][64:128, :], constant=0.0), writes=[bq[hb_]])
                    S.op(POOL, "memset", dict(ap=qz[hb_][1][0:64, :], constant=0.0), writes=[bq[hb_]])
                acc = sb(pA, nm("accA"), [65, 2, sq], F32)
                bacc = Buf()
                vt_all = sb(pA, nm("vtall"), [128, nt, 2, 65], BF16)
                bvta = Buf()
                vrow = sb(pA, nm("vrow"), [128, 2 * PADK], F32)
                bvrow = Buf()
                Pt = Ring([sb(pA, nm("Pt"), [128, 2, 512], BF16) for _ in range(3)])
                recr = Ring([sb(pA, nm("rec"), [128, 512], F32) for _ in range(2)])
                thr = Ring([sb(pA, nm("tha"), [128, 512], F32) for _ in range(1)])
                sgr = Ring([sb(pA, nm("sga"), [128, 512], F32) for _ in range(1)])
                tnr = Ring([sb(pA, nm("tn"), [128, 512], F32) for _ in range(1)])
                yTr = Ring([sb(pA, nm("yTa"), [128, 512], BF16) for _ in range(2)])
                scA = Ring([ps(pA, nm("scA"), [128, 2, 512], F32) for _ in range(2)])
                opsr = Ring([ps(pA, nm("ops"), [128, 2, 512], F32) for _ in range(2)])
                if s == 0:
                    S.dma("dma_start", dict(out=vrow[:], in_=VROW[:, :]), writes=[bvrow])

                kc0, kn = (PADK, sq) if s != 0 else (0, sext)

                def load_v(hp):
                    hb = hp % 2
                    if s != 0:
                        S.op(POOL, "memset", dict(ap=vaT[hb][:, 0:PADK], constant=0.0), writes=[bv[hb]])
                        S.op(POOL, "memset", dict(ap=vaT[hb][:, PADK + sq:sext], constant=0.0), writes=[bv[hb]])
                    S.dma("dma_start", dict(out=vaT[hb][:, kc0:kc0 + kn], in_=SC["VA"][hp * 128:(hp + 1) * 128, kc0:kc0 + kn]), reads=[SCB["VA"]], writes=[bv[hb]])
                    if s == 0:
                        S.op(DVE, "tensor_tensor", dict(out=vaT[hb][:, 0:PADK], in0=vaT[hb][:, 0:PADK], in1=vrow[:, 0:PADK], op=ALU.mult), reads=[bvrow, bv[hb]], writes=[bv[hb]])
                        S.op(DVE, "tensor_tensor", dict(out=vaT[hb][:, PADK + sq:sext], in0=vaT[hb][:, PADK + sq:sext], in1=vrow[:, PADK:2 * PADK], op=ALU.mult), reads=[bvrow, bv[hb]], writes=[bv[hb]])

                def load_pair(hp):
                    hb = hp % 2
                    if s != 0:
                        S.op(POOL, "memset", dict(ap=kaT[hb][:, 0:PADK], constant=0.0), writes=[bk[hb]])
                        S.op(POOL, "memset", dict(ap=kaT[hb][:, PADK + sq:sext], constant=0.0), writes=[bk[hb]])
                    for hl in range(2):
                        h = 2 * hp + hl
                        S.dma("dma_start", dict(out=kaT[hb][hl * 64:hl * 64 + 48, kc0:kc0 + kn], in_=SC["KA"][hp * 128 + hl * 64 + 16:hp * 128 + hl * 64 + 64, kc0:kc0 + kn]), reads=[SCB["KA"]], writes=[bk[hb]])
                        S.dma("dma_start", dict(out=kaT[hb][hl * 64 + 48:hl * 64 + 56, kc0:kc0 + kn], in_=SC["KAr"][h * 8:h * 8 + 8, kc0:kc0 + kn]), reads=[SCB["KAr"]], writes=[bk[hb]])
                        S.dma("dma_start", dict(out=kaT[hb][hl * 64 + 56:hl * 64 + 64, kc0:kc0 + kn], in_=SC["KAr"][64 + h * 8:64 + h * 8 + 8, kc0:kc0 + kn]), reads=[SCB["KAr"]], writes=[bk[hb]])
                        S.dma("dma_start", dict(out=qz[hb][hl][hl * 64:hl * 64 + 48, :], in_=SC["QA"][hp * 128 + hl * 64 + 16:hp * 128 + hl * 64 + 64, 0:sq]), reads=[SCB["QA"]], writes=[bq[hb]])
                        S.dma("dma_start", dict(out=qz[hb][hl][hl * 64 + 48:hl * 64 + 56, :], in_=SC["QAr"][h * 8:h * 8 + 8, 0:sq]), reads=[SCB["QAr"]], writes=[bq[hb]])
                        S.dma("dma_start", dict(out=qz[hb][hl][hl * 64 + 56:hl * 64 + 64, :], in_=SC["QAr"][64 + h * 8:64 + h * 8 + 8, 0:sq]), reads=[SCB["QAr"]], writes=[bq[hb]])

                def tile_geom(tile):
                    (pi, dil, r, t, nblk, je0) = tile
                    e0 = r + dil * (je0 + 128 * t)
                    ksl = slice(e0, e0 + dil * 127 + 1, dil)
                    mlo, mhi = max(t - 1, 0), min(t, nblk - 1)
                    nq = (mhi - mlo + 1) * 128
                    qs = r + dil * 128 * mlo
                    qsl = slice(qs, qs + dil * (nq - 1) + 1, dil)
                    boff = 0 if t >= 1 else 128
                    return ksl, qsl, nq, boff

                pairs = list(range(4)) if lvl >= 4 else []
                if pairs:
                    load_pair(0)
                    load_v(0)
                for hp in pairs:
                    hb = hp % 2
                    if hp + 1 < 4:
                        load_pair(hp + 1)
                    S.dma("dma_start", dict(out=gaT[hb][:], in_=SC["GA"][hp * 128:(hp + 1) * 128, 0:sq]), reads=[SCB["GA"]], writes=[bg[hb]])
                    S.op(POOL, "memset", dict(ap=acc[:], constant=0.0), writes=[bacc])
                    for t4 in range(0, nt, 4):
                        n4 = min(4, nt - t4)
                        o_t, o_b = opsr.next()
                        pv = o_t[:, 0, 0:256].bitcast(BF16).rearrange("p (a b) -> p a b", b=128)
                        for j in range(n4):
                            ksl = tile_geom(tiles[t4 + j])[0]
                            S.op(PE, "transpose", dict(out=pv[:, j, :], in_=vaT[hb][:, ksl], identity=ident[:]), reads=[bv[hb], bconst], writes=[o_b])
                        eng = ACT if (t4 // 4) % 2 == 0 else DVE
                        if eng == ACT:
                            S.op(ACT, "activation", dict(out=vt_all[:, t4:t4 + n4, :, 0:64], in_=pv[:, 0:n4, :].rearrange("p a (h d) -> p a h d", h=2), func=AF.Copy), reads=[o_b], writes=[bvta])
                        else:
                            S.op(DVE, "tensor_copy", dict(out=vt_all[:, t4:t4 + n4, :, 0:64], in_=pv[:, 0:n4, :].rearrange("p a (h d) -> p a h d", h=2)), reads=[o_b], writes=[bvta])
                    for hl in range(2):
                        S.op(DVE, "tensor_copy", dict(out=vt_all[:, :, hl, 64], in_=am[:, 0:nt]), reads=[bconst], writes=[bvta])
                    if hp + 1 < 4:
                        load_v(hp + 1)
                    units = [list(range(u, min(u + 2, nt))) for u in range(0, nt, 2)]

                    def scores(unit):
                        sc_t, sc_b = scA.next()
                        geo = []
                        for j, ti in enumerate(unit):
                            ksl, qsl, nq, boff = tile_geom(tiles[ti])
                            geo.append((ti, qsl, nq))
                            for hl in range(2):
                                S.op(PE, "matmul", dict(out=sc_t[:, hl, j * 256:j * 256 + nq], lhsT=kaT[hb][:, ksl], rhs=qz[hb][hl][:, qsl], start=True, stop=False), reads=[bk[hb], bq[hb]], writes=[sc_b])
                                S.op(PE, "matmul", dict(out=sc_t[:, hl, j * 256:j * 256 + nq], lhsT=ident[:], rhs=bandneg[:, boff:boff + nq], start=False, stop=True), reads=[bconst], writes=[sc_b])
                        p_t, p_b = Pt.next()
                        if len(unit) == 2 and all(g_[2] == 256 for g_ in geo):
                            S.op(ACT, "activation", dict(out=p_t[:], in_=sc_t[:], func=AF.Exp, scale=0.125), reads=[sc_b], writes=[p_b])
                        else:
                            for j, (ti, qsl, nq) in enumerate(geo):
                                S.op(ACT, "activation", dict(out=p_t[:, :, j * 256:j * 256 + nq], in_=sc_t[:, :, j * 256:j * 256 + nq], func=AF.Exp, scale=0.125), reads=[sc_b], writes=[p_b])
                        return geo, p_t, p_b

                    def av(geo, p_t, p_b):
                        o_t, o_b = opsr.next()
                        for j, (ti, qsl, nq) in enumerate(geo):
                            for hl in range(2):
                                for bi in range(nq // 128):
                                    c = j * 256 + bi * 128
                                    S.op(PE, "matmul", dict(out=o_t[0:65, hl, c:c + 128], lhsT=vt_all[:, ti, hl, :], rhs=p_t[:, hl, c:c + 128], start=True, stop=True), reads=[bvta, p_b], writes=[o_b])
                        for j, (ti, qsl, nq) in enumerate(geo):
                            S.op(DVE, "tensor_tensor", dict(out=acc[:, :, qsl], in0=o_t[0:65, :, j * 256:j * 256 + nq], in1=acc[:, :, qsl], op=ALU.add), reads=[o_b, bacc], writes=[bacc])

                    inflight = {0: scores(units[0])}
                    for u in range(len(units)):
                        if u + 1 < len(units):
                            inflight[u + 1] = scores(units[u + 1])
                        if u >= 1:
                            av(*inflight.pop(u - 1))
                    av(*inflight.pop(len(units) - 1))
                    for qc in range(sq // 512):
                        q0 = qc * 512
                        d_t, d_b = opsr.next()
                        dv = d_t[:, 0, :]
                        nv = d_t[:, 1, :]
                        for h2 in range(2):
                            qa_, qb_ = q0 + h2 * 256, q0 + (h2 + 1) * 256
                            for hl in range(2):
                                S.op(PE, "matmul", dict(out=dv[:, h2 * 256:(h2 + 1) * 256], lhsT=sels[:, hl, :], rhs=acc[:, hl, qa_:qb_], start=(hl == 0), stop=(hl == 1)), reads=[bacc, bconst], writes=[d_b])
                        for h2 in range(2):
                            qa_, qb_ = q0 + h2 * 256, q0 + (h2 + 1) * 256
                            for hl in range(2):
                                S.op(PE, "matmul", dict(out=nv[:, h2 * 256:(h2 + 1) * 256], lhsT=sels[:, 2 + hl, :], rhs=acc[:, hl, qa_:qb_], start=(hl == 0), stop=(hl == 1)), reads=[bacc, bconst], writes=[d_b])
                        rc_t, rc_b = recr.next()
                        S.op(DVE, "reciprocal", dict(out=rc_t[:], in_=dv), reads=[d_b], writes=[rc_b])
                        th_t, th_b = thr.next()
                        S.op(ACT, "activation", dict(out=th_t[:], in_=gaT[hb][:, q0:q0 + 512], func=AF.Tanh, scale=0.5), reads=[bg[hb]], writes=[th_b])
                        sg_t, sg_b = sgr.next()
                        S.op(DVE, "scalar_tensor_tensor", dict(out=sg_t[:], in0=th_t[:], scalar=1.0, in1=gaT[hb][:, q0:q0 + 512], op0=ALU.add, op1=ALU.mult), reads=[th_b, bg[hb]], writes=[sg_b])
                        tn_t, tn_b = tnr.next()
                        S.op(DVE, "tensor_tensor", dict(out=tn_t[:], in0=nv, in1=rc_t[:], op=ALU.mult), reads=[d_b, rc_b], writes=[tn_b])
                        y_t, y_b = yTr.next()
                        S.op(DVE, "tensor_tensor", dict(out=y_t[:], in0=tn_t[:], in1=sg_t[:], op=ALU.mult), reads=[tn_b, sg_b], writes=[y_b])
                        S.dma("dma_start", dict(out=SC["YM"][hp * 128:(hp + 1) * 128, q0:q0 + 512], in_=y_t[:]), reads=[y_b], writes=[SCB["YM"]])
                S.barrier()

            with contextlib.ExitStack() as p3:
                ymr = Ring([sb(p3, nm("ym"), [128, 8, 512], BF16) for _ in range(2)])
                xr = Ring([sb(p3, nm("x3"), [128, D], F32) for _ in range(8)])
                tmr = Ring([sb(p3, nm("tm3"), [128, D], F32) for _ in range(3)])
                outr = Ring([sb(p3, nm("o3"), [128, D], F32) for _ in range(4)])
                st3 = Ring([sb(p3, nm("st3"), [128, 6], F32) for _ in range(8)])
                junk3 = sb(p3, nm("junk3"), [128, 512], F32)
                bj3 = Buf()
                po = Ring([ps(p3, nm("po"), [128, 2, 512], F32) for _ in range(4)])
                nq3 = (sq // 512) if lvl >= 5 else 0

                def p3_loads(qc):
                    q0 = qc * 512
                    ym_t, ym_b = ymr.next()
                    S.dma("dma_start", dict(out=ym_t[:], in_=SC["YM"][:, q0:q0 + 512].rearrange("(kc p) t -> p kc t", p=128)), reads=[SCB["YM"]], writes=[ym_b])
                    xs = []
                    for i in range(4):
                        row = q0 + i * 128
                        x_t, x_b = xr.next()
                        S.dma("dma_start", dict(out=x_t[:], in_=X[job["xrow"] + row:job["xrow"] + row + 128, :]), writes=[x_b])
                        xs.append((x_t, x_b))
                    return ym_t, ym_b, xs

                ld3 = {}
                if nq3:
                    ld3[0] = p3_loads(0)
                for qc in range(nq3):
                    q0 = qc * 512
                    ym_t, ym_b, xs = ld3.pop(qc)
                    if qc + 1 < nq3:
                        ld3[qc + 1] = p3_loads(qc + 1)
                    for i in range(4):
                        row = q0 + i * 128
                        x_t, x_b = xs[i]
                        po_t, po_b = po.next()
                        for hf in range(2):
                            for kc in range(8):
                                S.op(PE, "matmul", dict(out=po_t[:, hf, :], lhsT=ym_t[:, kc, i * 128:(i + 1) * 128], rhs=wout[:, kc, hf * 512:(hf + 1) * 512], start=(kc == 0), stop=(kc == 7)), reads=[ym_b, bwout], writes=[po_b])
                        s_t, s_b = st3.next()
                        for hf in range(2):
                            S.op(ACT, "activation", dict(out=junk3[:], in_=po_t[:, hf, :], func=AF.Square, accum_out=s_t[:, hf:hf + 1]), reads=[po_b], writes=[bj3, s_b])
                        S.op(DVE, "tensor_tensor", dict(out=s_t[:, 2:3], in0=s_t[:, 0:1], in1=s_t[:, 1:2], op=ALU.add), reads=[s_b], writes=[s_b])
                        S.op(DVE, "tensor_scalar", dict(out=s_t[:, 3:4], in0=s_t[:, 2:3], scalar1=1.0 / D, scalar2=EPS, op0=ALU.mult, op1=ALU.add), reads=[s_b], writes=[s_b])
                        S.op(POOL, "tensor_tensor", dict(out=s_t[:, 4:5], in0=s_t[:, 3:4], in1=mhalf[:], op=ALU.pow), reads=[s_b, bconst], writes=[s_b])
                        tm_t, tm_b = tmr.next()
                        for hf in range(2):
                            S.op(DVE, "scalar_tensor_tensor", dict(out=tm_t[:, hf * 512:(hf + 1) * 512], in0=po_t[:, hf, :], scalar=s_t[:, 4:5], in1=gg[s][:, hf * 512:(hf + 1) * 512], op0=ALU.mult, op1=ALU.mult), reads=[po_b, s_b, bgg[s]], writes=[tm_b])
                        o_t, o_b = outr.next()
                        S.op(DVE, "tensor_tensor", dict(out=o_t[:], in0=tm_t[:], in1=x_t[:], op=ALU.add), reads=[tm_b, x_b], writes=[o_b])
                        yrow = job["yrow"] + row
                        out_dmas.append(S.dma("dma_start", dict(out=Y[yrow:yrow + 128, :], in_=o_t[:]), reads=[o_b], writes=[Buf()]))
                S.barrier()

        S.run(final_wait=out_dmas)
    return nc


_NC_CACHE = {}


def _rope_tables(pos):
    half = 8
    inv = (np.float32(500000.0) ** (-np.arange(half, dtype=np.float32) / np.float32(half))).astype(np.float32)
    ang = pos.astype(np.float32)[None, :] * inv[:, None]
    cos = np.cos(ang).astype(np.float32)
    sin = np.sin(ang).astype(np.float32)
    idx = np.arange(128) % 8
    C = cos[idx]
    Sg = sin[idx].copy()
    Sg[:64] *= -1.0
    return np.ascontiguousarray(C), np.ascontiguousarray(Sg)


def _amask(valid_ext, sq):
    tiles = amask_tiles(sq)
    m = np.zeros((128, len(tiles)), np.float32)
    p = np.arange(128)
    for ti, (pi, dil, r, t, nblk, je0) in enumerate(tiles):
        e = r + dil * (je0 + 128 * t + p)
        m[:, ti] = valid_ext[e]
    return m


def prep_inputs(x_prompt, x_sample, c_prompt, c_sample, w_in, w_out, g_pre, g_post,
                w_ada, b_ada, lam_q1, lam_k1, lam_q2, lam_k2, g_sub):
    f32 = np.float32
    x_prompt = np.asarray(x_prompt, f32)
    x_sample = np.asarray(x_sample, f32)
    c_prompt = np.asarray(c_prompt, f32)
    c_sample = np.asarray(c_sample, f32)
    w_in0 = np.ascontiguousarray(np.asarray(w_in, f32)[0])
    w_out0 = np.ascontiguousarray(np.asarray(w_out, f32)[0])
    w_ada0 = np.ascontiguousarray(np.asarray(w_ada, f32)[0])
    b_ada0 = np.asarray(b_ada, f32)[0]
    g_pre0 = np.asarray(g_pre, f32)[0]
    g_post0 = np.asarray(g_post, f32)[0]
    g_sub0 = np.asarray(g_sub, f32)[0]

    offs = dict(qa=0, ka=512, qb=2048, kb=2560)
    w_rot = np.zeros((D, 4, 192), f32)
    for ti, nme in enumerate(("qa", "ka", "qb", "kb")):
        x1 = np.array([offs[nme] + h * 64 + i for h in range(8) for i in range(8)])
        x2 = x1 + 8
        w_rot[:, ti, 0:64] = w_in0[:, x1]
        w_rot[:, ti, 64:128] = w_in0[:, x2]
        w_rot[:, ti, 128:192] = w_in0[:, x1]
    w_rot = np.ascontiguousarray(w_rot.reshape(D, 768))

    g_pre_t = np.ascontiguousarray(g_pre0.reshape(8, 128).T)
    b_ada_t = np.ascontiguousarray(b_ada0[:2048].reshape(16, 128).T)
    b_gate = np.ascontiguousarray(b_ada0[2048:3072].reshape(1, D))
    g_post_r = np.ascontiguousarray(g_post0.reshape(1, D))
    g_sub_t = np.ascontiguousarray(g_sub0.reshape(128, 1))
    lam_v = np.ascontiguousarray(np.stack([np.asarray(v, f32)[0] for v in (lam_q1, lam_k1, lam_q2, lam_k2)], 0))
    ident = np.eye(128, dtype=f32)
    p = np.arange(128)[:, None]
    f = np.arange(128)[None, :]
    band = np.concatenate([(p <= f), (p >= f)], axis=1).astype(f32)
    band = np.ascontiguousarray(np.concatenate([band, band], axis=1))
    sels = np.zeros((65, 4, 128), f32)
    sels[64, 0, 0:64] = 2.0
    sels[64, 1, 64:128] = 2.0
    sels[np.arange(64), 2, np.arange(64)] = 1.0
    sels[np.arange(64), 3, 64 + np.arange(64)] = 1.0
    sels = np.ascontiguousarray(sels.reshape(65, 512))
    valid1 = np.zeros(DSEQ + 2 * PADK, f32)
    valid1[PADK:PADK + DSEQ] = 1.0
    am1 = _amask(valid1, DSEQ)

    in_maps = []
    for c in range(8):
        psq, hf = c // 2, c % 2
        q0 = hf * HALF
        o0 = (1 - hf) * HALF
        xa = np.zeros((NROWS, D), f32)
        pos = np.zeros(NROWS, np.int64)
        xa[ROW_OWN:ROW_OWN + HALF] = x_prompt[psq, q0:q0 + HALF]
        pos[ROW_OWN:ROW_OWN + HALF] = np.arange(q0, q0 + HALF)
        xa[ROW_OTHER:ROW_OTHER + HALF] = x_prompt[psq, o0:o0 + HALF]
        pos[ROW_OTHER:ROW_OTHER + HALF] = np.arange(o0, o0 + HALF)
        hpos = np.concatenate([np.arange(q0 - PADK, q0), np.arange(q0 + HALF, q0 + HALF + PADK)])
        hval = (hpos >= 0) & (hpos < SEQ)
        xa[ROW_HALO:ROW_HALO + 2 * PADK][hval] = x_prompt[psq, hpos[hval]]
        pos[ROW_HALO:ROW_HALO + 2 * PADK] = np.clip(hpos, 0, SEQ - 1)
        xa[ROW_S0:ROW_S0 + DSEQ] = x_sample[2 * c]
        pos[ROW_S0:ROW_S0 + DSEQ] = np.arange(DSEQ)
        xa[ROW_S1:ROW_S1 + DSEQ] = x_sample[2 * c + 1]
        pos[ROW_S1:ROW_S1 + DSEQ] = np.arange(DSEQ)
        C, Sg = _rope_tables(pos)
        cs = np.stack([c_prompt[psq], c_sample[2 * c], c_sample[2 * c + 1]], axis=-1)
        c_t = np.ascontiguousarray(cs.reshape(8, 128, 3).transpose(1, 0, 2).reshape(128, 24))
        valid0 = np.ones(HALF + 2 * PADK, f32)
        valid0[0:PADK] = hval[:PADK]
        valid0[PADK + HALF:] = hval[PADK:]
        am0 = _amask(valid0, HALF)
        vrow = np.ascontiguousarray(np.broadcast_to(hval.astype(f32)[None, :], (128, 2 * PADK)))
        in_maps.append({
            "x_all": xa, "rope_c": C, "rope_s": Sg, "c_t": c_t, "w_in": w_in0, "w_rot": w_rot,
            "w_out": w_out0, "w_ada": w_ada0, "g_pre_t": g_pre_t, "b_ada_t": b_ada_t, "b_gate": b_gate,
            "g_post": g_post_r, "g_sub_t": g_sub_t, "lam_v": lam_v, "ident": ident, "band": band,
            "sels": sels, "amask0": am0, "amask1": am1, "vrow": vrow,
        })

    return in_maps


def kernel(**inputs):
    f32 = np.float32
    in_maps = prep_inputs(**inputs)
    if "nc" not in _NC_CACHE:
        _NC_CACHE["nc"] = build_program()
    nc = _NC_CACHE["nc"]
    res = run_bass_kernel_spmd(nc, in_maps, core_ids=list(range(8)))
    y_prompt = np.zeros((4, SEQ, D), f32)
    y_sample = np.zeros((16, DSEQ, D), f32)
    for c in range(8):
        y = np.asarray(res.results[c]["y_all"], f32)
        psq, hf = c // 2, c % 2
        y_prompt[psq, hf * HALF:(hf + 1) * HALF] = y[0:HALF]
        y_sample[2 * c] = y[HALF:HALF + DSEQ]
        y_sample[2 * c + 1] = y[HALF + DSEQ:HALF + 2 * DSEQ]
    return (y_prompt, y_sample)
```

```python
import contextlib
import numpy as np
import concourse.bass as bass
import concourse.mybir as mybir
from concourse.bass_utils import run_bass_kernel_spmd

F32 = mybir.dt.float32
BF16 = mybir.dt.bfloat16
AF = mybir.ActivationFunctionType
ALU = mybir.AluOpType

PE, ACT, DVE, POOL, SP = "tensor", "scalar", "vector", "gpsimd", "sync"
ENGINES = (PE, ACT, DVE, POOL, SP)

D = 1024
SEQ = 8192
DSEQ = 2048
HALF = 4096
PADK = 1024
PATTERNS = ((128, 1), (512, 4), (2048, 16))
EPS = 1e-6
LAM_INIT = 0.2
NROWS = HALF + HALF + 2 * PADK + 2 * DSEQ
ROW_OWN, ROW_OTHER, ROW_HALO, ROW_S0, ROW_S1 = 0, 4096, 8192, 10240, 12288
NOUT = HALF + 2 * DSEQ
WCOLS = 4096 + 4 * 192


class Buf:
    __slots__ = ("last_write", "reads")

    def __init__(self):
        self.last_write = None
        self.reads = []


class Rec:
    __slots__ = ("eng", "fn", "deps", "is_dma", "signal", "count", "dsem", "dval", "epoch")

    def __init__(self, eng, fn, is_dma):
        self.eng, self.fn, self.is_dma = eng, fn, is_dma
        self.deps = []
        self.signal = False
        self.count = 0
        self.dsem = None
        self.dval = 0
        self.epoch = 0


class Sched:
    EPOCH_MAX = 30000

    def __init__(self, nc, n_dma_sems=20):
        self.nc = nc
        self.q = {e: [] for e in ENGINES}
        self.n_dma_sems = n_dma_sems
        self.dma_rr = {e: 0 for e in ENGINES}
        self.dma_last = {}
        self.last_compute = {}

    def op(self, eng, meth, kw, reads=(), writes=(), dma=False, extra=()):
        r = Rec(eng, (meth, kw), dma)
        deps = list(extra)
        for b in reads:
            if b.last_write is not None:
                deps.append(b.last_write)
        for b in writes:
            if b.last_write is not None:
                deps.append(b.last_write)
            deps.extend(b.reads)
        if dma:
            slot = self.dma_rr[eng] % self.n_dma_sems
            self.dma_rr[eng] += 1
            prev = self.dma_last.get((eng, slot))
            if prev is not None:
                deps.append(prev)
            self.dma_last[(eng, slot)] = r
            r.dsem = (eng, slot)
        seen = set()
        for d in deps:
            if d is r or id(d) in seen:
                continue
            if (not d.is_dma) and (not dma) and d.eng == PE and eng == PE:
                continue
            seen.add(id(d))
            r.deps.append(d)
        for b in reads:
            b.reads.append(r)
        for b in writes:
            b.last_write = r
            b.reads = []
        self.q[eng].append(r)
        if not dma:
            self.last_compute[eng] = r
        return r

    def dma(self, meth, kw, reads=(), writes=(), eng=SP):
        return self.op(eng, meth, kw, reads, writes, dma=True)

    def barrier(self):
        deps = list(self.last_compute.values()) + list(self.dma_last.values())
        for e in ENGINES:
            r = Rec(e, None, False)
            for d in deps:
                if (not d.is_dma) and d.eng == e:
                    continue
                r.deps.append(d)
            self.q[e].append(r)

    def run(self, final_wait=()):
        nc = self.nc
        for e in ENGINES:
            for r in self.q[e]:
                for d in r.deps:
                    if not d.is_dma:
                        d.signal = True
        n_epochs = {}
        for e in ENGINES:
            cnt, ep = 0, 0
            for r in self.q[e]:
                if r.is_dma or r.fn is None:
                    continue
                if r.signal:
                    if cnt >= self.EPOCH_MAX:
                        ep += 1
                        cnt = 0
                    cnt += 1
                    r.count = cnt
                    r.epoch = ep
            n_epochs[e] = ep + 1
        dcount = {}
        for e in ENGINES:
            for r in self.q[e]:
                if r.is_dma:
                    dcount[r.dsem] = dcount.get(r.dsem, 0) + 16
                    r.dval = dcount[r.dsem]
        with contextlib.ExitStack() as st:
            csem = {}
            for e in ENGINES:
                for ep in range(n_epochs[e]):
                    if any((not r.is_dma) and r.signal and r.epoch == ep for r in self.q[e]):
                        csem[(e, ep)] = st.enter_context(nc.semaphore(f"c_{e}_{ep}"))
            dsem = {}
            for key in dcount:
                dsem[key] = st.enter_context(nc.semaphore(f"d_{key[0]}_{key[1]}"))
            block = st.enter_context(nc.Block())

            def emit(e, engine):
                waited = {}

                def do_wait(d):
                    if d.is_dma:
                        s, v, k = dsem[d.dsem], d.dval, ("d",) + d.dsem
                    else:
                        s, v, k = csem[(d.eng, d.epoch)], d.count, ("c", d.eng, d.epoch)
                    if waited.get(k, 0) >= v:
                        return
                    waited[k] = v
                    engine.wait_ge(s, v)

                for r in self.q[e]:
                    for d in r.deps:
                        do_wait(d)
                    if r.fn is None:
                        continue
                    ins = getattr(engine, r.fn[0])(**r.fn[1])
                    if r.is_dma:
                        ins.then_inc(dsem[r.dsem], 16)
                    elif r.signal:
                        ins.then_inc(csem[(e, r.epoch)], 1)
                if e == SP:
                    for d in final_wait:
                        do_wait(d)

            @block.tensor
            def _(eng):
                emit(PE, eng)

            @block.scalar
            def _(eng):
                emit(ACT, eng)

            @block.vector
            def _(eng):
                emit(DVE, eng)

            @block.gpsimd
            def _(eng):
                emit(POOL, eng)

            @block.sync
            def _(eng):
                emit(SP, eng)


class Ring:
    def __init__(self, tiles):
        self.tiles = tiles
        self.bufs = [Buf() for _ in tiles]
        self.i = 0

    def next(self):
        k = self.i % len(self.tiles)
        self.i += 1
        return self.tiles[k], self.bufs[k]


def amask_tiles(sq):
    out = []
    for pi, (w, dil) in enumerate(PATTERNS):
        nblk = sq // dil // 128
        je0 = PADK // dil - 64
        for r in range(dil):
            for t in range(nblk + 1):
                out.append((pi, dil, r, t, nblk, je0))
    return out


def build_program(stop_after=None, jobs=(0, 1, 2)):
    nc = bass.Bass("TRN2", target_bir_lowering=False)
    lvl = {'p0': 0, 'w': 1, 'p1': 2, '2b': 3, '2a': 4, None: 5}[stop_after]

    def din(name, shape, dt=F32):
        return nc.dram_tensor(name, list(shape), dt, kind="ExternalInput").ap()

    X = din("x_all", [NROWS, D])
    CT = din("rope_c", [128, NROWS])
    ST = din("rope_s", [128, NROWS])
    CTT = din("c_t", [128, 24])
    W_IN = din("w_in", [D, 4096])
    W_ROT = din("w_rot", [D, 768])
    W_OUT = din("w_out", [D, D])
    W_ADA = din("w_ada", [D, 3 * D])
    GPRE = din("g_pre_t", [128, 8])
    BADA = din("b_ada_t", [128, 16])
    BGATE = din("b_gate", [1, D])
    GPOST = din("g_post", [1, D])
    GSUB = din("g_sub_t", [128, 1])
    LAMV = din("lam_v", [4, 64])
    IDENT = din("ident", [128, 128])
    BAND = din("band", [128, 512])
    SELS = din("sels", [65, 512])
    AM0 = din("amask0", [128, 117])
    VROW = din("vrow", [128, 2 * PADK])
    AM1 = din("amask1", [128, 69])
    Y = nc.dram_tensor("y_all", [NOUT, D], F32, kind="ExternalOutput").ap()

    JOBS = [
        dict(s=0, sq=HALF, skv=SEQ, sext=HALF + 2 * PADK, xrow=ROW_OWN, yrow=0, am=AM0),
        dict(s=1, sq=DSEQ, skv=DSEQ, sext=DSEQ + 2 * PADK, xrow=ROW_S0, yrow=HALF, am=AM1),
        dict(s=2, sq=DSEQ, skv=DSEQ, sext=DSEQ + 2 * PADK, xrow=ROW_S1, yrow=HALF + DSEQ, am=AM1),
    ]
    def scr(name, rows, cols):
        return nc.dram_tensor(name, [rows, cols], BF16, kind="Internal").ap()

    SC = dict(
        QA=scr("s_qa", 512, HALF), QAr=scr("s_qar", 128, HALF), GA=scr("s_ga", 512, HALF),
        KA=scr("s_ka", 512, HALF + 2 * PADK), KAr=scr("s_kar", 128, HALF + 2 * PADK),
        VA=scr("s_va", 512, HALF + 2 * PADK),
        QB=scr("s_qb", 512, HALF), QBr=scr("s_qbr", 128, HALF), GB=scr("s_gb", 512, HALF),
        KB=scr("s_kb", 512, SEQ), KBr=scr("s_kbr", 128, SEQ), VB=scr("s_vb", 512, SEQ),
        YM=scr("s_ym", 1024, HALF),
    )
    SCB = {k: Buf() for k in SC}
    WS = scr("s_w", D, WCOLS)
    bWS = Buf()

    S = Sched(nc)
    out_dmas = []
    top = contextlib.ExitStack()
    with top:
        def sb(st, name, shape, dt):
            return st.enter_context(nc.sbuf_tensor("sb_" + name, list(shape), dt))

        def ps(st, name, shape, dt):
            return st.enter_context(nc.psum_tensor("ps_" + name, list(shape), dt))

        uid = [0]

        def nm(p):
            uid[0] += 1
            return f"{p}{uid[0]}"

        ident_f = sb(top, "ident_f", [128, 128], F32)
        ident = sb(top, "ident", [128, 128], BF16)
        band_f = sb(top, "band_f", [128, 512], F32)
        bandneg = sb(top, "bandneg", [128, 256], BF16)
        sels = sb(top, "sels", [65, 4, 128], F32)
        am0 = sb(top, "am0", [128, 117], F32)
        am1 = sb(top, "am1", [128, 69], F32)
        ones_f = sb(top, "ones_f", [128, 128], F32)
        mhalf = sb(top, "mhalf", [128, 1], F32)
        zer = sb(top, "zer", [1, 512], BF16)
        gsub2 = sb(top, "gsub2", [128, 1], F32)
        neglam = sb(top, "neglam", [128, 1], F32)
        lamt = sb(top, "lamt", [128, 4, 64], F32)
        lamj = sb(top, "lamj", [128, 64], F32)
        lams = sb(top, "lams", [128, 4], F32)
        gg = [sb(top, f"gg{s}", [128, D], F32) for s in range(3)]
        gsT = sb(top, "gsT", [128, 8, 4], F32)
        shT = sb(top, "shT", [128, 8, 4], F32)
        wout = sb(top, "wout", [128, 8, D], BF16)
        bconst = Buf()
        bgg = [Buf() for _ in range(3)]
        bmod = Buf()
        bwout = Buf()

        biasA = sb(top, "biasA", [128, 40, 4], F32)
        bbiasA = Buf()
        S.dma("dma_start", dict(out=ident_f[:], in_=IDENT[:, :]), writes=[bconst])
        S.dma("dma_start", dict(out=band_f[:], in_=BAND[:, :]), writes=[bconst])
        S.dma("dma_start", dict(out=sels[:].rearrange("p a b -> p (a b)"), in_=SELS[:, :]), writes=[bconst])
        S.dma("dma_start", dict(out=am0[:], in_=AM0[:, :]), writes=[bconst])
        S.dma("dma_start", dict(out=am1[:], in_=AM1[:, :]), writes=[bconst])
        S.dma("dma_start", dict(out=gsub2[:], in_=GSUB[:, :]), writes=[bconst])
        for i in range(4):
            S.dma("dma_start", dict(out=lamt[:, i, :], in_=LAMV[i:i + 1, :].broadcast_to([128, 64])), writes=[bconst])
        S.op(DVE, "tensor_copy", dict(out=ident[:], in_=ident_f[:]), reads=[bconst], writes=[bconst])
        S.op(DVE, "tensor_scalar", dict(out=bandneg[:], in0=band_f[:, 0:256], scalar1=30000.0, scalar2=-30000.0, op0=ALU.mult, op1=ALU.add), reads=[bconst], writes=[bconst])
        S.op(POOL, "memset", dict(ap=ones_f[:], constant=1.0), writes=[bconst])
        S.op(POOL, "memset", dict(ap=mhalf[:], constant=-0.5), writes=[bconst])
        S.op(POOL, "memset", dict(ap=zer[:], constant=0.0), writes=[bconst])
        S.op(DVE, "tensor_scalar", dict(out=gsub2[:], in0=gsub2[:], scalar1=(1.0 - LAM_INIT) * 0.5, scalar2=None, op0=ALU.mult), reads=[bconst], writes=[bconst])
        for i in range(2):
            S.op(DVE, "tensor_tensor", dict(out=lamj[:], in0=lamt[:, 2 * i, :], in1=lamt[:, 2 * i + 1, :], op=ALU.mult), reads=[bconst], writes=[bconst])
            S.op(ACT, "activation", dict(out=lamj[:], in_=lamj[:], func=AF.Identity, accum_out=lams[:, i:i + 1]), reads=[bconst], writes=[bconst])
        S.op(ACT, "activation", dict(out=lams[:, 2:4], in_=lams[:, 0:2], func=AF.Exp), reads=[bconst], writes=[bconst])
        S.op(DVE, "tensor_tensor", dict(out=neglam[:], in0=lams[:, 3:4], in1=lams[:, 2:3], op=ALU.subtract), reads=[bconst], writes=[bconst])
        S.op(DVE, "tensor_scalar", dict(out=neglam[:], in0=neglam[:], scalar1=-LAM_INIT, scalar2=None, op0=ALU.add), reads=[bconst], writes=[bconst])

        with contextlib.ExitStack() as p0:
            stg = [sb(p0, f"stg0_{i}", [128, 8, 512], F32) for i in range(2)]
            bstg = [Buf() for _ in range(2)]
            ct = sb(p0, "ct", [128, 24], F32)
            th = sb(p0, "th0", [128, 24], F32)
            scT = sb(p0, "scT", [128, 8, 4], F32)
            screp = sb(p0, "screp", [128, 24, 128], F32)
            gpre = sb(p0, "gpre", [128, 8], F32)
            bada = sb(p0, "bada", [128, 16], F32)
            bgate = sb(p0, "bgate", [128, D], F32)
            gpost = sb(p0, "gpost", [128, D], F32)
            tmpm = sb(p0, "tmpm", [128, 8], F32)
            tmpg = sb(p0, "tmpg", [128, 512], F32)
            pmod = ps(p0, "pmod", [128, 16, 4], F32)
            pg = [ps(p0, f"pg{i}", [128, 512], F32) for i in range(3)]
            b0 = Buf()
            bpm = Buf()
            bpg = [Buf() for _ in range(3)]
            S.dma("dma_start", dict(out=ct[:], in_=CTT[:, :]), writes=[b0])
            S.dma("dma_start", dict(out=gpre[:], in_=GPRE[:, :]), writes=[b0])
            S.dma("dma_start", dict(out=bada[:], in_=BADA[:, :]), writes=[b0])
            S.dma("dma_start", dict(out=bgate[:], in_=BGATE[0:1, :].broadcast_to([128, D])), writes=[b0])
            S.dma("dma_start", dict(out=gpost[:], in_=GPOST[0:1, :].broadcast_to([128, D])), writes=[b0])
            S.op(ACT, "activation", dict(out=th[:], in_=ct[:], func=AF.Tanh, scale=0.5), reads=[b0], writes=[b0])
            S.op(DVE, "scalar_tensor_tensor", dict(out=th[:], in0=th[:], scalar=1.0, in1=ct[:], op0=ALU.add, op1=ALU.mult), reads=[b0], writes=[b0])
            S.op(POOL, "memset", dict(ap=scT[:], constant=0.0), writes=[b0])
            S.op(POOL, "memset", dict(ap=shT[:], constant=0.0), writes=[bmod])
            S.op(POOL, "memset", dict(ap=gsT[:], constant=0.0), writes=[bmod])
            S.op(DVE, "tensor_scalar", dict(out=scT[:, :, 0:3], in0=th[:].rearrange("p (a b) -> p a b", b=3), scalar1=0.5, scalar2=None, op0=ALU.mult), reads=[b0], writes=[b0])
            for i in range(24):
                kc, s = divmod(i, 3)
                S.op(DVE, "tensor_scalar", dict(out=screp[:, i, :], in0=ones_f[:], scalar1=scT[:, kc, s:s + 1], scalar2=None, op0=ALU.mult), reads=[b0, bconst], writes=[b0])
            for pc in range(6):
                k = pc % 2
                S.dma("dma_start", dict(out=stg[k][:], in_=W_ADA[:, pc * 512:(pc + 1) * 512].rearrange("(kc p) n -> p kc n", p=128)), writes=[bstg[k]])
                if pc < 4:
                    for fb in range(4):
                        blk = pc * 4 + fb
                        for kc in range(8):
                            S.op(PE, "matmul", dict(out=pmod[:, blk, :], lhsT=stg[k][:, kc, fb * 128:(fb + 1) * 128], rhs=scT[:, kc, :], start=(kc == 0), stop=(kc == 7)), reads=[bstg[k], b0], writes=[bpm])
                else:
                    hf = pc - 4
                    for s in range(3):
                        for h2 in range(2):
                            for kc in range(8):
                                S.op(PE, "matmul", dict(out=pg[s][:, h2 * 256:(h2 + 1) * 256], lhsT=screp[:, kc * 3 + s, :], rhs=stg[k][:, kc, h2 * 256:(h2 + 1) * 256], start=(kc == 0), stop=(kc == 7)), reads=[bstg[k], b0], writes=[bpg[s]])
                        S.op(DVE, "tensor_tensor", dict(out=tmpg[:], in0=pg[s][:], in1=bgate[:, hf * 512:(hf + 1) * 512], op=ALU.add), reads=[bpg[s], b0], writes=[b0])
                        S.op(DVE, "tensor_tensor", dict(out=gg[s][:, hf * 512:(hf + 1) * 512], in0=tmpg[:], in1=gpost[:, hf * 512:(hf + 1) * 512], op=ALU.mult), reads=[b0], writes=[bgg[s]])
            for s in range(3):
                S.op(DVE, "tensor_tensor", dict(out=shT[:, :, s], in0=pmod[:, 0:8, s], in1=bada[:, 0:8], op=ALU.add), reads=[bpm, b0], writes=[bmod])
                S.op(DVE, "scalar_tensor_tensor", dict(out=tmpm[:], in0=pmod[:, 8:16, s], scalar=1.0, in1=bada[:, 8:16], op0=ALU.add, op1=ALU.add), reads=[bpm, b0], writes=[b0])
                S.op(DVE, "tensor_tensor", dict(out=gsT[:, :, s], in0=tmpm[:], in1=gpre[:], op=ALU.mult), reads=[b0], writes=[bmod])
            for pc in range(2):
                k = pc % 2
                S.dma("dma_start", dict(out=stg[k][:], in_=W_OUT[:, pc * 512:(pc + 1) * 512].rearrange("(kc p) n -> p kc n", p=128)), writes=[bstg[k]])
                S.op(DVE, "tensor_copy", dict(out=wout[:, :, pc * 512:(pc + 1) * 512], in_=stg[k][:]), reads=[bstg[k]], writes=[bwout])
            wbr = Ring([sb(p0, f"wbst{i}", [128, 8, 512], BF16) for i in range(2)])
            pb = ps(p0, "pb0", [128, 40, 4], F32)
            bpb = Buf()
            pieces = [(W_IN, pc * 512, 512, pc * 512) for pc in range(8)] + [(W_ROT, 0, 384, 4096), (W_ROT, 384, 384, 4096 + 384)]
            for pi_, (wsrc, c0, ncol, dcol) in enumerate(pieces):
                k = pi_ % 2
                S.dma("dma_start", dict(out=stg[k][:, :, 0:ncol], in_=wsrc[:, c0:c0 + ncol].rearrange("(kc p) n -> p kc n", p=128)), writes=[bstg[k]])
                wb_t, wb_b = wbr.next()
                S.op(ACT, "activation", dict(out=wb_t[:, 0:4, 0:ncol], in_=stg[k][:, 0:4, 0:ncol], func=AF.Copy), reads=[bstg[k]], writes=[wb_b])
                S.op(DVE, "tensor_copy", dict(out=wb_t[:, 4:8, 0:ncol], in_=stg[k][:, 4:8, 0:ncol]), reads=[bstg[k]], writes=[wb_b])
                S.dma("dma_start", dict(out=WS[:, dcol:dcol + ncol].rearrange("(kc p) n -> p kc n", p=128), in_=wb_t[:, :, 0:ncol]), reads=[wb_b], writes=[bWS])
                if pi_ < 8:
                    cols = [(fb * 128, pi_ * 4 + fb) for fb in range(4)]
                else:
                    t0 = (pi_ - 8) * 2
                    cols = [(0, 32 + 2 * t0), (64, 33 + 2 * t0), (192, 34 + 2 * t0), (256, 35 + 2 * t0)]
                for (co, bcol) in cols:
                    for kc in range(8):
                        S.op(PE, "matmul", dict(out=pb[:, bcol, :], lhsT=stg[k][:, kc, co:co + 128], rhs=shT[:, kc, :], start=(kc == 0), stop=(kc == 7)), reads=[bstg[k], bmod], writes=[bpb])
            S.op(DVE, "tensor_copy", dict(out=biasA[:], in_=pb[:]), reads=[bpb], writes=[bbiasA])
            S.barrier()

        for job in [JOBS[j_] for j_ in jobs] if stop_after != 'p0' else []:
            s, sq, skv, sext = job["s"], job["sq"], job["skv"], job["sext"]
            am = am0 if job["am"] is AM0 else am1
            with contextlib.ExitStack() as p1:
                wp = sb(p1, nm("wp"), [128, 8, WCOLS], BF16)
                WPIECE = 1216
                bwp_l = [Buf() for _ in range(-(-WCOLS // WPIECE))]

                def bwp_of(c0):
                    return bwp_l[c0 // WPIECE]
                gsrep = sb(p1, nm("gsrep"), [128, 8, 128], F32)
                bgsrep = Buf()
                if lvl >= 1:
                    for c0 in range(0, WCOLS, WPIECE):
                        c1 = min(c0 + WPIECE, WCOLS)
                        S.dma("dma_start", dict(out=wp[:, :, c0:c1], in_=WS[:, c0:c1].rearrange("(kc p) n -> p kc n", p=128)), reads=[bWS], writes=[bwp_of(c0)])
                    for kc in range(8):
                        S.op(DVE, "tensor_scalar", dict(out=gsrep[:, kc, :], in0=ones_f[:], scalar1=gsT[:, kc, s:s + 1], scalar2=None, op0=ALU.mult), reads=[bmod, bconst], writes=[bgsrep])
                biasT = biasA[:, :, s]
                bbias = bbiasA

                with contextlib.ExitStack() as pp:
                    xt = Ring([sb(pp, nm("xt"), [128, D], F32) for _ in range(4)])
                    junk = sb(pp, nm("junk"), [128, D], F32)
                    bjunk = Buf()
                    xn = Ring([sb(pp, nm("xn"), [128, D], F32) for _ in range(4)])
                    xnT = Ring([sb(pp, nm("xnT"), [128, 8, 512], BF16) for _ in range(2)])
                    stat = Ring([sb(pp, nm("stat"), [128, 4], F32) for _ in range(8)])
                    ost = Ring([sb(pp, nm("ost"), [128, 4, 512], BF16) for _ in range(3)])
                    cst = Ring([sb(pp, nm("cst"), [128, 2, 512], F32) for _ in range(2)])
                    t12 = Ring([sb(pp, nm("t12"), [128, 2, 512], F32) for _ in range(2)])
                    rot = Ring([sb(pp, nm("rot"), [128, 512], BF16) for _ in range(2)])
                    pT = Ring([ps(pp, nm("pT"), [128, D], F32) for _ in range(2)])
                    pacc = Ring([ps(pp, nm("pacc"), [128, 512], F32) for _ in range(4)])

                    G = dict(qa=(0, 0), ka=(512, 4), va=(1024, 8), ga=(1536, 12), qb=(2048, 16), kb=(2560, 20), vb=(3072, 24), gb=(3584, 28))
                    RT = dict(qa=0, ka=1, qb=2, kb=3)
                    segs = []
                    full_main = [("qa", "QA", 0), ("ka", "KA", PADK), ("va", "VA", PADK), ("ga", "GA", 0), ("qb", "QB", 0), ("kb", "KB", 0), ("vb", "VB", 0), ("gb", "GB", 0)]
                    full_rot = [("qa", "QAr", 0), ("ka", "KAr", PADK), ("qb", "QBr", 0), ("kb", "KBr", 0)]
                    segs.append((job["xrow"], sq, full_main, full_rot))
                    if s == 0:
                        segs.append((ROW_OTHER, HALF, [("kb", "KB", HALF), ("vb", "VB", HALF)], [("kb", "KBr", HALF)]))
                        segs.append((ROW_HALO, PADK, [("ka", "KA", 0), ("va", "VA", 0)], [("ka", "KAr", 0)]))
                        segs.append((ROW_HALO + PADK, PADK, [("ka", "KA", PADK + HALF), ("va", "VA", PADK + HALF)], [("ka", "KAr", PADK + HALF)]))
                    evi = [0]
                    chunks = []
                    for (xr0, T, mains, rots) in (segs if lvl >= 2 else []):
                        for ch in range(T // 512):
                            chunks.append((xr0 + ch * 512, ch, mains, rots))

                    def emit_loads(cidx):
                        r0, ch, mains, rots = chunks[cidx]
                        xs = []
                        for i in range(4):
                            x_t, x_b = xt.next()
                            S.dma("dma_start", dict(out=x_t[:], in_=X[r0 + i * 128:r0 + (i + 1) * 128, :]), writes=[x_b])
                            xs.append((x_t, x_b))
                        c_t, c_b = cst.next()
                        if rots:
                            S.dma("dma_start", dict(out=c_t[:, 0, :], in_=CT[:, r0:r0 + 512]), writes=[c_b])
                            S.dma("dma_start", dict(out=c_t[:, 1, :], in_=ST[:, r0:r0 + 512]), writes=[c_b])
                        return xs, c_t, c_b

                    def emit_prologue(cidx, xs):
                        xT_t, xT_b = xnT.next()
                        for i, (x_t, x_b) in enumerate(xs):
                            st_t, st_b = stat.next()
                            S.op(ACT, "activation", dict(out=junk[:], in_=x_t[:], func=AF.Square, accum_out=st_t[:, 0:1]), reads=[x_b], writes=[bjunk, st_b])
                            S.op(DVE, "tensor_scalar", dict(out=st_t[:, 1:2], in0=st_t[:, 0:1], scalar1=1.0 / D, scalar2=EPS, op0=ALU.mult, op1=ALU.add), reads=[st_b], writes=[st_b])
                            S.op(POOL, "tensor_tensor", dict(out=st_t[:, 2:3], in0=st_t[:, 1:2], in1=mhalf[:], op=ALU.pow), reads=[st_b, bconst], writes=[st_b])
                            xn_t, xn_b = xn.next()
                            S.op(DVE, "tensor_scalar", dict(out=xn_t[:], in0=x_t[:], scalar1=st_t[:, 2:3], scalar2=None, op0=ALU.mult), reads=[x_b, st_b], writes=[xn_b])
                            pT_t, pT_b = pT.next()
                            for kc in range(8):
                                S.op(PE, "transpose", dict(out=pT_t[:, kc * 128:(kc + 1) * 128], in_=xn_t[:, kc * 128:(kc + 1) * 128], identity=ident_f[:]), reads=[xn_b, bconst], writes=[pT_b])
                            S.op(DVE, "tensor_tensor", dict(out=xT_t[:, :, i * 128:(i + 1) * 128], in0=pT_t[:].rearrange("p (a b) -> p a b", b=128), in1=gsrep[:], op=ALU.mult), reads=[pT_b, bgsrep], writes=[xT_b])
                        return xT_t, xT_b

                    def emit_main(cidx, xT_t, xT_b, c_t, c_b):
                        r0, ch, mains, rots = chunks[cidx]
                        for (gname, skey, dcol0) in mains:
                            wcol, bcol = G[gname]
                            o_t, o_b = ost.next()
                            for b in range(4):
                                pa_t, pa_b = pacc.next()
                                c0 = wcol + b * 128
                                bc = bcol + b
                                for kc in range(8):
                                    S.op(PE, "matmul", dict(out=pa_t[:], lhsT=wp[:, kc, c0:c0 + 128], rhs=xT_t[:, kc, :], start=(kc == 0), stop=(kc == 7)), reads=[bwp_of(c0), bwp_of(c0 + 127), xT_b], writes=[pa_b])
                                evi[0] += 1
                                if evi[0] % 4 != 0:
                                    S.op(ACT, "activation", dict(out=o_t[:, b, :], in_=pa_t[:], func=AF.Identity, bias=biasT[:, bc:bc + 1]), reads=[pa_b, bbias], writes=[o_b])
                                else:
                                    S.op(DVE, "tensor_scalar", dict(out=o_t[:, b, :], in0=pa_t[:], scalar1=biasT[:, bc:bc + 1], scalar2=None, op0=ALU.add), reads=[pa_b, bbias], writes=[o_b])
                            dc = dcol0 + ch * 512
                            S.dma("dma_start", dict(out=SC[skey][:, dc:dc + 512].rearrange("(b p) t -> p b t", p=128), in_=o_t[:]), reads=[o_b], writes=[SCB[skey]])
                        for (gname, skey, dcol0) in rots:
                            ti = RT[gname]
                            wc = 4096 + ti * 192
                            bc = 32 + 2 * ti
                            p1_t, p1_b = pacc.next()
                            p2_t, p2_b = pacc.next()
                            for kc in range(8):
                                S.op(PE, "matmul", dict(out=p1_t[:], lhsT=wp[:, kc, wc:wc + 128], rhs=xT_t[:, kc, :], start=(kc == 0), stop=(kc == 7)), reads=[bwp_of(wc), bwp_of(wc + 127), xT_b], writes=[p1_b])
                            for kc in range(8):
                                S.op(PE, "matmul", dict(out=p2_t[:], lhsT=wp[:, kc, wc + 64:wc + 192], rhs=xT_t[:, kc, :], start=(kc == 0), stop=(kc == 7)), reads=[bwp_of(wc + 64), bwp_of(wc + 191), xT_b], writes=[p2_b])
                            t_t, t_b = t12.next()
                            S.op(DVE, "scalar_tensor_tensor", dict(out=t_t[:, 0, :], in0=p1_t[:], scalar=biasT[:, bc:bc + 1], in1=c_t[:, 0, :], op0=ALU.add, op1=ALU.mult), reads=[p1_b, c_b, bbias], writes=[t_b])
                            S.op(DVE, "scalar_tensor_tensor", dict(out=t_t[:, 1, :], in0=p2_t[:], scalar=biasT[:, bc + 1:bc + 2], in1=c_t[:, 1, :], op0=ALU.add, op1=ALU.mult), reads=[p2_b, c_b, bbias], writes=[t_b])
                            r_t, r_b = rot.next()
                            S.op(DVE, "tensor_tensor", dict(out=r_t[:], in0=t_t[:, 0, :], in1=t_t[:, 1, :], op=ALU.add), reads=[t_b], writes=[r_b])
                            dc = dcol0 + ch * 512
                            S.dma("dma_start", dict(out=SC[skey][:, dc:dc + 512], in_=r_t[:]), reads=[r_b], writes=[SCB[skey]])

                    loaded = {}
                    if chunks:
                        loaded[0] = emit_loads(0)
                    for cidx in range(len(chunks)):
                        xs, c_t, c_b = loaded.pop(cidx)
                        xT_t, xT_b = emit_prologue(cidx, xs)
                        if cidx + 1 < len(chunks):
                            loaded[cidx + 1] = emit_loads(cidx + 1)
                        emit_main(cidx, xT_t, xT_b, c_t, c_b)
                    S.barrier()

            with contextlib.ExitStack() as pB:
                nkt = skv // 128
                nqc = sq // 512
                kbT = [sb(pB, nm("kbT"), [128, skv], BF16) for _ in range(2)]
                vbT = [sb(pB, nm("vbT"), [128, skv], BF16) for _ in range(2)]
                qbT = [sb(pB, nm("qbT"), [128, sq], BF16) for _ in range(2)]
                gbT = [sb(pB, nm("gbT"), [128, sq], BF16) for _ in range(2)]
                bk, bv, bq, bg = ([Buf(), Buf()] for _ in range(4))
                vtok = sb(pB, nm("vtok"), [128, nkt, 128], BF16)
                bvt = Buf()
                Pr = Ring([sb(pB, nm("P"), [128, 1024], BF16) for _ in range(6)])
                asum = [[sb(pB, nm("asum"), [128, 1024], F32) for _ in range(2)] for _ in range(2)]
                basum = [[Buf() for _ in range(2)] for _ in range(2)]
                oS = [[sb(pB, nm("oS"), [128, 512], F32) for _ in range(2)] for _ in range(2)]
                boS = [[Buf() for _ in range(2)] for _ in range(2)]
                thr = Ring([sb(pB, nm("thb"), [128, 512], F32) for _ in range(2)])
                sgr = Ring([sb(pB, nm("sgb"), [128, 512], BF16) for _ in range(3)])
                yTr = Ring([sb(pB, nm("yTb"), [128, 512], BF16) for _ in range(3)])
                rsr = Ring([sb(pB, nm("rs"), [128, 16], F32) for _ in range(3)])
                rr = Ring([sb(pB, nm("rr"), [128, 8], F32) for _ in range(8)])
                t0r = Ring([sb(pB, nm("t0"), [128, 128], F32) for _ in range(3)])
                orr = Ring([sb(pB, nm("o"), [128, 128], F32) for _ in range(8)])
                onr = Ring([sb(pB, nm("on"), [128, 128], BF16) for _ in range(8)])
                junkb = sb(pB, nm("junkb"), [128, 128], F32)
                bjb = Buf()
                scr_ = Ring([ps(pB, nm("sc"), [128, 1024], F32) for _ in range(2)])
                oT = [ps(pB, nm("oT"), [128, 512], F32) for _ in range(2)]
                boT = [Buf() for _ in range(2)]
                tpb = ps(pB, nm("tpb"), [128, 512], F32)
                btp = Buf()
                pTv = ps(pB, nm("pTv"), [128, 8, 128], BF16)
                bpv = Buf()

                def load_head(g):
                    hb = g % 2
                    for (dst, dbuf, main, rotk, ncols) in ((kbT[hb], bk[hb], "KB", "KBr", skv), (qbT[hb], bq[hb], "QB", "QBr", sq)):
                        for m in range(2):
                            h = 2 * g + m
                            S.dma("dma_start", dict(out=dst[m * 64:m * 64 + 48, 0:ncols], in_=SC[main][g * 128 + m * 64 + 16:g * 128 + m * 64 + 64, 0:ncols]), reads=[SCB[main]], writes=[dbuf])
                            S.dma("dma_start", dict(out=dst[m * 64 + 48:m * 64 + 56, 0:ncols], in_=SC[rotk][h * 8:h * 8 + 8, 0:ncols]), reads=[SCB[rotk]], writes=[dbuf])
                            S.dma("dma_start", dict(out=dst[m * 64 + 56:m * 64 + 64, 0:ncols], in_=SC[rotk][64 + h * 8:64 + h * 8 + 8, 0:ncols]), reads=[SCB[rotk]], writes=[dbuf])
                    S.dma("dma_start", dict(out=vbT[hb][:], in_=SC["VB"][g * 128:(g + 1) * 128, 0:skv]), reads=[SCB["VB"]], writes=[bv[hb]])
                    S.dma("dma_start", dict(out=gbT[hb][:], in_=SC["GB"][g * 128:(g + 1) * 128, 0:sq]), reads=[SCB["GB"]], writes=[bg[hb]])

                def make_epilogue(g, q0, par, sg_t, sg_b):
                    steps = []
                    rs_t, rs_b = rsr.next()
                    y_t, y_b = yTr.next()
                    state = {}

                    def rowsums():
                        for m in range(2):
                            for qb in range(4):
                                c = 256 + 2 * (m * 4 + qb)
                                for j in range(2):
                                    S.op(PE, "matmul", dict(out=tpb[:, c:c + 2], lhsT=asum[par][j][:, m * 512 + qb * 128:m * 512 + (qb + 1) * 128], rhs=ones_f[:, 0:2], start=(j == 0), stop=(j == 1)), reads=[basum[par][j], bconst], writes=[btp])
                        S.op(DVE, "reciprocal", dict(out=rs_t[:, 0:8], in_=tpb[:, 256:272:2]), reads=[btp], writes=[rs_b])
                        S.op(DVE, "tensor_scalar", dict(out=rs_t[:, 8:12], in0=rs_t[:, 4:8], scalar1=neglam[:, 0:1], scalar2=None, op0=ALU.mult), reads=[rs_b, bconst], writes=[rs_b])
                    steps.append(rowsums)

                    def st1(qb):
                        def f():
                            for m in range(2):
                                S.op(PE, "transpose", dict(out=tpb[:, m * 128:(m + 1) * 128], in_=oS[par][m][:, qb * 128:(qb + 1) * 128], identity=ident_f[:]), reads=[boS[par][m], bconst], writes=[btp])
                            r_t, r_b = rr.next()
                            t0_t, t0_b = t0r.next()
                            S.op(DVE, "tensor_scalar", dict(out=t0_t[:], in0=tpb[:, 0:128], scalar1=rs_t[:, qb:qb + 1], scalar2=None, op0=ALU.mult), reads=[btp, rs_b], writes=[t0_b])
                            o_t, o_b = orr.next()
                            S.op(DVE, "scalar_tensor_tensor", dict(out=o_t[:], in0=tpb[:, 128:256], scalar=rs_t[:, 8 + qb:9 + qb], in1=t0_t[:], op0=ALU.mult, op1=ALU.add), reads=[btp, rs_b, t0_b], writes=[o_b])
                            state[qb] = dict(r=(r_t, r_b), o=(o_t, o_b))
                        return f

                    def st2(qb):
                        def f():
                            (r_t, r_b), (o_t, o_b) = state[qb]["r"], state[qb]["o"]
                            S.op(ACT, "activation", dict(out=junkb[:], in_=o_t[:], func=AF.Square, accum_out=r_t[:, 3:4]), reads=[o_b], writes=[bjb, r_b])
                        return f

                    def st3(qb):
                        def f():
                            r_t, r_b = state[qb]["r"]
                            S.op(DVE, "tensor_scalar", dict(out=r_t[:, 4:5], in0=r_t[:, 3:4], scalar1=1.0 / 128, scalar2=EPS, op0=ALU.mult, op1=ALU.add), reads=[r_b], writes=[r_b])
                            S.op(POOL, "tensor_tensor", dict(out=r_t[:, 5:6], in0=r_t[:, 4:5], in1=mhalf[:], op=ALU.pow), reads=[r_b, bconst], writes=[r_b])
                        return f

                    def st4(qb):
                        def f():
                            (r_t, r_b), (o_t, o_b) = state[qb]["r"], state[qb]["o"]
                            on_t, on_b = onr.next()
                            S.op(ACT, "activation", dict(out=on_t[:], in_=o_t[:], func=AF.Identity, scale=r_t[:, 5:6]), reads=[o_b, r_b], writes=[on_b])
                            state[qb]["on"] = (on_t, on_b)
                        return f

                    def st5(qb, last):
                        def f():
                            on_t, on_b = state[qb]["on"]
                            ev = pTv[:, 0, :]
                            S.op(PE, "transpose", dict(out=ev, in_=on_t[:], identity=ident[:]), reads=[on_b, bconst], writes=[bpv])
                            S.op(DVE, "scalar_tensor_tensor", dict(out=y_t[:, qb * 128:(qb + 1) * 128], in0=ev, scalar=gsub2[:, 0:1], in1=sg_t[:, qb * 128:(qb + 1) * 128], op0=ALU.mult, op1=ALU.mult), reads=[bpv, sg_b, bconst], writes=[y_b])
                            if last:
                                S.dma("dma_start", dict(out=SC["YM"][512 + g * 128:512 + (g + 1) * 128, q0:q0 + 512], in_=y_t[:]), reads=[y_b], writes=[SCB["YM"]])
                        return f

                    for stage in (st1, st2, st3, st4):
                        for qb in range(4):
                            steps.append(stage(qb))
                    for qb in range(4):
                        steps.append(st5(qb, qb == 3))
                    return steps

                pending = []
                steps_per_iter = -(-21 // max(nkt - 3, 1))
                heads = list(range(4)) if lvl >= 3 else []
                if heads:
                    load_head(0)
                for g in heads:
                    hb = g % 2
                    if g + 1 < 4:
                        load_head(g + 1)
                    for k8 in range(nkt // 8):
                        for j in range(8):
                            kt = k8 * 8 + j
                            S.op(PE, "transpose", dict(out=pTv[:, j, :], in_=vbT[hb][:, kt * 128:(kt + 1) * 128], identity=ident[:]), reads=[bv[hb], bconst], writes=[bpv])
                        S.op(ACT, "activation", dict(out=vtok[:, k8 * 8:k8 * 8 + 8, :], in_=pTv[:], func=AF.Copy), reads=[bpv], writes=[bvt])
                    for qc in range(nqc):
                        q0 = qc * 512
                        par = qc % 2
                        th_t, th_b = thr.next()
                        sg_t, sg_b = sgr.next()
                        S.op(ACT, "activation", dict(out=th_t[:], in_=gbT[hb][:, q0:q0 + 512], func=AF.Tanh, scale=0.5), reads=[bg[hb]], writes=[th_b])
                        S.op(DVE, "scalar_tensor_tensor", dict(out=sg_t[:], in0=th_t[:], scalar=1.0, in1=gbT[hb][:, q0:q0 + 512], op0=ALU.add, op1=ALU.mult), reads=[th_b, bg[hb]], writes=[sg_b])

                        def scores(kt):
                            sc_t, sc_b = scr_.next()
                            for m in range(2):
                                S.op(PE, "matmul", dict(out=sc_t[:, m * 512:(m + 1) * 512], lhsT=kbT[hb][m * 64:(m + 1) * 64, kt * 128:(kt + 1) * 128], rhs=qbT[hb][m * 64:(m + 1) * 64, q0:q0 + 512], start=True, stop=True), reads=[bk[hb], bq[hb]], writes=[sc_b])
                            p_t, p_b = Pr.next()
                            S.op(ACT, "activation", dict(out=p_t[:], in_=sc_t[:], func=AF.Exp, scale=0.125), reads=[sc_b], writes=[p_b])
                            return p_t, p_b

                        def av(kt, p_t, p_b):
                            for m in range(2):
                                S.op(PE, "matmul", dict(out=oT[m][:], lhsT=vtok[:, kt, :], rhs=p_t[:, m * 512:(m + 1) * 512], start=(kt == 0), stop=(kt == nkt - 1)), reads=[p_b, bvt], writes=[boT[m]])
                            j = kt % 2
                            if kt < 2:
                                S.op(DVE, "tensor_copy", dict(out=asum[par][j][:], in_=p_t[:]), reads=[p_b], writes=[basum[par][j]])
                            else:
                                S.op(DVE, "tensor_tensor", dict(out=asum[par][j][:], in0=asum[par][j][:], in1=p_t[:], op=ALU.add), reads=[p_b, basum[par][j]], writes=[basum[par][j]])

                        ptiles = {0: scores(0)}
                        for kt in range(nkt):
                            if kt + 1 < nkt:
                                ptiles[kt + 1] = scores(kt + 1)
                            if kt >= 1:
                                av(kt - 1, *ptiles.pop(kt - 1))
                            if kt >= 1:
                                for _ in range(steps_per_iter):
                                    if pending:
                                        pending.pop(0)()
                        av(nkt - 1, *ptiles.pop(nkt - 1))
                        for m in range(2):
                            S.op(ACT, "activation", dict(out=oS[par][m][:], in_=oT[m][:], func=AF.Copy), reads=[boT[m]], writes=[boS[par][m]])
                        while pending:
                            pending.pop(0)()
                        pending.extend(make_epilogue(g, q0, par, sg_t, sg_b))
                while pending:
                    pending.pop(0)()
                S.barrier()

            with contextlib.ExitStack() as pA:
                tiles = amask_tiles(sq)
                nt = len(tiles)
                kaT = [sb(pA, nm("kaT"), [128, sext], BF16) for _ in range(2)]
                vaT1 = sb(pA, nm("vaT"), [128, sext], BF16)
                vaT = [vaT1, vaT1]
                bv1 = Buf()
                qz = [[sb(pA, nm("qz"), [128, sq], BF16) for _ in range(2)] for _ in range(2)]
                gaT1 = sb(pA, nm("gaT"), [128, sq], BF16)
                gaT = [gaT1, gaT1]
                bg1 = Buf()
                bk, bv, bq, bg = ([Buf(), Buf()] for _ in range(4))
                bv = [bv1, bv1]
                bg = [bg1, bg1]
                for hb_ in range(2):
                    S.op(POOL, "memset", dict(ap=qz[hb_][0][64:128, :], constant=0.0), writes=[bq[hb_]])
                    S.op(POOL, "memset", dict(ap=qz[hb_][1][0:64, :], constant=0.0), writes=[bq[hb_]])
                acc = sb(pA, nm("accA"), [65, 2, sq], F32)
                bacc = Buf()
                vt_all = sb(pA, nm("vtall"), [128, nt, 2, 65], BF16)
                bvta = Buf()
                vrow = sb(pA, nm("vrow"), [128, 2 * PADK], F32)
                bvrow = Buf()
                Pt = Ring([sb(pA, nm("Pt"), [128, 2, 512], BF16) for _ in range(3)])
                recr = Ring([sb(pA, nm("rec"), [128, 512], F32) for _ in range(2)])
                thr = Ring([sb(pA, nm("tha"), [128, 512], F32) for _ in range(1)])
                sgr = Ring([sb(pA, nm("sga"), [128, 512], F32) for _ in range(1)])
                tnr = Ring([sb(pA, nm("tn"), [128, 512], F32) for _ in range(1)])
                yTr = Ring([sb(pA, nm("yTa"), [128, 512], BF16) for _ in range(2)])
                scA = Ring([ps(pA, nm("scA"), [128, 2, 512], F32) for _ in range(2)])
                opsr = Ring([ps(pA, nm("ops"), [128, 2, 512], F32) for _ in range(2)])
                if s == 0:
                    S.dma("dma_start", dict(out=vrow[:], in_=VROW[:, :]), writes=[bvrow])

                kc0, kn = (PADK, sq) if s != 0 else (0, sext)

                def load_v(hp):
                    hb = hp % 2
                    if s != 0:
                        S.op(POOL, "memset", dict(ap=vaT[hb][:, 0:PADK], constant=0.0), writes=[bv[hb]])
                        S.op(POOL, "memset", dict(ap=vaT[hb][:, PADK + sq:sext], constant=0.0), writes=[bv[hb]])
                    S.dma("dma_start", dict(out=vaT[hb][:, kc0:kc0 + kn], in_=SC["VA"][hp * 128:(hp + 1) * 128, kc0:kc0 + kn]), reads=[SCB["VA"]], writes=[bv[hb]])
                    if s == 0:
                        S.op(DVE, "tensor_tensor", dict(out=vaT[hb][:, 0:PADK], in0=vaT[hb][:, 0:PADK], in1=vrow[:, 0:PADK], op=ALU.mult), reads=[bvrow, bv[hb]], writes=[bv[hb]])
                        S.op(DVE, "tensor_tensor", dict(out=vaT[hb][:, PADK + sq:sext], in0=vaT[hb][:, PADK + sq:sext], in1=vrow[:, PADK:2 * PADK], op=ALU.mult), reads=[bvrow, bv[hb]], writes=[bv[hb]])

                def load_pair(hp):
                    hb = hp % 2
                    if s != 0:
                        S.op(POOL, "memset", dict(ap=kaT[hb][:, 0:PADK], constant=0.0), writes=[bk[hb]])
                        S.op(POOL, "memset", dict(ap=kaT[hb][:, PADK + sq:sext], constant=0.0), writes=[bk[hb]])
                    for hl in range(2):
                        h = 2 * hp + hl
                        S.dma("dma_start", dict(out=kaT[hb][hl * 64:hl * 64 + 48, kc0:kc0 + kn], in_=SC["KA"][hp * 128 + hl * 64 + 16:hp * 128 + hl * 64 + 64, kc0:kc0 + kn]), reads=[SCB["KA"]], writes=[bk[hb]])
                        S.dma("dma_start", dict(out=kaT[hb][hl * 64 + 48:hl * 64 + 56, kc0:kc0 + kn], in_=SC["KAr"][h * 8:h * 8 + 8, kc0:kc0 + kn]), reads=[SCB["KAr"]], writes=[bk[hb]])
                        S.dma("dma_start", dict(out=kaT[hb][hl * 64 + 56:hl * 64 + 64, kc0:kc0 + kn], in_=SC["KAr"][64 + h * 8:64 + h * 8 + 8, kc0:kc0 + kn]), reads=[SCB["KAr"]], writes=[bk[hb]])
                        S.dma("dma_start", dict(out=qz[hb][hl][hl * 64:hl * 64 + 48, :], in_=SC["QA"][hp * 128 + hl * 64 + 16:hp * 128 + hl * 64 + 64, 0:sq]), reads=[SCB["QA"]], writes=[bq[hb]])
                        S.dma("dma_start", dict(out=qz[hb][hl][hl * 64 + 48:hl * 64 + 56, :], in_=SC["QAr"][h * 8:h * 8 + 8, 0:sq]), reads=[SCB["QAr"]], writes=[bq[hb]])
                        S.dma("dma_start", dict(out=qz[hb][hl][hl * 64 + 56:hl * 64 + 64, :], in_=SC["QAr"][64 + h * 8:64 + h * 8 + 8, 0:sq]), reads=[SCB["QAr"]], writes=[bq[hb]])

                def tile_geom(tile):
                    (pi, dil, r, t, nblk, je0) = tile
                    e0 = r + dil * (je0 + 128 * t)
                    ksl = slice(e0, e0 + dil * 127 + 1, dil)
                    mlo, mhi = max(t - 1, 0), min(t, nblk - 1)
                    nq = (mhi - mlo + 1) * 128
                    qs = r + dil * 128 * mlo
                    qsl = slice(qs, qs + dil * (nq - 1) + 1, dil)
                    boff = 0 if t >= 1 else 128
                    return ksl, qsl, nq, boff

                pairs = list(range(4)) if lvl >= 4 else []
                if pairs:
                    load_pair(0)
                    load_v(0)
                for hp in pairs:
                    hb = hp % 2
                    if hp + 1 < 4:
                        load_pair(hp + 1)
                    S.dma("dma_start", dict(out=gaT[hb][:], in_=SC["GA"][hp * 128:(hp + 1) * 128, 0:sq]), reads=[SCB["GA"]], writes=[bg[hb]])
                    S.op(POOL, "memset", dict(ap=acc[:], constant=0.0), writes=[bacc])
                    for t4 in range(0, nt, 8):
                        n4 = min(8, nt - t4)
                        o_t, o_b = opsr.next()
                        pv = o_t[:, 0, :].bitcast(BF16).rearrange("p (a b) -> p a b", b=128)
                        for j in range(n4):
                            ksl = tile_geom(tiles[t4 + j])[0]
                            S.op(PE, "transpose", dict(out=pv[:, j, :], in_=vaT[hb][:, ksl], identity=ident[:]), reads=[bv[hb], bconst], writes=[o_b])
                        eng = ACT if (t4 // 8) % 2 == 0 else DVE
                        if eng == ACT:
                            S.op(ACT, "activation", dict(out=vt_all[:, t4:t4 + n4, :, 0:64], in_=pv[:, 0:n4, :].rearrange("p a (h d) -> p a h d", h=2), func=AF.Copy), reads=[o_b], writes=[bvta])
                        else:
                            S.op(DVE, "tensor_copy", dict(out=vt_all[:, t4:t4 + n4, :, 0:64], in_=pv[:, 0:n4, :].rearrange("p a (h d) -> p a h d", h=2)), reads=[o_b], writes=[bvta])
                    for hl in range(2):
                        S.op(DVE, "tensor_copy", dict(out=vt_all[:, :, hl, 64], in_=am[:, 0:nt]), reads=[bconst], writes=[bvta])
                    if hp + 1 < 4:
                        load_v(hp + 1)
                    units = [list(range(u, min(u + 2, nt))) for u in range(0, nt, 2)]

                    def scores(unit):
                        sc_t, sc_b = scA.next()
                        geo = []
                        for j, ti in enumerate(unit):
                            ksl, qsl, nq, boff = tile_geom(tiles[ti])
                            geo.append((ti, qsl, nq))
                            for hl in range(2):
                                S.op(PE, "matmul", dict(out=sc_t[:, hl, j * 256:j * 256 + nq], lhsT=kaT[hb][:, ksl], rhs=qz[hb][hl][:, qsl], start=True, stop=False), reads=[bk[hb], bq[hb]], writes=[sc_b])
                                S.op(PE, "matmul", dict(out=sc_t[:, hl, j * 256:j * 256 + nq], lhsT=ident[:], rhs=bandneg[:, boff:boff + nq], start=False, stop=True), reads=[bconst], writes=[sc_b])
                        p_t, p_b = Pt.next()
                        if len(unit) == 2 and all(g_[2] == 256 for g_ in geo):
                            S.op(ACT, "activation", dict(out=p_t[:], in_=sc_t[:], func=AF.Exp, scale=0.125), reads=[sc_b], writes=[p_b])
                        else:
                            for j, (ti, qsl, nq) in enumerate(geo):
                                S.op(ACT, "activation", dict(out=p_t[:, :, j * 256:j * 256 + nq], in_=sc_t[:, :, j * 256:j * 256 + nq], func=AF.Exp, scale=0.125), reads=[sc_b], writes=[p_b])
                        return geo, p_t, p_b

                    def av(geo, p_t, p_b):
                        o_t, o_b = opsr.next()
                        for j, (ti, qsl, nq) in enumerate(geo):
                            for hl in range(2):
                                for bi in range(nq // 128):
                                    c = j * 256 + bi * 128
                                    S.op(PE, "matmul", dict(out=o_t[0:65, hl, c:c + 128], lhsT=vt_all[:, ti, hl, :], rhs=p_t[:, hl, c:c + 128], start=True, stop=True), reads=[bvta, p_b], writes=[o_b])
                        for j, (ti, qsl, nq) in enumerate(geo):
                            S.op(DVE, "tensor_tensor", dict(out=acc[:, :, qsl], in0=o_t[0:65, :, j * 256:j * 256 + nq], in1=acc[:, :, qsl], op=ALU.add), reads=[o_b, bacc], writes=[bacc])

                    inflight = {0: scores(units[0])}
                    for u in range(len(units)):
                        if u + 1 < len(units):
                            inflight[u + 1] = scores(units[u + 1])
                        if u >= 1:
                            av(*inflight.pop(u - 1))
                    av(*inflight.pop(len(units) - 1))
                    for qc in range(sq // 512):
                        q0 = qc * 512
                        d_t, d_b = opsr.next()
                        dv = d_t[:, 0, :]
                        nv = d_t[:, 1, :]
                        for h2 in range(2):
                            qa_, qb_ = q0 + h2 * 256, q0 + (h2 + 1) * 256
                            for hl in range(2):
                                S.op(PE, "matmul", dict(out=dv[:, h2 * 256:(h2 + 1) * 256], lhsT=sels[:, hl, :], rhs=acc[:, hl, qa_:qb_], start=(hl == 0), stop=(hl == 1)), reads=[bacc, bconst], writes=[d_b])
                        for h2 in range(2):
                            qa_, qb_ = q0 + h2 * 256, q0 + (h2 + 1) * 256
                            for hl in range(2):
                                S.op(PE, "matmul", dict(out=nv[:, h2 * 256:(h2 + 1) * 256], lhsT=sels[:, 2 + hl, :], rhs=acc[:, hl, qa_:qb_], start=(hl == 0), stop=(hl == 1)), reads=[bacc, bconst], writes=[d_b])
                        rc_t, rc_b = recr.next()
                        S.op(DVE, "reciprocal", dict(out=rc_t[:], in_=dv), reads=[d_b], writes=[rc_b])
                        th_t, th_b = thr.next()
                        S.op(ACT, "activation", dict(out=th_t[:], in_=gaT[hb][:, q0:q0 + 512], func=AF.Tanh, scale=0.5), reads=[bg[hb]], writes=[th_b])
                        sg_t, sg_b = sgr.next()
                        S.op(DVE, "scalar_tensor_tensor", dict(out=sg_t[:], in0=th_t[:], scalar=1.0, in1=gaT[hb][:, q0:q0 + 512], op0=ALU.add, op1=ALU.mult), reads=[th_b, bg[hb]], writes=[sg_b])
                        tn_t, tn_b = tnr.next()
                        S.op(DVE, "tensor_tensor", dict(out=tn_t[:], in0=nv, in1=rc_t[:], op=ALU.mult), reads=[d_b, rc_b], writes=[tn_b])
                        y_t, y_b = yTr.next()
                        S.op(DVE, "tensor_tensor", dict(out=y_t[:], in0=tn_t[:], in1=sg_t[:], op=ALU.mult), reads=[tn_b, sg_b], writes=[y_b])
                        S.dma("dma_start", dict(out=SC["YM"][hp * 128:(hp + 1) * 128, q0:q0 + 512], in_=y_t[:]), reads=[y_b], writes=[SCB["YM"]])
                S.barrier()

            with contextlib.ExitStack() as p3:
                ymr = Ring([sb(p3, nm("ym"), [128, 8, 512], BF16) for _ in range(2)])
                xr = Ring([sb(p3, nm("x3"), [128, D], F32) for _ in range(8)])
                tmr = Ring([sb(p3, nm("tm3"), [128, D], F32) for _ in range(3)])
                outr = Ring([sb(p3, nm("o3"), [128, D], F32) for _ in range(4)])
                st3 = Ring([sb(p3, nm("st3"), [128, 6], F32) for _ in range(8)])
                junk3 = sb(p3, nm("junk3"), [128, 512], F32)
                bj3 = Buf()
                po = Ring([ps(p3, nm("po"), [128, 2, 512], F32) for _ in range(4)])
                nq3 = (sq // 512) if lvl >= 5 else 0

                def p3_loads(qc):
                    q0 = qc * 512
                    ym_t, ym_b = ymr.next()
                    S.dma("dma_start", dict(out=ym_t[:], in_=SC["YM"][:, q0:q0 + 512].rearrange("(kc p) t -> p kc t", p=128)), reads=[SCB["YM"]], writes=[ym_b])
                    xs = []
                    for i in range(4):
                        row = q0 + i * 128
                        x_t, x_b = xr.next()
                        S.dma("dma_start", dict(out=x_t[:], in_=X[job["xrow"] + row:job["xrow"] + row + 128, :]), writes=[x_b])
                        xs.append((x_t, x_b))
                    return ym_t, ym_b, xs

                ld3 = {}
                if nq3:
                    ld3[0] = p3_loads(0)
                for qc in range(nq3):
                    q0 = qc * 512
                    ym_t, ym_b, xs = ld3.pop(qc)
                    if qc + 1 < nq3:
                        ld3[qc + 1] = p3_loads(qc + 1)
                    for i in range(4):
                        row = q0 + i * 128
                        x_t, x_b = xs[i]
                        po_t, po_b = po.next()
                        for hf in range(2):
                            for kc in range(8):
                                S.op(PE, "matmul", dict(out=po_t[:, hf, :], lhsT=ym_t[:, kc, i * 128:(i + 1) * 128], rhs=wout[:, kc, hf * 512:(hf + 1) * 512], start=(kc == 0), stop=(kc == 7)), reads=[ym_b, bwout], writes=[po_b])
                        s_t, s_b = st3.next()
                        for hf in range(2):
                            S.op(ACT, "activation", dict(out=junk3[:], in_=po_t[:, hf, :], func=AF.Square, accum_out=s_t[:, hf:hf + 1]), reads=[po_b], writes=[bj3, s_b])
                        S.op(DVE, "tensor_tensor", dict(out=s_t[:, 2:3], in0=s_t[:, 0:1], in1=s_t[:, 1:2], op=ALU.add), reads=[s_b], writes=[s_b])
                        S.op(DVE, "tensor_scalar", dict(out=s_t[:, 3:4], in0=s_t[:, 2:3], scalar1=1.0 / D, scalar2=EPS, op0=ALU.mult, op1=ALU.add), reads=[s_b], writes=[s_b])
                        S.op(POOL, "tensor_tensor", dict(out=s_t[:, 4:5], in0=s_t[:, 3:4], in1=mhalf[:], op=ALU.pow), reads=[s_b, bconst], writes=[s_b])
                        tm_t, tm_b = tmr.next()
                        for hf in range(2):
                            S.op(DVE, "scalar_tensor_tensor", dict(out=tm_t[:, hf * 512:(hf + 1) * 512], in0=po_t[:, hf, :], scalar=s_t[:, 4:5], in1=gg[s][:, hf * 512:(hf + 1) * 512], op0=ALU.mult, op1=ALU.mult), reads=[po_b, s_b, bgg[s]], writes=[tm_b])
                        o_t, o_b = outr.next()
                        S.op(DVE, "tensor_tensor", dict(out=o_t[:], in0=tm_t[:], in1=x_t[:], op=ALU.add), reads=[tm_b, x_b], writes=[o_b])
                        yrow = job["yrow"] + row
                        out_dmas.append(S.dma("dma_start", dict(out=Y[yrow:yrow + 128, :], in_=o_t[:]), reads=[o_b], writes=[Buf()]))
                S.barrier()

        S.run(final_wait=out_dmas)
    return nc


_NC_CACHE = {}


def _rope_tables(pos):
    half = 8
    inv = (np.float32(500000.0) ** (-np.arange(half, dtype=np.float32) / np.float32(half))).astype(np.float32)
    ang = pos.astype(np.float32)[None, :] * inv[:, None]
    cos = np.cos(ang).astype(np.float32)
    sin = np.sin(ang).astype(np.float32)
    idx = np.arange(128) % 8
    C = cos[idx]
    Sg = sin[idx].copy()
    Sg[:64] *= -1.0
    return np.ascontiguousarray(C), np.ascontiguousarray(Sg)


def _amask(valid_ext, sq):
    tiles = amask_tiles(sq)
    m = np.zeros((128, len(tiles)), np.float32)
    p = np.arange(128)
    for ti, (pi, dil, r, t, nblk, je0) in enumerate(tiles):
        e = r + dil * (je0 + 128 * t + p)
        m[:, ti] = valid_ext[e]
    return m


def prep_inputs(x_prompt, x_sample, c_prompt, c_sample, w_in, w_out, g_pre, g_post,
                w_ada, b_ada, lam_q1, lam_k1, lam_q2, lam_k2, g_sub):
    f32 = np.float32
    x_prompt = np.asarray(x_prompt, f32)
    x_sample = np.asarray(x_sample, f32)
    c_prompt = np.asarray(c_prompt, f32)
    c_sample = np.asarray(c_sample, f32)
    w_in0 = np.ascontiguousarray(np.asarray(w_in, f32)[0])
    w_out0 = np.ascontiguousarray(np.asarray(w_out, f32)[0])
    w_ada0 = np.ascontiguousarray(np.asarray(w_ada, f32)[0])
    b_ada0 = np.asarray(b_ada, f32)[0]
    g_pre0 = np.asarray(g_pre, f32)[0]
    g_post0 = np.asarray(g_post, f32)[0]
    g_sub0 = np.asarray(g_sub, f32)[0]

    offs = dict(qa=0, ka=512, qb=2048, kb=2560)
    w_rot = np.zeros((D, 4, 192), f32)
    for ti, nme in enumerate(("qa", "ka", "qb", "kb")):
        x1 = np.array([offs[nme] + h * 64 + i for h in range(8) for i in range(8)])
        x2 = x1 + 8
        w_rot[:, ti, 0:64] = w_in0[:, x1]
        w_rot[:, ti, 64:128] = w_in0[:, x2]
        w_rot[:, ti, 128:192] = w_in0[:, x1]
    w_rot = np.ascontiguousarray(w_rot.reshape(D, 768))

    g_pre_t = np.ascontiguousarray(g_pre0.reshape(8, 128).T)
    b_ada_t = np.ascontiguousarray(b_ada0[:2048].reshape(16, 128).T)
    b_gate = np.ascontiguousarray(b_ada0[2048:3072].reshape(1, D))
    g_post_r = np.ascontiguousarray(g_post0.reshape(1, D))
    g_sub_t = np.ascontiguousarray(g_sub0.reshape(128, 1))
    lam_v = np.ascontiguousarray(np.stack([np.asarray(v, f32)[0] for v in (lam_q1, lam_k1, lam_q2, lam_k2)], 0))
    ident = np.eye(128, dtype=f32)
    p = np.arange(128)[:, None]
    f = np.arange(128)[None, :]
    band = np.concatenate([(p <= f), (p >= f)], axis=1).astype(f32)
    band = np.ascontiguousarray(np.concatenate([band, band], axis=1))
    sels = np.zeros((65, 4, 128), f32)
    sels[64, 0, 0:64] = 2.0
    sels[64, 1, 64:128] = 2.0
    sels[np.arange(64), 2, np.arange(64)] = 1.0
    sels[np.arange(64), 3, 64 + np.arange(64)] = 1.0
    sels = np.ascontiguousarray(sels.reshape(65, 512))
    valid1 = np.zeros(DSEQ + 2 * PADK, f32)
    valid1[PADK:PADK + DSEQ] = 1.0
    am1 = _amask(valid1, DSEQ)

    in_maps = []
    for c in range(8):
        psq, hf = c // 2, c % 2
        q0 = hf * HALF
        o0 = (1 - hf) * HALF
        xa = np.zeros((NROWS, D), f32)
        pos = np.zeros(NROWS, np.int64)
        xa[ROW_OWN:ROW_OWN + HALF] = x_prompt[psq, q0:q0 + HALF]
        pos[ROW_OWN:ROW_OWN + HALF] = np.arange(q0, q0 + HALF)
        xa[ROW_OTHER:ROW_OTHER + HALF] = x_prompt[psq, o0:o0 + HALF]
        pos[ROW_OTHER:ROW_OTHER + HALF] = np.arange(o0, o0 + HALF)
        hpos = np.concatenate([np.arange(q0 - PADK, q0), np.arange(q0 + HALF, q0 + HALF + PADK)])
        hval = (hpos >= 0) & (hpos < SEQ)
        xa[ROW_HALO:ROW_HALO + 2 * PADK][hval] = x_prompt[psq, hpos[hval]]
        pos[ROW_HALO:ROW_HALO + 2 * PADK] = np.clip(hpos, 0, SEQ - 1)
        xa[ROW_S0:ROW_S0 + DSEQ] = x_sample[2 * c]
        pos[ROW_S0:ROW_S0 + DSEQ] = np.arange(DSEQ)
        xa[ROW_S1:ROW_S1 + DSEQ] = x_sample[2 * c + 1]
        pos[ROW_S1:ROW_S1 + DSEQ] = np.arange(DSEQ)
        C, Sg = _rope_tables(pos)
        cs = np.stack([c_prompt[psq], c_sample[2 * c], c_sample[2 * c + 1]], axis=-1)
        c_t = np.ascontiguousarray(cs.reshape(8, 128, 3).transpose(1, 0, 2).reshape(128, 24))
        valid0 = np.ones(HALF + 2 * PADK, f32)
        valid0[0:PADK] = hval[:PADK]
        valid0[PADK + HALF:] = hval[PADK:]
        am0 = _amask(valid0, HALF)
        vrow = np.ascontiguousarray(np.broadcast_to(hval.astype(f32)[None, :], (128, 2 * PADK)))
        in_maps.append({
            "x_all": xa, "rope_c": C, "rope_s": Sg, "c_t": c_t, "w_in": w_in0, "w_rot": w_rot,
            "w_out": w_out0, "w_ada": w_ada0, "g_pre_t": g_pre_t, "b_ada_t": b_ada_t, "b_gate": b_gate,
            "g_post": g_post_r, "g_sub_t": g_sub_t, "lam_v": lam_v, "ident": ident, "band": band,
            "sels": sels, "amask0": am0, "amask1": am1, "vrow": vrow,
        })

    return in_maps


def kernel(**inputs):
    f32 = np.float32
    in_maps = prep_inputs(**inputs)
    if "nc" not in _NC_CACHE:
        _NC_CACHE["nc"] = build_program()
    nc = _NC_CACHE["nc"]
    res = run_bass_kernel_spmd(nc, in_maps, core_ids=list(range(8)))
    y_prompt = np.zeros((4, SEQ, D), f32)
    y_sample = np.zeros((16, DSEQ, D), f32)
    for c in range(8):
        y = np.asarray(res.results[c]["y_all"], f32)
        psq, hf = c // 2, c % 2
        y_prompt[psq, hf * HALF:(hf + 1) * HALF] = y[0:HALF]
        y_sample[2 * c] = y[HALF:HALF + DSEQ]
        y_sample[2 * c + 1] = y[HALF + DSEQ:HALF + 2 * DSEQ]
    return (y_prompt, y_sample)
```

```python
import contextlib
import numpy as np
import concourse.bass as bass
import concourse.mybir as mybir
from concourse.bass_utils import run_bass_kernel_spmd

F32 = mybir.dt.float32
BF16 = mybir.dt.bfloat16
AF = mybir.ActivationFunctionType
ALU = mybir.AluOpType

PE, ACT, DVE, POOL, SP = "tensor", "scalar", "vector", "gpsimd", "sync"
ENGINES = (PE, ACT, DVE, POOL, SP)

D = 1024
SEQ = 8192
DSEQ = 2048
HALF = 4096
PADK = 1024
PATTERNS = ((128, 1), (512, 4), (2048, 16))
EPS = 1e-6
LAM_INIT = 0.2
NROWS = HALF + HALF + 2 * PADK + 2 * DSEQ
ROW_OWN, ROW_OTHER, ROW_HALO, ROW_S0, ROW_S1 = 0, 4096, 8192, 10240, 12288
NOUT = HALF + 2 * DSEQ
WCOLS = 4096 + 4 * 192


class Buf:
    __slots__ = ("last_write", "reads")

    def __init__(self):
        self.last_write = None
        self.reads = []


class Rec:
    __slots__ = ("eng", "fn", "deps", "is_dma", "signal", "count", "dsem", "dval", "epoch")

    def __init__(self, eng, fn, is_dma):
        self.eng, self.fn, self.is_dma = eng, fn, is_dma
        self.deps = []
        self.signal = False
        self.count = 0
        self.dsem = None
        self.dval = 0
        self.epoch = 0


class Sched:
    EPOCH_MAX = 30000

    def __init__(self, nc, n_dma_sems=20):
        self.nc = nc
        self.q = {e: [] for e in ENGINES}
        self.n_dma_sems = n_dma_sems
        self.dma_rr = {e: 0 for e in ENGINES}
        self.dma_last = {}
        self.last_compute = {}

    def op(self, eng, meth, kw, reads=(), writes=(), dma=False, extra=()):
        r = Rec(eng, (meth, kw), dma)
        deps = list(extra)
        for b in reads:
            if b.last_write is not None:
                deps.append(b.last_write)
        for b in writes:
            if b.last_write is not None:
                deps.append(b.last_write)
            deps.extend(b.reads)
        if dma:
            slot = self.dma_rr[eng] % self.n_dma_sems
            self.dma_rr[eng] += 1
            prev = self.dma_last.get((eng, slot))
            if prev is not None:
                deps.append(prev)
            self.dma_last[(eng, slot)] = r
            r.dsem = (eng, slot)
        seen = set()
        for d in deps:
            if d is r or id(d) in seen:
                continue
            if (not d.is_dma) and (not dma) and d.eng == PE and eng == PE:
                continue
            seen.add(id(d))
            r.deps.append(d)
        for b in reads:
            b.reads.append(r)
        for b in writes:
            b.last_write = r
            b.reads = []
        self.q[eng].append(r)
        if not dma:
            self.last_compute[eng] = r
        return r

    def dma(self, meth, kw, reads=(), writes=(), eng=SP):
        return self.op(eng, meth, kw, reads, writes, dma=True)

    def barrier(self):
        deps = list(self.last_compute.values()) + list(self.dma_last.values())
        for e in ENGINES:
            r = Rec(e, None, False)
            for d in deps:
                if (not d.is_dma) and d.eng == e:
                    continue
                r.deps.append(d)
            self.q[e].append(r)

    def run(self, final_wait=()):
        nc = self.nc
        for e in ENGINES:
            for r in self.q[e]:
                for d in r.deps:
                    if not d.is_dma:
                        d.signal = True
        n_epochs = {}
        for e in ENGINES:
            cnt, ep = 0, 0
            for r in self.q[e]:
                if r.is_dma or r.fn is None:
                    continue
                if r.signal:
                    if cnt >= self.EPOCH_MAX:
                        ep += 1
                        cnt = 0
                    cnt += 1
                    r.count = cnt
                    r.epoch = ep
            n_epochs[e] = ep + 1
        dcount = {}
        for e in ENGINES:
            for r in self.q[e]:
                if r.is_dma:
                    dcount[r.dsem] = dcount.get(r.dsem, 0) + 16
                    r.dval = dcount[r.dsem]
        with contextlib.ExitStack() as st:
            csem = {}
            for e in ENGINES:
                for ep in range(n_epochs[e]):
                    if any((not r.is_dma) and r.signal and r.epoch == ep for r in self.q[e]):
                        csem[(e, ep)] = st.enter_context(nc.semaphore(f"c_{e}_{ep}"))
            dsem = {}
            for key in dcount:
                dsem[key] = st.enter_context(nc.semaphore(f"d_{key[0]}_{key[1]}"))
            block = st.enter_context(nc.Block())

            def emit(e, engine):
                waited = {}

                def do_wait(d):
                    if d.is_dma:
                        s, v, k = dsem[d.dsem], d.dval, ("d",) + d.dsem
                    else:
                        s, v, k = csem[(d.eng, d.epoch)], d.count, ("c", d.eng, d.epoch)
                    if waited.get(k, 0) >= v:
                        return
                    waited[k] = v
                    engine.wait_ge(s, v)

                for r in self.q[e]:
                    for d in r.deps:
                        do_wait(d)
                    if r.fn is None:
                        continue
                    ins = getattr(engine, r.fn[0])(**r.fn[1])
                    if r.is_dma:
                        ins.then_inc(dsem[r.dsem], 16)
                    elif r.signal:
                        ins.then_inc(csem[(e, r.epoch)], 1)
                if e == SP:
                    for d in final_wait:
                        do_wait(d)

            @block.tensor
            def _(eng):
                emit(PE, eng)

            @block.scalar
            def _(eng):
                emit(ACT, eng)

            @block.vector
            def _(eng):
                emit(DVE, eng)

            @block.gpsimd
            def _(eng):
                emit(POOL, eng)

            @block.sync
            def _(eng):
                emit(SP, eng)


class Ring:
    def __init__(self, tiles):
        self.tiles = tiles
        self.bufs = [Buf() for _ in tiles]
        self.i = 0

    def next(self):
        k = self.i % len(self.tiles)
        self.i += 1
        return self.tiles[k], self.bufs[k]


def amask_tiles(sq):
    out = []
    for pi, (w, dil) in enumerate(PATTERNS):
        nblk = sq // dil // 128
        je0 = PADK // dil - 64
        for r in range(dil):
            for t in range(nblk + 1):
                out.append((pi, dil, r, t, nblk, je0))
    return out


def build_program(stop_after=None, jobs=(0, 1, 2)):
    nc = bass.Bass("TRN2", target_bir_lowering=False)
    lvl = {'p0': 0, 'w': 1, 'p1': 2, '2b': 3, '2a': 4, None: 5}[stop_after]

    def din(name, shape, dt=F32):
        return nc.dram_tensor(name, list(shape), dt, kind="ExternalInput").ap()

    X = din("x_all", [NROWS, D])
    CT = din("rope_c", [128, NROWS])
    ST = din("rope_s", [128, NROWS])
    CTT = din("c_t", [128, 24])
    W_IN = din("w_in", [D, 4096])
    W_ROT = din("w_rot", [D, 768])
    W_OUT = din("w_out", [D, D])
    W_ADA = din("w_ada", [D, 3 * D])
    GPRE = din("g_pre_t", [128, 8])
    BADA = din("b_ada_t", [128, 16])
    BGATE = din("b_gate", [1, D])
    GPOST = din("g_post", [1, D])
    GSUB = din("g_sub_t", [128, 1])
    LAMV = din("lam_v", [4, 64])
    IDENT = din("ident", [128, 128])
    BAND = din("band", [128, 512])
    SELS = din("sels", [65, 512])
    AM0 = din("amask0", [128, 117])
    VROW = din("vrow", [128, 2 * PADK])
    AM1 = din("amask1", [128, 69])
    Y = nc.dram_tensor("y_all", [NOUT, D], F32, kind="ExternalOutput").ap()

    JOBS = [
        dict(s=0, sq=HALF, skv=SEQ, sext=HALF + 2 * PADK, xrow=ROW_OWN, yrow=0, am=AM0),
        dict(s=1, sq=DSEQ, skv=DSEQ, sext=DSEQ + 2 * PADK, xrow=ROW_S0, yrow=HALF, am=AM1),
        dict(s=2, sq=DSEQ, skv=DSEQ, sext=DSEQ + 2 * PADK, xrow=ROW_S1, yrow=HALF + DSEQ, am=AM1),
    ]
    def scr(name, rows, cols):
        return nc.dram_tensor(name, [rows, cols], BF16, kind="Internal").ap()

    SC = dict(
        QA=scr("s_qa", 512, HALF), QAr=scr("s_qar", 128, HALF), GA=scr("s_ga", 512, HALF),
        KA=scr("s_ka", 512, HALF + 2 * PADK), KAr=scr("s_kar", 128, HALF + 2 * PADK),
        VA=scr("s_va", 512, HALF + 2 * PADK),
        QB=scr("s_qb", 512, HALF), QBr=scr("s_qbr", 128, HALF), GB=scr("s_gb", 512, HALF),
        KB=scr("s_kb", 512, SEQ), KBr=scr("s_kbr", 128, SEQ), VB=scr("s_vb", 512, SEQ),
        YM=scr("s_ym", 1024, HALF),
    )
    SCB = {k: Buf() for k in SC}
    WS = scr("s_w", D, WCOLS)
    bWS = Buf()

    S = Sched(nc)
    out_dmas = []
    top = contextlib.ExitStack()
    with top:
        def sb(st, name, shape, dt):
            return st.enter_context(nc.sbuf_tensor("sb_" + name, list(shape), dt))

        def ps(st, name, shape, dt):
            return st.enter_context(nc.psum_tensor("ps_" + name, list(shape), dt))

        uid = [0]

        def nm(p):
            uid[0] += 1
            return f"{p}{uid[0]}"

        ident_f = sb(top, "ident_f", [128, 128], F32)
        ident = sb(top, "ident", [128, 128], BF16)
        band_f = sb(top, "band_f", [128, 512], F32)
        bandneg = sb(top, "bandneg", [128, 256], BF16)
        sels = sb(top, "sels", [65, 4, 128], F32)
        am0 = sb(top, "am0", [128, 117], F32)
        am1 = sb(top, "am1", [128, 69], F32)
        ones_f = sb(top, "ones_f", [128, 128], F32)
        mhalf = sb(top, "mhalf", [128, 1], F32)
        zer = sb(top, "zer", [1, 512], BF16)
        gsub2 = sb(top, "gsub2", [128, 1], F32)
        neglam = sb(top, "neglam", [128, 1], F32)
        lamt = sb(top, "lamt", [128, 4, 64], F32)
        lamj = sb(top, "lamj", [128, 64], F32)
        lams = sb(top, "lams", [128, 4], F32)
        gg = [sb(top, f"gg{s}", [128, D], F32) for s in range(3)]
        gsT = sb(top, "gsT", [128, 8, 4], F32)
        shT = sb(top, "shT", [128, 8, 4], F32)
        wout = sb(top, "wout", [128, 8, D], BF16)
        bconst = Buf()
        bgg = [Buf() for _ in range(3)]
        bmod = Buf()
        bwout = Buf()

        biasA = sb(top, "biasA", [128, 40, 4], F32)
        bbiasA = Buf()
        S.dma("dma_start", dict(out=ident_f[:], in_=IDENT[:, :]), writes=[bconst])
        S.dma("dma_start", dict(out=band_f[:], in_=BAND[:, :]), writes=[bconst])
        S.dma("dma_start", dict(out=sels[:].rearrange("p a b -> p (a b)"), in_=SELS[:, :]), writes=[bconst])
        S.dma("dma_start", dict(out=am0[:], in_=AM0[:, :]), writes=[bconst])
        S.dma("dma_start", dict(out=am1[:], in_=AM1[:, :]), writes=[bconst])
        S.dma("dma_start", dict(out=gsub2[:], in_=GSUB[:, :]), writes=[bconst])
        for i in range(4):
            S.dma("dma_start", dict(out=lamt[:, i, :], in_=LAMV[i:i + 1, :].broadcast_to([128, 64])), writes=[bconst])
        S.op(DVE, "tensor_copy", dict(out=ident[:], in_=ident_f[:]), reads=[bconst], writes=[bconst])
        S.op(DVE, "tensor_scalar", dict(out=bandneg[:], in0=band_f[:, 0:256], scalar1=30000.0, scalar2=-30000.0, op0=ALU.mult, op1=ALU.add), reads=[bconst], writes=[bconst])
        S.op(POOL, "memset", dict(ap=ones_f[:], constant=1.0), writes=[bconst])
        S.op(POOL, "memset", dict(ap=mhalf[:], constant=-0.5), writes=[bconst])
        S.op(POOL, "memset", dict(ap=zer[:], constant=0.0), writes=[bconst])
        S.op(DVE, "tensor_scalar", dict(out=gsub2[:], in0=gsub2[:], scalar1=(1.0 - LAM_INIT) * 0.5, scalar2=None, op0=ALU.mult), reads=[bconst], writes=[bconst])
        for i in range(2):
            S.op(DVE, "tensor_tensor", dict(out=lamj[:], in0=lamt[:, 2 * i, :], in1=lamt[:, 2 * i + 1, :], op=ALU.mult), reads=[bconst], writes=[bconst])
            S.op(ACT, "activation", dict(out=lamj[:], in_=lamj[:], func=AF.Identity, accum_out=lams[:, i:i + 1]), reads=[bconst], writes=[bconst])
        S.op(ACT, "activation", dict(out=lams[:, 2:4], in_=lams[:, 0:2], func=AF.Exp), reads=[bconst], writes=[bconst])
        S.op(DVE, "tensor_tensor", dict(out=neglam[:], in0=lams[:, 3:4], in1=lams[:, 2:3], op=ALU.subtract), reads=[bconst], writes=[bconst])
        S.op(DVE, "tensor_scalar", dict(out=neglam[:], in0=neglam[:], scalar1=-LAM_INIT, scalar2=None, op0=ALU.add), reads=[bconst], writes=[bconst])

        with contextlib.ExitStack() as p0:
            stg = [sb(p0, f"stg0_{i}", [128, 8, 512], F32) for i in range(2)]
            bstg = [Buf() for _ in range(2)]
            ct = sb(p0, "ct", [128, 24], F32)
            th = sb(p0, "th0", [128, 24], F32)
            scT = sb(p0, "scT", [128, 8, 4], F32)
            screp = sb(p0, "screp", [128, 24, 128], F32)
            gpre = sb(p0, "gpre", [128, 8], F32)
            bada = sb(p0, "bada", [128, 16], F32)
            bgate = sb(p0, "bgate", [128, D], F32)
            gpost = sb(p0, "gpost", [128, D], F32)
            tmpm = sb(p0, "tmpm", [128, 8], F32)
            tmpg = sb(p0, "tmpg", [128, 512], F32)
            pmod = ps(p0, "pmod", [128, 16, 4], F32)
            pg = [ps(p0, f"pg{i}", [128, 512], F32) for i in range(3)]
            b0 = Buf()
            bpm = Buf()
            bpg = [Buf() for _ in range(3)]
            S.dma("dma_start", dict(out=ct[:], in_=CTT[:, :]), writes=[b0])
            S.dma("dma_start", dict(out=gpre[:], in_=GPRE[:, :]), writes=[b0])
            S.dma("dma_start", dict(out=bada[:], in_=BADA[:, :]), writes=[b0])
            S.dma("dma_start", dict(out=bgate[:], in_=BGATE[0:1, :].broadcast_to([128, D])), writes=[b0])
            S.dma("dma_start", dict(out=gpost[:], in_=GPOST[0:1, :].broadcast_to([128, D])), writes=[b0])
            S.op(ACT, "activation", dict(out=th[:], in_=ct[:], func=AF.Tanh, scale=0.5), reads=[b0], writes=[b0])
            S.op(DVE, "scalar_tensor_tensor", dict(out=th[:], in0=th[:], scalar=1.0, in1=ct[:], op0=ALU.add, op1=ALU.mult), reads=[b0], writes=[b0])
            S.op(POOL, "memset", dict(ap=scT[:], constant=0.0), writes=[b0])
            S.op(POOL, "memset", dict(ap=shT[:], constant=0.0), writes=[bmod])
            S.op(POOL, "memset", dict(ap=gsT[:], constant=0.0), writes=[bmod])
            S.op(DVE, "tensor_scalar", dict(out=scT[:, :, 0:3], in0=th[:].rearrange("p (a b) -> p a b", b=3), scalar1=0.5, scalar2=None, op0=ALU.mult), reads=[b0], writes=[b0])
            for i in range(24):
                kc, s = divmod(i, 3)
                S.op(DVE, "tensor_scalar", dict(out=screp[:, i, :], in0=ones_f[:], scalar1=scT[:, kc, s:s + 1], scalar2=None, op0=ALU.mult), reads=[b0, bconst], writes=[b0])
            for pc in range(6):
                k = pc % 2
                S.dma("dma_start", dict(out=stg[k][:], in_=W_ADA[:, pc * 512:(pc + 1) * 512].rearrange("(kc p) n -> p kc n", p=128)), writes=[bstg[k]])
                if pc < 4:
                    for fb in range(4):
                        blk = pc * 4 + fb
                        for kc in range(8):
                            S.op(PE, "matmul", dict(out=pmod[:, blk, :], lhsT=stg[k][:, kc, fb * 128:(fb + 1) * 128], rhs=scT[:, kc, :], start=(kc == 0), stop=(kc == 7)), reads=[bstg[k], b0], writes=[bpm])
                else:
                    hf = pc - 4
                    for s in range(3):
                        for h2 in range(2):
                            for kc in range(8):
                                S.op(PE, "matmul", dict(out=pg[s][:, h2 * 256:(h2 + 1) * 256], lhsT=screp[:, kc * 3 + s, :], rhs=stg[k][:, kc, h2 * 256:(h2 + 1) * 256], start=(kc == 0), stop=(kc == 7)), reads=[bstg[k], b0], writes=[bpg[s]])
                        S.op(DVE, "tensor_tensor", dict(out=tmpg[:], in0=pg[s][:], in1=bgate[:, hf * 512:(hf + 1) * 512], op=ALU.add), reads=[bpg[s], b0], writes=[b0])
                        S.op(DVE, "tensor_tensor", dict(out=gg[s][:, hf * 512:(hf + 1) * 512], in0=tmpg[:], in1=gpost[:, hf * 512:(hf + 1) * 512], op=ALU.mult), reads=[b0], writes=[bgg[s]])
            for s in range(3):
                S.op(DVE, "tensor_tensor", dict(out=shT[:, :, s], in0=pmod[:, 0:8, s], in1=bada[:, 0:8], op=ALU.add), reads=[bpm, b0], writes=[bmod])
                S.op(DVE, "scalar_tensor_tensor", dict(out=tmpm[:], in0=pmod[:, 8:16, s], scalar=1.0, in1=bada[:, 8:16], op0=ALU.add, op1=ALU.add), reads=[bpm, b0], writes=[b0])
                S.op(DVE, "tensor_tensor", dict(out=gsT[:, :, s], in0=tmpm[:], in1=gpre[:], op=ALU.mult), reads=[b0], writes=[bmod])
            for pc in range(2):
                k = pc % 2
                S.dma("dma_start", dict(out=stg[k][:], in_=W_OUT[:, pc * 512:(pc + 1) * 512].rearrange("(kc p) n -> p kc n", p=128)), writes=[bstg[k]])
                S.op(DVE, "tensor_copy", dict(out=wout[:, :, pc * 512:(pc + 1) * 512], in_=stg[k][:]), reads=[bstg[k]], writes=[bwout])
            wbr = Ring([sb(p0, f"wbst{i}", [128, 8, 512], BF16) for i in range(2)])
            pb = ps(p0, "pb0", [128, 40, 4], F32)
            bpb = Buf()
            pieces = [(W_IN, pc * 512, 512, pc * 512) for pc in range(8)] + [(W_ROT, 0, 384, 4096), (W_ROT, 384, 384, 4096 + 384)]
            for pi_, (wsrc, c0, ncol, dcol) in enumerate(pieces):
                k = pi_ % 2
                S.dma("dma_start", dict(out=stg[k][:, :, 0:ncol], in_=wsrc[:, c0:c0 + ncol].rearrange("(kc p) n -> p kc n", p=128)), writes=[bstg[k]])
                wb_t, wb_b = wbr.next()
                S.op(ACT, "activation", dict(out=wb_t[:, 0:4, 0:ncol], in_=stg[k][:, 0:4, 0:ncol], func=AF.Copy), reads=[bstg[k]], writes=[wb_b])
                S.op(DVE, "tensor_copy", dict(out=wb_t[:, 4:8, 0:ncol], in_=stg[k][:, 4:8, 0:ncol]), reads=[bstg[k]], writes=[wb_b])
                S.dma("dma_start", dict(out=WS[:, dcol:dcol + ncol].rearrange("(kc p) n -> p kc n", p=128), in_=wb_t[:, :, 0:ncol]), reads=[wb_b], writes=[bWS])
                if pi_ < 8:
                    cols = [(fb * 128, pi_ * 4 + fb) for fb in range(4)]
                else:
                    t0 = (pi_ - 8) * 2
                    cols = [(0, 32 + 2 * t0), (64, 33 + 2 * t0), (192, 34 + 2 * t0), (256, 35 + 2 * t0)]
                for (co, bcol) in cols:
                    for kc in range(8):
                        S.op(PE, "matmul", dict(out=pb[:, bcol, :], lhsT=stg[k][:, kc, co:co + 128], rhs=shT[:, kc, :], start=(kc == 0), stop=(kc == 7)), reads=[bstg[k], bmod], writes=[bpb])
            S.op(DVE, "tensor_copy", dict(out=biasA[:], in_=pb[:]), reads=[bpb], writes=[bbiasA])
            S.barrier()

        for job in [JOBS[j_] for j_ in jobs] if stop_after != 'p0' else []:
            s, sq, skv, sext = job["s"], job["sq"], job["skv"], job["sext"]
            am = am0 if job["am"] is AM0 else am1
            with contextlib.ExitStack() as p1:
                wp = sb(p1, nm("wp"), [128, 8, WCOLS], BF16)
                WPIECE = 1216
                bwp_l = [Buf() for _ in range(-(-WCOLS // WPIECE))]

                def bwp_of(c0):
                    return bwp_l[c0 // WPIECE]
                gsrep = sb(p1, nm("gsrep"), [128, 8, 128], F32)
                bgsrep = Buf()
                if lvl >= 1:
                    for c0 in range(0, WCOLS, WPIECE):
                        c1 = min(c0 + WPIECE, WCOLS)
                        S.dma("dma_start", dict(out=wp[:, :, c0:c1], in_=WS[:, c0:c1].rearrange("(kc p) n -> p kc n", p=128)), reads=[bWS], writes=[bwp_of(c0)])
                    for kc in range(8):
                        S.op(DVE, "tensor_scalar", dict(out=gsrep[:, kc, :], in0=ones_f[:], scalar1=gsT[:, kc, s:s + 1], scalar2=None, op0=ALU.mult), reads=[bmod, bconst], writes=[bgsrep])
                biasT = biasA[:, :, s]
                bbias = bbiasA

                with contextlib.ExitStack() as pp:
                    xt = Ring([sb(pp, nm("xt"), [128, D], F32) for _ in range(4)])
                    junk = sb(pp, nm("junk"), [128, D], F32)
                    bjunk = Buf()
                    xn = Ring([sb(pp, nm("xn"), [128, D], F32) for _ in range(4)])
                    xnT = Ring([sb(pp, nm("xnT"), [128, 8, 512], BF16) for _ in range(2)])
                    stat = Ring([sb(pp, nm("stat"), [128, 4], F32) for _ in range(8)])
                    ost = Ring([sb(pp, nm("ost"), [128, 4, 512], BF16) for _ in range(3)])
                    cst = Ring([sb(pp, nm("cst"), [128, 2, 512], F32) for _ in range(3)])
                    t12 = Ring([sb(pp, nm("t12"), [128, 2, 512], F32) for _ in range(1)])
                    rot = Ring([sb(pp, nm("rot"), [128, 512], BF16) for _ in range(2)])
                    pT = Ring([ps(pp, nm("pT"), [128, D], F32) for _ in range(2)])
                    pacc = Ring([ps(pp, nm("pacc"), [128, 512], F32) for _ in range(4)])

                    G = dict(qa=(0, 0), ka=(512, 4), va=(1024, 8), ga=(1536, 12), qb=(2048, 16), kb=(2560, 20), vb=(3072, 24), gb=(3584, 28))
                    RT = dict(qa=0, ka=1, qb=2, kb=3)
                    segs = []
                    full_main = [("qa", "QA", 0), ("ka", "KA", PADK), ("va", "VA", PADK), ("ga", "GA", 0), ("qb", "QB", 0), ("kb", "KB", 0), ("vb", "VB", 0), ("gb", "GB", 0)]
                    full_rot = [("qa", "QAr", 0), ("ka", "KAr", PADK), ("qb", "QBr", 0), ("kb", "KBr", 0)]
                    segs.append((job["xrow"], sq, full_main, full_rot))
                    if s == 0:
                        segs.append((ROW_OTHER, HALF, [("kb", "KB", HALF), ("vb", "VB", HALF)], [("kb", "KBr", HALF)]))
                        segs.append((ROW_HALO, PADK, [("ka", "KA", 0), ("va", "VA", 0)], [("ka", "KAr", 0)]))
                        segs.append((ROW_HALO + PADK, PADK, [("ka", "KA", PADK + HALF), ("va", "VA", PADK + HALF)], [("ka", "KAr", PADK + HALF)]))
                    evi = [0]
                    chunks = []
                    for (xr0, T, mains, rots) in (segs if lvl >= 2 else []):
                        for ch in range(T // 512):
                            chunks.append((xr0 + ch * 512, ch, mains, rots))

                    def emit_loads(cidx):
                        r0, ch, mains, rots = chunks[cidx]
                        xs = []
                        for i in range(4):
                            x_t, x_b = xt.next()
                            S.dma("dma_start", dict(out=x_t[:], in_=X[r0 + i * 128:r0 + (i + 1) * 128, :]), writes=[x_b])
                            xs.append((x_t, x_b))
                        c_t, c_b = cst.next()
                        if rots:
                            S.dma("dma_start", dict(out=c_t[:, 0, :], in_=CT[:, r0:r0 + 512]), writes=[c_b])
                            S.dma("dma_start", dict(out=c_t[:, 1, :], in_=ST[:, r0:r0 + 512]), writes=[c_b])
                        return xs, c_t, c_b

                    def emit_prologue(cidx, xs):
                        xT_t, xT_b = xnT.next()
                        for i, (x_t, x_b) in enumerate(xs):
                            st_t, st_b = stat.next()
                            S.op(ACT, "activation", dict(out=junk[:], in_=x_t[:], func=AF.Square, accum_out=st_t[:, 0:1]), reads=[x_b], writes=[bjunk, st_b])
                            S.op(DVE, "tensor_scalar", dict(out=st_t[:, 1:2], in0=st_t[:, 0:1], scalar1=1.0 / D, scalar2=EPS, op0=ALU.mult, op1=ALU.add), reads=[st_b], writes=[st_b])
                            S.op(POOL, "tensor_tensor", dict(out=st_t[:, 2:3], in0=st_t[:, 1:2], in1=mhalf[:], op=ALU.pow), reads=[st_b, bconst], writes=[st_b])
                            xn_t, xn_b = xn.next()
                            S.op(DVE, "tensor_scalar", dict(out=xn_t[:], in0=x_t[:], scalar1=st_t[:, 2:3], scalar2=None, op0=ALU.mult), reads=[x_b, st_b], writes=[xn_b])
                            pT_t, pT_b = pT.next()
                            for kc in range(8):
                                S.op(PE, "transpose", dict(out=pT_t[:, kc * 128:(kc + 1) * 128], in_=xn_t[:, kc * 128:(kc + 1) * 128], identity=ident_f[:]), reads=[xn_b, bconst], writes=[pT_b])
                            S.op(DVE, "tensor_tensor", dict(out=xT_t[:, :, i * 128:(i + 1) * 128], in0=pT_t[:].rearrange("p (a b) -> p a b", b=128), in1=gsrep[:], op=ALU.mult), reads=[pT_b, bgsrep], writes=[xT_b])
                        return xT_t, xT_b

                    def emit_main(cidx, xT_t, xT_b, c_t, c_b, hook):
                        r0, ch, mains, rots = chunks[cidx]
                        hook_at = (len(mains) + 1) // 2
                        for gi, (gname, skey, dcol0) in enumerate(mains):
                            if gi == hook_at:
                                hook()
                                hook = None
                            wcol, bcol = G[gname]
                            o_t, o_b = ost.next()
                            for b in range(4):
                                pa_t, pa_b = pacc.next()
                                c0 = wcol + b * 128
                                bc = bcol + b
                                for kc in range(8):
                                    S.op(PE, "matmul", dict(out=pa_t[:], lhsT=wp[:, kc, c0:c0 + 128], rhs=xT_t[:, kc, :], start=(kc == 0), stop=(kc == 7)), reads=[bwp_of(c0), bwp_of(c0 + 127), xT_b], writes=[pa_b])
                                evi[0] += 1
                                if evi[0] % 4 != 0:
                                    S.op(ACT, "activation", dict(out=o_t[:, b, :], in_=pa_t[:], func=AF.Identity, bias=biasT[:, bc:bc + 1]), reads=[pa_b, bbias], writes=[o_b])
                                else:
                                    S.op(DVE, "tensor_scalar", dict(out=o_t[:, b, :], in0=pa_t[:], scalar1=biasT[:, bc:bc + 1], scalar2=None, op0=ALU.add), reads=[pa_b, bbias], writes=[o_b])
                            dc = dcol0 + ch * 512
                            S.dma("dma_start", dict(out=SC[skey][:, dc:dc + 512].rearrange("(b p) t -> p b t", p=128), in_=o_t[:]), reads=[o_b], writes=[SCB[skey]])
                        if hook is not None:
                            hook()
                        for (gname, skey, dcol0) in rots:
                            ti = RT[gname]
                            wc = 4096 + ti * 192
                            bc = 32 + 2 * ti
                            p1_t, p1_b = pacc.next()
                            p2_t, p2_b = pacc.next()
                            for kc in range(8):
                                S.op(PE, "matmul", dict(out=p1_t[:], lhsT=wp[:, kc, wc:wc + 128], rhs=xT_t[:, kc, :], start=(kc == 0), stop=(kc == 7)), reads=[bwp_of(wc), bwp_of(wc + 127), xT_b], writes=[p1_b])
                            for kc in range(8):
                                S.op(PE, "matmul", dict(out=p2_t[:], lhsT=wp[:, kc, wc + 64:wc + 192], rhs=xT_t[:, kc, :], start=(kc == 0), stop=(kc == 7)), reads=[bwp_of(wc + 64), bwp_of(wc + 191), xT_b], writes=[p2_b])
                            t_t, t_b = t12.next()
                            S.op(DVE, "scalar_tensor_tensor", dict(out=t_t[:, 0, :], in0=p1_t[:], scalar=biasT[:, bc:bc + 1], in1=c_t[:, 0, :], op0=ALU.add, op1=ALU.mult), reads=[p1_b, c_b, bbias], writes=[t_b])
                            S.op(DVE, "scalar_tensor_tensor", dict(out=t_t[:, 1, :], in0=p2_t[:], scalar=biasT[:, bc + 1:bc + 2], in1=c_t[:, 1, :], op0=ALU.add, op1=ALU.mult), reads=[p2_b, c_b, bbias], writes=[t_b])
                            r_t, r_b = rot.next()
                            S.op(DVE, "tensor_tensor", dict(out=r_t[:], in0=t_t[:, 0, :], in1=t_t[:, 1, :], op=ALU.add), reads=[t_b], writes=[r_b])
                            dc = dcol0 + ch * 512
                            S.dma("dma_start", dict(out=SC[skey][:, dc:dc + 512], in_=r_t[:]), reads=[r_b], writes=[SCB[skey]])

                    loaded, ready = {}, {}
                    nch = len(chunks)
                    if nch:
                        loaded[0] = emit_loads(0)
                        xs, c_t, c_b = loaded.pop(0)
                        ready[0] = (emit_prologue(0, xs), c_t, c_b)
                        if nch > 1:
                            loaded[1] = emit_loads(1)
                    for cidx in range(nch):
                        (xT_t, xT_b), c_t, c_b = ready.pop(cidx)

                        def hook(cidx=cidx):
                            if cidx + 1 < nch:
                                xs1, c1_t, c1_b = loaded.pop(cidx + 1)
                                ready[cidx + 1] = (emit_prologue(cidx + 1, xs1), c1_t, c1_b)
                                if cidx + 2 < nch:
                                    loaded[cidx + 2] = emit_loads(cidx + 2)
                        emit_main(cidx, xT_t, xT_b, c_t, c_b, hook)
                    S.barrier()

            with contextlib.ExitStack() as pB:
                nkt = skv // 128
                nqc = sq // 512
                kbT = [sb(pB, nm("kbT"), [128, skv], BF16) for _ in range(2)]
                vbT = [sb(pB, nm("vbT"), [128, skv], BF16) for _ in range(2)]
                qbT = [sb(pB, nm("qbT"), [128, sq], BF16) for _ in range(2)]
                gbT = [sb(pB, nm("gbT"), [128, sq], BF16) for _ in range(2)]
                bk, bv, bq, bg = ([Buf(), Buf()] for _ in range(4))
                vtok = sb(pB, nm("vtok"), [128, nkt, 128], BF16)
                bvt = Buf()
                Pr = Ring([sb(pB, nm("P"), [128, 1024], BF16) for _ in range(6)])
                asum = [[sb(pB, nm("asum"), [128, 1024], F32) for _ in range(2)] for _ in range(2)]
                basum = [[Buf() for _ in range(2)] for _ in range(2)]
                oS = [[sb(pB, nm("oS"), [128, 512], F32) for _ in range(2)] for _ in range(2)]
                boS = [[Buf() for _ in range(2)] for _ in range(2)]
                thr = Ring([sb(pB, nm("thb"), [128, 512], F32) for _ in range(2)])
                sgr = Ring([sb(pB, nm("sgb"), [128, 512], BF16) for _ in range(3)])
                yTr = Ring([sb(pB, nm("yTb"), [128, 512], BF16) for _ in range(3)])
                rsr = Ring([sb(pB, nm("rs"), [128, 16], F32) for _ in range(3)])
                rr = Ring([sb(pB, nm("rr"), [128, 8], F32) for _ in range(8)])
                t0r = Ring([sb(pB, nm("t0"), [128, 128], F32) for _ in range(3)])
                orr = Ring([sb(pB, nm("o"), [128, 128], F32) for _ in range(8)])
                onr = Ring([sb(pB, nm("on"), [128, 128], BF16) for _ in range(8)])
                junkb = sb(pB, nm("junkb"), [128, 128], F32)
                bjb = Buf()
                scr_ = Ring([ps(pB, nm("sc"), [128, 1024], F32) for _ in range(2)])
                oT = [ps(pB, nm("oT"), [128, 512], F32) for _ in range(2)]
                boT = [Buf() for _ in range(2)]
                tpb = ps(pB, nm("tpb"), [128, 512], F32)
                btp = Buf()
                pTv = ps(pB, nm("pTv"), [128, 8, 128], BF16)
                bpv = Buf()

                def load_head(g):
                    hb = g % 2
                    for (dst, dbuf, main, rotk, ncols) in ((kbT[hb], bk[hb], "KB", "KBr", skv), (qbT[hb], bq[hb], "QB", "QBr", sq)):
                        for m in range(2):
                            h = 2 * g + m
                            S.dma("dma_start", dict(out=dst[m * 64:m * 64 + 48, 0:ncols], in_=SC[main][g * 128 + m * 64 + 16:g * 128 + m * 64 + 64, 0:ncols]), reads=[SCB[main]], writes=[dbuf])
                            S.dma("dma_start", dict(out=dst[m * 64 + 48:m * 64 + 56, 0:ncols], in_=SC[rotk][h * 8:h * 8 + 8, 0:ncols]), reads=[SCB[rotk]], writes=[dbuf])
                            S.dma("dma_start", dict(out=dst[m * 64 + 56:m * 64 + 64, 0:ncols], in_=SC[rotk][64 + h * 8:64 + h * 8 + 8, 0:ncols]), reads=[SCB[rotk]], writes=[dbuf])
                    S.dma("dma_start", dict(out=vbT[hb][:], in_=SC["VB"][g * 128:(g + 1) * 128, 0:skv]), reads=[SCB["VB"]], writes=[bv[hb]])
                    S.dma("dma_start", dict(out=gbT[hb][:], in_=SC["GB"][g * 128:(g + 1) * 128, 0:sq]), reads=[SCB["GB"]], writes=[bg[hb]])

                def make_epilogue(g, q0, par, sg_t, sg_b):
                    steps = []
                    rs_t, rs_b = rsr.next()
                    y_t, y_b = yTr.next()
                    state = {}

                    def rowsums():
                        for m in range(2):
                            for qb in range(4):
                                c = 256 + 2 * (m * 4 + qb)
                                for j in range(2):
                                    S.op(PE, "matmul", dict(out=tpb[:, c:c + 2], lhsT=asum[par][j][:, m * 512 + qb * 128:m * 512 + (qb + 1) * 128], rhs=ones_f[:, 0:2], start=(j == 0), stop=(j == 1)), reads=[basum[par][j], bconst], writes=[btp])
                        S.op(DVE, "reciprocal", dict(out=rs_t[:, 0:8], in_=tpb[:, 256:272:2]), reads=[btp], writes=[rs_b])
                        S.op(DVE, "tensor_scalar", dict(out=rs_t[:, 8:12], in0=rs_t[:, 4:8], scalar1=neglam[:, 0:1], scalar2=None, op0=ALU.mult), reads=[rs_b, bconst], writes=[rs_b])
                    steps.append(rowsums)

                    def st1(qb):
                        def f():
                            for m in range(2):
                                S.op(PE, "transpose", dict(out=tpb[:, m * 128:(m + 1) * 128], in_=oS[par][m][:, qb * 128:(qb + 1) * 128], identity=ident_f[:]), reads=[boS[par][m], bconst], writes=[btp])
                            r_t, r_b = rr.next()
                            t0_t, t0_b = t0r.next()
                            S.op(DVE, "tensor_scalar", dict(out=t0_t[:], in0=tpb[:, 0:128], scalar1=rs_t[:, qb:qb + 1], scalar2=None, op0=ALU.mult), reads=[btp, rs_b], writes=[t0_b])
                            o_t, o_b = orr.next()
                            S.op(DVE, "scalar_tensor_tensor", dict(out=o_t[:], in0=tpb[:, 128:256], scalar=rs_t[:, 8 + qb:9 + qb], in1=t0_t[:], op0=ALU.mult, op1=ALU.add), reads=[btp, rs_b, t0_b], writes=[o_b])
                            state[qb] = dict(r=(r_t, r_b), o=(o_t, o_b))
                        return f

                    def st2(qb):
                        def f():
                            (r_t, r_b), (o_t, o_b) = state[qb]["r"], state[qb]["o"]
                            S.op(ACT, "activation", dict(out=junkb[:], in_=o_t[:], func=AF.Square, accum_out=r_t[:, 3:4]), reads=[o_b], writes=[bjb, r_b])
                        return f

                    def st3(qb):
                        def f():
                            r_t, r_b = state[qb]["r"]
                            S.op(DVE, "tensor_scalar", dict(out=r_t[:, 4:5], in0=r_t[:, 3:4], scalar1=1.0 / 128, scalar2=EPS, op0=ALU.mult, op1=ALU.add), reads=[r_b], writes=[r_b])
                            S.op(POOL, "tensor_tensor", dict(out=r_t[:, 5:6], in0=r_t[:, 4:5], in1=mhalf[:], op=ALU.pow), reads=[r_b, bconst], writes=[r_b])
                        return f

                    def st4(qb):
                        def f():
                            (r_t, r_b), (o_t, o_b) = state[qb]["r"], state[qb]["o"]
                            on_t, on_b = onr.next()
                            S.op(ACT, "activation", dict(out=on_t[:], in_=o_t[:], func=AF.Identity, scale=r_t[:, 5:6]), reads=[o_b, r_b], writes=[on_b])
                            state[qb]["on"] = (on_t, on_b)
                        return f

                    def st5(qb, last):
                        def f():
                            on_t, on_b = state[qb]["on"]
                            ev = pTv[:, 0, :]
                            S.op(PE, "transpose", dict(out=ev, in_=on_t[:], identity=ident[:]), reads=[on_b, bconst], writes=[bpv])
                            S.op(DVE, "scalar_tensor_tensor", dict(out=y_t[:, qb * 128:(qb + 1) * 128], in0=ev, scalar=gsub2[:, 0:1], in1=sg_t[:, qb * 128:(qb + 1) * 128], op0=ALU.mult, op1=ALU.mult), reads=[bpv, sg_b, bconst], writes=[y_b])
                            if last:
                                S.dma("dma_start", dict(out=SC["YM"][512 + g * 128:512 + (g + 1) * 128, q0:q0 + 512], in_=y_t[:]), reads=[y_b], writes=[SCB["YM"]])
                        return f

                    for stage in (st1, st2, st3, st4):
                        for qb in range(4):
                            steps.append(stage(qb))
                    for qb in range(4):
                        steps.append(st5(qb, qb == 3))
                    return steps

                pending = []
                steps_per_iter = -(-21 // max(nkt - 3, 1))
                heads = list(range(4)) if lvl >= 3 else []
                if heads:
                    load_head(0)
                for g in heads:
                    hb = g % 2
                    if g + 1 < 4:
                        load_head(g + 1)
                    for k8 in range(nkt // 8):
                        for j in range(8):
                            kt = k8 * 8 + j
                            S.op(PE, "transpose", dict(out=pTv[:, j, :], in_=vbT[hb][:, kt * 128:(kt + 1) * 128], identity=ident[:]), reads=[bv[hb], bconst], writes=[bpv])
                        S.op(ACT, "activation", dict(out=vtok[:, k8 * 8:k8 * 8 + 8, :], in_=pTv[:], func=AF.Copy), reads=[bpv], writes=[bvt])
                    for qc in range(nqc):
                        q0 = qc * 512
                        par = qc % 2
                        th_t, th_b = thr.next()
                        sg_t, sg_b = sgr.next()
                        S.op(ACT, "activation", dict(out=th_t[:], in_=gbT[hb][:, q0:q0 + 512], func=AF.Tanh, scale=0.5), reads=[bg[hb]], writes=[th_b])
                        S.op(DVE, "scalar_tensor_tensor", dict(out=sg_t[:], in0=th_t[:], scalar=1.0, in1=gbT[hb][:, q0:q0 + 512], op0=ALU.add, op1=ALU.mult), reads=[th_b, bg[hb]], writes=[sg_b])

                        def scores(kt):
                            sc_t, sc_b = scr_.next()
                            for m in range(2):
                                S.op(PE, "matmul", dict(out=sc_t[:, m * 512:(m + 1) * 512], lhsT=kbT[hb][m * 64:(m + 1) * 64, kt * 128:(kt + 1) * 128], rhs=qbT[hb][m * 64:(m + 1) * 64, q0:q0 + 512], start=True, stop=True), reads=[bk[hb], bq[hb]], writes=[sc_b])
                            p_t, p_b = Pr.next()
                            S.op(ACT, "activation", dict(out=p_t[:], in_=sc_t[:], func=AF.Exp, scale=0.125), reads=[sc_b], writes=[p_b])
                            return p_t, p_b

                        def av(kt, p_t, p_b):
                            for m in range(2):
                                S.op(PE, "matmul", dict(out=oT[m][:], lhsT=vtok[:, kt, :], rhs=p_t[:, m * 512:(m + 1) * 512], start=(kt == 0), stop=(kt == nkt - 1)), reads=[p_b, bvt], writes=[boT[m]])
                            j = kt % 2
                            if kt < 2:
                                S.op(DVE, "tensor_copy", dict(out=asum[par][j][:], in_=p_t[:]), reads=[p_b], writes=[basum[par][j]])
                            else:
                                S.op(DVE, "tensor_tensor", dict(out=asum[par][j][:], in0=asum[par][j][:], in1=p_t[:], op=ALU.add), reads=[p_b, basum[par][j]], writes=[basum[par][j]])

                        ptiles = {0: scores(0)}
                        for kt in range(nkt):
                            if kt + 1 < nkt:
                                ptiles[kt + 1] = scores(kt + 1)
                            if kt >= 1:
                                av(kt - 1, *ptiles.pop(kt - 1))
                            if kt >= 1:
                                for _ in range(steps_per_iter):
                                    if pending:
                                        pending.pop(0)()
                        av(nkt - 1, *ptiles.pop(nkt - 1))
                        for m in range(2):
                            S.op(ACT, "activation", dict(out=oS[par][m][:], in_=oT[m][:], func=AF.Copy), reads=[boT[m]], writes=[boS[par][m]])
                        while pending:
                            pending.pop(0)()
                        pending.extend(make_epilogue(g, q0, par, sg_t, sg_b))
                while pending:
                    pending.pop(0)()
                S.barrier()

            with contextlib.ExitStack() as pA:
                tiles = amask_tiles(sq)
                nt = len(tiles)
                kaT = [sb(pA, nm("kaT"), [128, sext], BF16) for _ in range(2)]
                vaT1 = sb(pA, nm("vaT"), [128, sext], BF16)
                vaT = [vaT1, vaT1]
                bv1 = Buf()
                qz = [[sb(pA, nm("qz"), [128, sq], BF16) for _ in range(2)] for _ in range(2)]
                gaT1 = sb(pA, nm("gaT"), [128, sq], BF16)
                gaT = [gaT1, gaT1]
                bg1 = Buf()
                bk, bv, bq, bg = ([Buf(), Buf()] for _ in range(4))
                bv = [bv1, bv1]
                bg = [bg1, bg1]
                for hb_ in range(2):
                    S.op(POOL, "memset", dict(ap=qz[hb_][0][64:128, :], constant=0.0), writes=[bq[hb_]])
                    S.op(POOL, "memset", dict(ap=qz[hb_][1][0:64, :], constant=0.0), writes=[bq[hb_]])
                acc = sb(pA, nm("accA"), [65, 2, sq], F32)
                bacc = Buf()
                vt_all = sb(pA, nm("vtall"), [128, nt, 2, 65], BF16)
                bvta = Buf()
                vrow = sb(pA, nm("vrow"), [128, 2 * PADK], F32)
                bvrow = Buf()
                Pt = Ring([sb(pA, nm("Pt"), [128, 2, 512], BF16) for _ in range(3)])
                recr = Ring([sb(pA, nm("rec"), [128, 512], F32) for _ in range(2)])
                thr = Ring([sb(pA, nm("tha"), [128, 512], F32) for _ in range(1)])
                sgr = Ring([sb(pA, nm("sga"), [128, 512], F32) for _ in range(1)])
                tnr = Ring([sb(pA, nm("tn"), [128, 512], F32) for _ in range(1)])
                yTr = Ring([sb(pA, nm("yTa"), [128, 512], BF16) for _ in range(2)])
                scA = Ring([ps(pA, nm("scA"), [128, 2, 512], F32) for _ in range(2)])
                opsr = Ring([ps(pA, nm("ops"), [128, 2, 512], F32) for _ in range(2)])
                if s == 0:
                    S.dma("dma_start", dict(out=vrow[:], in_=VROW[:, :]), writes=[bvrow])

                kc0, kn = (PADK, sq) if s != 0 else (0, sext)

                def load_v(hp):
                    hb = hp % 2
                    if s != 0:
                        S.op(POOL, "memset", dict(ap=vaT[hb][:, 0:PADK], constant=0.0), writes=[bv[hb]])
                        S.op(POOL, "memset", dict(ap=vaT[hb][:, PADK + sq:sext], constant=0.0), writes=[bv[hb]])
                    S.dma("dma_start", dict(out=vaT[hb][:, kc0:kc0 + kn], in_=SC["VA"][hp * 128:(hp + 1) * 128, kc0:kc0 + kn]), reads=[SCB["VA"]], writes=[bv[hb]])
                    if s == 0:
                        S.op(DVE, "tensor_tensor", dict(out=vaT[hb][:, 0:PADK], in0=vaT[hb][:, 0:PADK], in1=vrow[:, 0:PADK], op=ALU.mult), reads=[bvrow, bv[hb]], writes=[bv[hb]])
                        S.op(DVE, "tensor_tensor", dict(out=vaT[hb][:, PADK + sq:sext], in0=vaT[hb][:, PADK + sq:sext], in1=vrow[:, PADK:2 * PADK], op=ALU.mult), reads=[bvrow, bv[hb]], writes=[bv[hb]])

                def load_pair(hp):
                    hb = hp % 2
                    if s != 0:
                        S.op(POOL, "memset", dict(ap=kaT[hb][:, 0:PADK], constant=0.0), writes=[bk[hb]])
                        S.op(POOL, "memset", dict(ap=kaT[hb][:, PADK + sq:sext], constant=0.0), writes=[bk[hb]])
                    for hl in range(2):
                        h = 2 * hp + hl
                        S.dma("dma_start", dict(out=kaT[hb][hl * 64:hl * 64 + 48, kc0:kc0 + kn], in_=SC["KA"][hp * 128 + hl * 64 + 16:hp * 128 + hl * 64 + 64, kc0:kc0 + kn]), reads=[SCB["KA"]], writes=[bk[hb]])
                        S.dma("dma_start", dict(out=kaT[hb][hl * 64 + 48:hl * 64 + 56, kc0:kc0 + kn], in_=SC["KAr"][h * 8:h * 8 + 8, kc0:kc0 + kn]), reads=[SCB["KAr"]], writes=[bk[hb]])
                        S.dma("dma_start", dict(out=kaT[hb][hl * 64 + 56:hl * 64 + 64, kc0:kc0 + kn], in_=SC["KAr"][64 + h * 8:64 + h * 8 + 8, kc0:kc0 + kn]), reads=[SCB["KAr"]], writes=[bk[hb]])
                        S.dma("dma_start", dict(out=qz[hb][hl][hl * 64:hl * 64 + 48, :], in_=SC["QA"][hp * 128 + hl * 64 + 16:hp * 128 + hl * 64 + 64, 0:sq]), reads=[SCB["QA"]], writes=[bq[hb]])
                        S.dma("dma_start", dict(out=qz[hb][hl][hl * 64 + 48:hl * 64 + 56, :], in_=SC["QAr"][h * 8:h * 8 + 8, 0:sq]), reads=[SCB["QAr"]], writes=[bq[hb]])
                        S.dma("dma_start", dict(out=qz[hb][hl][hl * 64 + 56:hl * 64 + 64, :], in_=SC["QAr"][64 + h * 8:64 + h * 8 + 8, 0:sq]), reads=[SCB["QAr"]], writes=[bq[hb]])

                def tile_geom(tile):
                    (pi, dil, r, t, nblk, je0) = tile
                    e0 = r + dil * (je0 + 128 * t)
                    ksl = slice(e0, e0 + dil * 127 + 1, dil)
                    mlo, mhi = max(t - 1, 0), min(t, nblk - 1)
                    nq = (mhi - mlo + 1) * 128
                    qs = r + dil * 128 * mlo
                    qsl = slice(qs, qs + dil * (nq - 1) + 1, dil)
                    boff = 0 if t >= 1 else 128
                    return ksl, qsl, nq, boff

                pairs = list(range(4)) if lvl >= 4 else []
                if pairs:
                    load_pair(0)
                    load_v(0)
                for hp in pairs:
                    hb = hp % 2
                    if hp + 1 < 4:
                        load_pair(hp + 1)
                    S.dma("dma_start", dict(out=gaT[hb][:], in_=SC["GA"][hp * 128:(hp + 1) * 128, 0:sq]), reads=[SCB["GA"]], writes=[bg[hb]])
                    S.op(POOL, "memset", dict(ap=acc[:], constant=0.0), writes=[bacc])
                    for t4 in range(0, nt, 8):
                        n4 = min(8, nt - t4)
                        o_t, o_b = opsr.next()
                        pv = o_t[:, 0, :].bitcast(BF16).rearrange("p (a b) -> p a b", b=128)
                        for j in range(n4):
                            ksl = tile_geom(tiles[t4 + j])[0]
                            S.op(PE, "transpose", dict(out=pv[:, j, :], in_=vaT[hb][:, ksl], identity=ident[:]), reads=[bv[hb], bconst], writes=[o_b])
                        eng = ACT if (t4 // 8) % 2 == 0 else DVE
                        if eng == ACT:
                            S.op(ACT, "activation", dict(out=vt_all[:, t4:t4 + n4, :, 0:64], in_=pv[:, 0:n4, :].rearrange("p a (h d) -> p a h d", h=2), func=AF.Copy), reads=[o_b], writes=[bvta])
                        else:
                            S.op(DVE, "tensor_copy", dict(out=vt_all[:, t4:t4 + n4, :, 0:64], in_=pv[:, 0:n4, :].rearrange("p a (h d) -> p a h d", h=2)), reads=[o_b], writes=[bvta])
                    for hl in range(2):
                        S.op(DVE, "tensor_copy", dict(out=vt_all[:, :, hl, 64], in_=am[:, 0:nt]), reads=[bconst], writes=[bvta])
                    if hp + 1 < 4:
                        load_v(hp + 1)
                    units = [list(range(u, min(u + 2, nt))) for u in range(0, nt, 2)]

                    def scores(unit):
                        sc_t, sc_b = scA.next()
                        geo = []
                        for j, ti in enumerate(unit):
                            ksl, qsl, nq, boff = tile_geom(tiles[ti])
                            geo.append((ti, qsl, nq))
                            for hl in range(2):
                                S.op(PE, "matmul", dict(out=sc_t[:, hl, j * 256:j * 256 + nq], lhsT=kaT[hb][:, ksl], rhs=qz[hb][hl][:, qsl], start=True, stop=False), reads=[bk[hb], bq[hb]], writes=[sc_b])
                                S.op(PE, "matmul", dict(out=sc_t[:, hl, j * 256:j * 256 + nq], lhsT=ident[:], rhs=bandneg[:, boff:boff + nq], start=False, stop=True), reads=[bconst], writes=[sc_b])
                        p_t, p_b = Pt.next()
                        if len(unit) == 2 and all(g_[2] == 256 for g_ in geo):
                            S.op(ACT, "activation", dict(out=p_t[:], in_=sc_t[:], func=AF.Exp, scale=0.125), reads=[sc_b], writes=[p_b])
                        else:
                            for j, (ti, qsl, nq) in enumerate(geo):
                                S.op(ACT, "activation", dict(out=p_t[:, :, j * 256:j * 256 + nq], in_=sc_t[:, :, j * 256:j * 256 + nq], func=AF.Exp, scale=0.125), reads=[sc_b], writes=[p_b])
                        return geo, p_t, p_b

                    def av(geo, p_t, p_b):
                        o_t, o_b = opsr.next()
                        for j, (ti, qsl, nq) in enumerate(geo):
                            for hl in range(2):
                                for bi in range(nq // 128):
                                    c = j * 256 + bi * 128
                                    S.op(PE, "matmul", dict(out=o_t[0:65, hl, c:c + 128], lhsT=vt_all[:, ti, hl, :], rhs=p_t[:, hl, c:c + 128], start=True, stop=True), reads=[bvta, p_b], writes=[o_b])
                        for j, (ti, qsl, nq) in enumerate(geo):
                            S.op(DVE, "tensor_tensor", dict(out=acc[:, :, qsl], in0=o_t[0:65, :, j * 256:j * 256 + nq], in1=acc[:, :, qsl], op=ALU.add), reads=[o_b, bacc], writes=[bacc])

                    inflight = {0: scores(units[0])}
                    for u in range(len(units)):
                        if u + 1 < len(units):
                            inflight[u + 1] = scores(units[u + 1])
                        if u >= 1:
                            av(*inflight.pop(u - 1))
                    av(*inflight.pop(len(units) - 1))
                    for qc in range(sq // 512):
                        q0 = qc * 512
                        d_t, d_b = opsr.next()
                        dv = d_t[:, 0, :]
                        nv = d_t[:, 1, :]
                        for h2 in range(2):
                            qa_, qb_ = q0 + h2 * 256, q0 + (h2 + 1) * 256
                            for hl in range(2):
                                S.op(PE, "matmul", dict(out=dv[:, h2 * 256:(h2 + 1) * 256], lhsT=sels[:, hl, :], rhs=acc[:, hl, qa_:qb_], start=(hl == 0), stop=(hl == 1)), reads=[bacc, bconst], writes=[d_b])
                        for h2 in range(2):
                            qa_, qb_ = q0 + h2 * 256, q0 + (h2 + 1) * 256
                            for hl in range(2):
                                S.op(PE, "matmul", dict(out=nv[:, h2 * 256:(h2 + 1) * 256], lhsT=sels[:, 2 + hl, :], rhs=acc[:, hl, qa_:qb_], start=(hl == 0), stop=(hl == 1)), reads=[bacc, bconst], writes=[d_b])
                        rc_t, rc_b = recr.next()
                        S.op(DVE, "reciprocal", dict(out=rc_t[:], in_=dv), reads=[d_b], writes=[rc_b])
                        th_t, th_b = thr.next()
                        S.op(ACT, "activation", dict(out=th_t[:], in_=gaT[hb][:, q0:q0 + 512], func=AF.Tanh, scale=0.5), reads=[bg[hb]], writes=[th_b])
                        sg_t, sg_b = sgr.next()
                        S.op(DVE, "scalar_tensor_tensor", dict(out=sg_t[:], in0=th_t[:], scalar=1.0, in1=gaT[hb][:, q0:q0 + 512], op0=ALU.add, op1=ALU.mult), reads=[th_b, bg[hb]], writes=[sg_b])
                        tn_t, tn_b = tnr.next()
                        S.op(DVE, "tensor_tensor", dict(out=tn_t[:], in0=nv, in1=rc_t[:], op=ALU.mult), reads=[d_b, rc_b], writes=[tn_b])
                        y_t, y_b = yTr.next()
                        S.op(DVE, "tensor_tensor", dict(out=y_t[:], in0=tn_t[:], in1=sg_t[:], op=ALU.mult), reads=[tn_b, sg_b], writes=[y_b])
                        S.dma("dma_start", dict(out=SC["YM"][hp * 128:(hp + 1) * 128, q0:q0 + 512], in_=y_t[:]), reads=[y_b], writes=[SCB["YM"]])
                S.barrier()

            with contextlib.ExitStack() as p3:
                ymr = Ring([sb(p3, nm("ym"), [128, 8, 512], BF16) for _ in range(2)])
                xr = Ring([sb(p3, nm("x3"), [128, D], F32) for _ in range(8)])
                tmr = Ring([sb(p3, nm("tm3"), [128, D], F32) for _ in range(3)])
                outr = Ring([sb(p3, nm("o3"), [128, D], F32) for _ in range(4)])
                st3 = Ring([sb(p3, nm("st3"), [128, 6], F32) for _ in range(8)])
                junk3 = sb(p3, nm("junk3"), [128, 512], F32)
                bj3 = Buf()
                po = Ring([ps(p3, nm("po"), [128, 2, 512], F32) for _ in range(4)])
                nq3 = (sq // 512) if lvl >= 5 else 0

                def p3_loads(qc):
                    q0 = qc * 512
                    ym_t, ym_b = ymr.next()
                    S.dma("dma_start", dict(out=ym_t[:], in_=SC["YM"][:, q0:q0 + 512].rearrange("(kc p) t -> p kc t", p=128)), reads=[SCB["YM"]], writes=[ym_b])
                    xs = []
                    for i in range(4):
                        row = q0 + i * 128
                        x_t, x_b = xr.next()
                        S.dma("dma_start", dict(out=x_t[:], in_=X[job["xrow"] + row:job["xrow"] + row + 128, :]), writes=[x_b])
                        xs.append((x_t, x_b))
                    return ym_t, ym_b, xs

                ld3 = {}
                if nq3:
                    ld3[0] = p3_loads(0)
                for qc in range(nq3):
                    q0 = qc * 512
                    ym_t, ym_b, xs = ld3.pop(qc)
                    if qc + 1 < nq3:
                        ld3[qc + 1] = p3_loads(qc + 1)
                    for i in range(4):
                        row = q0 + i * 128
                        x_t, x_b = xs[i]
                        po_t, po_b = po.next()
                        for hf in range(2):
                            for kc in range(8):
                                S.op(PE, "matmul", dict(out=po_t[:, hf, :], lhsT=ym_t[:, kc, i * 128:(i + 1) * 128], rhs=wout[:, kc, hf * 512:(hf + 1) * 512], start=(kc == 0), stop=(kc == 7)), reads=[ym_b, bwout], writes=[po_b])
                        s_t, s_b = st3.next()
                        for hf in range(2):
                            S.op(ACT, "activation", dict(out=junk3[:], in_=po_t[:, hf, :], func=AF.Square, accum_out=s_t[:, hf:hf + 1]), reads=[po_b], writes=[bj3, s_b])
                        S.op(DVE, "tensor_tensor", dict(out=s_t[:, 2:3], in0=s_t[:, 0:1], in1=s_t[:, 1:2], op=ALU.add), reads=[s_b], writes=[s_b])
                        S.op(DVE, "tensor_scalar", dict(out=s_t[:, 3:4], in0=s_t[:, 2:3], scalar1=1.0 / D, scalar2=EPS, op0=ALU.mult, op1=ALU.add), reads=[s_b], writes=[s_b])
                        S.op(POOL, "tensor_tensor", dict(out=s_t[:, 4:5], in0=s_t[:, 3:4], in1=mhalf[:], op=ALU.pow), reads=[s_b, bconst], writes=[s_b])
                        tm_t, tm_b = tmr.next()
                        for hf in range(2):
                            S.op(DVE, "scalar_tensor_tensor", dict(out=tm_t[:, hf * 512:(hf + 1) * 512], in0=po_t[:, hf, :], scalar=s_t[:, 4:5], in1=gg[s][:, hf * 512:(hf + 1) * 512], op0=ALU.mult, op1=ALU.mult), reads=[po_b, s_b, bgg[s]], writes=[tm_b])
                        o_t, o_b = outr.next()
                        S.op(DVE, "tensor_tensor", dict(out=o_t[:], in0=tm_t[:], in1=x_t[:], op=ALU.add), reads=[tm_b, x_b], writes=[o_b])
                        yrow = job["yrow"] + row
                        out_dmas.append(S.dma("dma_start", dict(out=Y[yrow:yrow + 128, :], in_=o_t[:]), reads=[o_b], writes=[Buf()]))
                S.barrier()

        S.run(final_wait=out_dmas)
    return nc


_NC_CACHE = {}


def _rope_tables(pos):
    half = 8
    inv = (np.float32(500000.0) ** (-np.arange(half, dtype=np.float32) / np.float32(half))).astype(np.float32)
    ang = pos.astype(np.float32)[None, :] * inv[:, None]
    cos = np.cos(ang).astype(np.float32)
    sin = np.sin(ang).astype(np.float32)
    idx = np.arange(128) % 8
    C = cos[idx]
    Sg = sin[idx].copy()
    Sg[:64] *= -1.0
    return np.ascontiguousarray(C), np.ascontiguousarray(Sg)


def _amask(valid_ext, sq):
    tiles = amask_tiles(sq)
    m = np.zeros((128, len(tiles)), np.float32)
    p = np.arange(128)
    for ti, (pi, dil, r, t, nblk, je0) in enumerate(tiles):
        e = r + dil * (je0 + 128 * t + p)
        m[:, ti] = valid_ext[e]
    return m


def prep_inputs(x_prompt, x_sample, c_prompt, c_sample, w_in, w_out, g_pre, g_post,
                w_ada, b_ada, lam_q1, lam_k1, lam_q2, lam_k2, g_sub):
    f32 = np.float32
    x_prompt = np.asarray(x_prompt, f32)
    x_sample = np.asarray(x_sample, f32)
    c_prompt = np.asarray(c_prompt, f32)
    c_sample = np.asarray(c_sample, f32)
    w_in0 = np.ascontiguousarray(np.asarray(w_in, f32)[0])
    w_out0 = np.ascontiguousarray(np.asarray(w_out, f32)[0])
    w_ada0 = np.ascontiguousarray(np.asarray(w_ada, f32)[0])
    b_ada0 = np.asarray(b_ada, f32)[0]
    g_pre0 = np.asarray(g_pre, f32)[0]
    g_post0 = np.asarray(g_post, f32)[0]
    g_sub0 = np.asarray(g_sub, f32)[0]

    offs = dict(qa=0, ka=512, qb=2048, kb=2560)
    w_rot = np.zeros((D, 4, 192), f32)
    for ti, nme in enumerate(("qa", "ka", "qb", "kb")):
        x1 = np.array([offs[nme] + h * 64 + i for h in range(8) for i in range(8)])
        x2 = x1 + 8
        w_rot[:, ti, 0:64] = w_in0[:, x1]
        w_rot[:, ti, 64:128] = w_in0[:, x2]
        w_rot[:, ti, 128:192] = w_in0[:, x1]
    w_rot = np.ascontiguousarray(w_rot.reshape(D, 768))

    g_pre_t = np.ascontiguousarray(g_pre0.reshape(8, 128).T)
    b_ada_t = np.ascontiguousarray(b_ada0[:2048].reshape(16, 128).T)
    b_gate = np.ascontiguousarray(b_ada0[2048:3072].reshape(1, D))
    g_post_r = np.ascontiguousarray(g_post0.reshape(1, D))
    g_sub_t = np.ascontiguousarray(g_sub0.reshape(128, 1))
    lam_v = np.ascontiguousarray(np.stack([np.asarray(v, f32)[0] for v in (lam_q1, lam_k1, lam_q2, lam_k2)], 0))
    ident = np.eye(128, dtype=f32)
    p = np.arange(128)[:, None]
    f = np.arange(128)[None, :]
    band = np.concatenate([(p <= f), (p >= f)], axis=1).astype(f32)
    band = np.ascontiguousarray(np.concatenate([band, band], axis=1))
    sels = np.zeros((65, 4, 128), f32)
    sels[64, 0, 0:64] = 2.0
    sels[64, 1, 64:128] = 2.0
    sels[np.arange(64), 2, np.arange(64)] = 1.0
    sels[np.arange(64), 3, 64 + np.arange(64)] = 1.0
    sels = np.ascontiguousarray(sels.reshape(65, 512))
    valid1 = np.zeros(DSEQ + 2 * PADK, f32)
    valid1[PADK:PADK + DSEQ] = 1.0
    am1 = _amask(valid1, DSEQ)

    in_maps = []
    for c in range(8):
        psq, hf = c // 2, c % 2
        q0 = hf * HALF
        o0 = (1 - hf) * HALF
        xa = np.zeros((NROWS, D), f32)
        pos = np.zeros(NROWS, np.int64)
        xa[ROW_OWN:ROW_OWN + HALF] = x_prompt[psq, q0:q0 + HALF]
        pos[ROW_OWN:ROW_OWN + HALF] = np.arange(q0, q0 + HALF)
        xa[ROW_OTHER:ROW_OTHER + HALF] = x_prompt[psq, o0:o0 + HALF]
        pos[ROW_OTHER:ROW_OTHER + HALF] = np.arange(o0, o0 + HALF)
        hpos = np.concatenate([np.arange(q0 - PADK, q0), np.arange(q0 + HALF, q0 + HALF + PADK)])
        hval = (hpos >= 0) & (hpos < SEQ)
        xa[ROW_HALO:ROW_HALO + 2 * PADK][hval] = x_prompt[psq, hpos[hval]]
        pos[ROW_HALO:ROW_HALO + 2 * PADK] = np.clip(hpos, 0, SEQ - 1)
        xa[ROW_S0:ROW_S0 + DSEQ] = x_sample[2 * c]
        pos[ROW_S0:ROW_S0 + DSEQ] = np.arange(DSEQ)
        xa[ROW_S1:ROW_S1 + DSEQ] = x_sample[2 * c + 1]
        pos[ROW_S1:ROW_S1 + DSEQ] = np.arange(DSEQ)
        C, Sg = _rope_tables(pos)
        cs = np.stack([c_prompt[psq], c_sample[2 * c], c_sample[2 * c + 1]], axis=-1)
        c_t = np.ascontiguousarray(cs.reshape(8, 128, 3).transpose(1, 0, 2).reshape(128, 24))
        valid0 = np.ones(HALF + 2 * PADK, f32)
        valid0[0:PADK] = hval[:PADK]
        valid0[PADK + HALF:] = hval[PADK:]
        am0 = _amask(valid0, HALF)
        vrow = np.ascontiguousarray(np.broadcast_to(hval.astype(f32)[None, :], (128, 2 * PADK)))
        in_maps.append({
            "x_all": xa, "rope_c": C, "rope_s": Sg, "c_t": c_t, "w_in": w_in0, "w_rot": w_rot,
            "w_out": w_out0, "w_ada": w_ada0, "g_pre_t": g_pre_t, "b_ada_t": b_ada_t, "b_gate": b_gate,
            "g_post": g_post_r, "g_sub_t": g_sub_t, "lam_v": lam_v, "ident": ident, "band": band,
            "sels": sels, "amask0": am0, "amask1": am1, "vrow": vrow,
        })

    return in_maps


def kernel(**inputs):
    f32 = np.float32
    in_maps = prep_inputs(**inputs)
    if "nc" not in _NC_CACHE:
        _NC_CACHE["nc"] = build_program()
    nc = _NC_CACHE["nc"]
    res = run_bass_kernel_spmd(nc, in_maps, core_ids=list(range(8)))
    y_prompt = np.zeros((4, SEQ, D), f32)
    y_sample = np.zeros((16, DSEQ, D), f32)
    for c in range(8):
        y = np.asarray(res.results[c]["y_all"], f32)
        psq, hf = c // 2, c % 2
        y_prompt[psq, hf * HALF:(hf + 1) * HALF] = y[0:HALF]
        y_sample[2 * c] = y[HALF:HALF + DSEQ]
        y_sample[2 * c + 1] = y[HALF + DSEQ:HALF + 2 * DSEQ]
    return (y_prompt, y_sample)
```

```python
import contextlib
import numpy as np
import concourse.bass as bass
import concourse.mybir as mybir
from concourse.bass_utils import run_bass_kernel_spmd

F32 = mybir.dt.float32
BF16 = mybir.dt.bfloat16
AF = mybir.ActivationFunctionType
ALU = mybir.AluOpType

PE, ACT, DVE, POOL, SP = "tensor", "scalar", "vector", "gpsimd", "sync"
ENGINES = (PE, ACT, DVE, POOL, SP)

D = 1024
SEQ = 8192
DSEQ = 2048
HALF = 4096
PADK = 1024
PATTERNS = ((128, 1), (512, 4), (2048, 16))
EPS = 1e-6
LAM_INIT = 0.2
NROWS = HALF + HALF + 2 * PADK + 2 * DSEQ
ROW_OWN, ROW_OTHER, ROW_HALO, ROW_S0, ROW_S1 = 0, 4096, 8192, 10240, 12288
NOUT = HALF + 2 * DSEQ
WCOLS = 4096 + 4 * 192


class Buf:
    __slots__ = ("last_write", "reads")

    def __init__(self):
        self.last_write = None
        self.reads = []


class Rec:
    __slots__ = ("eng", "fn", "deps", "is_dma", "signal", "count", "dsem", "dval", "epoch")

    def __init__(self, eng, fn, is_dma):
        self.eng, self.fn, self.is_dma = eng, fn, is_dma
        self.deps = []
        self.signal = False
        self.count = 0
        self.dsem = None
        self.dval = 0
        self.epoch = 0


class Sched:
    EPOCH_MAX = 30000

    def __init__(self, nc, n_dma_sems=20):
        self.nc = nc
        self.q = {e: [] for e in ENGINES}
        self.n_dma_sems = n_dma_sems
        self.dma_rr = {e: 0 for e in ENGINES}
        self.dma_last = {}
        self.last_compute = {}

    def op(self, eng, meth, kw, reads=(), writes=(), dma=False, extra=()):
        r = Rec(eng, (meth, kw), dma)
        deps = list(extra)
        for b in reads:
            if b.last_write is not None:
                deps.append(b.last_write)
        for b in writes:
            if b.last_write is not None:
                deps.append(b.last_write)
            deps.extend(b.reads)
        if dma:
            slot = self.dma_rr[eng] % self.n_dma_sems
            self.dma_rr[eng] += 1
            prev = self.dma_last.get((eng, slot))
            if prev is not None:
                deps.append(prev)
            self.dma_last[(eng, slot)] = r
            r.dsem = (eng, slot)
        seen = set()
        for d in deps:
            if d is r or id(d) in seen:
                continue
            if (not d.is_dma) and (not dma) and d.eng == PE and eng == PE:
                continue
            seen.add(id(d))
            r.deps.append(d)
        for b in reads:
            b.reads.append(r)
        for b in writes:
            b.last_write = r
            b.reads = []
        self.q[eng].append(r)
        if not dma:
            self.last_compute[eng] = r
        return r

    def dma(self, meth, kw, reads=(), writes=(), eng=SP):
        return self.op(eng, meth, kw, reads, writes, dma=True)

    def barrier(self):
        deps = list(self.last_compute.values()) + list(self.dma_last.values())
        for e in ENGINES:
            r = Rec(e, None, False)
            for d in deps:
                if (not d.is_dma) and d.eng == e:
                    continue
                r.deps.append(d)
            self.q[e].append(r)

    def run(self, final_wait=()):
        nc = self.nc
        for e in ENGINES:
            for r in self.q[e]:
                for d in r.deps:
                    if not d.is_dma:
                        d.signal = True
        n_epochs = {}
        for e in ENGINES:
            cnt, ep = 0, 0
            for r in self.q[e]:
                if r.is_dma or r.fn is None:
                    continue
                if r.signal:
                    if cnt >= self.EPOCH_MAX:
                        ep += 1
                        cnt = 0
                    cnt += 1
                    r.count = cnt
                    r.epoch = ep
            n_epochs[e] = ep + 1
        dcount = {}
        for e in ENGINES:
            for r in self.q[e]:
                if r.is_dma:
                    dcount[r.dsem] = dcount.get(r.dsem, 0) + 16
                    r.dval = dcount[r.dsem]
        with contextlib.ExitStack() as st:
            csem = {}
            for e in ENGINES:
                for ep in range(n_epochs[e]):
                    if any((not r.is_dma) and r.signal and r.epoch == ep for r in self.q[e]):
                        csem[(e, ep)] = st.enter_context(nc.semaphore(f"c_{e}_{ep}"))
            dsem = {}
            for key in dcount:
                dsem[key] = st.enter_context(nc.semaphore(f"d_{key[0]}_{key[1]}"))
            block = st.enter_context(nc.Block())

            def emit(e, engine):
                waited = {}

                def do_wait(d):
                    if d.is_dma:
                        s, v, k = dsem[d.dsem], d.dval, ("d",) + d.dsem
                    else:
                        s, v, k = csem[(d.eng, d.epoch)], d.count, ("c", d.eng, d.epoch)
                    if waited.get(k, 0) >= v:
                        return
                    waited[k] = v
                    engine.wait_ge(s, v)

                for r in self.q[e]:
                    for d in r.deps:
                        do_wait(d)
                    if r.fn is None:
                        continue
                    ins = getattr(engine, r.fn[0])(**r.fn[1])
                    if r.is_dma:
                        ins.then_inc(dsem[r.dsem], 16)
                    elif r.signal:
                        ins.then_inc(csem[(e, r.epoch)], 1)
                if e == SP:
                    for d in final_wait:
                        do_wait(d)

            @block.tensor
            def _(eng):
                emit(PE, eng)

            @block.scalar
            def _(eng):
                emit(ACT, eng)

            @block.vector
            def _(eng):
                emit(DVE, eng)

            @block.gpsimd
            def _(eng):
                emit(POOL, eng)

            @block.sync
            def _(eng):
                emit(SP, eng)


class Ring:
    def __init__(self, tiles):
        self.tiles = tiles
        self.bufs = [Buf() for _ in tiles]
        self.i = 0

    def next(self):
        k = self.i % len(self.tiles)
        self.i += 1
        return self.tiles[k], self.bufs[k]


def amask_tiles(sq):
    out = []
    for pi, (w, dil) in enumerate(PATTERNS):
        nblk = sq // dil // 128
        je0 = PADK // dil - 64
        for r in range(dil):
            for t in range(nblk + 1):
                out.append((pi, dil, r, t, nblk, je0))
    return out


def build_program(stop_after=None, jobs=(0, 1, 2)):
    nc = bass.Bass("TRN2", target_bir_lowering=False)
    lvl = {'p0': 0, 'w': 1, 'p1': 2, '2b': 3, '2a': 4, None: 5}[stop_after]

    def din(name, shape, dt=F32):
        return nc.dram_tensor(name, list(shape), dt, kind="ExternalInput").ap()

    X = din("x_all", [NROWS, D])
    CT = din("rope_c", [128, NROWS])
    ST = din("rope_s", [128, NROWS])
    CTT = din("c_t", [128, 24])
    W_IN = din("w_in", [D, 4096])
    W_ROT = din("w_rot", [D, 768])
    W_OUT = din("w_out", [D, D])
    W_ADA = din("w_ada", [D, 3 * D])
    GPRE = din("g_pre_t", [128, 8])
    BADA = din("b_ada_t", [128, 16])
    BGATE = din("b_gate", [1, D])
    GPOST = din("g_post", [1, D])
    GSUB = din("g_sub_t", [128, 1])
    LAMV = din("lam_v", [4, 64])
    IDENT = din("ident", [128, 128])
    BAND = din("band", [128, 512])
    SELS = din("sels", [65, 512])
    AM0 = din("amask0", [128, 117])
    VROW = din("vrow", [128, 2 * PADK])
    AM1 = din("amask1", [128, 69])
    Y = nc.dram_tensor("y_all", [NOUT, D], F32, kind="ExternalOutput").ap()

    JOBS = [
        dict(s=0, sq=HALF, skv=SEQ, sext=HALF + 2 * PADK, xrow=ROW_OWN, yrow=0, am=AM0),
        dict(s=1, sq=DSEQ, skv=DSEQ, sext=DSEQ + 2 * PADK, xrow=ROW_S0, yrow=HALF, am=AM1),
        dict(s=2, sq=DSEQ, skv=DSEQ, sext=DSEQ + 2 * PADK, xrow=ROW_S1, yrow=HALF + DSEQ, am=AM1),
    ]
    def scr(name, rows, cols):
        return nc.dram_tensor(name, [rows, cols], BF16, kind="Internal").ap()

    SC = dict(
        QA=scr("s_qa", 512, HALF), QAr=scr("s_qar", 128, HALF), GA=scr("s_ga", 512, HALF),
        KA=scr("s_ka", 512, HALF + 2 * PADK), KAr=scr("s_kar", 128, HALF + 2 * PADK),
        VA=scr("s_va", 512, HALF + 2 * PADK),
        QB=scr("s_qb", 512, HALF), QBr=scr("s_qbr", 128, HALF), GB=scr("s_gb", 512, HALF),
        KB=scr("s_kb", 512, SEQ), KBr=scr("s_kbr", 128, SEQ), VB=scr("s_vb", 512, SEQ),
        YM=scr("s_ym", 1024, HALF),
    )
    SCB = {k: Buf() for k in SC}
    WS = scr("s_w", D, WCOLS)
    bWS = Buf()

    S = Sched(nc)
    out_dmas = []
    top = contextlib.ExitStack()
    with top:
        def sb(st, name, shape, dt):
            return st.enter_context(nc.sbuf_tensor("sb_" + name, list(shape), dt))

        def ps(st, name, shape, dt):
            return st.enter_context(nc.psum_tensor("ps_" + name, list(shape), dt))

        uid = [0]

        def nm(p):
            uid[0] += 1
            return f"{p}{uid[0]}"

        ident_f = sb(top, "ident_f", [128, 128], F32)
        ident = sb(top, "ident", [128, 128], BF16)
        band_f = sb(top, "band_f", [128, 512], F32)
        bandneg = sb(top, "bandneg", [128, 256], BF16)
        sels = sb(top, "sels", [65, 4, 128], F32)
        am0 = sb(top, "am0", [128, 117], F32)
        am1 = sb(top, "am1", [128, 69], F32)
        ones_f = sb(top, "ones_f", [128, 128], F32)
        mhalf = sb(top, "mhalf", [128, 1], F32)
        zer = sb(top, "zer", [1, 512], BF16)
        gsub2 = sb(top, "gsub2", [128, 1], F32)
        neglam = sb(top, "neglam", [128, 1], F32)
        lamt = sb(top, "lamt", [128, 4, 64], F32)
        lamj = sb(top, "lamj", [128, 64], F32)
        lams = sb(top, "lams", [128, 4], F32)
        gg = [sb(top, f"gg{s}", [128, D], F32) for s in range(3)]
        gsT = sb(top, "gsT", [128, 8, 4], F32)
        shT = sb(top, "shT", [128, 8, 4], F32)
        wout = sb(top, "wout", [128, 8, D], BF16)
        bconst = Buf()
        bgg = [Buf() for _ in range(3)]
        bmod = Buf()
        bwout = Buf()

        biasA = sb(top, "biasA", [128, 40, 4], F32)
        bbiasA = Buf()
        S.dma("dma_start", dict(out=ident_f[:], in_=IDENT[:, :]), writes=[bconst])
        S.dma("dma_start", dict(out=band_f[:], in_=BAND[:, :]), writes=[bconst])
        S.dma("dma_start", dict(out=sels[:].rearrange("p a b -> p (a b)"), in_=SELS[:, :]), writes=[bconst])
        S.dma("dma_start", dict(out=am0[:], in_=AM0[:, :]), writes=[bconst])
        S.dma("dma_start", dict(out=am1[:], in_=AM1[:, :]), writes=[bconst])
        S.dma("dma_start", dict(out=gsub2[:], in_=GSUB[:, :]), writes=[bconst])
        for i in range(4):
            S.dma("dma_start", dict(out=lamt[:, i, :], in_=LAMV[i:i + 1, :].broadcast_to([128, 64])), writes=[bconst])
        S.op(DVE, "tensor_copy", dict(out=ident[:], in_=ident_f[:]), reads=[bconst], writes=[bconst])
        S.op(DVE, "tensor_scalar", dict(out=bandneg[:], in0=band_f[:, 0:256], scalar1=30000.0, scalar2=-30000.0, op0=ALU.mult, op1=ALU.add), reads=[bconst], writes=[bconst])
        S.op(POOL, "memset", dict(ap=ones_f[:], constant=1.0), writes=[bconst])
        S.op(POOL, "memset", dict(ap=mhalf[:], constant=-0.5), writes=[bconst])
        S.op(POOL, "memset", dict(ap=zer[:], constant=0.0), writes=[bconst])
        S.op(DVE, "tensor_scalar", dict(out=gsub2[:], in0=gsub2[:], scalar1=(1.0 - LAM_INIT) * 0.5, scalar2=None, op0=ALU.mult), reads=[bconst], writes=[bconst])
        for i in range(2):
            S.op(DVE, "tensor_tensor", dict(out=lamj[:], in0=lamt[:, 2 * i, :], in1=lamt[:, 2 * i + 1, :], op=ALU.mult), reads=[bconst], writes=[bconst])
            S.op(ACT, "activation", dict(out=lamj[:], in_=lamj[:], func=AF.Identity, accum_out=lams[:, i:i + 1]), reads=[bconst], writes=[bconst])
        S.op(ACT, "activation", dict(out=lams[:, 2:4], in_=lams[:, 0:2], func=AF.Exp), reads=[bconst], writes=[bconst])
        S.op(DVE, "tensor_tensor", dict(out=neglam[:], in0=lams[:, 3:4], in1=lams[:, 2:3], op=ALU.subtract), reads=[bconst], writes=[bconst])
        S.op(DVE, "tensor_scalar", dict(out=neglam[:], in0=neglam[:], scalar1=-LAM_INIT, scalar2=None, op0=ALU.add), reads=[bconst], writes=[bconst])

        with contextlib.ExitStack() as p0:
            stg = [sb(p0, f"stg0_{i}", [128, 8, 512], F32) for i in range(2)]
            bstg = [Buf() for _ in range(2)]
            ct = sb(p0, "ct", [128, 24], F32)
            th = sb(p0, "th0", [128, 24], F32)
            scT = sb(p0, "scT", [128, 8, 4], F32)
            screp = sb(p0, "screp", [128, 24, 128], F32)
            gpre = sb(p0, "gpre", [128, 8], F32)
            bada = sb(p0, "bada", [128, 16], F32)
            bgate = sb(p0, "bgate", [128, D], F32)
            gpost = sb(p0, "gpost", [128, D], F32)
            tmpm = sb(p0, "tmpm", [128, 8], F32)
            tmpg = sb(p0, "tmpg", [128, 512], F32)
            pmod = ps(p0, "pmod", [128, 16, 4], F32)
            pg = [ps(p0, f"pg{i}", [128, 512], F32) for i in range(3)]
            b0 = Buf()
            bpm = Buf()
            bpg = [Buf() for _ in range(3)]
            S.dma("dma_start", dict(out=ct[:], in_=CTT[:, :]), writes=[b0])
            S.dma("dma_start", dict(out=gpre[:], in_=GPRE[:, :]), writes=[b0])
            S.dma("dma_start", dict(out=bada[:], in_=BADA[:, :]), writes=[b0])
            S.dma("dma_start", dict(out=bgate[:], in_=BGATE[0:1, :].broadcast_to([128, D])), writes=[b0])
            S.dma("dma_start", dict(out=gpost[:], in_=GPOST[0:1, :].broadcast_to([128, D])), writes=[b0])
            S.op(ACT, "activation", dict(out=th[:], in_=ct[:], func=AF.Tanh, scale=0.5), reads=[b0], writes=[b0])
            S.op(DVE, "scalar_tensor_tensor", dict(out=th[:], in0=th[:], scalar=1.0, in1=ct[:], op0=ALU.add, op1=ALU.mult), reads=[b0], writes=[b0])
            S.op(POOL, "memset", dict(ap=scT[:], constant=0.0), writes=[b0])
            S.op(POOL, "memset", dict(ap=shT[:], constant=0.0), writes=[bmod])
            S.op(POOL, "memset", dict(ap=gsT[:], constant=0.0), writes=[bmod])
            S.op(DVE, "tensor_scalar", dict(out=scT[:, :, 0:3], in0=th[:].rearrange("p (a b) -> p a b", b=3), scalar1=0.5, scalar2=None, op0=ALU.mult), reads=[b0], writes=[b0])
            for i in range(24):
                kc, s = divmod(i, 3)
                S.op(DVE, "tensor_scalar", dict(out=screp[:, i, :], in0=ones_f[:], scalar1=scT[:, kc, s:s + 1], scalar2=None, op0=ALU.mult), reads=[b0, bconst], writes=[b0])
            for pc in range(6):
                k = pc % 2
                S.dma("dma_start", dict(out=stg[k][:], in_=W_ADA[:, pc * 512:(pc + 1) * 512].rearrange("(kc p) n -> p kc n", p=128)), writes=[bstg[k]])
                if pc < 4:
                    for fb in range(4):
                        blk = pc * 4 + fb
                        for kc in range(8):
                            S.op(PE, "matmul", dict(out=pmod[:, blk, :], lhsT=stg[k][:, kc, fb * 128:(fb + 1) * 128], rhs=scT[:, kc, :], start=(kc == 0), stop=(kc == 7)), reads=[bstg[k], b0], writes=[bpm])
                else:
                    hf = pc - 4
                    for s in range(3):
                        for h2 in range(2):
                            for kc in range(8):
                                S.op(PE, "matmul", dict(out=pg[s][:, h2 * 256:(h2 + 1) * 256], lhsT=screp[:, kc * 3 + s, :], rhs=stg[k][:, kc, h2 * 256:(h2 + 1) * 256], start=(kc == 0), stop=(kc == 7)), reads=[bstg[k], b0], writes=[bpg[s]])
                        S.op(DVE, "tensor_tensor", dict(out=tmpg[:], in0=pg[s][:], in1=bgate[:, hf * 512:(hf + 1) * 512], op=ALU.add), reads=[bpg[s], b0], writes=[b0])
                        S.op(DVE, "tensor_tensor", dict(out=gg[s][:, hf * 512:(hf + 1) * 512], in0=tmpg[:], in1=gpost[:, hf * 512:(hf + 1) * 512], op=ALU.mult), reads=[b0], writes=[bgg[s]])
            for s in range(3):
                S.op(DVE, "tensor_tensor", dict(out=shT[:, :, s], in0=pmod[:, 0:8, s], in1=bada[:, 0:8], op=ALU.add), reads=[bpm, b0], writes=[bmod])
                S.op(DVE, "scalar_tensor_tensor", dict(out=tmpm[:], in0=pmod[:, 8:16, s], scalar=1.0, in1=bada[:, 8:16], op0=ALU.add, op1=ALU.add), reads=[bpm, b0], writes=[b0])
                S.op(DVE, "tensor_tensor", dict(out=gsT[:, :, s], in0=tmpm[:], in1=gpre[:], op=ALU.mult), reads=[b0], writes=[bmod])
            for pc in range(2):
                k = pc % 2
                S.dma("dma_start", dict(out=stg[k][:], in_=W_OUT[:, pc * 512:(pc + 1) * 512].rearrange("(kc p) n -> p kc n", p=128)), writes=[bstg[k]])
                S.op(DVE, "tensor_copy", dict(out=wout[:, :, pc * 512:(pc + 1) * 512], in_=stg[k][:]), reads=[bstg[k]], writes=[bwout])
            wbr = Ring([sb(p0, f"wbst{i}", [128, 8, 512], BF16) for i in range(2)])
            pb = ps(p0, "pb0", [128, 40, 4], F32)
            bpb = Buf()
            pieces = [(W_IN, pc * 512, 512, pc * 512) for pc in range(8)] + [(W_ROT, 0, 384, 4096), (W_ROT, 384, 384, 4096 + 384)]
            for pi_, (wsrc, c0, ncol, dcol) in enumerate(pieces):
                k = pi_ % 2
                S.dma("dma_start", dict(out=stg[k][:, :, 0:ncol], in_=wsrc[:, c0:c0 + ncol].rearrange("(kc p) n -> p kc n", p=128)), writes=[bstg[k]])
                wb_t, wb_b = wbr.next()
                S.op(ACT, "activation", dict(out=wb_t[:, 0:4, 0:ncol], in_=stg[k][:, 0:4, 0:ncol], func=AF.Copy), reads=[bstg[k]], writes=[wb_b])
                S.op(DVE, "tensor_copy", dict(out=wb_t[:, 4:8, 0:ncol], in_=stg[k][:, 4:8, 0:ncol]), reads=[bstg[k]], writes=[wb_b])
                S.dma("dma_start", dict(out=WS[:, dcol:dcol + ncol].rearrange("(kc p) n -> p kc n", p=128), in_=wb_t[:, :, 0:ncol]), reads=[wb_b], writes=[bWS])
                if pi_ < 8:
                    cols = [(fb * 128, pi_ * 4 + fb) for fb in range(4)]
                else:
                    t0 = (pi_ - 8) * 2
                    cols = [(0, 32 + 2 * t0), (64, 33 + 2 * t0), (192, 34 + 2 * t0), (256, 35 + 2 * t0)]
                for (co, bcol) in cols:
                    for kc in range(8):
                        S.op(PE, "matmul", dict(out=pb[:, bcol, :], lhsT=stg[k][:, kc, co:co + 128], rhs=shT[:, kc, :], start=(kc == 0), stop=(kc == 7)), reads=[bstg[k], bmod], writes=[bpb])
            S.op(DVE, "tensor_copy", dict(out=biasA[:], in_=pb[:]), reads=[bpb], writes=[bbiasA])
            S.barrier()

        for job in [JOBS[j_] for j_ in jobs] if stop_after != 'p0' else []:
            s, sq, skv, sext = job["s"], job["sq"], job["skv"], job["sext"]
            am = am0 if job["am"] is AM0 else am1
            with contextlib.ExitStack() as p1:
                wp = sb(p1, nm("wp"), [128, 8, WCOLS], BF16)
                WPIECE = 1216
                bwp_l = [Buf() for _ in range(-(-WCOLS // WPIECE))]

                def bwp_of(c0):
                    return bwp_l[c0 // WPIECE]
                gsrep = sb(p1, nm("gsrep"), [128, 8, 128], F32)
                bgsrep = Buf()
                if lvl >= 1:
                    for c0 in range(0, WCOLS, WPIECE):
                        c1 = min(c0 + WPIECE, WCOLS)
                        S.dma("dma_start", dict(out=wp[:, :, c0:c1], in_=WS[:, c0:c1].rearrange("(kc p) n -> p kc n", p=128)), reads=[bWS], writes=[bwp_of(c0)])
                    for kc in range(8):
                        S.op(DVE, "tensor_scalar", dict(out=gsrep[:, kc, :], in0=ones_f[:], scalar1=gsT[:, kc, s:s + 1], scalar2=None, op0=ALU.mult), reads=[bmod, bconst], writes=[bgsrep])
                biasT = biasA[:, :, s]
                bbias = bbiasA

                with contextlib.ExitStack() as pp:
                    xt = Ring([sb(pp, nm("xt"), [128, D], F32) for _ in range(4)])
                    junk = sb(pp, nm("junk"), [128, D], F32)
                    bjunk = Buf()
                    xn = Ring([sb(pp, nm("xn"), [128, D], F32) for _ in range(4)])
                    xnT = Ring([sb(pp, nm("xnT"), [128, 8, 512], BF16) for _ in range(2)])
                    stat = Ring([sb(pp, nm("stat"), [128, 4], F32) for _ in range(8)])
                    ost = Ring([sb(pp, nm("ost"), [128, 4, 512], BF16) for _ in range(3)])
                    cst = Ring([sb(pp, nm("cst"), [128, 2, 512], F32) for _ in range(3)])
                    t12 = Ring([sb(pp, nm("t12"), [128, 2, 512], F32) for _ in range(1)])
                    rot = Ring([sb(pp, nm("rot"), [128, 512], BF16) for _ in range(2)])
                    pT = Ring([ps(pp, nm("pT"), [128, D], F32) for _ in range(2)])
                    pacc = Ring([ps(pp, nm("pacc"), [128, 512], F32) for _ in range(4)])

                    G = dict(qa=(0, 0), ka=(512, 4), va=(1024, 8), ga=(1536, 12), qb=(2048, 16), kb=(2560, 20), vb=(3072, 24), gb=(3584, 28))
                    RT = dict(qa=0, ka=1, qb=2, kb=3)
                    segs = []
                    full_main = [("qa", "QA", 0), ("ka", "KA", PADK), ("va", "VA", PADK), ("ga", "GA", 0), ("qb", "QB", 0), ("kb", "KB", 0), ("vb", "VB", 0), ("gb", "GB", 0)]
                    full_rot = [("qa", "QAr", 0), ("ka", "KAr", PADK), ("qb", "QBr", 0), ("kb", "KBr", 0)]
                    segs.append((job["xrow"], sq, full_main, full_rot))
                    if s == 0:
                        segs.append((ROW_OTHER, HALF, [("kb", "KB", HALF), ("vb", "VB", HALF)], [("kb", "KBr", HALF)]))
                        segs.append((ROW_HALO, PADK, [("ka", "KA", 0), ("va", "VA", 0)], [("ka", "KAr", 0)]))
                        segs.append((ROW_HALO + PADK, PADK, [("ka", "KA", PADK + HALF), ("va", "VA", PADK + HALF)], [("ka", "KAr", PADK + HALF)]))
                    evi = [0]
                    chunks = []
                    for (xr0, T, mains, rots) in (segs if lvl >= 2 else []):
                        for ch in range(T // 512):
                            chunks.append((xr0 + ch * 512, ch, mains, rots))

                    def emit_loads(cidx):
                        r0, ch, mains, rots = chunks[cidx]
                        xs = []
                        for i in range(4):
                            x_t, x_b = xt.next()
                            S.dma("dma_start", dict(out=x_t[:], in_=X[r0 + i * 128:r0 + (i + 1) * 128, :]), writes=[x_b])
                            xs.append((x_t, x_b))
                        c_t, c_b = cst.next()
                        if rots:
                            S.dma("dma_start", dict(out=c_t[:, 0, :], in_=CT[:, r0:r0 + 512]), writes=[c_b])
                            S.dma("dma_start", dict(out=c_t[:, 1, :], in_=ST[:, r0:r0 + 512]), writes=[c_b])
                        return xs, c_t, c_b

                    def emit_prologue(cidx, xs):
                        xT_t, xT_b = xnT.next()
                        for i, (x_t, x_b) in enumerate(xs):
                            st_t, st_b = stat.next()
                            S.op(ACT, "activation", dict(out=junk[:], in_=x_t[:], func=AF.Square, accum_out=st_t[:, 0:1]), reads=[x_b], writes=[bjunk, st_b])
                            S.op(DVE, "tensor_scalar", dict(out=st_t[:, 1:2], in0=st_t[:, 0:1], scalar1=1.0 / D, scalar2=EPS, op0=ALU.mult, op1=ALU.add), reads=[st_b], writes=[st_b])
                            S.op(POOL, "tensor_tensor", dict(out=st_t[:, 2:3], in0=st_t[:, 1:2], in1=mhalf[:], op=ALU.pow), reads=[st_b, bconst], writes=[st_b])
                            xn_t, xn_b = xn.next()
                            S.op(DVE, "tensor_scalar", dict(out=xn_t[:], in0=x_t[:], scalar1=st_t[:, 2:3], scalar2=None, op0=ALU.mult), reads=[x_b, st_b], writes=[xn_b])
                            pT_t, pT_b = pT.next()
                            for kc in range(8):
                                S.op(PE, "transpose", dict(out=pT_t[:, kc * 128:(kc + 1) * 128], in_=xn_t[:, kc * 128:(kc + 1) * 128], identity=ident_f[:]), reads=[xn_b, bconst], writes=[pT_b])
                            S.op(DVE, "tensor_tensor", dict(out=xT_t[:, :, i * 128:(i + 1) * 128], in0=pT_t[:].rearrange("p (a b) -> p a b", b=128), in1=gsrep[:], op=ALU.mult), reads=[pT_b, bgsrep], writes=[xT_b])
                        return xT_t, xT_b

                    def emit_main(cidx, xT_t, xT_b, c_t, c_b, hook):
                        r0, ch, mains, rots = chunks[cidx]
                        hook_at = (len(mains) + 1) // 2
                        for gi, (gname, skey, dcol0) in enumerate(mains):
                            if gi == hook_at:
                                hook()
                                hook = None
                            wcol, bcol = G[gname]
                            o_t, o_b = ost.next()
                            for b in range(4):
                                pa_t, pa_b = pacc.next()
                                c0 = wcol + b * 128
                                bc = bcol + b
                                for kc in range(8):
                                    S.op(PE, "matmul", dict(out=pa_t[:], lhsT=wp[:, kc, c0:c0 + 128], rhs=xT_t[:, kc, :], start=(kc == 0), stop=(kc == 7)), reads=[bwp_of(c0), bwp_of(c0 + 127), xT_b], writes=[pa_b])
                                evi[0] += 1
                                if evi[0] % 4 != 0:
                                    S.op(ACT, "activation", dict(out=o_t[:, b, :], in_=pa_t[:], func=AF.Identity, bias=biasT[:, bc:bc + 1]), reads=[pa_b, bbias], writes=[o_b])
                                else:
                                    S.op(DVE, "tensor_scalar", dict(out=o_t[:, b, :], in0=pa_t[:], scalar1=biasT[:, bc:bc + 1], scalar2=None, op0=ALU.add), reads=[pa_b, bbias], writes=[o_b])
                            dc = dcol0 + ch * 512
                            S.dma("dma_start", dict(out=SC[skey][:, dc:dc + 512].rearrange("(b p) t -> p b t", p=128), in_=o_t[:]), reads=[o_b], writes=[SCB[skey]])
                        if hook is not None:
                            hook()
                        for (gname, skey, dcol0) in rots:
                            ti = RT[gname]
                            wc = 4096 + ti * 192
                            bc = 32 + 2 * ti
                            p1_t, p1_b = pacc.next()
                            p2_t, p2_b = pacc.next()
                            for kc in range(8):
                                S.op(PE, "matmul", dict(out=p1_t[:], lhsT=wp[:, kc, wc:wc + 128], rhs=xT_t[:, kc, :], start=(kc == 0), stop=(kc == 7)), reads=[bwp_of(wc), bwp_of(wc + 127), xT_b], writes=[p1_b])
                            for kc in range(8):
                                S.op(PE, "matmul", dict(out=p2_t[:], lhsT=wp[:, kc, wc + 64:wc + 192], rhs=xT_t[:, kc, :], start=(kc == 0), stop=(kc == 7)), reads=[bwp_of(wc + 64), bwp_of(wc + 191), xT_b], writes=[p2_b])
                            t_t, t_b = t12.next()
                            S.op(DVE, "scalar_tensor_tensor", dict(out=t_t[:, 0, :], in0=p1_t[:], scalar=biasT[:, bc:bc + 1], in1=c_t[:, 0, :], op0=ALU.add, op1=ALU.mult), reads=[p1_b, c_b, bbias], writes=[t_b])
                            S.op(DVE, "scalar_tensor_tensor", dict(out=t_t[:, 1, :], in0=p2_t[:], scalar=biasT[:, bc + 1:bc + 2], in1=c_t[:, 1, :], op0=ALU.add, op1=ALU.mult), reads=[p2_b, c_b, bbias], writes=[t_b])
                            r_t, r_b = rot.next()
                            S.op(DVE, "tensor_tensor", dict(out=r_t[:], in0=t_t[:, 0, :], in1=t_t[:, 1, :], op=ALU.add), reads=[t_b], writes=[r_b])
                            dc = dcol0 + ch * 512
                            S.dma("dma_start", dict(out=SC[skey][:, dc:dc + 512], in_=r_t[:]), reads=[r_b], writes=[SCB[skey]])

                    loaded, ready = {}, {}
                    nch = len(chunks)
                    if nch:
                        loaded[0] = emit_loads(0)
                        xs, c_t, c_b = loaded.pop(0)
                        ready[0] = (emit_prologue(0, xs), c_t, c_b)
                        if nch > 1:
                            loaded[1] = emit_loads(1)
                    for cidx in range(nch):
                        (xT_t, xT_b), c_t, c_b = ready.pop(cidx)

                        def hook(cidx=cidx):
                            if cidx + 1 < nch:
                                xs1, c1_t, c1_b = loaded.pop(cidx + 1)
                                ready[cidx + 1] = (emit_prologue(cidx + 1, xs1), c1_t, c1_b)
                                if cidx + 2 < nch:
                                    loaded[cidx + 2] = emit_loads(cidx + 2)
                        emit_main(cidx, xT_t, xT_b, c_t, c_b, hook)
                    S.barrier()

            with contextlib.ExitStack() as pB:
                nkt = skv // 128
                nqc = sq // 512
                kbT = [sb(pB, nm("kbT"), [128, skv], BF16) for _ in range(2)]
                vbT = [sb(pB, nm("vbT"), [128, skv], BF16) for _ in range(2)]
                qbT = [sb(pB, nm("qbT"), [128, sq], BF16) for _ in range(2)]
                padq = s != 0
                qb2 = [sb(pB, nm("qb2"), [128, sq], BF16) for _ in range(2)] if padq else None
                gbT = [sb(pB, nm("gbT"), [128, sq], BF16) for _ in range(2)]
                bk, bv, bq, bg = ([Buf(), Buf()] for _ in range(4))
                vtok = sb(pB, nm("vtok"), [128, nkt, 128], BF16)
                if padq:
                    for hb_ in range(2):
                        S.op(POOL, "memset", dict(ap=qbT[hb_][64:128, :], constant=0.0), writes=[bq[hb_]])
                        S.op(POOL, "memset", dict(ap=qb2[hb_][0:64, :], constant=0.0), writes=[bq[hb_]])
                bvt = Buf()
                Pr = Ring([sb(pB, nm("P"), [128, 1024], BF16) for _ in range(6)])
                asum = [[sb(pB, nm("asum"), [128, 1024], F32) for _ in range(2)] for _ in range(2)]
                basum = [[Buf() for _ in range(2)] for _ in range(2)]
                oS = [[sb(pB, nm("oS"), [128, 512], F32) for _ in range(2)] for _ in range(2)]
                boS = [[Buf() for _ in range(2)] for _ in range(2)]
                thr = Ring([sb(pB, nm("thb"), [128, 512], F32) for _ in range(2)])
                sgr = Ring([sb(pB, nm("sgb"), [128, 512], BF16) for _ in range(3)])
                yTr = Ring([sb(pB, nm("yTb"), [128, 512], BF16) for _ in range(3)])
                rsr = Ring([sb(pB, nm("rs"), [128, 16], F32) for _ in range(3)])
                rr = Ring([sb(pB, nm("rr"), [128, 8], F32) for _ in range(8)])
                t0r = Ring([sb(pB, nm("t0"), [128, 128], F32) for _ in range(3)])
                orr = Ring([sb(pB, nm("o"), [128, 128], F32) for _ in range(8)])
                onr = Ring([sb(pB, nm("on"), [128, 128], BF16) for _ in range(8)])
                junkb = sb(pB, nm("junkb"), [128, 128], F32)
                bjb = Buf()
                scr_ = Ring([ps(pB, nm("sc"), [128, 1024], F32) for _ in range(2)])
                oT = [ps(pB, nm("oT"), [128, 512], F32) for _ in range(2)]
                boT = [Buf() for _ in range(2)]
                tpb = ps(pB, nm("tpb"), [128, 512], F32)
                btp = Buf()
                pTv = ps(pB, nm("pTv"), [128, 8, 128], BF16)
                bpv = Buf()

                def load_head(g):
                    hb = g % 2
                    for (dst0, dbuf, main, rotk, ncols) in ((kbT[hb], bk[hb], "KB", "KBr", skv), (qbT[hb], bq[hb], "QB", "QBr", sq)):
                        for m in range(2):
                            h = 2 * g + m
                            dst = qb2[hb] if (padq and main == "QB" and m == 1) else dst0
                            S.dma("dma_start", dict(out=dst[m * 64:m * 64 + 48, 0:ncols], in_=SC[main][g * 128 + m * 64 + 16:g * 128 + m * 64 + 64, 0:ncols]), reads=[SCB[main]], writes=[dbuf])
                            S.dma("dma_start", dict(out=dst[m * 64 + 48:m * 64 + 56, 0:ncols], in_=SC[rotk][h * 8:h * 8 + 8, 0:ncols]), reads=[SCB[rotk]], writes=[dbuf])
                            S.dma("dma_start", dict(out=dst[m * 64 + 56:m * 64 + 64, 0:ncols], in_=SC[rotk][64 + h * 8:64 + h * 8 + 8, 0:ncols]), reads=[SCB[rotk]], writes=[dbuf])
                    S.dma("dma_start", dict(out=vbT[hb][:], in_=SC["VB"][g * 128:(g + 1) * 128, 0:skv]), reads=[SCB["VB"]], writes=[bv[hb]])
                    S.dma("dma_start", dict(out=gbT[hb][:], in_=SC["GB"][g * 128:(g + 1) * 128, 0:sq]), reads=[SCB["GB"]], writes=[bg[hb]])

                def make_epilogue(g, q0, par, sg_t, sg_b):
                    steps = []
                    rs_t, rs_b = rsr.next()
                    y_t, y_b = yTr.next()
                    state = {}

                    def rowsums():
                        for m in range(2):
                            for qb in range(4):
                                c = 256 + 2 * (m * 4 + qb)
                                for j in range(2):
                                    S.op(PE, "matmul", dict(out=tpb[:, c:c + 2], lhsT=asum[par][j][:, m * 512 + qb * 128:m * 512 + (qb + 1) * 128], rhs=ones_f[:, 0:2], start=(j == 0), stop=(j == 1)), reads=[basum[par][j], bconst], writes=[btp])
                        S.op(DVE, "reciprocal", dict(out=rs_t[:, 0:8], in_=tpb[:, 256:272:2]), reads=[btp], writes=[rs_b])
                        S.op(DVE, "tensor_scalar", dict(out=rs_t[:, 8:12], in0=rs_t[:, 4:8], scalar1=neglam[:, 0:1], scalar2=None, op0=ALU.mult), reads=[rs_b, bconst], writes=[rs_b])
                    steps.append(rowsums)

                    def st1(qb):
                        def f():
                            for m in range(2):
                                S.op(PE, "transpose", dict(out=tpb[:, m * 128:(m + 1) * 128], in_=oS[par][m][:, qb * 128:(qb + 1) * 128], identity=ident_f[:]), reads=[boS[par][m], bconst], writes=[btp])
                            r_t, r_b = rr.next()
                            t0_t, t0_b = t0r.next()
                            S.op(DVE, "tensor_scalar", dict(out=t0_t[:], in0=tpb[:, 0:128], scalar1=rs_t[:, qb:qb + 1], scalar2=None, op0=ALU.mult), reads=[btp, rs_b], writes=[t0_b])
                            o_t, o_b = orr.next()
                            S.op(DVE, "scalar_tensor_tensor", dict(out=o_t[:], in0=tpb[:, 128:256], scalar=rs_t[:, 8 + qb:9 + qb], in1=t0_t[:], op0=ALU.mult, op1=ALU.add), reads=[btp, rs_b, t0_b], writes=[o_b])
                            state[qb] = dict(r=(r_t, r_b), o=(o_t, o_b))
                        return f

                    def st2(qb):
                        def f():
                            (r_t, r_b), (o_t, o_b) = state[qb]["r"], state[qb]["o"]
                            S.op(ACT, "activation", dict(out=junkb[:], in_=o_t[:], func=AF.Square, accum_out=r_t[:, 3:4]), reads=[o_b], writes=[bjb, r_b])
                        return f

                    def st3(qb):
                        def f():
                            r_t, r_b = state[qb]["r"]
                            S.op(DVE, "tensor_scalar", dict(out=r_t[:, 4:5], in0=r_t[:, 3:4], scalar1=1.0 / 128, scalar2=EPS, op0=ALU.mult, op1=ALU.add), reads=[r_b], writes=[r_b])
                            S.op(POOL, "tensor_tensor", dict(out=r_t[:, 5:6], in0=r_t[:, 4:5], in1=mhalf[:], op=ALU.pow), reads=[r_b, bconst], writes=[r_b])
                        return f

                    def st4(qb):
                        def f():
                            (r_t, r_b), (o_t, o_b) = state[qb]["r"], state[qb]["o"]
                            on_t, on_b = onr.next()
                            S.op(ACT, "activation", dict(out=on_t[:], in_=o_t[:], func=AF.Identity, scale=r_t[:, 5:6]), reads=[o_b, r_b], writes=[on_b])
                            state[qb]["on"] = (on_t, on_b)
                        return f

                    def st5(qb, last):
                        def f():
                            on_t, on_b = state[qb]["on"]
                            ev = pTv[:, 0, :]
                            S.op(PE, "transpose", dict(out=ev, in_=on_t[:], identity=ident[:]), reads=[on_b, bconst], writes=[bpv])
                            S.op(DVE, "scalar_tensor_tensor", dict(out=y_t[:, qb * 128:(qb + 1) * 128], in0=ev, scalar=gsub2[:, 0:1], in1=sg_t[:, qb * 128:(qb + 1) * 128], op0=ALU.mult, op1=ALU.mult), reads=[bpv, sg_b, bconst], writes=[y_b])
                            if last:
                                S.dma("dma_start", dict(out=SC["YM"][512 + g * 128:512 + (g + 1) * 128, q0:q0 + 512], in_=y_t[:]), reads=[y_b], writes=[SCB["YM"]])
                        return f

                    for stage in (st1, st2, st3, st4):
                        for qb in range(4):
                            steps.append(stage(qb))
                    for qb in range(4):
                        steps.append(st5(qb, qb == 3))
                    return steps

                pending = []
                steps_per_iter = -(-21 // max(nkt - 3, 1))
                heads = list(range(4)) if lvl >= 3 else []
                if heads:
                    load_head(0)
                for g in heads:
                    hb = g % 2
                    if g + 1 < 4:
                        load_head(g + 1)
                    for k8 in range(nkt // 8):
                        for j in range(8):
                            kt = k8 * 8 + j
                            S.op(PE, "transpose", dict(out=pTv[:, j, :], in_=vbT[hb][:, kt * 128:(kt + 1) * 128], identity=ident[:]), reads=[bv[hb], bconst], writes=[bpv])
                        S.op(ACT, "activation", dict(out=vtok[:, k8 * 8:k8 * 8 + 8, :], in_=pTv[:], func=AF.Copy), reads=[bpv], writes=[bvt])
                    for qc in range(nqc):
                        q0 = qc * 512
                        par = qc % 2
                        th_t, th_b = thr.next()
                        sg_t, sg_b = sgr.next()
                        S.op(ACT, "activation", dict(out=th_t[:], in_=gbT[hb][:, q0:q0 + 512], func=AF.Tanh, scale=0.5), reads=[bg[hb]], writes=[th_b])
                        S.op(DVE, "scalar_tensor_tensor", dict(out=sg_t[:], in0=th_t[:], scalar=1.0, in1=gbT[hb][:, q0:q0 + 512], op0=ALU.add, op1=ALU.mult), reads=[th_b, bg[hb]], writes=[sg_b])

                        def scores(kt):
                            sc_t, sc_b = scr_.next()
                            for m in range(2):
                                S.op(PE, "matmul", dict(out=sc_t[:, m * 512:(m + 1) * 512], lhsT=(kbT[hb][:, kt * 128:(kt + 1) * 128] if padq else kbT[hb][m * 64:(m + 1) * 64, kt * 128:(kt + 1) * 128]), rhs=((qbT[hb] if m == 0 else qb2[hb])[:, q0:q0 + 512] if padq else qbT[hb][m * 64:(m + 1) * 64, q0:q0 + 512]), start=True, stop=True), reads=[bk[hb], bq[hb]], writes=[sc_b])
                            p_t, p_b = Pr.next()
                            S.op(ACT, "activation", dict(out=p_t[:], in_=sc_t[:], func=AF.Exp, scale=0.125), reads=[sc_b], writes=[p_b])
                            return p_t, p_b

                        def av(kt, p_t, p_b):
                            for m in range(2):
                                S.op(PE, "matmul", dict(out=oT[m][:], lhsT=vtok[:, kt, :], rhs=p_t[:, m * 512:(m + 1) * 512], start=(kt == 0), stop=(kt == nkt - 1)), reads=[p_b, bvt], writes=[boT[m]])
                            j = kt % 2
                            if kt < 2:
                                S.op(DVE, "tensor_copy", dict(out=asum[par][j][:], in_=p_t[:]), reads=[p_b], writes=[basum[par][j]])
                            else:
                                S.op(DVE, "tensor_tensor", dict(out=asum[par][j][:], in0=asum[par][j][:], in1=p_t[:], op=ALU.add), reads=[p_b, basum[par][j]], writes=[basum[par][j]])

                        ptiles = {0: scores(0)}
                        for kt in range(nkt):
                            if kt + 1 < nkt:
                                ptiles[kt + 1] = scores(kt + 1)
                            if kt >= 1:
                                av(kt - 1, *ptiles.pop(kt - 1))
                            if kt >= 1:
                                for _ in range(steps_per_iter):
                                    if pending:
                                        pending.pop(0)()
                        av(nkt - 1, *ptiles.pop(nkt - 1))
                        for m in range(2):
                            S.op(ACT, "activation", dict(out=oS[par][m][:], in_=oT[m][:], func=AF.Copy), reads=[boT[m]], writes=[boS[par][m]])
                        while pending:
                            pending.pop(0)()
                        pending.extend(make_epilogue(g, q0, par, sg_t, sg_b))
                while pending:
                    pending.pop(0)()
                S.barrier()

            with contextlib.ExitStack() as pA:
                tiles = amask_tiles(sq)
                nt = len(tiles)
                kaT = [sb(pA, nm("kaT"), [128, sext], BF16) for _ in range(2)]
                vaT1 = sb(pA, nm("vaT"), [128, sext], BF16)
                vaT = [vaT1, vaT1]
                bv1 = Buf()
                qz = [[sb(pA, nm("qz"), [128, sq], BF16) for _ in range(2)] for _ in range(2)]
                gaT1 = sb(pA, nm("gaT"), [128, sq], BF16)
                gaT = [gaT1, gaT1]
                bg1 = Buf()
                bk, bv, bq, bg = ([Buf(), Buf()] for _ in range(4))
                bv = [bv1, bv1]
                bg = [bg1, bg1]
                for hb_ in range(2):
                    S.op(POOL, "memset", dict(ap=qz[hb_][0][64:128, :], constant=0.0), writes=[bq[hb_]])
                    S.op(POOL, "memset", dict(ap=qz[hb_][1][0:64, :], constant=0.0), writes=[bq[hb_]])
                acc = sb(pA, nm("accA"), [65, 2, sq], F32)
                bacc = Buf()
                vt_all = sb(pA, nm("vtall"), [128, nt, 2, 65], BF16)
                bvta = Buf()
                vrow = sb(pA, nm("vrow"), [128, 2 * PADK], F32)
                bvrow = Buf()
                Pt = Ring([sb(pA, nm("Pt"), [128, 2, 512], BF16) for _ in range(3)])
                recr = Ring([sb(pA, nm("rec"), [128, 512], F32) for _ in range(2)])
                thr = Ring([sb(pA, nm("tha"), [128, 512], F32) for _ in range(1)])
                sgr = Ring([sb(pA, nm("sga"), [128, 512], F32) for _ in range(1)])
                tnr = Ring([sb(pA, nm("tn"), [128, 512], F32) for _ in range(1)])
                yTr = Ring([sb(pA, nm("yTa"), [128, 512], BF16) for _ in range(2)])
                scA = Ring([ps(pA, nm("scA"), [128, 2, 512], F32) for _ in range(2)])
                opsr = Ring([ps(pA, nm("ops"), [128, 2, 512], F32) for _ in range(2)])
                if s == 0:
                    S.dma("dma_start", dict(out=vrow[:], in_=VROW[:, :]), writes=[bvrow])

                kc0, kn = (PADK, sq) if s != 0 else (0, sext)

                def load_v(hp):
                    hb = hp % 2
                    if s != 0:
                        S.op(POOL, "memset", dict(ap=vaT[hb][:, 0:PADK], constant=0.0), writes=[bv[hb]])
                        S.op(POOL, "memset", dict(ap=vaT[hb][:, PADK + sq:sext], constant=0.0), writes=[bv[hb]])
                    S.dma("dma_start", dict(out=vaT[hb][:, kc0:kc0 + kn], in_=SC["VA"][hp * 128:(hp + 1) * 128, kc0:kc0 + kn]), reads=[SCB["VA"]], writes=[bv[hb]])
                    if s == 0:
                        S.op(DVE, "tensor_tensor", dict(out=vaT[hb][:, 0:PADK], in0=vaT[hb][:, 0:PADK], in1=vrow[:, 0:PADK], op=ALU.mult), reads=[bvrow, bv[hb]], writes=[bv[hb]])
                        S.op(DVE, "tensor_tensor", dict(out=vaT[hb][:, PADK + sq:sext], in0=vaT[hb][:, PADK + sq:sext], in1=vrow[:, PADK:2 * PADK], op=ALU.mult), reads=[bvrow, bv[hb]], writes=[bv[hb]])

                def load_pair(hp):
                    hb = hp % 2
                    if s != 0:
                        S.op(POOL, "memset", dict(ap=kaT[hb][:, 0:PADK], constant=0.0), writes=[bk[hb]])
                        S.op(POOL, "memset", dict(ap=kaT[hb][:, PADK + sq:sext], constant=0.0), writes=[bk[hb]])
                    for hl in range(2):
                        h = 2 * hp + hl
                        S.dma("dma_start", dict(out=kaT[hb][hl * 64:hl * 64 + 48, kc0:kc0 + kn], in_=SC["KA"][hp * 128 + hl * 64 + 16:hp * 128 + hl * 64 + 64, kc0:kc0 + kn]), reads=[SCB["KA"]], writes=[bk[hb]])
                        S.dma("dma_start", dict(out=kaT[hb][hl * 64 + 48:hl * 64 + 56, kc0:kc0 + kn], in_=SC["KAr"][h * 8:h * 8 + 8, kc0:kc0 + kn]), reads=[SCB["KAr"]], writes=[bk[hb]])
                        S.dma("dma_start", dict(out=kaT[hb][hl * 64 + 56:hl * 64 + 64, kc0:kc0 + kn], in_=SC["KAr"][64 + h * 8:64 + h * 8 + 8, kc0:kc0 + kn]), reads=[SCB["KAr"]], writes=[bk[hb]])
                        S.dma("dma_start", dict(out=qz[hb][hl][hl * 64:hl * 64 + 48, :], in_=SC["QA"][hp * 128 + hl * 64 + 16:hp * 128 + hl * 64 + 64, 0:sq]), reads=[SCB["QA"]], writes=[bq[hb]])
                        S.dma("dma_start", dict(out=qz[hb][hl][hl * 64 + 48:hl * 64 + 56, :], in_=SC["QAr"][h * 8:h * 8 + 8, 0:sq]), reads=[SCB["QAr"]], writes=[bq[hb]])
                        S.dma("dma_start", dict(out=qz[hb][hl][hl * 64 + 56:hl * 64 + 64, :], in_=SC["QAr"][64 + h * 8:64 + h * 8 + 8, 0:sq]), reads=[SCB["QAr"]], writes=[bq[hb]])

                def tile_geom(tile):
                    (pi, dil, r, t, nblk, je0) = tile
                    e0 = r + dil * (je0 + 128 * t)
                    ksl = slice(e0, e0 + dil * 127 + 1, dil)
                    mlo, mhi = max(t - 1, 0), min(t, nblk - 1)
                    nq = (mhi - mlo + 1) * 128
                    qs = r + dil * 128 * mlo
                    qsl = slice(qs, qs + dil * (nq - 1) + 1, dil)
                    boff = 0 if t >= 1 else 128
                    return ksl, qsl, nq, boff

                pairs = list(range(4)) if lvl >= 4 else []
                if pairs:
                    load_pair(0)
                    load_v(0)
                for hp in pairs:
                    hb = hp % 2
                    if hp + 1 < 4:
                        load_pair(hp + 1)
                    S.dma("dma_start", dict(out=gaT[hb][:], in_=SC["GA"][hp * 128:(hp + 1) * 128, 0:sq]), reads=[SCB["GA"]], writes=[bg[hb]])
                    S.op(POOL, "memset", dict(ap=acc[:], constant=0.0), writes=[bacc])
                    for t4 in range(0, nt, 8):
                        n4 = min(8, nt - t4)
                        o_t, o_b = opsr.next()
                        pv = o_t[:, 0, :].bitcast(BF16).rearrange("p (a b) -> p a b", b=128)
                        for j in range(n4):
                            ksl = tile_geom(tiles[t4 + j])[0]
                            S.op(PE, "transpose", dict(out=pv[:, j, :], in_=vaT[hb][:, ksl], identity=ident[:]), reads=[bv[hb], bconst], writes=[o_b])
                        eng = ACT if (t4 // 8) % 2 == 0 else DVE
                        if eng == ACT:
                            S.op(ACT, "activation", dict(out=vt_all[:, t4:t4 + n4, :, 0:64], in_=pv[:, 0:n4, :].rearrange("p a (h d) -> p a h d", h=2), func=AF.Copy), reads=[o_b], writes=[bvta])
                        else:
                            S.op(DVE, "tensor_copy", dict(out=vt_all[:, t4:t4 + n4, :, 0:64], in_=pv[:, 0:n4, :].rearrange("p a (h d) -> p a h d", h=2)), reads=[o_b], writes=[bvta])
                    for hl in range(2):
                        S.op(DVE, "tensor_copy", dict(out=vt_all[:, :, hl, 64], in_=am[:, 0:nt]), reads=[bconst], writes=[bvta])
                    if hp + 1 < 4:
                        load_v(hp + 1)
                    units = [list(range(u, min(u + 2, nt))) for u in range(0, nt, 2)]

                    def scores(unit):
                        sc_t, sc_b = scA.next()
                        geo = []
                        for j, ti in enumerate(unit):
                            ksl, qsl, nq, boff = tile_geom(tiles[ti])
                            geo.append((ti, qsl, nq))
                            for hl in range(2):
                                S.op(PE, "matmul", dict(out=sc_t[:, hl, j * 256:j * 256 + nq], lhsT=kaT[hb][:, ksl], rhs=qz[hb][hl][:, qsl], start=True, stop=False), reads=[bk[hb], bq[hb]], writes=[sc_b])
                                S.op(PE, "matmul", dict(out=sc_t[:, hl, j * 256:j * 256 + nq], lhsT=ident[:], rhs=bandneg[:, boff:boff + nq], start=False, stop=True), reads=[bconst], writes=[sc_b])
                        p_t, p_b = Pt.next()
                        if len(unit) == 2 and all(g_[2] == 256 for g_ in geo):
                            S.op(ACT, "activation", dict(out=p_t[:], in_=sc_t[:], func=AF.Exp, scale=0.125), reads=[sc_b], writes=[p_b])
                        else:
                            for j, (ti, qsl, nq) in enumerate(geo):
                                S.op(ACT, "activation", dict(out=p_t[:, :, j * 256:j * 256 + nq], in_=sc_t[:, :, j * 256:j * 256 + nq], func=AF.Exp, scale=0.125), reads=[sc_b], writes=[p_b])
                        return geo, p_t, p_b

                    def av(geo, p_t, p_b):
                        o_t, o_b = opsr.next()
                        for j, (ti, qsl, nq) in enumerate(geo):
                            for hl in range(2):
                                for bi in range(nq // 128):
                                    c = j * 256 + bi * 128
                                    S.op(PE, "matmul", dict(out=o_t[0:65, hl, c:c + 128], lhsT=vt_all[:, ti, hl, :], rhs=p_t[:, hl, c:c + 128], start=True, stop=True), reads=[bvta, p_b], writes=[o_b])
                        for j, (ti, qsl, nq) in enumerate(geo):
                            S.op(DVE, "tensor_tensor", dict(out=acc[:, :, qsl], in0=o_t[0:65, :, j * 256:j * 256 + nq], in1=acc[:, :, qsl], op=ALU.add), reads=[o_b, bacc], writes=[bacc])

                    inflight = {0: scores(units[0])}
                    for u in range(len(units)):
                        if u + 1 < len(units):
                            inflight[u + 1] = scores(units[u + 1])
                        if u >= 1:
                            av(*inflight.pop(u - 1))
                    av(*inflight.pop(len(units) - 1))
                    for qc in range(sq // 512):
                        q0 = qc * 512
                        d_t, d_b = opsr.next()
                        dv = d_t[:, 0, :]
                        nv = d_t[:, 1, :]
                        for h2 in range(2):
                            qa_, qb_ = q0 + h2 * 256, q0 + (h2 + 1) * 256
                            for hl in range(2):
                                S.op(PE, "matmul", dict(out=dv[:, h2 * 256:(h2 + 1) * 256], lhsT=sels[:, hl, :], rhs=acc[:, hl, qa_:qb_], start=(hl == 0), stop=(hl == 1)), reads=[bacc, bconst], writes=[d_b])
                        for h2 in range(2):
                            qa_, qb_ = q0 + h2 * 256, q0 + (h2 + 1) * 256
                            for hl in range(2):
                                S.op(PE, "matmul", dict(out=nv[:, h2 * 256:(h2 + 1) * 256], lhsT=sels[:, 2 + hl, :], rhs=acc[:, hl, qa_:qb_], start=(hl == 0), stop=(hl == 1)), reads=[bacc, bconst], writes=[d_b])
                        rc_t, rc_b = recr.next()
                        S.op(DVE, "reciprocal", dict(out=rc_t[:], in_=dv), reads=[d_b], writes=[rc_b])
                        th_t, th_b = thr.next()
                        S.op(ACT, "activation", dict(out=th_t[:], in_=gaT[hb][:, q0:q0 + 512], func=AF.Tanh, scale=0.5), reads=[bg[hb]], writes=[th_b])
                        sg_t, sg_b = sgr.next()
                        S.op(DVE, "scalar_tensor_tensor", dict(out=sg_t[:], in0=th_t[:], scalar=1.0, in1=gaT[hb][:, q0:q0 + 512], op0=ALU.add, op1=ALU.mult), reads=[th_b, bg[hb]], writes=[sg_b])
                        tn_t, tn_b = tnr.next()
                        S.op(DVE, "tensor_tensor", dict(out=tn_t[:], in0=nv, in1=rc_t[:], op=ALU.mult), reads=[d_b, rc_b], writes=[tn_b])
                        y_t, y_b = yTr.next()
                        S.op(DVE, "tensor_tensor", dict(out=y_t[:], in0=tn_t[:], in1=sg_t[:], op=ALU.mult), reads=[tn_b, sg_b], writes=[y_b])
                        S.dma("dma_start", dict(out=SC["YM"][hp * 128:(hp + 1) * 128, q0:q0 + 512], in_=y_t[:]), reads=[y_b], writes=[SCB["YM"]])
                S.barrier()

            with contextlib.ExitStack() as p3:
                ymr = Ring([sb(p3, nm("ym"), [128, 8, 512], BF16) for _ in range(2)])
                xr = Ring([sb(p3, nm("x3"), [128, D], F32) for _ in range(8)])
                tmr = Ring([sb(p3, nm("tm3"), [128, D], F32) for _ in range(3)])
                outr = Ring([sb(p3, nm("o3"), [128, D], F32) for _ in range(4)])
                st3 = Ring([sb(p3, nm("st3"), [128, 6], F32) for _ in range(8)])
                junk3 = sb(p3, nm("junk3"), [128, 512], F32)
                bj3 = Buf()
                po = Ring([ps(p3, nm("po"), [128, 2, 512], F32) for _ in range(4)])
                nq3 = (sq // 512) if lvl >= 5 else 0

                def p3_loads(qc):
                    q0 = qc * 512
                    ym_t, ym_b = ymr.next()
                    S.dma("dma_start", dict(out=ym_t[:], in_=SC["YM"][:, q0:q0 + 512].rearrange("(kc p) t -> p kc t", p=128)), reads=[SCB["YM"]], writes=[ym_b])
                    xs = []
                    for i in range(4):
                        row = q0 + i * 128
                        x_t, x_b = xr.next()
                        S.dma("dma_start", dict(out=x_t[:], in_=X[job["xrow"] + row:job["xrow"] + row + 128, :]), writes=[x_b])
                        xs.append((x_t, x_b))
                    return ym_t, ym_b, xs

                ld3 = {}
                if nq3:
                    ld3[0] = p3_loads(0)
                for qc in range(nq3):
                    q0 = qc * 512
                    ym_t, ym_b, xs = ld3.pop(qc)
                    if qc + 1 < nq3:
                        ld3[qc + 1] = p3_loads(qc + 1)
                    for i in range(4):
                        row = q0 + i * 128
                        x_t, x_b = xs[i]
                        po_t, po_b = po.next()
                        for hf in range(2):
                            for kc in range(8):
                                S.op(PE, "matmul", dict(out=po_t[:, hf, :], lhsT=ym_t[:, kc, i * 128:(i + 1) * 128], rhs=wout[:, kc, hf * 512:(hf + 1) * 512], start=(kc == 0), stop=(kc == 7)), reads=[ym_b, bwout], writes=[po_b])
                        s_t, s_b = st3.next()
                        for hf in range(2):
                            S.op(ACT, "activation", dict(out=junk3[:], in_=po_t[:, hf, :], func=AF.Square, accum_out=s_t[:, hf:hf + 1]), reads=[po_b], writes=[bj3, s_b])
                        S.op(DVE, "tensor_tensor", dict(out=s_t[:, 2:3], in0=s_t[:, 0:1], in1=s_t[:, 1:2], op=ALU.add), reads=[s_b], writes=[s_b])
                        S.op(DVE, "tensor_scalar", dict(out=s_t[:, 3:4], in0=s_t[:, 2:3], scalar1=1.0 / D, scalar2=EPS, op0=ALU.mult, op1=ALU.add), reads=[s_b], writes=[s_b])
                        S.op(POOL, "tensor_tensor", dict(out=s_t[:, 4:5], in0=s_t[:, 3:4], in1=mhalf[:], op=ALU.pow), reads=[s_b, bconst], writes=[s_b])
                        tm_t, tm_b = tmr.next()
                        for hf in range(2):
                            S.op(DVE, "scalar_tensor_tensor", dict(out=tm_t[:, hf * 512:(hf + 1) * 512], in0=po_t[:, hf, :], scalar=s_t[:, 4:5], in1=gg[s][:, hf * 512:(hf + 1) * 512], op0=ALU.mult, op1=ALU.mult), reads=[po_b, s_b, bgg[s]], writes=[tm_b])
                        o_t, o_b = outr.next()
                        S.op(DVE, "tensor_tensor", dict(out=o_t[:], in0=tm_t[:], in1=x_t[:], op=ALU.add), reads=[tm_b, x_b], writes=[o_b])
                        yrow = job["yrow"] + row
                        out_dmas.append(S.dma("dma_start", dict(out=Y[yrow:yrow + 128, :], in_=o_t[:]), reads=[o_b], writes=[Buf()]))
                S.barrier()

        S.run(final_wait=out_dmas)
    return nc


_NC_CACHE = {}


def _rope_tables(pos):
    half = 8
    inv = (np.float32(500000.0) ** (-np.arange(half, dtype=np.float32) / np.float32(half))).astype(np.float32)
    ang = pos.astype(np.float32)[None, :] * inv[:, None]
    cos = np.cos(ang).astype(np.float32)
    sin = np.sin(ang).astype(np.float32)
    idx = np.arange(128) % 8
    C = cos[idx]
    Sg = sin[idx].copy()
    Sg[:64] *= -1.0
    return np.ascontiguousarray(C), np.ascontiguousarray(Sg)


def _amask(valid_ext, sq):
    tiles = amask_tiles(sq)
    m = np.zeros((128, len(tiles)), np.float32)
    p = np.arange(128)
    for ti, (pi, dil, r, t, nblk, je0) in enumerate(tiles):
        e = r + dil * (je0 + 128 * t + p)
        m[:, ti] = valid_ext[e]
    return m


def prep_inputs(x_prompt, x_sample, c_prompt, c_sample, w_in, w_out, g_pre, g_post,
                w_ada, b_ada, lam_q1, lam_k1, lam_q2, lam_k2, g_sub):
    f32 = np.float32
    x_prompt = np.asarray(x_prompt, f32)
    x_sample = np.asarray(x_sample, f32)
    c_prompt = np.asarray(c_prompt, f32)
    c_sample = np.asarray(c_sample, f32)
    w_in0 = np.ascontiguousarray(np.asarray(w_in, f32)[0])
    w_out0 = np.ascontiguousarray(np.asarray(w_out, f32)[0])
    w_ada0 = np.ascontiguousarray(np.asarray(w_ada, f32)[0])
    b_ada0 = np.asarray(b_ada, f32)[0]
    g_pre0 = np.asarray(g_pre, f32)[0]
    g_post0 = np.asarray(g_post, f32)[0]
    g_sub0 = np.asarray(g_sub, f32)[0]

    offs = dict(qa=0, ka=512, qb=2048, kb=2560)
    w_rot = np.zeros((D, 4, 192), f32)
    for ti, nme in enumerate(("qa", "ka", "qb", "kb")):
        x1 = np.array([offs[nme] + h * 64 + i for h in range(8) for i in range(8)])
        x2 = x1 + 8
        w_rot[:, ti, 0:64] = w_in0[:, x1]
        w_rot[:, ti, 64:128] = w_in0[:, x2]
        w_rot[:, ti, 128:192] = w_in0[:, x1]
    w_rot = np.ascontiguousarray(w_rot.reshape(D, 768))

    g_pre_t = np.ascontiguousarray(g_pre0.reshape(8, 128).T)
    b_ada_t = np.ascontiguousarray(b_ada0[:2048].reshape(16, 128).T)
    b_gate = np.ascontiguousarray(b_ada0[2048:3072].reshape(1, D))
    g_post_r = np.ascontiguousarray(g_post0.reshape(1, D))
    g_sub_t = np.ascontiguousarray(g_sub0.reshape(128, 1))
    lam_v = np.ascontiguousarray(np.stack([np.asarray(v, f32)[0] for v in (lam_q1, lam_k1, lam_q2, lam_k2)], 0))
    ident = np.eye(128, dtype=f32)
    p = np.arange(128)[:, None]
    f = np.arange(128)[None, :]
    band = np.concatenate([(p <= f), (p >= f)], axis=1).astype(f32)
    band = np.ascontiguousarray(np.concatenate([band, band], axis=1))
    sels = np.zeros((65, 4, 128), f32)
    sels[64, 0, 0:64] = 2.0
    sels[64, 1, 64:128] = 2.0
    sels[np.arange(64), 2, np.arange(64)] = 1.0
    sels[np.arange(64), 3, 64 + np.arange(64)] = 1.0
    sels = np.ascontiguousarray(sels.reshape(65, 512))
    valid1 = np.zeros(DSEQ + 2 * PADK, f32)
    valid1[PADK:PADK + DSEQ] = 1.0
    am1 = _amask(valid1, DSEQ)

    in_maps = []
    for c in range(8):
        psq, hf = c // 2, c % 2
        q0 = hf * HALF
        o0 = (1 - hf) * HALF
        xa = np.zeros((NROWS, D), f32)
        pos = np.zeros(NROWS, np.int64)
        xa[ROW_OWN:ROW_OWN + HALF] = x_prompt[psq, q0:q0 + HALF]
        pos[ROW_OWN:ROW_OWN + HALF] = np.arange(q0, q0 + HALF)
        xa[ROW_OTHER:ROW_OTHER + HALF] = x_prompt[psq, o0:o0 + HALF]
        pos[ROW_OTHER:ROW_OTHER + HALF] = np.arange(o0, o0 + HALF)
        hpos = np.concatenate([np.arange(q0 - PADK, q0), np.arange(q0 + HALF, q0 + HALF + PADK)])
        hval = (hpos >= 0) & (hpos < SEQ)
        xa[ROW_HALO:ROW_HALO + 2 * PADK][hval] = x_prompt[psq, hpos[hval]]
        pos[ROW_HALO:ROW_HALO + 2 * PADK] = np.clip(hpos, 0, SEQ - 1)
        xa[ROW_S0:ROW_S0 + DSEQ] = x_sample[2 * c]
        pos[ROW_S0:ROW_S0 + DSEQ] = np.arange(DSEQ)
        xa[ROW_S1:ROW_S1 + DSEQ] = x_sample[2 * c + 1]
        pos[ROW_S1:ROW_S1 + DSEQ] = np.arange(DSEQ)
        C, Sg = _rope_tables(pos)
        cs = np.stack([c_prompt[psq], c_sample[2 * c], c_sample[2 * c + 1]], axis=-1)
        c_t = np.ascontiguousarray(cs.reshape(8, 128, 3).transpose(1, 0, 2).reshape(128, 24))
        valid0 = np.ones(HALF + 2 * PADK, f32)
        valid0[0:PADK] = hval[:PADK]
        valid0[PADK + HALF:] = hval[PADK:]
        am0 = _amask(valid0, HALF)
        vrow = np.ascontiguousarray(np.broadcast_to(hval.astype(f32)[None, :], (128, 2 * PADK)))
        in_maps.append({
            "x_all": xa, "rope_c": C, "rope_s": Sg, "c_t": c_t, "w_in": w_in0, "w_rot": w_rot,
            "w_out": w_out0, "w_ada": w_ada0, "g_pre_t": g_pre_t, "b_ada_t": b_ada_t, "b_gate": b_gate,
            "g_post": g_post_r, "g_sub_t": g_sub_t, "lam_v": lam_v, "ident": ident, "band": band,
            "sels": sels, "amask0": am0, "amask1": am1, "vrow": vrow,
        })

    return in_maps


def kernel(**inputs):
    f32 = np.float32
    in_maps = prep_inputs(**inputs)
    if "nc" not in _NC_CACHE:
        _NC_CACHE["nc"] = build_program()
    nc = _NC_CACHE["nc"]
    res = run_bass_kernel_spmd(nc, in_maps, core_ids=list(range(8)))
    y_prompt = np.zeros((4, SEQ, D), f32)
    y_sample = np.zeros((16, DSEQ, D), f32)
    for c in range(8):
        y = np.asarray(res.results[c]["y_all"], f32)
        psq, hf = c // 2, c % 2
        y_prompt[psq, hf * HALF:(hf + 1) * HALF] = y[0:HALF]
        y_sample[2 * c] = y[HALF:HALF + DSEQ]
        y_sample[2 * c + 1] = y[HALF + DSEQ:HALF + 2 * DSEQ]
    return (y_prompt, y_sample)
```
